# Optimizing a Trainium2 kernel written in Bass

```python
import jax, jax.numpy as jnp
from jax import lax
import numpy as np

D_MODEL = 1024
BATCH = 8
SEQ = 2048
DEPTH = 2
DEC_BATCH = 128
DEC_SEQ = 1
PAST_LEN = 8192
PAGE_SIZE = 128

N_PAIRS = DEPTH // 2
HEAD_DIM = 64
ROT_DIM = HEAD_DIM // 4
ROPE_THETA = 500000.0
D_FF = 4 * D_MODEL
EPS = 1e-6
BLOCK = 128

CONV_CH = D_MODEL // 2
CONV_WIDTH = 31
SC_CH = D_MODEL // 2
SC_WIDTH = 3
IN_COLS_CONV = 2 * CONV_CH + 3 * SC_CH
MIX_COLS_CONV = CONV_CH + SC_CH

SWA_HEADS = 8
SWA_KV = 2
SWA_WINDOW = 128
DIL_HEADS = 4
DIL_KV = 1
DIL_PATTERNS = ((128, 1), (512, 4), (2048, 16))
IN_COLS_ATTN = (SWA_HEADS + 2 * SWA_KV) * HEAD_DIM + len(DIL_PATTERNS) * (DIL_HEADS + 2 * DIL_KV) * HEAD_DIM
MIX_COLS_ATTN = (SWA_HEADS + DIL_HEADS) * HEAD_DIM

kernel_name = 'hybrid_conv_swa_dilated_decode_step'


def rmsnorm(x, g):
    xf = x.astype(jnp.float32)
    y = xf * lax.rsqrt(jnp.mean(xf * xf, -1, keepdims=True) + EPS)
    return (y * g.astype(jnp.float32)).astype(x.dtype)


def layernorm(x, g, b):
    xf = x.astype(jnp.float32)
    mu = jnp.mean(xf, -1, keepdims=True)
    var = jnp.mean(jnp.square(xf - mu), -1, keepdims=True)
    y = (xf - mu) * lax.rsqrt(var + EPS) * g.astype(jnp.float32) + b.astype(jnp.float32)
    return y.astype(x.dtype)


def split_cols(u, sizes):
    out, off = [], 0
    for s in sizes:
        out.append(u[..., off:off + s])
        off += s
    return out


def rope(x, pos):
    half = ROT_DIM // 2
    inv = ROPE_THETA ** (-jnp.arange(half, dtype=jnp.float32) / half)
    ang = pos.astype(jnp.float32)[:, None] * inv[None, :]
    cos, sin = jnp.cos(ang)[:, None, :], jnp.sin(ang)[:, None, :]
    xf = x.astype(jnp.float32)
    x1, x2 = xf[..., :half], xf[..., half:ROT_DIM]
    y = jnp.concatenate([x1 * cos - x2 * sin, x2 * cos + x1 * sin, xf[..., ROT_DIM:]], -1)
    return y.astype(x.dtype)


def dwconv_valid(x, w):
    return lax.conv_general_dilated(x, w.astype(x.dtype)[:, None, :], window_strides=(1,), padding='VALID',
                                    dimension_numbers=('NWC', 'WIO', 'NWC'), feature_group_count=x.shape[-1])


def sqrelu_mlp(h, w1, w2):
    return jnp.square(jax.nn.relu(h @ w1)) @ w2


def attend(q, k, v, mask, sink=None):
    s = jnp.einsum('...qhgd,...khd->...hgqk', q.astype(jnp.float32), k.astype(jnp.float32)) * (HEAD_DIM ** -0.5)
    s = jnp.where(mask, s, -jnp.inf)
    m = jnp.max(s, -1)
    if sink is not None:
        sk = sink.astype(jnp.float32).reshape(q.shape[-3], q.shape[-2], 1)
        m = jnp.maximum(m, sk)
    p = jnp.exp(s - m[..., None])
    den = jnp.sum(p, -1)
    if sink is not None:
        den = den + jnp.exp(sk - m)
    o = jnp.einsum('...hgqk,...khd->...qhgd', p, v.astype(jnp.float32))
    o = o / jnp.moveaxis(den, -1, -3)[..., None]
    lse = jnp.moveaxis(m + jnp.log(den), -1, -3)
    return o.astype(q.dtype), lse


def banded_attend(q, k, v, window, sink=None):
    b, n = q.shape[:2]
    blk = min(BLOCK, n)
    nb = -(-n // blk)
    npad = nb * blk - n
    nprev = -(-window // blk)
    qb = jnp.pad(q, [(0, 0), (0, npad)] + [(0, 0)] * (q.ndim - 2)).reshape((b, nb, blk) + q.shape[2:])

    def windows(t):
        tp = jnp.pad(t, [(0, 0), (nprev * blk, npad)] + [(0, 0)] * (t.ndim - 2))
        return jnp.concatenate([tp[:, j * blk: j * blk + nb * blk].reshape((b, nb, blk) + t.shape[2:])
                                for j in range(nprev + 1)], axis=2)

    kw, vw = windows(k), windows(v)
    qi = jnp.arange(nb)[:, None] * blk + jnp.arange(blk)[None, :]
    ki = (jnp.arange(nb)[:, None] - nprev) * blk + jnp.arange((nprev + 1) * blk)[None, :]
    dist = qi[:, :, None] - ki[:, None, :]
    mask = (ki[:, None, :] >= 0) & (dist >= 0) & (dist <= window)
    o, lse = attend(qb, kw, vw, mask[:, None, None], sink)
    o = o.reshape((b, nb * blk) + q.shape[2:])[:, :n]
    lse = lse.reshape((b, nb * blk) + q.shape[2:4])[:, :n]
    return o, lse


def dilated_prompt(q, k, v, window, dil):
    b, s = q.shape[:2]

    def split(t):
        return jnp.moveaxis(t.reshape((b, s // dil, dil) + t.shape[2:]), 2, 1).reshape((b * dil, s // dil) + t.shape[2:])

    def merge(t):
        return jnp.moveaxis(t.reshape((b, dil, s // dil) + t.shape[2:]), 1, 2).reshape((b, s) + t.shape[2:])

    o, lse = banded_attend(split(q), split(k), split(v), window // dil)
    return merge(o), merge(lse)


def dilated_sample(q, kc, vc, window, dil, buf_len):
    ds = q.shape[1]
    steps = jnp.arange(window // dil + 1)
    idx = buf_len + jnp.arange(ds)[:, None] - steps[None, :] * dil
    valid = idx >= 0
    idx = jnp.maximum(idx, 0)
    kg = jnp.take(kc, idx, axis=1)
    vg = jnp.take(vc, idx, axis=1)
    o, lse = attend(q[:, :, None], kg, vg, valid[:, None, None, None, :])
    return o[:, :, 0], lse[:, :, 0]


def window_sample(q, kc, vc, window, sink):
    ds = q.shape[1]
    kpos = jnp.arange(kc.shape[1])
    qpos = kc.shape[1] - ds + jnp.arange(ds)
    dist = qpos[:, None] - kpos[None, :]
    mask = (dist >= 0) & (dist <= window)
    return attend(q, kc, vc, mask, sink)


def conv_mixers(h, buf_a, buf_b, w_in, a_w, a_b, a_ln_g, a_ln_b, b_w, w_out):
    a_val, a_gate, b_x, b_gb, b_gc = split_cols(h @ w_in, [CONV_CH, CONV_CH, SC_CH, SC_CH, SC_CH])
    ga = jnp.concatenate([buf_a, a_val * jax.nn.sigmoid(a_gate)], 1)
    ya = jax.nn.silu(layernorm(dwconv_valid(ga, a_w) + a_b, a_ln_g, a_ln_b))
    zb = jnp.concatenate([buf_b, b_gc * b_x], 1)
    yb = b_gb * dwconv_valid(zb, b_w)
    y = jnp.concatenate([ya, yb], -1) @ w_out
    return y, ga[:, -(CONV_WIDTH - 1):], zb[:, -(SC_WIDTH - 1):]


def attn_qkv(h, pos, w_in):
    b, t, _ = h.shape
    sizes = [SWA_HEADS * HEAD_DIM, SWA_KV * HEAD_DIM, SWA_KV * HEAD_DIM] + \
            [DIL_HEADS * HEAD_DIM, DIL_KV * HEAD_DIM, DIL_KV * HEAD_DIM] * len(DIL_PATTERNS)
    parts = split_cols(h @ w_in, sizes)

    def qkv(pq, pk, pv, nq, nkv):
        q = rope(pq.reshape(b, t, nq, HEAD_DIM), pos).reshape(b, t, nkv, nq // nkv, HEAD_DIM)
        k = rope(pk.reshape(b, t, nkv, HEAD_DIM), pos)
        v = pv.reshape(b, t, nkv, HEAD_DIM)
        return q, k, v

    swa = qkv(parts[0], parts[1], parts[2], SWA_HEADS, SWA_KV)
    dil = [qkv(parts[3 + 3 * i], parts[4 + 3 * i], parts[5 + 3 * i], DIL_HEADS, DIL_KV) for i in range(len(DIL_PATTERNS))]
    return swa, dil


def attn_merge(o_swa, dil_outs, w_out):
    b, t = o_swa.shape[:2]
    wts = jax.nn.softmax(jnp.stack([l for _, l in dil_outs]), axis=0)[..., None]
    o_dil = jnp.sum(wts * jnp.stack([o.astype(jnp.float32) for o, _ in dil_outs]), axis=0)
    mix = jnp.concatenate([o_swa.reshape(b, t, -1), o_dil.astype(o_swa.dtype).reshape(b, t, -1)], -1)
    return mix @ w_out


def attn_prompt(h, w_in, sinks, w_out):
    t = h.shape[1]
    (q, k, v), dil = attn_qkv(h, jnp.arange(t), w_in)
    o_swa, _ = banded_attend(q, k, v, SWA_WINDOW, sinks)
    dil_outs = [dilated_prompt(dq, dk, dv, w, d) for (dq, dk, dv), (w, d) in zip(dil, DIL_PATTERNS)]
    y = attn_merge(o_swa, dil_outs, w_out)
    swa_state = jnp.stack([k, v], 2)[:, -min(SWA_WINDOW, t):]
    dil_states = [jnp.stack([dk, dv], 2)[:, -min(w, t):] for (_, dk, dv), (w, _) in zip(dil, DIL_PATTERNS)]
    return y, swa_state, dil_states


def attn_sample(h, swa_cache, dil_caches, w_in, sinks, w_out):
    ds = h.shape[1]
    (q, k, v), dil = attn_qkv(h, PAST_LEN + jnp.arange(ds), w_in)
    cat = jnp.concatenate([swa_cache, jnp.stack([k, v], 2)], 1)
    o_swa, _ = window_sample(q, cat[:, :, 0], cat[:, :, 1], SWA_WINDOW, sinks)
    swa_state = cat[:, -min(SWA_WINDOW, PAST_LEN + ds):]
    dil_outs, dil_states = [], []
    for (dq, dk, dv), (w, d), c in zip(dil, DIL_PATTERNS, dil_caches):
        cd = jnp.concatenate([c, jnp.stack([dk, dv], 2)], 1)
        dil_outs.append(dilated_sample(dq, cd[:, :, 0], cd[:, :, 1], w, d, c.shape[1]))
        dil_states.append(cd[:, -min(w, PAST_LEN + ds):])
    y = attn_merge(o_swa, dil_outs, w_out)
    return y, swa_state, dil_states


def setup_inputs(seed: int = 0) -> dict:
    key = jax.random.key(seed)
    keys = list(jax.random.split(key, 24))

    def nrm(shape, scale):
        return jax.random.normal(keys.pop(), shape, jnp.float32) * scale

    l_swa = min(SWA_WINDOW, PAST_LEN)
    l_dil = [min(w, PAST_LEN) for w, _ in DIL_PATTERNS]
    return {
        'x_prompt': nrm((BATCH, SEQ, D_MODEL), 1.0),
        'x_sample': nrm((DEC_BATCH, DEC_SEQ, D_MODEL), 1.0),
        'state_conv_a': nrm((N_PAIRS, DEC_BATCH, CONV_WIDTH - 1, CONV_CH), 0.5),
        'state_conv_b': nrm((N_PAIRS, DEC_BATCH, SC_WIDTH - 1, SC_CH), 0.5),
        'cache_swa_kv': nrm((N_PAIRS, DEC_BATCH, l_swa, 2, SWA_KV, HEAD_DIM), 1.0),
        'cache_dil0_kv': nrm((N_PAIRS, DEC_BATCH, l_dil[0], 2, DIL_KV, HEAD_DIM), 1.0),
        'cache_dil1_kv': nrm((N_PAIRS, DEC_BATCH, l_dil[1], 2, DIL_KV, HEAD_DIM), 1.0),
        'cache_dil2_kv': nrm((N_PAIRS, DEC_BATCH, l_dil[2], 2, DIL_KV, HEAD_DIM), 1.0),
        'norm_g': 1.0 + nrm((DEPTH, 4, D_MODEL), 0.05),
        'w_in_conv': nrm((N_PAIRS, D_MODEL, IN_COLS_CONV), D_MODEL ** -0.5),
        'conv_a_w': nrm((N_PAIRS, CONV_WIDTH, CONV_CH), CONV_WIDTH ** -0.5),
        'conv_a_b': nrm((N_PAIRS, CONV_CH), 0.02),
        'conv_a_ln_g': 1.0 + nrm((N_PAIRS, CONV_CH), 0.05),
        'conv_a_ln_b': nrm((N_PAIRS, CONV_CH), 0.02),
        'conv_b_w': nrm((N_PAIRS, SC_WIDTH, SC_CH), SC_WIDTH ** -0.5),
        'w_out_conv': nrm((N_PAIRS, MIX_COLS_CONV, D_MODEL), MIX_COLS_CONV ** -0.5),
        'w_in_attn': nrm((N_PAIRS, D_MODEL, IN_COLS_ATTN), D_MODEL ** -0.5),
        'attn_sinks': nrm((N_PAIRS, SWA_HEADS), 0.5),
        'w_out_attn': nrm((N_PAIRS, MIX_COLS_ATTN, D_MODEL), MIX_COLS_ATTN ** -0.5),
        'mlp_w1': nrm((DEPTH, D_MODEL, D_FF), D_MODEL ** -0.5),
        'mlp_w2': nrm((DEPTH, D_FF, D_MODEL), D_FF ** -0.5),
    }


def reference(x_prompt, x_sample, state_conv_a, state_conv_b, cache_swa_kv, cache_dil0_kv, cache_dil1_kv,
              cache_dil2_kv, norm_g, w_in_conv, conv_a_w, conv_a_b, conv_a_ln_g, conv_a_ln_b, conv_b_w,
              w_out_conv, w_in_attn, attn_sinks, w_out_attn, mlp_w1, mlp_w2):
    names = ('a_p', 'a_s', 'b_p', 'b_s', 'swa_p', 'swa_s', 'd0_p', 'd0_s', 'd1_p', 'd1_s', 'd2_p', 'd2_s')
    st = {n: [] for n in names}
    hp, hs = x_prompt, x_sample
    for layer in range(DEPTH):
        p = layer // 2
        g = norm_g[layer]
        up, us = rmsnorm(hp, g[0]), rmsnorm(hs, g[0])
        if layer % 2 == 0:
            cw = (w_in_conv[p], conv_a_w[p], conv_a_b[p], conv_a_ln_g[p], conv_a_ln_b[p], conv_b_w[p], w_out_conv[p])
            za = jnp.zeros((up.shape[0], CONV_WIDTH - 1, CONV_CH), up.dtype)
            zb = jnp.zeros((up.shape[0], SC_WIDTH - 1, SC_CH), up.dtype)
            yp, a_p, b_p = conv_mixers(up, za, zb, *cw)
            ys, a_s, b_s = conv_mixers(us, state_conv_a[p], state_conv_b[p], *cw)
            st['a_p'].append(a_p); st['a_s'].append(a_s)
            st['b_p'].append(b_p); st['b_s'].append(b_s)
        else:
            yp, swa_p, dil_p = attn_prompt(up, w_in_attn[p], attn_sinks[p], w_out_attn[p])
            ys, swa_s, dil_s = attn_sample(us, cache_swa_kv[p], (cache_dil0_kv[p], cache_dil1_kv[p], cache_dil2_kv[p]),
                                           w_in_attn[p], attn_sinks[p], w_out_attn[p])
            st['swa_p'].append(swa_p); st['swa_s'].append(swa_s)
            for i in range(len(DIL_PATTERNS)):
                st['d%d_p' % i].append(dil_p[i]); st['d%d_s' % i].append(dil_s[i])
        hp = hp + rmsnorm(yp, g[1])
        hs = hs + rmsnorm(ys, g[1])
        hp = hp + rmsnorm(sqrelu_mlp(rmsnorm(hp, g[2]), mlp_w1[layer], mlp_w2[layer]), g[3])
        hs = hs + rmsnorm(sqrelu_mlp(rmsnorm(hs, g[2]), mlp_w1[layer], mlp_w2[layer]), g[3])
    return (hp, hs,
            jnp.stack(st['a_p']), jnp.stack(st['a_s']),
            jnp.stack(st['b_p']), jnp.stack(st['b_s']),
            jnp.stack(st['swa_p']), jnp.stack(st['swa_s']),
            jnp.stack(st['d0_p']), jnp.stack(st['d0_s']),
            jnp.stack(st['d1_p']), jnp.stack(st['d1_s']),
            jnp.stack(st['d2_p']), jnp.stack(st['d2_s']))
```

```python
import os
import numpy as np
from contextlib import ExitStack
import concourse.bass as bass
import concourse.mybir as mybir
from concourse.bass_utils import run_bass_kernel_spmd

F32 = mybir.dt.float32
BF16 = mybir.dt.bfloat16
ALU = mybir.AluOpType
AF = mybir.ActivationFunctionType
AX = mybir.AxisListType

ENGS = ("pe", "act", "dve", "pool", "sp")

NTOK = 2048
NS = 16
NT = NTOK + NS
TBS = [(0, 512), (512, 512), (1024, 512), (1536, 512), (2048, 16)]
HALVES = [(0, [0, 1]), (1024, [2, 3, 4])]
HW_ = 1040
EPS = 1e-6
STOP = int(os.environ.get("MK_STOP", "99"))
SAFE_WAR = int(os.environ.get("MK_SAFE_WAR", "1"))
M4A = int(os.environ.get("MK_4A", "63"))
POOL_ADD = int(os.environ.get("MK_POOL_ADD", "0"))
L1STOP = int(os.environ.get("MK_L1STOP", "99"))
NOCOPY = int(os.environ.get("MK_NOCOPY", "0"))


class H:
    __slots__ = ("name", "w", "r", "excl")

    def __init__(self, name="", excl=False):
        self.name = name
        self.w = None
        self.r = []
        self.excl = excl


class HD(dict):
    def __missing__(self, k):
        v = H(str(k))
        self[k] = v
        return v


class Op:
    __slots__ = ("eng", "fn", "deps", "signal", "tick", "dma", "sem", "val", "fenced")


class Prog:
    def __init__(self, nc, ndma=8):
        self.nc = nc
        self.ops = {e: [] for e in ENGS}
        self.all = []
        self.ndma = ndma

    def op(self, eng, fn, reads=(), writes=(), dma=False):
        o = Op()
        o.eng, o.fn, o.dma, o.signal, o.tick, o.sem, o.val, o.fenced = eng, fn, dma, dma, 0, None, 0, False
        deps = []
        if any(h.excl for h in reads):
            writes = list(writes) + [h for h in reads if h.excl and h not in writes]
            reads = [h for h in reads if not h.excl]

        def add(p, kind):
            if p is None or p is o:
                return
            if p.eng == eng and not p.dma and not dma:
                if eng == "pe" or (kind == "WAR" and not SAFE_WAR):
                    return
            if p not in deps:
                deps.append(p)

        for h in reads:
            add(h.w, "RAW")
        for h in writes:
            add(h.w, "WAW")
            for r in h.r:
                add(r, "WAR")
        for h in reads:
            h.r.append(o)
        for h in writes:
            h.w = o
            h.r = []
        o.deps = deps
        self.ops[eng].append(o)
        self.all.append(o)
        return o

    def fence(self):
        lasts = [self.ops[e][-1] for e in ENGS if self.ops[e] and self.ops[e][-1].fn is not None]
        dmas = [o for o in self.all if o.dma and not o.fenced]
        for o in dmas:
            o.fenced = True
        for e in ENGS:
            o = Op()
            o.eng, o.fn, o.dma, o.signal, o.tick, o.sem, o.val, o.fenced = e, None, False, False, 0, None, 0, True
            o.deps = [p for p in lasts if p.eng != e or p.dma] + [d for d in dmas if d not in lasts]
            self.ops[e].append(o)
            self.all.append(o)

    def emit(self, stack):
        nc = self.nc
        for o in self.all:
            for d in o.deps:
                d.signal = True
        esem = {e: stack.enter_context(nc.semaphore("es_" + e)) for e in ENGS}
        dsem = {e: [stack.enter_context(nc.semaphore("ds_%s_%d" % (e, i))) for i in range(self.ndma)]
                for e in ENGS if any(o.dma for o in self.ops[e])}
        for e in ENGS:
            c = 0
            nd = 0
            dmas = []
            for o in self.ops[e]:
                if o.dma:
                    o.sem = dsem[e][nd % self.ndma]
                    o.val = 16 * (nd // self.ndma + 1)
                    if nd >= self.ndma:
                        prev = dmas[nd - self.ndma]
                        if prev not in o.deps:
                            o.deps.append(prev)
                    dmas.append(o)
                    nd += 1
                elif o.signal:
                    c += 1
                    o.tick = c
        block = stack.enter_context(nc.Block())
        prog = self
        self.stats = {}

        def section(e):
            def body(eng):
                known = {}
                nwait = 0
                for o in prog.ops[e]:
                    for d in o.deps:
                        if d.dma:
                            sem, val = d.sem, d.val
                        else:
                            sem, val = esem[d.eng], d.tick
                        key = id(sem)
                        if known.get(key, 0) >= val:
                            continue
                        eng.wait_ge(sem, val)
                        nwait += 1
                        known[key] = val
                    if o.fn is None:
                        continue
                    ins = o.fn(eng)
                    if o.dma:
                        ins.then_inc(o.sem, 16)
                    elif o.signal:
                        ins.then_inc(esem[e], 1)
                last = {}
                for o in prog.ops[e]:
                    if o.dma:
                        last[id(o.sem)] = (o.sem, o.val)
                for sem, val in last.values():
                    if known.get(id(sem), 0) < val:
                        eng.wait_ge(sem, val)
                prog.stats[e] = (len(prog.ops[e]), nwait)
            return body

        block.tensor(section("pe"))
        block.scalar(section("act"))
        block.vector(section("dve"))
        block.gpsimd(section("pool"))
        block.sync(section("sp"))


class Arena:
    def __init__(self, t32, cap_bytes):
        self.t32 = t32
        self.t16 = t32.bitcast(BF16)
        self.cap = cap_bytes
        self.off = 0
        self.peak = 0

    def alloc(self, ncols, dtype):
        esz = 4 if dtype == F32 else 2
        off = (self.off + 31) // 32 * 32
        nb = ncols * esz
        assert off + nb <= self.cap, ("arena overflow", off, nb, self.cap)
        self.off = off + nb
        self.peak = max(self.peak, self.off)
        if dtype == F32:
            return self.t32[:, off // 4: off // 4 + ncols]
        return self.t16[:, off // 2: off // 2 + ncols]

    def mark(self):
        return self.off

    def reset(self, m=0):
        self.off = m


def bcast_ap(ap, dims):
    return bass.AP(ap.tensor, ap.offset, [list(ap.ap[0])] + [[s, n] for s, n in dims])


def build_program():
    nc = bass.Bass("TRN2", target_bir_lowering=False)

    def din(name, shape):
        return nc.dram_tensor(name, list(shape), F32, kind="ExternalInput").ap()

    def dout(name, shape):
        return nc.dram_tensor(name, list(shape), F32, kind="ExternalOutput").ap()

    xp = din("xp", [NTOK, 1024])
    xs = din("xs", [NS, 1024])
    sca = din("sca", [NS * 30, 512])
    scb = din("scb", [NS * 2, 512])
    c_swa = din("c_swa", [NS, 128, 256])
    c_d = [din("c_d0", [NS, 128, 128]), din("c_d1", [NS, 512, 128]), din("c_d2", [NS, 2048, 128])]
    prm1 = din("prm1", [88, 128])
    prm2 = din("prm2", [124, 128])
    sinks = din("sinks", [1, 8])
    cst32_d = din("cst32", [128, 512])
    cstb_d = din("cstb", [128, 768])
    cs_d = din("cs", [128, 2, NT])
    w_in_conv = din("w_in_conv", [1024, 2560])
    w_out_conv = din("w_out_conv", [1024, 1024])
    wq_d = din("wq", [1024, 1280])
    wk_d = din("wk", [1024, 384])
    wv_d = din("wv", [1024, 320])
    wvd_d = din("wvd", [1024, 640])
    w_out_attn = din("w_out_attn", [768, 1024])
    w1_d = din("w1", [2, 1024, 4096])
    w2_d = din("w2", [2, 4096, 1024])

    y_p = dout("y_p", [NTOK, 1024])
    y_s = dout("y_s", [NS, 1024])
    a_p = dout("a_p", [30, 512])
    a_s = dout("a_s", [NS, 30, 512])
    b_p = dout("b_p", [2, 512])
    b_s = dout("b_s", [NS, 2, 512])
    swa_p = dout("swa_p", [128, 256])
    swa_s = dout("swa_s", [NS, 128, 256])
    d_p = [dout("d0_p", [128, 128]), dout("d1_p", [512, 128]), dout("d2_p", [2048, 128])]
    d_s = [dout("d0_s", [NS, 128, 128]), dout("d1_s", [NS, 512, 128]), dout("d2_s", [NS, 2048, 128])]

    st = ExitStack()
    with st:
        P = Prog(nc)
        sb = lambda name, shape, dt: st.enter_context(nc.sbuf_tensor(name, list(shape), dt))
        hT = sb("hT", [128, 8, NT], F32)
        cst32 = sb("cst32s", [128, 512], F32)
        cstb = sb("cstbs", [128, 768], BF16)
        prmT = sb("prmT", [128, 88], F32)
        wAT = sb("wAT", [128, 124], F32)
        smalls = sb("smalls", [128, 32], F32)
        halo = sb("halo", [128, 8, 30], BF16)
        NSLOT = 2
        SLOT = 4096
        wslots = [sb("wslot%d" % i, [128, SLOT], BF16) for i in range(NSLOT)]
        ARENA_BYTES = (nc.sbuf_bytes_remaining // 64) * 64 - 256
        A = Arena(sb("arena", [128, ARENA_BYTES // 4], F32), ARENA_BYTES)
        psb = [st.enter_context(nc.psum_tensor("psb%d" % i, [128, 512], F32)) for i in range(8)]
        hps = [H("ps%d" % i, excl=True) for i in range(8)]

        ident = cst32[:, 0:128]
        selh = [cst32[:, 128:256], cst32[:, 256:384]]
        ones32 = cst32[:, 384:512]
        prot = cstb[:, 0:128]
        onesb = cstb[:, 128:256]
        maskpd = cstb[:, 256:512]
        identb = cstb[:, 512:640]
        epsc = smalls[:, 0:1]

        hh = HD()
        hconst = hh["const"]

        def MM(ps_ap, lhsT, rhs, start, stop, reads, writes):
            P.op("pe", lambda e: e.matmul(ps_ap, lhsT=lhsT, rhs=rhs, start=start, stop=stop), reads=reads, writes=writes)

        def TR(out, in_, idn, reads, writes):
            P.op("pe", lambda e: e.transpose(out=out, in_=in_, identity=idn), reads=reads, writes=writes)

        def ACT(out, in_, func, reads, writes, bias=None, scale=None):
            kw = {}
            if bias is not None:
                kw["bias"] = bias
            if scale is not None:
                kw["scale"] = scale
            P.op("act", lambda e: e.activation(out=out, in_=in_, func=func, **kw), reads=reads, writes=writes)

        def TT(out, in0, in1, op, reads, writes, eng="dve"):
            P.op(eng, lambda e: e.tensor_tensor(out=out, in0=in0, in1=in1, op=op), reads=reads, writes=writes)

        def STT(out, in0, scalar, in1, op0, op1, reads, writes):
            P.op("dve", lambda e: e.scalar_tensor_tensor(out=out, in0=in0, scalar=scalar, in1=in1, op0=op0, op1=op1), reads=reads, writes=writes)

        def TS(out, in0, s1, op0, reads, writes, s2=None, op1=None, eng="dve"):
            if op1 is None:
                P.op(eng, lambda e: e.tensor_scalar(out=out, in0=in0, scalar1=s1, scalar2=None, op0=op0), reads=reads, writes=writes)
            else:
                P.op(eng, lambda e: e.tensor_scalar(out=out, in0=in0, scalar1=s1, scalar2=s2, op0=op0, op1=op1), reads=reads, writes=writes)

        def CP(out, in_, reads, writes, eng="dve"):
            P.op(eng, lambda e: e.tensor_copy(out=out, in_=in_), reads=reads, writes=writes)

        def RED(out, in_, reads, writes):
            P.op("dve", lambda e: e.tensor_reduce(out=out, in_=in_, axis=AX.X, op=ALU.add), reads=reads, writes=writes)

        def RECIP(out, in_, reads, writes):
            P.op("dve", lambda e: e.reciprocal(out=out, in_=in_), reads=reads, writes=writes)

        def MEMSET(ap, val, writes, eng="dve"):
            P.op(eng, lambda e: e.memset(ap, val), writes=writes)

        def DMA(eng, out, in_, reads=(), writes=()):
            P.op(eng, lambda e: e.dma_start(out=out, in_=in_), reads=reads, writes=writes, dma=True)

        class Rot:
            def __init__(self, banks):
                self.banks = banks
                self.i = 0

            def next(self):
                b = self.banks[self.i % len(self.banks)]
                self.i += 1
                return psb[b], hps[b]

        wrot = [0]

        NSL = [NSLOT]

        def wslot_next():
            i = wrot[0] % NSL[0]
            wrot[0] += 1
            return wslots[i], hh[("wslot", i)]

        def gcol(l, w, c):
            j = (l * 4 + w) * 8 + c
            return prmT[:, j:j + 1]

        DMA("sp", cst32[:], cst32_d, writes=[hconst])
        DMA("pool", cstb[:], cstb_d, writes=[hconst])
        MEMSET(smalls[:, 0:8], EPS, [hh["smalls"]])
        DMA("sp", smalls[:, 8:16], bass.AP(sinks.tensor, 0, [[0, 128], [1, 8]]), writes=[hh["sinks"]])
        ACT(smalls[:, 16:24], smalls[:, 8:16], AF.Exp, [hh["sinks"]], [hh["expsink"]])
        m0 = A.mark()
        p1 = A.alloc(128, F32)
        p2 = A.alloc(128, F32)
        DMA("sp", p1[0:88, :], prm1, writes=[hh["p1"]])
        DMA("sp", p2[0:124, :], prm2, writes=[hh["p2"]])
        TR(psb[0][:, 0:88], p1[0:88, :], ident[0:88, 0:88], [hh["p1"], hconst], [hps[0]])
        ACT(prmT[:], psb[0][:, 0:88], AF.Copy, [hps[0]], [hconst])
        TR(psb[1][:, 0:124], p2[0:124, :], ident[0:124, 0:124], [hh["p2"], hconst], [hps[1]])
        ACT(wAT[:], psb[1][:, 0:124], AF.Copy, [hps[1]], [hconst])

        xin = [A.alloc(1024, F32), A.alloc(1024, F32)]
        rot = Rot([2, 3, 4, 5])
        for t in range(17):
            xi = xin[t % 2]
            hx = hh[("xin", t % 2)]
            rows = 128 if t < 16 else NS
            src = xp[t * 128:(t + 1) * 128, :] if t < 16 else xs
            DMA("sp", xi[0:rows, :], src, writes=[hx])
            for g in range(2):
                ps, hp = rot.next()
                for i in range(4):
                    c = g * 4 + i
                    TR(ps[:, i * rows:(i + 1) * rows], xi[0:rows, c * 128:(c + 1) * 128], ident[0:rows, 0:rows], [hx, hconst], [hp])
                dst = hT[:, g * 4:(g + 1) * 4, t * 128:t * 128 + rows]
                srcp = ps[:, 0:4 * rows].rearrange("p (i r) -> p i r", i=4)
                wr = [hh[("hT", g * 4 + i, t // 4 if t < 16 else 4)] for i in range(4)]
                if g == 0:
                    ACT(dst, srcp, AF.Copy, [hp], wr)
                else:
                    CP(dst, srcp, [hp], wr)
        PHASE1_FENCE = True

        statrot = Rot([6, 7])
        mmrot = Rot([0, 1, 2, 3, 4, 5])

        def rstd_from_ps(ps_stat, hstat, N, scale, out_rstd, hout, tmp, htmp):
            ACT(tmp[:, 0:N], ps_stat[:, 0:N], AF.Sqrt, [hstat, hh["smalls"]], [htmp], bias=epsc, scale=scale)
            RECIP(out_rstd[:, 0:N], tmp[:, 0:N], [htmp], [hout])

        def sumsq_stat(src_fn, rd_fn, N, sqbuf):
            ps, hp = statrot.next()
            for c in range(8):
                sq = sqbuf[c % 2]
                hsq = hh[("sq", c % 2)]
                ACT(sq[:, 0:N], src_fn(c), AF.Square, rd_fn(c), [hsq])
                MM(ps[:, 0:N], onesb, sq[:, 0:N], c == 0, c == 7, [hsq, hconst], [hp])
            return ps, hp

        def norm_block(l, w, tb, dstT, dst_col0, sqbuf, tmp, rstd, htb=None):
            c0, N = TBS[tb]
            ps, hp = sumsq_stat(lambda c: hT[:, c, c0:c0 + N], lambda c: [hh[("hT", c, tb)]], N, sqbuf)
            rstd_from_ps(ps, hp, N, 1.0 / 1024, rstd, hh["rstd"], tmp, hh["rtmp"])
            for c in range(8):
                STT(dstT[:, c, c0 - dst_col0:c0 - dst_col0 + N], hT[:, c, c0:c0 + N], gcol(l, w, c), rstd[:, 0:N], ALU.mult, ALU.mult,
                    [hh[("hT", c, tb)], hh["rstd"], hconst], [hh[("uT", c, tb if htb is None else htb)]])

        def push_slots(n):
            nb = len(wslots)
            wslots.extend([A.alloc(SLOT, BF16) for _ in range(n)])
            NSL[0] = len(wslots)
            return nb

        def pop_slots(nb):
            del wslots[nb:]
            NSL[0] = len(wslots)

        PRE = {}

        def wkey(wd, row0, Kc, cols):
            return (wd.tensor.name, int(wd.offset), row0, Kc, tuple(cols))

        def preload(wd, row0, Kc, cols):
            assert NSL[0] == NSLOT
            PRE[wkey(wd, row0, Kc, cols)] = load_wchunks(wd, row0, Kc, cols)

        def load_wchunks(wd, row0, Kc, cols):
            k_ = wkey(wd, row0, Kc, cols)
            if k_ in PRE:
                return PRE.pop(k_)
            slot, hs = wslot_next()
            n = len(cols)
            assert n * Kc * 128 <= SLOT
            v = slot[:, 0:n * Kc * 128].rearrange("p (i k m) -> p i k m", i=n, k=Kc)
            for i, col0 in enumerate(cols):
                src = wd[row0:row0 + Kc * 128, col0:col0 + 128].rearrange("(k p) m -> p k m", p=128)
                DMA("pool", v[:, i], src, writes=[hs])
            return v, hs

        def postnorm_residual(l, w, tb, ysb, ycol0, sqbuf, tmp, rstd, ytmp, stat=None):
            c0, N = TBS[tb]
            lc = c0 - ycol0
            if stat is not None:
                ps, hp = stat[tb]
            else:
                ps, hp = sumsq_stat(lambda c: ysb[:, c, lc:lc + N], lambda c: [hh[("ysb", c, tb)]], N, sqbuf)
            rstd_from_ps(ps, hp, N, 1.0 / 1024, rstd, hh["rstd"], tmp, hh["rtmp"])
            for c in range(8):
                yt = ytmp[c % 2]
                hyt = hh[("ytmp", c % 2)]
                STT(yt[:, 0:N], ysb[:, c, lc:lc + N], gcol(l, w, c), rstd[:, 0:N], ALU.mult, ALU.mult,
                    [hh[("ysb", c, tb)], hh["rstd"], hconst], [hyt])
                TT(hT[:, c, c0:c0 + N], hT[:, c, c0:c0 + N], yt[:, 0:N], ALU.add, [hyt, hh[("hT", c, tb)]], [hh[("hT", c, tb)]],
                   eng=("pool" if POOL_ADD else "dve"))

        mmrot5 = Rot([0, 1, 2, 3, 4])

        def out_proj(wd, row0, Kc, rhs_fn, tbs, ysb, ycol0, accumulate=False, sqbuf=None):
            stat = None
            if sqbuf is not None:
                stat = {tb: (psb[5 + i], hps[5 + i]) for i, tb in enumerate(tbs)}
            rot_ = mmrot5 if sqbuf is not None else mmrot
            for m in range(8):
                v, hs = load_wchunks(wd, row0, Kc, [m * 128])
                for tb in tbs:
                    c0, N = TBS[tb]
                    ps, hp = rot_.next()
                    for k in range(Kc):
                        rap, rh = rhs_fn(k, tb)
                        MM(ps[:, 0:N], v[:, 0, k, :], rap, k == 0, k == Kc - 1, [hs] + rh, [hp])
                    dst = ysb[:, m, c0 - ycol0:c0 - ycol0 + N]
                    if not accumulate:
                        ACT(dst, ps[:, 0:N], AF.Copy, [hp], [hh[("ysb", m, tb)]])
                    else:
                        TT(dst, dst, ps[:, 0:N], ALU.add, [hp, hh[("ysb", m, tb)]], [hh[("ysb", m, tb)]])
                    if stat is not None:
                        sq = sqbuf[(m + tb) % 2]
                        hsq = hh[("sq", (m + tb) % 2)]
                        ACT(sq[:, 0:N], dst, AF.Square, [hh[("ysb", m, tb)]], [hsq])
                        pst, hpst = stat[tb]
                        MM(pst[:, 0:N], onesb, sq[:, 0:N], m == 0, m == 7, [hsq, hconst], [hpst])
            return stat

        def pre_l0a():
            preload(w_in_conv, 0, 8, [0, 512, 1024, 2048])
            preload(w_in_conv, 0, 8, [1536])

        def pre_outproj(wd, Kc):
            preload(wd, 0, Kc, [0])
            preload(wd, 0, Kc, [128])

        def pre_mlp(l):
            preload(w1_d[l], 0, 8, [0, 128, 256, 384])
            preload(w1_d[l], 0, 8, [512, 640, 768, 896])

        def pre_l1a():
            preload(wq_d, 0, 8, [0, 128, 256, 384])
            preload(wq_d, 0, 8, [512, 640, 768, 896])

        def mlp_half(l, col0, tbs, pre_next=None):
            m1 = A.mark()
            W = HW_
            uT = A.alloc(8 * W, BF16).rearrange("p (c n) -> p c n", c=8)
            hid = A.alloc(16 * W, BF16).rearrange("p (c n) -> p c n", c=16)
            ysb = A.alloc(8 * W, F32).rearrange("p (c n) -> p c n", c=8)
            sqbuf = [A.alloc(512, BF16), A.alloc(512, BF16)]
            rl = [A.alloc(512, BF16), A.alloc(512, BF16)]
            tmp = A.alloc(512, F32)
            rstd = A.alloc(512, F32)
            ytmp = [A.alloc(512, F32), A.alloc(512, F32)]
            nb_ = push_slots(2)
            for tb in tbs:
                norm_block(l, 2, tb, uT, col0, sqbuf, tmp, rstd)
            for hf in range(2):
                for mg in range(4):
                    ms = [hf * 16 + mg * 4 + i for i in range(4)]
                    v, hs = load_wchunks(w1_d[l], 0, 8, [m * 128 for m in ms])
                    for i, m in enumerate(ms):
                        for tb in tbs:
                            c0, N = TBS[tb]
                            lc = c0 - col0
                            ps, hp = mmrot.next()
                            for k in range(8):
                                MM(ps[:, 0:N], v[:, i, k, :], uT[:, k, lc:lc + N], k == 0, k == 7, [hs, hh[("uT", k, tb)]], [hp])
                            r = rl[(m + tb) % 2]
                            hr = hh[("rl", (m + tb) % 2)]
                            ACT(r[:, 0:N], ps[:, 0:N], AF.Relu, [hp], [hr])
                            TT(hid[:, m % 16, lc:lc + N], r[:, 0:N], r[:, 0:N], ALU.mult, [hr], [hh[("hid", m % 16, tb)]])
                stat_ = out_proj(w2_d[l], hf * 2048, 16,
                                 lambda k, tb: (hid[:, k, TBS[tb][0] - col0:TBS[tb][0] - col0 + TBS[tb][1]], [hh[("hid", k, tb)]]),
                                 tbs, ysb, col0, accumulate=(hf == 1), sqbuf=(sqbuf if hf == 1 else None))
            for tb in tbs:
                postnorm_residual(l, 3, tb, ysb, col0, sqbuf, tmp, rstd, ytmp, stat=stat_)
            pop_slots(nb_)
            if pre_next is not None:
                pre_next()
            P.fence()
            A.reset(m1)

        def layer0_half(hi, col0, tbs):
            m1 = A.mark()
            W = HW_
            uT = A.alloc(8 * W, BF16).rearrange("p (c n) -> p c n", c=8)
            mixT = uT
            ga = A.alloc(4 * (30 + W), BF16).rearrange("p (c n) -> p c n", c=4)
            zb = A.alloc(4 * (2 + W), BF16).rearrange("p (c n) -> p c n", c=4)
            gb = A.alloc(4 * W, BF16).rearrange("p (c n) -> p c n", c=4)
            m2 = A.mark()
            diag = A.alloc(31 * 128, BF16).rearrange("p (j m) -> p j m", j=31)
            diagB = A.alloc(12 * 128, BF16).rearrange("p (j m) -> p j m", j=12)
            cvo = A.alloc(4 * W, F32).rearrange("p (c n) -> p c n", c=4)
            reg32 = A.alloc(2048, F32)
            xq16 = A.t16[:, 2 * reg32.offset:2 * reg32.offset + 4096]
            xb = xq16[:, 0:2048].rearrange("p (c n) -> p c n", c=4)
            sq4 = xq16[:, 2048:4096].rearrange("p (c n) -> p c n", c=4)
            sqbuf = [A.alloc(512, BF16), A.alloc(512, BF16)]
            tmpa = A.alloc(512, F32)
            tmpg = A.alloc(512, F32)
            tmp = A.alloc(512, F32)
            rstd = A.alloc(512, F32)
            mean = A.alloc(512, F32)
            msq = A.alloc(512, F32)
            t1 = [A.alloc(512, F32), A.alloc(512, F32)]
            tails = A.alloc(4 * 64, F32).rearrange("p (c n) -> p c n", c=4)
            sAT = A.alloc(4 * NS * 30, F32).rearrange("p (c n) -> p c n", c=4)
            sBT = A.alloc(4 * NS * 2, F32).rearrange("p (c n) -> p c n", c=4)
            stg = A.alloc(512, F32)
            gas = A.alloc(4 * NS, F32).rearrange("p (c n) -> p c n", c=4)
            prodA = A.alloc(NS * 30, F32)
            hga = [hh[("ga", c)] for c in range(4)]
            hzb = [hh[("zb", c)] for c in range(4)]
            nb2a = push_slots(1)

            if hi == 0:
                MEMSET(ga[:, :, 0:30], 0.0, hga)
                MEMSET(zb[:, :, 0:2], 0.0, hzb)
            else:
                CP(ga[:, :, 0:30], halo[:, 0:4, 0:30], [hh["halo"]], hga)
                CP(zb[:, :, 0:2], halo[:, 4:8, 0:2], [hh["halo"]], hzb)

            for tb in tbs:
                norm_block(0, 0, tb, uT, col0, sqbuf, tmp, rstd)

            for c in range(4):
                mcols = [c * 128, 512 + c * 128, 1024 + c * 128, 2048 + c * 128, 1536 + c * 128]
                v, hs = load_wchunks(w_in_conv, 0, 8, mcols[0:4])
                v2, hs2 = load_wchunks(w_in_conv, 0, 8, mcols[4:5])
                for tb in tbs:
                    c0, N = TBS[tb]
                    lc = c0 - col0
                    pss = []
                    for i in range(5):
                        ps, hp = mmrot.next()
                        vv, hv, ii = (v, hs, i) if i < 4 else (v2, hs2, 0)
                        for k in range(8):
                            MM(ps[:, 0:N], vv[:, ii, k, :], uT[:, k, lc:lc + N], k == 0, k == 7, [hv, hh[("uT", k, tb)]], [hp])
                        pss.append((ps, hp))
                    (pa, ha), (pg, hg), (px, hx_), (pc, hc), (pb, hb) = pss
                    ACT(tmpg[:, 0:N], pg[:, 0:N], AF.Sigmoid, [hg], [hh["tmpg"]])
                    TT(ga[:, c, 30 + lc:30 + lc + N], pa[:, 0:N], tmpg[:, 0:N], ALU.mult, [ha, hh["tmpg"]], [hga[c]])
                    if tb == 3:
                        TT(tails[:, c, 0:30], pa[:, 482:512], tmpg[:, 482:512], ALU.mult, [ha, hh["tmpg"]], [hh[("tails", c)]])
                    if tb == 4:
                        TT(gas[:, c, :], pa[:, 0:NS], tmpg[:, 0:NS], ALU.mult, [ha, hh["tmpg"]], [hh[("gas", c)]])
                    ACT(tmpa[:, 0:N], px[:, 0:N], AF.Copy, [hx_], [hh["tmpa"]])
                    TT(zb[:, c, 2 + lc:2 + lc + N], pc[:, 0:N], tmpa[:, 0:N], ALU.mult, [hc, hh["tmpa"]], [hzb[c]])
                    if tb == 3:
                        TT(tails[:, c, 32:34], pc[:, 510:512], tmpa[:, 510:512], ALU.mult, [hc, hh["tmpa"]], [hh[("tails", c)]])
                    if tb == 4:
                        TT(tails[:, c, 40:56], pc[:, 0:NS], tmpa[:, 0:NS], ALU.mult, [hc, hh["tmpa"]], [hh[("tails", c)]])
                    ACT(gb[:, c, lc:lc + N], pb[:, 0:N], AF.Copy, [hb], [hh[("gb", c)]])

            if hi == 1:
                stA = reg32.rearrange("p (g n) -> p g n", g=4)
                stB = tmp
                hstA = [hh[("xb", c)] for c in range(4)] + [hh[("sq4", c)] for c in range(4)]
                DMA("sp", stA[0:120, :, :], sca.rearrange("(g r) n -> r g n", r=120), writes=hstA)
                DMA("sp", stB[0:32, :], scb, writes=[hh["rtmp"]])
                for c in range(4):
                    ps, hp = mmrot.next()
                    for g in range(4):
                        TR(ps[:, g * 120:(g + 1) * 120], stA[0:120, g, c * 128:(c + 1) * 128], ident[0:120, 0:120], hstA + [hconst], [hp])
                    ACT(sAT[:, c, :], ps[:, 0:480], AF.Copy, [hp], [hh[("sAT", c)]])
                ps, hp = mmrot.next()
                for c in range(4):
                    TR(ps[:, c * 32:(c + 1) * 32], stB[0:32, c * 128:(c + 1) * 128], ident[0:32, 0:32], [hh["rtmp"], hconst], [hp])
                ACT(sBT[:, :, :], ps[:, 0:128].rearrange("p (c n) -> p c n", c=4), AF.Copy, [hp], [hh["sBT"]])

            for j in range(3):
                for c in range(4):
                    TS(diagB[:, j * 4 + c, :], identb, prmT[:, 76 + j * 4 + c:77 + j * 4 + c], ALU.mult, [hconst], [hh["diagB"]])
            for c in range(4):
                for j in range(31):
                    if j % 2 == 0:
                        TS(diag[:, j, :], identb, wAT[:, j * 4 + c:j * 4 + c + 1], ALU.mult, [hconst], [hh[("diag", j)]])
                    else:
                        ACT(diag[:, j, :], identb, AF.Copy, [hconst], [hh[("diag", j)]], scale=wAT[:, j * 4 + c:j * 4 + c + 1])
                for tb in tbs:
                    c0, N = TBS[tb]
                    lc = c0 - col0
                    hcv = hh[("cvo", c, tb)]
                    if tb < 4:
                        ps, hp = mmrot.next()
                        for j in range(31):
                            MM(ps[:, 0:N], diag[:, j, :], ga[:, c, lc + j:lc + j + N], j == 0, j == 30, [hh[("diag", j)], hga[c]], [hp])
                        ACT(cvo[:, c, lc:lc + N], ps[:, 0:N], AF.Identity, [hp, hconst], [hcv], bias=prmT[:, 64 + c:65 + c], scale=1.0)
                    else:
                        wv_ = bcast_ap(wAT[:, c:c + 1], [(0, NS), (4, 30)])
                        pA = prodA.rearrange("p (b j) -> p b j", b=NS)
                        TT(pA, sAT[:, c, :].rearrange("p (b j) -> p b j", b=NS), wv_, ALU.mult, [hh[("sAT", c)], hconst], [hh["prodA"]])
                        RED(cvo[:, c, lc:lc + NS], pA, [hh["prodA"]], [hcv])
                        STT(cvo[:, c, lc:lc + NS], gas[:, c, :], wAT[:, 120 + c:121 + c], cvo[:, c, lc:lc + NS], ALU.mult, ALU.add,
                            [hh[("gas", c)], hcv, hconst], [hcv])
                        TS(cvo[:, c, lc:lc + NS], cvo[:, c, lc:lc + NS], prmT[:, 64 + c:65 + c], ALU.add, [hcv, hconst], [hcv])
            for tb in tbs:
                c0, N = TBS[tb]
                lc = c0 - col0
                psm, hpm = statrot.next()
                pse, hpe = statrot.next()
                for c in range(4):
                    CP(xb[:, c, 0:N], cvo[:, c, lc:lc + N], [hh[("cvo", c, tb)]], [hh[("xb", c)]])
                    ACT(sq4[:, c, 0:N], cvo[:, c, lc:lc + N], AF.Square, [hh[("cvo", c, tb)]], [hh[("sq4", c)]])
                for c in range(4):
                    MM(psm[:, 0:N], onesb, xb[:, c, 0:N], c == 0, c == 3, [hh[("xb", c)], hconst], [hpm])
                for c in range(4):
                    MM(pse[:, 0:N], onesb, sq4[:, c, 0:N], c == 0, c == 3, [hh[("sq4", c)], hconst], [hpe])
                ACT(mean[:, 0:N], psm[:, 0:N], AF.Copy, [hpm], [hh["mean"]], scale=1.0 / 512)
                TT(msq[:, 0:N], mean[:, 0:N], mean[:, 0:N], ALU.mult, [hh["mean"]], [hh["msq"]])
                STT(msq[:, 0:N], pse[:, 0:N], 1.0 / 512, msq[:, 0:N], ALU.mult, ALU.subtract, [hpe, hh["msq"]], [hh["msq"]])
                ACT(tmp[:, 0:N], msq[:, 0:N], AF.Sqrt, [hh["msq"], hh["smalls"]], [hh["rtmp"]], bias=epsc, scale=1.0)
                RECIP(rstd[:, 0:N], tmp[:, 0:N], [hh["rtmp"]], [hh["rstd"]])
                for c in range(4):
                    tt = t1[c % 2]
                    ht = hh[("t1", c % 2)]
                    TT(tt[:, 0:N], cvo[:, c, lc:lc + N], mean[:, 0:N], ALU.subtract, [hh[("cvo", c, tb)], hh["mean"]], [ht])
                    TT(tt[:, 0:N], tt[:, 0:N], rstd[:, 0:N], ALU.mult, [ht, hh["rstd"]], [ht])
                    ACT(mixT[:, c, lc:lc + N], tt[:, 0:N], AF.Silu, [ht, hconst], [hh[("uT", c, tb)]],
                        bias=prmT[:, 72 + c:73 + c], scale=prmT[:, 68 + c:69 + c])
                for c in range(4):
                    hmx = hh[("uT", 4 + c, tb)]
                    if tb < 4:
                        ps, hp = mmrot.next()
                        for j in range(3):
                            MM(ps[:, 0:N], diagB[:, j * 4 + c, :], zb[:, c, lc + j:lc + j + N], j == 0, j == 2, [hh["diagB"], hzb[c]], [hp])
                        TT(mixT[:, 4 + c, lc:lc + N], ps[:, 0:N], gb[:, c, lc:lc + N], ALU.mult, [hp, hh[("gb", c)]], [hmx])
                    else:
                        tt = t1[c % 2]
                        ht = hh[("t1", c % 2)]
                        sb_ = sBT[:, c, :].rearrange("p (b j) -> p b j", j=2)
                        TS(tt[:, 0:NS], sb_[:, :, 0], prmT[:, 76 + c:77 + c], ALU.mult, [hh["sBT"], hconst], [ht])
                        STT(tt[:, 0:NS], sb_[:, :, 1], prmT[:, 80 + c:81 + c], tt[:, 0:NS], ALU.mult, ALU.add, [hh["sBT"], ht, hconst], [ht])
                        STT(tt[:, 0:NS], tails[:, c, 40:56], prmT[:, 84 + c:85 + c], tt[:, 0:NS], ALU.mult, ALU.add, [hh[("tails", c)], ht, hconst], [ht])
                        TT(mixT[:, 4 + c, lc:lc + NS], tt[:, 0:NS], gb[:, c, lc:lc + NS], ALU.mult, [ht, hh[("gb", c)]], [hmx])

            if hi == 1:
                def tm_out(src_fn, rows, dst, rd_fn):
                    ps, hp = mmrot.next()
                    for c in range(4):
                        TR(ps[0:rows, c * 128:(c + 1) * 128], src_fn(c), ident, rd_fn(c) + [hconst], [hp])
                    ACT(stg[0:rows, :], ps[0:rows, :], AF.Copy, [hp], [hh["stg"]])
                    DMA("sp", dst, stg[0:rows, :], reads=[hh["stg"]])
                tm_out(lambda c: tails[:, c, 0:30], 30, a_p, lambda c: [hh[("tails", c)]])
                tm_out(lambda c: tails[:, c, 32:34], 2, b_p, lambda c: [hh[("tails", c)]])
                DMA("sp", a_s[:, 0:29, :], sca.rearrange("(b j) n -> b j n", j=30)[:, 1:30, :])
                DMA("sp", b_s[:, 0:1, :], scb.rearrange("(b j) n -> b j n", j=2)[:, 1:2, :])
                tm_out(lambda c: gas[:, c, :], NS, a_s[:, 29, :], lambda c: [hh[("gas", c)]])
                tm_out(lambda c: tails[:, c, 40:56], NS, b_s[:, 1, :], lambda c: [hh[("tails", c)]])
            else:
                CP(halo[:, 0:4, 0:30], ga[:, :, 1024:1054], hga, [hh["halo"]])
                CP(halo[:, 4:8, 0:2], zb[:, :, 1024:1026], hzb, [hh["halo"]])
            pop_slots(nb2a)
            pre_outproj(w_out_conv, 8)
            P.fence()
            A.reset(m2)
            ysb = A.alloc(8 * W, F32).rearrange("p (c n) -> p c n", c=8)
            sqbuf2 = [A.alloc(512, BF16), A.alloc(512, BF16)]
            tmp2 = A.alloc(512, F32)
            rstd2 = A.alloc(512, F32)
            ytmp = [A.alloc(512, F32), A.alloc(512, F32)]
            nb_ = push_slots(2)
            stat_ = out_proj(w_out_conv, 0, 8,
                             lambda k, tb: (mixT[:, k, TBS[tb][0] - col0:TBS[tb][0] - col0 + TBS[tb][1]], [hh[("uT", k, tb)]]),
                             tbs, ysb, col0, sqbuf=sqbuf2)
            for tb in tbs:
                postnorm_residual(0, 1, tb, ysb, col0, sqbuf2, tmp2, rstd2, ytmp, stat=stat_)
            pop_slots(nb_)
            if STOP >= 2:
                pre_mlp(0)
            P.fence()
            A.reset(m1)

        RUN0 = STOP >= 1 and not int(os.environ.get('MK_SKIP0', '0'))
        if RUN0:
            pre_l0a()
        P.fence()
        A.reset(m0)
        if RUN0:
            for hi, (col0, tbs) in enumerate(HALVES):
                layer0_half(hi, col0, tbs)
                if STOP >= 2:
                    nxt = pre_l0a if hi == 0 else (pre_l1a if STOP >= 3 else None)
                    mlp_half(0, col0, tbs, pre_next=nxt)

        def load_wcols(wd, row0, Kc, col0, ncols):
            slot, hs = wslot_next()
            assert Kc * ncols <= SLOT
            v = slot[:, 0:Kc * ncols].rearrange("p (k m) -> p k m", k=Kc)
            DMA("pool", v, wd[row0:row0 + Kc * 128, col0:col0 + ncols].rearrange("(k p) m -> p k m", p=128), writes=[hs])
            return v, hs

        def layer1_mixer():
            mL = A.mark()
            QT = A.alloc(10 * NT, BF16).rearrange("p (c n) -> p c n", c=10)
            KT = A.alloc(3 * NT, BF16).rearrange("p (c n) -> p c n", c=3)
            Vd = A.alloc(16 * 5 * 128, BF16).rearrange("p (t k m) -> p t k m", t=16, k=5)
            QTs = A.alloc(10 * NS, F32).rearrange("p (c n) -> p c n", c=10)
            KTs = A.alloc(3 * NS, F32).rearrange("p (c n) -> p c n", c=3)
            VTs = A.alloc(5 * NS, F32).rearrange("p (c n) -> p c n", c=5)
            VsTM = A.alloc(320, F32)
            knew = A.alloc(384, F32)
            smix = A.alloc(6 * NS, BF16).rearrange("p (c n) -> p c n", c=6)
            mP = A.mark()
            uT = A.alloc(8 * HW_, BF16).rearrange("p (c n) -> p c n", c=8)
            sqbuf = [A.alloc(512, BF16), A.alloc(512, BF16)]
            tmp = A.alloc(512, F32)
            rstd = A.alloc(512, F32)
            cst = A.alloc(2 * HW_, F32).rearrange("p (c n) -> p c n", c=2)
            qbs = [A.alloc(512, BF16), A.alloc(512, BF16)]
            t1s = [A.alloc(512, F32), A.alloc(512, F32)]
            t2s = [A.alloc(512, F32), A.alloc(512, F32)]
            rctr = [0]
            K32 = A.alloc(512, F32)
            vst = A.alloc(320, F32)
            kst = A.alloc(128, F32)


            def rope(ps, hp, N, lc=0):
                i = rctr[0] % 2
                rctr[0] += 1
                qb, t1, t2 = qbs[i], t1s[i], t2s[i]
                hq, h1, h2 = hh[("qb", i)], hh[("t1", i)], hh[("t2", i)]
                ACT(qb[:, 0:N], ps[:, 0:N], AF.Copy, [hp], [hq])
                pr, hpr = rrot.next()
                MM(pr[:, 0:N], prot, qb[:, 0:N], True, True, [hq, hconst], [hpr])
                TT(t1[:, 0:N], ps[:, 0:N], cst[:, 0, lc:lc + N], ALU.mult, [hp, hh["cst"]], [h1])
                TT(t2[:, 0:N], pr[:, 0:N], cst[:, 1, lc:lc + N], ALU.mult, [hpr, hh["cst"]], [h2])
                return t1, t2, h1, h2

            def perm_write(dstT, m, rows, kind, tb, hw, R):
                t1, t2, h1, h2 = R
                c0, N = TBS[tb]
                r0, r1 = rows
                if kind == 1:
                    o = dstT[r0:r1, m, c0:c0 + N]
                    a, b = t1[r0:r1, 0:N], t2[r0:r1, 0:N]
                else:
                    o = dstT[r0:r1, m, 0:NTOK].rearrange("p (r i) -> p r i", r=kind)[:, :, c0 // kind:c0 // kind + N // kind]
                    a = t1[r0:r1, 0:N].rearrange("p (i r) -> p r i", r=kind)
                    b = t2[r0:r1, 0:N].rearrange("p (i r) -> p r i", r=kind)
                TT(o, a, b, ALU.add, [h1, h2], [hw])

            QKIND = [((1, 1),)] * 4 + [((1, 4),)] * 4 + [((16, 16),)] * 2
            KKIND = [(1, 1), (1, 4), (16, 16)]

            nbase = len(wslots)
            PASSES = [[0, 1], [2, 3, 4]]
            psrot = Rot([0, 1, 2])
            rrot = Rot([3, 4, 5])
            for tbs_ in PASSES:
                pc0 = TBS[tbs_[0]][0]
                PW = sum(TBS[tb][1] for tb in tbs_)
                for tb in tbs_:
                    norm_block(1, 0, tb, uT, pc0, sqbuf, tmp, rstd, htb="L1")
                DMA("sp", cst[:, :, 0:PW], cs_d[:, :, pc0:pc0 + PW], writes=[hh["cst"]])
                pend = [None]

                def flush():
                    if pend[0] is not None:
                        f_, a_ = pend[0]
                        pend[0] = None
                        f_(*a_)

                def post_q(ps, hp, m, tb):
                    c0, N = TBS[tb]
                    lc = c0 - pc0
                    R = rope(ps, hp, N, lc)
                    if tb == 4:
                        TT(QTs[:, m, :], R[0][:, 0:N], R[1][:, 0:N], ALU.add, [R[2], R[3]], [hh[("QTs", m)]])
                    else:
                        ka, kb = QKIND[m][0]
                        if ka == kb:
                            perm_write(QT, m, (0, 128), ka, tb, hh[("QT", m, tb)], R)
                        else:
                            perm_write(QT, m, (0, 64), ka, tb, hh[("QT", m, tb, 0)], R)
                            perm_write(QT, m, (64, 128), kb, tb, hh[("QT", m, tb, 1)], R)

                def post_k(ps, hp, kc, tb):
                    c0, N = TBS[tb]
                    lc = c0 - pc0
                    R = rope(ps, hp, N, lc)
                    if tb == 4:
                        TT(KTs[:, kc, :], R[0][:, 0:N], R[1][:, 0:N], ALU.add, [R[2], R[3]], [hh[("KTs", kc)]])
                        return
                    ka, kb = KKIND[kc]
                    if ka == kb:
                        perm_write(KT, kc, (0, 128), ka, tb, hh[("KT", kc, tb)], R)
                    else:
                        perm_write(KT, kc, (0, 64), ka, tb, hh[("KT", kc, tb, 0)], R)
                        perm_write(KT, kc, (64, 128), kb, tb, hh[("KT", kc, tb, 1)], R)
                    need = [t for t in range(4) if (M4A & 4) and (kc == 2 or (kc == 1 and 4 * tb + t >= 12) or (kc == 0 and 4 * tb + t == 15))]
                    if need:
                        TT(K32[:, 0:N], R[0][:, 0:N], R[1][:, 0:N], ALU.add, [R[2], R[3]], [hh["K32"]])
                    for t in need:
                        T = 4 * tb + t
                        pt, hpt = rrot.next()
                        TR(pt[:, 0:128], K32[:, t * 128:(t + 1) * 128], ident, [hh["K32"], hconst], [hpt])
                        ACT(kst[:, 0:128], pt[:, 0:128], AF.Copy, [hpt], [hh["kst"]])
                        if kc == 2:
                            DMA("sp", d_p[2][T * 128:(T + 1) * 128, 0:64], kst[:, 0:64], reads=[hh["kst"]], writes=[hh[("d2pk", T)]])
                        elif kc == 1:
                            DMA("sp", d_p[1][(T - 12) * 128:(T - 11) * 128, 0:64], kst[:, 64:128], reads=[hh["kst"]])
                            if T == 15:
                                DMA("sp", d_p[0][:, 0:64], kst[:, 0:64], reads=[hh["kst"]])
                        else:
                            DMA("sp", swa_p[:, 0:128], kst[:, 0:128], reads=[hh["kst"]])

                def main_mm(v, i, hs, tb):
                    c0, N = TBS[tb]
                    lc = c0 - pc0
                    ps, hp = psrot.next()
                    for k in range(8):
                        MM(ps[:, 0:N], v[:, i, k, :], uT[:, k, lc:lc + N], k == 0, k == 7, [hs, hh[("uT", k, "L1")]], [hp])
                    return ps, hp

                for mg in range(3):
                    ms = list(range(mg * 4, min(10, mg * 4 + 4)))
                    v, hs = load_wchunks(wq_d, 0, 8, [m * 128 for m in ms])
                    for i, m in enumerate(ms):
                        for tb in tbs_:
                            ps, hp = main_mm(v, i, hs, tb)
                            flush()
                            pend[0] = (post_q, (ps, hp, m, tb))
                v, hs = load_wchunks(wk_d, 0, 8, [0, 128, 256])
                for kc in range(3):
                    for tb in tbs_:
                        ps, hp = main_mm(v, kc, hs, tb)
                        flush()
                        pend[0] = (post_k, (ps, hp, kc, tb))
                flush()
                vw, hvw = load_wcols(wv_d, 0, 8, 0, 320)
                for tb in tbs_:
                    c0, N = TBS[tb]
                    lc = c0 - pc0
                    if tb == 4:
                        ps, hp = mmrot.next()
                        for k in range(8):
                            MM(ps[0:NS, 0:320], uT[:, k, lc:lc + NS], vw[:, k, :], k == 0, k == 7, [hvw, hh[("uT", k, "L1")]], [hp])
                        ACT(VsTM[0:NS, :], ps[0:NS, 0:320], AF.Copy, [hp], [hh["VsTM"]])
                        vd_, hvd = load_wchunks(wvd_d, 0, 8, [0, 128, 256, 384])
                        vd2, hvd2 = load_wchunks(wvd_d, 0, 8, [512])
                        for i in range(5):
                            vv, hv_, ii = (vd_, hvd, i) if i < 4 else (vd2, hvd2, 0)
                            ps, hp = mmrot.next()
                            for k in range(8):
                                MM(ps[:, 0:NS], vv[:, ii, k, :], uT[:, k, lc:lc + NS], k == 0, k == 7, [hv_, hh[("uT", k, "L1")]], [hp])
                            ACT(VTs[:, i, :], ps[:, 0:NS], AF.Copy, [hp], [hh[("VTs", i)]])
                        ps, hp = mmrot.next()
                        for kc in range(3):
                            TR(ps[0:NS, kc * 128:(kc + 1) * 128], KTs[:, kc, :], ident, [hh[("KTs", kc)], hconst], [hp])
                        ACT(knew[0:NS, :], ps[0:NS, 0:384], AF.Copy, [hp], [hh["knew"]])
                        DMA("sp", swa_s[:, 127, 0:128], knew[0:NS, 0:128], reads=[hh["knew"]])
                        DMA("sp", swa_s[:, 127, 128:256], VsTM[0:NS, 0:128], reads=[hh["VsTM"]])
                        for g, Wg in enumerate((128, 512, 2048)):
                            DMA("sp", d_s[g][:, Wg - 1, 0:64], knew[0:NS, 128 + 64 * g:192 + 64 * g], reads=[hh["knew"]])
                            DMA("sp", d_s[g][:, Wg - 1, 64:128], VsTM[0:NS, 128 + 64 * g:192 + 64 * g], reads=[hh["VsTM"]])
                        continue
                    for t in range(4):
                        T = 4 * tb + t
                        ps, hp = mmrot.next()
                        for k in range(8):
                            MM(ps[:, 0:320], uT[:, k, lc + t * 128:lc + (t + 1) * 128], vw[:, k, :], k == 0, k == 7, [hvw, hh[("uT", k, "L1")]], [hp])
                        src3 = ps[:, 0:192].rearrange("p (k m) -> p k m", k=3)
                        ACT(Vd[:, T, 0:3, 0:64], src3, AF.Copy, [hp], [hh[("Vd", T, 0)]])
                        CP(Vd[:, T, 0:3, 64:128], src3, [hp], [hh[("Vd", T, 1)]])
                        ACT(vst[:, 0:320], ps[:, 0:320], AF.Copy, [hp], [hh["vst"]])
                        DMA("sp", d_p[2][T * 128:(T + 1) * 128, 64:128], vst[:, 256:320], reads=[hh["vst"]], writes=[hh[("d2pv", T)]])
                        if T >= 12:
                            DMA("sp", d_p[1][(T - 12) * 128:(T - 11) * 128, 64:128], vst[:, 192:256], reads=[hh["vst"]])
                        if T == 15:
                            DMA("sp", d_p[0][:, 64:128], vst[:, 128:192], reads=[hh["vst"]])
                            DMA("sp", swa_p[:, 128:256], vst[:, 0:128], reads=[hh["vst"]])
                    for r in range(4):
                        ps, hp = mmrot.next()
                        for k in range(8):
                            MM(ps[:, 0:64], uT[:, k, lc + r:lc + 512:4], vw[:, k, 192:256], k == 0, k == 7, [hvw, hh[("uT", k, "L1")]], [hp])
                        ACT(Vd[:, 4 * r + tb, 3, 0:64], ps[:, 0:64], AF.Copy, [hp], [hh[("Vd3", r, tb, 0)]])
                        CP(Vd[:, 4 * r + tb, 3, 64:128], ps[:, 0:64], [hp], [hh[("Vd3", r, tb, 1)]])
            del wslots[nbase:]
            NSL[0] = len(wslots)
            if L1STOP > 3:
                pre_outproj(w_out_attn, 6)
            src = d_p[2].rearrange("(i r) n -> i r n", r=16)[:, :, 64:128]
            rds = [hh[("d2pv", T)] for T in range(16)]
            if not int(os.environ.get("MK_NORB", "0")):
                for r in range(16):
                    DMA("pool", Vd[:, r, 4, 0:64], src[:, r, :], reads=rds, writes=[hh[("Vd4a", r)]])
                    DMA("pool", Vd[:, r, 4, 64:128], src[:, r, :], reads=rds, writes=[hh[("Vd4b", r)]])
            P.fence()
            if L1STOP <= 1:
                A.reset(mL)
                return

            if not NOCOPY:
                DMA("act", swa_s[:, 0:127, :], c_swa[:, 1:128, :])
                for g, Wg in enumerate((128, 512, 2048)):
                    DMA("act", d_s[g][:, 0:Wg - 1, :], c_d[g][:, 1:Wg, :])
            A.reset(mP)
            mixT = A.alloc(6 * NT, BF16).rearrange("p (c n) -> p c n", c=6)
            PT = [A.alloc(256, BF16), A.alloc(256, BF16)]
            accN = A.alloc(NTOK, F32)
            accD = A.alloc(NTOK, F32)
            rec = A.alloc(512, F32)
            PT = PT + [A.alloc(256, BF16), A.alloc(256, BF16)]
            srot = Rot([0, 1, 6, 7])
            orot = Rot([2, 4])
            items = []

            def add_head(qc, base, kc, kv, kind, evac):
                cfg = dict(qc=qc, base=base, kc=kc, kv=kv, kind=kind, evac=evac)
                for T in range(16):
                    items.append((cfg, T))

            for h in range(8):
                kv, g = h // 4, h % 4
                mrows = slice((h % 2) * 64, (h % 2) * 64 + 64)
                mch = h // 2

                def evac(b, po, hpo, pd, hpd, h=h, mrows=mrows, mch=mch):
                    TS(rec[mrows, :], pd[mrows, :], smalls[mrows, 16 + h:17 + h], ALU.add, [hpd, hh["expsink"]], [hh["rec"]])
                    RECIP(rec[mrows, :], rec[mrows, :], [hh["rec"]], [hh["rec"]])
                    TT(mixT[mrows, mch, b * 512:(b + 1) * 512], po[mrows, :], rec[mrows, :], ALU.mult, [hpo, hh["rec"]], [hh[("mixT", mch, h % 2, b)]])
                add_head(g, kv * 64, 0, kv, 1, evac)
            for s_ in range(4):
                mrows = slice((s_ % 2) * 64, (s_ % 2) * 64 + 64)
                mch = 4 + s_ // 2
                for grp in range(3):
                    kind = (1, 4, 16)[grp]

                    def evac(b, po, hpo, pd, hpd, grp=grp, kind=kind, mrows=mrows, mch=mch, s_=s_):
                        if kind == 1:
                            on = accN[mrows, b * 512:(b + 1) * 512]
                            od = accD[mrows, b * 512:(b + 1) * 512]
                            sn, sd = po[mrows, :], pd[mrows, :]
                        elif kind == 4:
                            on = accN[mrows, b:NTOK:4]
                            od = accD[mrows, b:NTOK:4]
                            sn, sd = po[mrows, :], pd[mrows, :]
                        else:
                            on = accN[mrows, :].rearrange("p (i r) -> p i r", r=16)[:, :, 4 * b:4 * b + 4]
                            od = accD[mrows, :].rearrange("p (i r) -> p i r", r=16)[:, :, 4 * b:4 * b + 4]
                            sn = po[mrows, :].rearrange("p (t i) -> p i t", t=4)
                            sd = pd[mrows, :].rearrange("p (t i) -> p i t", t=4)
                        if grp == 0:
                            ACT(on, sn, AF.Copy, [hpo], [hh["accN"]])
                            CP(od, sd, [hpd], [hh["accD"]])
                        else:
                            TT(on, on, sn, ALU.add, [hpo, hh["accN"]], [hh["accN"]])
                            TT(od, od, sd, ALU.add, [hpd, hh["accD"]], [hh["accD"]])
                        if grp == 2 and b == 3:
                            for bb in range(4):
                                RECIP(rec[mrows, :], accD[mrows, bb * 512:(bb + 1) * 512], [hh["accD"]], [hh["rec"]])
                                TT(mixT[mrows, mch, bb * 512:(bb + 1) * 512], accN[mrows, bb * 512:(bb + 1) * 512], rec[mrows, :], ALU.mult,
                                   [hh["accN"], hh["rec"]], [hh[("mixT", mch, s_ % 2, bb)]])
                    if grp == 0:
                        add_head(4 + s_, 0, 1, 2, 1, evac)
                    elif grp == 1:
                        add_head(4 + s_, 64, 1, 3, 4, evac)
                    else:
                        add_head(8 + s_ // 2, (s_ % 2) * 64, 2, 4, 16, evac)

            state = {}

            def issue_S(i):
                cfg, T = items[i]
                kind, base = cfg["kind"], cfg["base"]
                rows = slice(base, base + 64)
                j = T if kind == 1 else (T % 4 if kind == 4 else 0)
                pss, hs_ = srot.next()
                lo = 0 if j > 0 else 128
                qsl = QT[rows, cfg["qc"], T * 128:(T + 1) * 128]
                if j > 0:
                    MM(pss[:, 0:128], KT[rows, cfg["kc"], (T - 1) * 128:T * 128], qsl, True, True, [], [hs_])
                MM(pss[:, 128:256], KT[rows, cfg["kc"], T * 128:(T + 1) * 128], qsl, True, True, [], [hs_])
                pt = PT[i % 4]
                hpt = hh[("PT", i % 4)]
                ACT(pt[:, lo:256], pss[:, lo:256], AF.Exp, [hs_], [hpt], scale=0.125)
                TT(pt[:, lo:256], pt[:, lo:256], maskpd[:, lo:256], ALU.mult, [hpt, hconst], [hpt], eng="pool")
                state[i] = (pt, hpt, j)

            def issue_PV(i):
                cfg, T = items[i]
                pt, hpt, j = state.pop(i)
                kv = cfg["kv"]
                if T % 4 == 0:
                    po, hpo = orot.next()
                    ob = orot.banks[(orot.i - 1) % 2]
                    cfg["o"] = (po, hpo, psb[ob + 1], hps[ob + 1])
                po, hpo, pd, hpd = cfg["o"]
                osl = slice((T % 4) * 128, (T % 4 + 1) * 128)
                if j > 0:
                    MM(po[:, osl], Vd[:, T - 1, kv, :], pt[:, 0:128], True, False, [hpt], [hpo])
                    MM(po[:, osl], Vd[:, T, kv, :], pt[:, 128:256], False, True, [hpt], [hpo])
                    MM(pd[:, osl], onesb, pt[:, 0:128], True, False, [hpt, hconst], [hpd])
                    MM(pd[:, osl], onesb, pt[:, 128:256], False, True, [hpt, hconst], [hpd])
                else:
                    MM(po[:, osl], Vd[:, T, kv, :], pt[:, 128:256], True, True, [hpt], [hpo])
                    MM(pd[:, osl], onesb, pt[:, 128:256], True, True, [hpt, hconst], [hpd])
                if T % 4 == 3:
                    cfg["evac"](T // 4, po, hpo, pd, hpd)

            LOOK = 2
            for i in range(len(items) + LOOK):
                if i < len(items):
                    issue_S(i)
                if i >= LOOK:
                    issue_PV(i - LOOK)
            P.fence()
            if L1STOP <= 2:
                A.reset(mL)
                return

            A.reset(mL)
            _skip = A.alloc(1, F32)
            A.reset(mL)
            Kc = A.alloc(NS * 64, F32).rearrange("p (b d) -> p b d", b=NS)
            Vc = A.alloc(NS * 128, F32).rearrange("p (b d) -> p b d", b=NS)
            Qs = A.alloc(256, F32)
            Qd = A.alloc(NS * 256, F32).rearrange("p (b c) -> p b c", b=NS)
            prod = A.alloc(512, F32)
            Sall = A.alloc(64, F32)
            Pn = A.alloc(64, F32)
            Pnew = A.alloc(64, F32)
            den = A.alloc(64, F32)
            num = A.alloc(64, F32)
            numD = A.alloc(64, F32)
            denD = A.alloc(64, F32)
            prT = A.alloc(NS, F32)
            outv = A.alloc(64, F32)
            assert A.off <= mP - 0 or True

            def v3(ap):
                return ap.rearrange("p (b s) -> p b s", s=4)

            groups = []
            for kv in range(2):
                groups.append(dict(heads=[(g, kv * 64) for g in range(4)], kc=0, kbase=kv * 64, vi=kv, cache=c_swa, kcol=kv * 64, vcol=128 + kv * 64,
                                   step=1, swa=kv))
            groups.append(dict(heads=[(4 + s, 0) for s in range(4)], kc=1, kbase=0, vi=2, cache=c_d[0], kcol=0, vcol=64, step=1, swa=None))
            groups.append(dict(heads=[(4 + s, 64) for s in range(4)], kc=1, kbase=64, vi=3, cache=c_d[1], kcol=0, vcol=64, step=4, swa=None))
            groups.append(dict(heads=[(8 + s // 2, (s % 2) * 64) for s in range(4)], kc=2, kbase=None, vi=4, cache=c_d[2], kcol=0, vcol=64, step=16, swa=None))
            for gi, G in enumerate(groups):
                cview = G["cache"].rearrange("b j n -> j b n")
                st_ = G["step"]
                for q4 in range(4):
                    bs = slice(4 * q4, 4 * q4 + 4)
                    DMA("sp", Kc[:, bs, :], cview[0:128 * st_:st_, bs, G["kcol"]:G["kcol"] + 64], writes=[hh[("Kc", q4)]])
                    DMA("sp", Vc[:, bs, 0:64], cview[0:128 * st_:st_, bs, G["vcol"]:G["vcol"] + 64], writes=[hh[("Vc", q4, 0)]])
                    DMA("sp", Vc[:, bs, 64:128], cview[0:128 * st_:st_, bs, G["vcol"]:G["vcol"] + 64], writes=[hh[("Vc", q4, 1)]])
                pq = [mmrot.next(), mmrot.next()]
                for s, (qc, qb_) in enumerate(G["heads"]):
                    ps, hp = pq[qb_ // 64]
                    TR(ps[0:NS, s * 64:(s + 1) * 64], QTs[qb_:qb_ + 64, qc, :], ident[qb_:qb_ + 64, qb_:qb_ + 64], [hh[("QTs", qc)], hconst], [hp])
                for s, (qc, qb_) in enumerate(G["heads"]):
                    ps, hp = pq[qb_ // 64]
                    ACT(Qs[0:NS, s * 64:(s + 1) * 64], ps[0:NS, s * 64:(s + 1) * 64], AF.Copy, [hp], [hh["Qs"]])
                TT(Qd[0:NS, :, :], bcast_ap(Qs[0:NS, 0:1], [(0, NS), (1, 256)]), bcast_ap(ident[0:NS, 0:1], [(1, NS), (0, 256)]), ALU.mult,
                   [hh["Qs"], hconst], [hh["Qd"]])
                for bp in range(NS // 2):
                    ps, hp = mmrot.next()
                    MM(ps[:, 0:512], ones32[0:NS, :], Qd[0:NS, 2 * bp:2 * bp + 2, :], True, True, [hh["Qd"], hconst], [hp])
                    TT(prod[:, :].rearrange("p (b s d) -> p b s d", b=2, s=4), ps[:, 0:512].rearrange("p (b s d) -> p b s d", b=2, s=4),
                       bcast_ap(Kc[:, 2 * bp, 0:1], [(64, 2), (0, 4), (1, 64)]), ALU.mult, [hp] + [hh[("Kc", q4)] for q4 in range(4)], [hh["prod"]])
                    o = Sall[:, 8 * bp:8 * bp + 8].rearrange("p (b s) -> p b s", b=2)
                    RED(o, prod[:, :].rearrange("p (b s d) -> p b s d", b=2, s=4), [hh["prod"]], [hh["Sall"]])
                psn, hpn = mmrot.next()
                for s, (qc, qb_) in enumerate(G["heads"]):
                    TT(prT[:, :], QTs[:, qc, :], KTs[:, G["kc"], :], ALU.mult, [hh[("QTs", qc)], hh[("KTs", G["kc"])]], [hh["prT"]])
                    MM(psn[:, s * NS:(s + 1) * NS], selh[qb_ // 64], prT[:, :], True, True, [hh["prT"], hconst], [hpn])
                ACT(Pn[:, :], Sall[:, :], AF.Exp, [hh["Sall"]], [hh["Pn"]], scale=0.125)
                ACT(Pnew[:, :].rearrange("p (b s) -> p s b", s=4), psn[:, 0:64].rearrange("p (s b) -> p s b", s=4), AF.Exp, [hpn], [hh["Pnew"]], scale=0.125)
                psd, hpd = mmrot.next()
                MM(psd[:, 0:64], ones32, Pn[:, :], True, True, [hh["Pn"], hconst], [hpd])
                TT(den[:, :], psd[:, 0:64], Pnew[:, :], ALU.add, [hpd, hh["Pnew"]], [hh["den"]])
                if G["swa"] is not None:
                    kv = G["swa"]
                    TT(v3(den[:, :]), v3(den[:, :]), bcast_ap(smalls[:, 16 + 4 * kv:17 + 4 * kv], [(0, NS), (1, 4)]), ALU.add,
                       [hh["den"], hh["expsink"]], [hh["den"]])
                psv, hpv = mmrot.next()
                for b in range(NS):
                    MM(psv[:, 4 * b:4 * b + 4], Vc[:, b, :], Pn[:, 4 * b:4 * b + 4], True, True, [hh[("Vc", q4, i)] for q4 in range(4) for i in range(2)] + [hh["Pn"]], [hpv])
                TT(v3(num[:, :]), v3(Pnew[:, :]), bcast_ap(VTs[:, G["vi"], 0:1], [(1, NS), (0, 4)]), ALU.mult, [hh["Pnew"], hh[("VTs", G["vi"])]], [hh["num"]])
                TT(num[:, :], num[:, :], psv[:, 0:64], ALU.add, [hh["num"], hpv], [hh["num"]])
                if G["swa"] is not None:
                    kv = G["swa"]
                    RECIP(den[:, :], den[:, :], [hh["den"]], [hh["den"]])
                    TT(outv[:, :], num[:, :], den[:, :], ALU.mult, [hh["num"], hh["den"]], [hh["outv"]])
                    for s in range(4):
                        h = kv * 4 + s
                        rows = slice((h % 2) * 64, (h % 2) * 64 + 64)
                        CP(smix[rows, h // 2, :], outv[rows, s:64:4], [hh["outv"]], [hh[("smix", h // 2, h % 2)]])
                else:
                    if gi == 2:
                        CP(numD[:, :], num[:, :], [hh["num"]], [hh["numD"]])
                        CP(denD[:, :], den[:, :], [hh["den"]], [hh["denD"]])
                    else:
                        TT(numD[:, :], numD[:, :], num[:, :], ALU.add, [hh["num"], hh["numD"]], [hh["numD"]])
                        TT(denD[:, :], denD[:, :], den[:, :], ALU.add, [hh["den"], hh["denD"]], [hh["denD"]])
            RECIP(denD[:, :], denD[:, :], [hh["denD"]], [hh["denD"]])
            TT(outv[:, :], numD[:, :], denD[:, :], ALU.mult, [hh["numD"], hh["denD"]], [hh["outv"]])
            for s in range(4):
                rows = slice((s % 2) * 64, (s % 2) * 64 + 64)
                CP(smix[rows, 4 + s // 2, :], outv[rows, s:64:4], [hh["outv"]], [hh[("smix", 4 + s // 2, s % 2)]])
            P.fence()
            if L1STOP <= 3:
                A.reset(mL)
                return

            A.reset(mL)
            W = HW_
            ysb = A.alloc(8 * W, F32).rearrange("p (c n) -> p c n", c=8)
            sqbuf2 = [A.alloc(512, BF16), A.alloc(512, BF16)]
            tmp2 = A.alloc(512, F32)
            rstd2 = A.alloc(512, F32)
            ytmp = [A.alloc(512, F32), A.alloc(512, F32)]
            nb_ = push_slots(2)
            assert A.off <= mP

            def rhs_fn(k, tb):
                if tb == 4:
                    return smix[:, k, :], []
                return mixT[:, k, TBS[tb][0]:TBS[tb][0] + 512], []
            for hi_, (col0, tbs) in enumerate(HALVES):
                stat_ = out_proj(w_out_attn, 0, 6, rhs_fn, tbs, ysb, col0, sqbuf=sqbuf2)
                for tb in tbs:
                    postnorm_residual(1, 1, tb, ysb, col0, sqbuf2, tmp2, rstd2, ytmp, stat=stat_)
                if hi_ == 1:
                    pop_slots(nb_)
                    if STOP >= 4:
                        pre_mlp(1)
                P.fence()
            A.reset(mL)

        if STOP >= 3:
            layer1_mixer()
        if STOP >= 4:
            for hi_, (col0, tbs) in enumerate(HALVES):
                mlp_half(1, col0, tbs, pre_next=(lambda: pre_mlp(1)) if hi_ == 0 else None)

        A.reset(m0)
        yo = [A.alloc(1024, F32), A.alloc(1024, F32)]
        rot = Rot([0, 1, 2, 3])
        for t in range(17):
            y = yo[t % 2]
            hy = hh[("yo", t % 2)]
            rows = 128 if t < 16 else NS
            tbk = t // 4 if t < 16 else 4
            for g in range(2):
                ps, hp = rot.next()
                for i in range(4):
                    c = g * 4 + i
                    TR(ps[0:rows, i * 128:(i + 1) * 128], hT[:, c, t * 128:t * 128 + rows], ident, [hh[("hT", c, tbk)], hconst], [hp])
                if g == 0:
                    ACT(y[0:rows, 0:512], ps[0:rows, :], AF.Copy, [hp], [hy])
                else:
                    CP(y[0:rows, 512:1024], ps[0:rows, :], [hp], [hy])
            dst = y_p[t * 128:(t + 1) * 128, :] if t < 16 else y_s
            DMA("sp", dst, y[0:rows, :], reads=[hy])

        assert not PRE, list(PRE.keys())
        P.emit(st)
        build_program.stats = dict(P.stats)
        build_program.arena_peak = A.peak
    return nc


def _consts():
    c32 = np.zeros((128, 512), np.float32)
    c32[:, 0:128] = np.eye(128, dtype=np.float32)
    c32[0:64, 128:256] = 1.0
    c32[64:128, 256:384] = 1.0
    c32[:, 384:512] = 1.0
    cb = np.zeros((128, 768), np.float32)
    prot = np.zeros((128, 128), np.float32)
    for base in (0, 64):
        for i in range(8):
            prot[base + i + 8, base + i] = 1.0
            prot[base + i, base + i + 8] = 1.0
    cb[:, 0:128] = prot
    cb[:, 128:256] = 1.0
    p = np.arange(128)[:, None]
    f = np.arange(128)[None, :]
    cb[:, 256:384] = (f <= p)
    cb[:, 384:512] = (f >= p)
    cb[:, 512:640] = np.eye(128, dtype=np.float32)
    half = 8
    inv = (np.float32(500000.0) ** (-np.arange(half, dtype=np.float32) / half)).astype(np.float32)
    pos = np.concatenate([np.arange(NTOK, dtype=np.float32), np.full(NS, 8192.0, np.float32)])
    ang = (pos[None, :] * inv[:, None]).astype(np.float32)
    cs = np.zeros((128, 2, NT), np.float32)
    cs[:, 0, :] = 1.0
    for base in (0, 64):
        cs[base:base + 8, 0] = np.cos(ang)
        cs[base + 8:base + 16, 0] = np.cos(ang)
        cs[base:base + 8, 1] = -np.sin(ang)
        cs[base + 8:base + 16, 1] = np.sin(ang)
    return c32, cb, cs


_NC_CACHE = {}


def kernel(x_prompt, x_sample, state_conv_a, state_conv_b, cache_swa_kv, cache_dil0_kv, cache_dil1_kv,
           cache_dil2_kv, norm_g, w_in_conv, conv_a_w, conv_a_b, conv_a_ln_g, conv_a_ln_b, conv_b_w,
           w_out_conv, w_in_attn, attn_sinks, w_out_attn, mlp_w1, mlp_w2):
    f = lambda a: np.ascontiguousarray(np.asarray(a, dtype=np.float32))
    x_prompt, x_sample = f(x_prompt), f(x_sample)
    if "nc" not in _NC_CACHE:
        _NC_CACHE["nc"] = build_program()
    nc = _NC_CACHE["nc"]
    c32, cb, cs = _consts()
    prm1 = np.concatenate([f(norm_g).reshape(64, 128), f(conv_a_b).reshape(4, 128), f(conv_a_ln_g).reshape(4, 128),
                           f(conv_a_ln_b).reshape(4, 128), f(conv_b_w).reshape(12, 128)], 0)
    prm2 = f(conv_a_w).reshape(124, 128)
    wi = f(w_in_attn)[0]
    qs = lambda h: wi[:, h * 64:(h + 1) * 64]
    qd = lambda g, h: wi[:, 768 + 384 * g + h * 64: 768 + 384 * g + (h + 1) * 64]
    kd = lambda g: wi[:, 768 + 384 * g + 256: 768 + 384 * g + 320]
    vd = lambda g: wi[:, 768 + 384 * g + 320: 768 + 384 * g + 384]
    ks = lambda kv: wi[:, 512 + kv * 64: 512 + (kv + 1) * 64]
    vs = lambda kv: wi[:, 640 + kv * 64: 640 + (kv + 1) * 64]
    wq = np.concatenate([np.concatenate([qs(g), qs(4 + g)], 1) for g in range(4)] +
                        [np.concatenate([qd(0, g), qd(1, g)], 1) for g in range(4)] +
                        [np.concatenate([qd(2, 0), qd(2, 1)], 1), np.concatenate([qd(2, 2), qd(2, 3)], 1)], 1)
    wk = np.concatenate([ks(0), ks(1), kd(0), kd(1), kd(2), kd(2)], 1)
    wv = np.concatenate([vs(0), vs(1), vd(0), vd(1), vd(2)], 1)
    wvd = np.concatenate([vs(0), vs(0), vs(1), vs(1), vd(0), vd(0), vd(1), vd(1), vd(2), vd(2)], 1)
    shared = {
        "prm1": f(prm1), "prm2": prm2, "sinks": f(attn_sinks).reshape(1, 8), "cst32": c32, "cstb": cb, "cs": cs,
        "w_in_conv": f(w_in_conv)[0], "w_out_conv": f(w_out_conv)[0], "wq": f(wq), "wk": f(wk), "wv": f(wv), "wvd": f(wvd),
        "w_out_attn": f(w_out_attn)[0], "w1": f(mlp_w1), "w2": f(mlp_w2),
    }
    sca = f(state_conv_a)[0]
    scb = f(state_conv_b)[0]
    cswa = f(cache_swa_kv)[0]
    cds = [f(cache_dil0_kv)[0], f(cache_dil1_kv)[0], f(cache_dil2_kv)[0]]
    in_maps = []
    for i in range(8):
        s = slice(NS * i, NS * (i + 1))
        m = dict(shared)
        m["xp"] = x_prompt[i]
        m["xs"] = x_sample[s, 0, :]
        m["sca"] = sca[s].reshape(NS * 30, 512)
        m["scb"] = scb[s].reshape(NS * 2, 512)
        m["c_swa"] = cswa[s].reshape(NS, 128, 256)
        for g in range(3):
            m["c_d%d" % g] = cds[g][s].reshape(NS, cds[g].shape[1], 128)
        in_maps.append(m)
    res = run_bass_kernel_spmd(nc, in_maps, core_ids=list(range(8)))
    R = res.results
    cat = lambda k: np.stack([np.asarray(r[k]) for r in R], 0)
    y_prompt = cat("y_p")
    y_sample = np.concatenate([np.asarray(r["y_s"]) for r in R], 0).reshape(128, 1, 1024)
    a_p = cat("a_p")[None]
    a_s = np.concatenate([np.asarray(r["a_s"]) for r in R], 0)[None]
    b_p = cat("b_p")[None]
    b_s = np.concatenate([np.asarray(r["b_s"]) for r in R], 0)[None]
    swa_p = cat("swa_p").reshape(1, 8, 128, 2, 2, 64)
    swa_s = np.concatenate([np.asarray(r["swa_s"]) for r in R], 0).reshape(1, 128, 128, 2, 2, 64)
    outs = [y_prompt, y_sample, a_p, a_s, b_p, b_s, swa_p, swa_s]
    for g, Wg in enumerate((128, 512, 2048)):
        outs.append(cat("d%d_p" % g).reshape(1, 8, Wg, 2, 1, 64))
        outs.append(np.concatenate([np.asarray(r["d%d_s" % g]) for r in R], 0).reshape(1, 128, Wg, 2, 1, 64))
    return tuple(np.ascontiguousarray(o, dtype=np.float32) for o in outs)
```

```python
import os
import numpy as np
from contextlib import ExitStack
import concourse.bass as bass
import concourse.mybir as mybir
from concourse.bass_utils import run_bass_kernel_spmd

F32 = mybir.dt.float32
BF16 = mybir.dt.bfloat16
ALU = mybir.AluOpType
AF = mybir.ActivationFunctionType
AX = mybir.AxisListType

ENGS = ("pe", "act", "dve", "pool", "sp")

NTOK = 2048
NS = 16
NT = NTOK + NS
TBS = [(0, 512), (512, 512), (1024, 512), (1536, 512), (2048, 16)]
HALVES = [(0, [0, 1]), (1024, [2, 3, 4])]
HW_ = 1040
EPS = 1e-6
STOP = int(os.environ.get("MK_STOP", "99"))
SAFE_WAR = int(os.environ.get("MK_SAFE_WAR", "1"))
M4A = int(os.environ.get("MK_4A", "63"))
POOL_ADD = int(os.environ.get("MK_POOL_ADD", "0"))
L1STOP = int(os.environ.get("MK_L1STOP", "99"))
NOCOPY = int(os.environ.get("MK_NOCOPY", "0"))


class H:
    __slots__ = ("name", "w", "r", "excl")

    def __init__(self, name="", excl=False):
        self.name = name
        self.w = None
        self.r = []
        self.excl = excl


class HD(dict):
    def __missing__(self, k):
        v = H(str(k))
        self[k] = v
        return v


class Op:
    __slots__ = ("eng", "fn", "deps", "signal", "tick", "dma", "sem", "val", "fenced")


class Prog:
    def __init__(self, nc, ndma=8):
        self.nc = nc
        self.ops = {e: [] for e in ENGS}
        self.all = []
        self.ndma = ndma

    def op(self, eng, fn, reads=(), writes=(), dma=False):
        o = Op()
        o.eng, o.fn, o.dma, o.signal, o.tick, o.sem, o.val, o.fenced = eng, fn, dma, dma, 0, None, 0, False
        deps = []
        if any(h.excl for h in reads):
            writes = list(writes) + [h for h in reads if h.excl and h not in writes]
            reads = [h for h in reads if not h.excl]

        def add(p, kind):
            if p is None or p is o:
                return
            if p.eng == eng and not p.dma and not dma:
                if eng == "pe" or (kind == "WAR" and not SAFE_WAR):
                    return
            if p not in deps:
                deps.append(p)

        for h in reads:
            add(h.w, "RAW")
        for h in writes:
            add(h.w, "WAW")
            for r in h.r:
                add(r, "WAR")
        for h in reads:
            h.r.append(o)
        for h in writes:
            h.w = o
            h.r = []
        o.deps = deps
        self.ops[eng].append(o)
        self.all.append(o)
        return o

    def fence(self):
        lasts = [self.ops[e][-1] for e in ENGS if self.ops[e] and self.ops[e][-1].fn is not None]
        dmas = [o for o in self.all if o.dma and not o.fenced]
        for o in dmas:
            o.fenced = True
        for e in ENGS:
            o = Op()
            o.eng, o.fn, o.dma, o.signal, o.tick, o.sem, o.val, o.fenced = e, None, False, False, 0, None, 0, True
            o.deps = [p for p in lasts if p.eng != e or p.dma] + [d for d in dmas if d not in lasts]
            self.ops[e].append(o)
            self.all.append(o)

    def emit(self, stack):
        nc = self.nc
        for o in self.all:
            for d in o.deps:
                d.signal = True
        esem = {e: stack.enter_context(nc.semaphore("es_" + e)) for e in ENGS}
        dsem = {e: [stack.enter_context(nc.semaphore("ds_%s_%d" % (e, i))) for i in range(self.ndma)]
                for e in ENGS if any(o.dma for o in self.ops[e])}
        for e in ENGS:
            c = 0
            nd = 0
            dmas = []
            for o in self.ops[e]:
                if o.dma:
                    o.sem = dsem[e][nd % self.ndma]
                    o.val = 16 * (nd // self.ndma + 1)
                    if nd >= self.ndma:
                        prev = dmas[nd - self.ndma]
                        if prev not in o.deps:
                            o.deps.append(prev)
                    dmas.append(o)
                    nd += 1
                elif o.signal:
                    c += 1
                    o.tick = c
        block = stack.enter_context(nc.Block())
        prog = self
        self.stats = {}

        def section(e):
            def body(eng):
                known = {}
                nwait = 0
                for o in prog.ops[e]:
                    for d in o.deps:
                        if d.dma:
                            sem, val = d.sem, d.val
                        else:
                            sem, val = esem[d.eng], d.tick
                        key = id(sem)
                        if known.get(key, 0) >= val:
                            continue
                        eng.wait_ge(sem, val)
                        nwait += 1
                        known[key] = val
                    if o.fn is None:
                        continue
                    ins = o.fn(eng)
                    if o.dma:
                        ins.then_inc(o.sem, 16)
                    elif o.signal:
                        ins.then_inc(esem[e], 1)
                last = {}
                for o in prog.ops[e]:
                    if o.dma:
                        last[id(o.sem)] = (o.sem, o.val)
                for sem, val in last.values():
                    if known.get(id(sem), 0) < val:
                        eng.wait_ge(sem, val)
                prog.stats[e] = (len(prog.ops[e]), nwait)
            return body

        block.tensor(section("pe"))
        block.scalar(section("act"))
        block.vector(section("dve"))
        block.gpsimd(section("pool"))
        block.sync(section("sp"))


class Arena:
    def __init__(self, t32, cap_bytes):
        self.t32 = t32
        self.t16 = t32.bitcast(BF16)
        self.cap = cap_bytes
        self.off = 0
        self.peak = 0

    def alloc(self, ncols, dtype):
        esz = 4 if dtype == F32 else 2
        off = (self.off + 31) // 32 * 32
        nb = ncols * esz
        assert off + nb <= self.cap, ("arena overflow", off, nb, self.cap)
        self.off = off + nb
        self.peak = max(self.peak, self.off)
        if dtype == F32:
            return self.t32[:, off // 4: off // 4 + ncols]
        return self.t16[:, off // 2: off // 2 + ncols]

    def mark(self):
        return self.off

    def reset(self, m=0):
        self.off = m


def bcast_ap(ap, dims):
    return bass.AP(ap.tensor, ap.offset, [list(ap.ap[0])] + [[s, n] for s, n in dims])


def build_program():
    nc = bass.Bass("TRN2", target_bir_lowering=False)

    def din(name, shape):
        return nc.dram_tensor(name, list(shape), F32, kind="ExternalInput").ap()

    def dout(name, shape):
        return nc.dram_tensor(name, list(shape), F32, kind="ExternalOutput").ap()

    xp = din("xp", [NTOK, 1024])
    xs = din("xs", [NS, 1024])
    sca = din("sca", [NS * 30, 512])
    scb = din("scb", [NS * 2, 512])
    c_swa = din("c_swa", [NS, 128, 256])
    c_d = [din("c_d0", [NS, 128, 128]), din("c_d1", [NS, 512, 128]), din("c_d2", [NS, 2048, 128])]
    prm1 = din("prm1", [88, 128])
    prm2 = din("prm2", [124, 128])
    sinks = din("sinks", [1, 8])
    cst32_d = din("cst32", [128, 512])
    cstb_d = din("cstb", [128, 768])
    cs_d = din("cs", [128, 2, NT])
    w_in_conv = din("w_in_conv", [1024, 2560])
    w_out_conv = din("w_out_conv", [1024, 1024])
    wq_d = din("wq", [1024, 1280])
    wk_d = din("wk", [1024, 384])
    wv_d = din("wv", [1024, 320])
    wvd_d = din("wvd", [1024, 640])
    w_out_attn = din("w_out_attn", [768, 1024])
    w1_d = din("w1", [2, 1024, 4096])
    w2_d = din("w2", [2, 4096, 1024])

    y_p = dout("y_p", [NTOK, 1024])
    y_s = dout("y_s", [NS, 1024])
    a_p = dout("a_p", [30, 512])
    a_s = dout("a_s", [NS, 30, 512])
    b_p = dout("b_p", [2, 512])
    b_s = dout("b_s", [NS, 2, 512])
    swa_p = dout("swa_p", [128, 256])
    swa_s = dout("swa_s", [NS, 128, 256])
    d_p = [dout("d0_p", [128, 128]), dout("d1_p", [512, 128]), dout("d2_p", [2048, 128])]
    d_s = [dout("d0_s", [NS, 128, 128]), dout("d1_s", [NS, 512, 128]), dout("d2_s", [NS, 2048, 128])]

    st = ExitStack()
    with st:
        P = Prog(nc)
        sb = lambda name, shape, dt: st.enter_context(nc.sbuf_tensor(name, list(shape), dt))
        hT = sb("hT", [128, 8, NT], F32)
        cst32 = sb("cst32s", [128, 512], F32)
        cstb = sb("cstbs", [128, 768], BF16)
        prmT = sb("prmT", [128, 88], F32)
        wAT = sb("wAT", [128, 124], F32)
        smalls = sb("smalls", [128, 32], F32)
        halo = sb("halo", [128, 8, 30], BF16)
        NSLOT = 2
        SLOT = 4096
        wslots = [sb("wslot%d" % i, [128, SLOT], BF16) for i in range(NSLOT)]
        ARENA_BYTES = (nc.sbuf_bytes_remaining // 64) * 64 - 256
        A = Arena(sb("arena", [128, ARENA_BYTES // 4], F32), ARENA_BYTES)
        psb = [st.enter_context(nc.psum_tensor("psb%d" % i, [128, 512], F32)) for i in range(8)]
        hps = [H("ps%d" % i, excl=True) for i in range(8)]

        ident = cst32[:, 0:128]
        selh = [cst32[:, 128:256], cst32[:, 256:384]]
        ones32 = cst32[:, 384:512]
        prot = cstb[:, 0:128]
        onesb = cstb[:, 128:256]
        maskpd = cstb[:, 256:512]
        identb = cstb[:, 512:640]
        epsc = smalls[:, 0:1]

        hh = HD()
        hconst = hh["const"]

        def MM(ps_ap, lhsT, rhs, start, stop, reads, writes):
            P.op("pe", lambda e: e.matmul(ps_ap, lhsT=lhsT, rhs=rhs, start=start, stop=stop), reads=reads, writes=writes)

        def TR(out, in_, idn, reads, writes):
            P.op("pe", lambda e: e.transpose(out=out, in_=in_, identity=idn), reads=reads, writes=writes)

        def ACT(out, in_, func, reads, writes, bias=None, scale=None):
            kw = {}
            if bias is not None:
                kw["bias"] = bias
            if scale is not None:
                kw["scale"] = scale
            P.op("act", lambda e: e.activation(out=out, in_=in_, func=func, **kw), reads=reads, writes=writes)

        def TT(out, in0, in1, op, reads, writes, eng="dve"):
            P.op(eng, lambda e: e.tensor_tensor(out=out, in0=in0, in1=in1, op=op), reads=reads, writes=writes)

        def STT(out, in0, scalar, in1, op0, op1, reads, writes):
            P.op("dve", lambda e: e.scalar_tensor_tensor(out=out, in0=in0, scalar=scalar, in1=in1, op0=op0, op1=op1), reads=reads, writes=writes)

        def TS(out, in0, s1, op0, reads, writes, s2=None, op1=None, eng="dve"):
            if op1 is None:
                P.op(eng, lambda e: e.tensor_scalar(out=out, in0=in0, scalar1=s1, scalar2=None, op0=op0), reads=reads, writes=writes)
            else:
                P.op(eng, lambda e: e.tensor_scalar(out=out, in0=in0, scalar1=s1, scalar2=s2, op0=op0, op1=op1), reads=reads, writes=writes)

        def CP(out, in_, reads, writes, eng="dve"):
            P.op(eng, lambda e: e.tensor_copy(out=out, in_=in_), reads=reads, writes=writes)

        def RED(out, in_, reads, writes):
            P.op("dve", lambda e: e.tensor_reduce(out=out, in_=in_, axis=AX.X, op=ALU.add), reads=reads, writes=writes)

        def RECIP(out, in_, reads, writes):
            P.op("dve", lambda e: e.reciprocal(out=out, in_=in_), reads=reads, writes=writes)

        def MEMSET(ap, val, writes, eng="dve"):
            P.op(eng, lambda e: e.memset(ap, val), writes=writes)

        def DMA(eng, out, in_, reads=(), writes=()):
            P.op(eng, lambda e: e.dma_start(out=out, in_=in_), reads=reads, writes=writes, dma=True)

        class Rot:
            def __init__(self, banks):
                self.banks = banks
                self.i = 0

            def next(self):
                b = self.banks[self.i % len(self.banks)]
                self.i += 1
                return psb[b], hps[b]

        wrot = [0]

        NSL = [NSLOT]

        def wslot_next():
            i = wrot[0] % NSL[0]
            wrot[0] += 1
            return wslots[i], hh[("wslot", i)]

        def gcol(l, w, c):
            j = (l * 4 + w) * 8 + c
            return prmT[:, j:j + 1]

        DMA("sp", cst32[:], cst32_d, writes=[hconst])
        DMA("pool", cstb[:], cstb_d, writes=[hconst])
        MEMSET(smalls[:, 0:8], EPS, [hh["smalls"]])
        DMA("sp", smalls[:, 8:16], bass.AP(sinks.tensor, 0, [[0, 128], [1, 8]]), writes=[hh["sinks"]])
        ACT(smalls[:, 16:24], smalls[:, 8:16], AF.Exp, [hh["sinks"]], [hh["expsink"]])
        m0 = A.mark()
        p1 = A.alloc(128, F32)
        p2 = A.alloc(128, F32)
        DMA("sp", p1[0:88, :], prm1, writes=[hh["p1"]])
        DMA("sp", p2[0:124, :], prm2, writes=[hh["p2"]])
        TR(psb[0][:, 0:88], p1[0:88, :], ident[0:88, 0:88], [hh["p1"], hconst], [hps[0]])
        ACT(prmT[:], psb[0][:, 0:88], AF.Copy, [hps[0]], [hconst])
        TR(psb[1][:, 0:124], p2[0:124, :], ident[0:124, 0:124], [hh["p2"], hconst], [hps[1]])
        ACT(wAT[:], psb[1][:, 0:124], AF.Copy, [hps[1]], [hconst])

        xin = [A.alloc(1024, F32), A.alloc(1024, F32)]
        rot = Rot([2, 3, 4, 5])
        for t in range(17):
            xi = xin[t % 2]
            hx = hh[("xin", t % 2)]
            rows = 128 if t < 16 else NS
            src = xp[t * 128:(t + 1) * 128, :] if t < 16 else xs
            DMA("sp", xi[0:rows, :], src, writes=[hx])
            for g in range(2):
                ps, hp = rot.next()
                for i in range(4):
                    c = g * 4 + i
                    TR(ps[:, i * rows:(i + 1) * rows], xi[0:rows, c * 128:(c + 1) * 128], ident[0:rows, 0:rows], [hx, hconst], [hp])
                dst = hT[:, g * 4:(g + 1) * 4, t * 128:t * 128 + rows]
                srcp = ps[:, 0:4 * rows].rearrange("p (i r) -> p i r", i=4)
                wr = [hh[("hT", g * 4 + i, t // 4 if t < 16 else 4)] for i in range(4)]
                if g == 0:
                    ACT(dst, srcp, AF.Copy, [hp], wr)
                else:
                    CP(dst, srcp, [hp], wr)
        PHASE1_FENCE = True

        statrot = Rot([6, 7])
        mmrot = Rot([0, 1, 2, 3, 4, 5])

        def rstd_from_ps(ps_stat, hstat, N, scale, out_rstd, hout, tmp, htmp):
            ACT(tmp[:, 0:N], ps_stat[:, 0:N], AF.Sqrt, [hstat, hh["smalls"]], [htmp], bias=epsc, scale=scale)
            RECIP(out_rstd[:, 0:N], tmp[:, 0:N], [htmp], [hout])

        def sumsq_stat(src_fn, rd_fn, N, sqbuf):
            ps, hp = statrot.next()
            for c in range(8):
                sq = sqbuf[c % 2]
                hsq = hh[("sq", c % 2)]
                ACT(sq[:, 0:N], src_fn(c), AF.Square, rd_fn(c), [hsq])
                MM(ps[:, 0:N], onesb, sq[:, 0:N], c == 0, c == 7, [hsq, hconst], [hp])
            return ps, hp

        def norm_block(l, w, tb, dstT, dst_col0, sqbuf, tmp, rstd, htb=None):
            c0, N = TBS[tb]
            ps, hp = sumsq_stat(lambda c: hT[:, c, c0:c0 + N], lambda c: [hh[("hT", c, tb)]], N, sqbuf)
            rstd_from_ps(ps, hp, N, 1.0 / 1024, rstd, hh["rstd"], tmp, hh["rtmp"])
            for c in range(8):
                STT(dstT[:, c, c0 - dst_col0:c0 - dst_col0 + N], hT[:, c, c0:c0 + N], gcol(l, w, c), rstd[:, 0:N], ALU.mult, ALU.mult,
                    [hh[("hT", c, tb)], hh["rstd"], hconst], [hh[("uT", c, tb if htb is None else htb)]])

        def push_slots(n):
            nb = len(wslots)
            wslots.extend([A.alloc(SLOT, BF16) for _ in range(n)])
            NSL[0] = len(wslots)
            return nb

        def pop_slots(nb):
            del wslots[nb:]
            NSL[0] = len(wslots)

        PRE = {}

        def wkey(wd, row0, Kc, cols):
            return (wd.tensor.name, int(wd.offset), row0, Kc, tuple(cols))

        def preload(wd, row0, Kc, cols):
            assert NSL[0] == NSLOT
            PRE[wkey(wd, row0, Kc, cols)] = load_wchunks(wd, row0, Kc, cols)

        def load_wchunks(wd, row0, Kc, cols):
            k_ = wkey(wd, row0, Kc, cols)
            if k_ in PRE:
                return PRE.pop(k_)
            slot, hs = wslot_next()
            n = len(cols)
            assert n * Kc * 128 <= SLOT
            v = slot[:, 0:n * Kc * 128].rearrange("p (i k m) -> p i k m", i=n, k=Kc)
            for i, col0 in enumerate(cols):
                src = wd[row0:row0 + Kc * 128, col0:col0 + 128].rearrange("(k p) m -> p k m", p=128)
                DMA("pool", v[:, i], src, writes=[hs])
            return v, hs

        def postnorm_residual(l, w, tb, ysb, ycol0, sqbuf, tmp, rstd, ytmp, stat=None):
            c0, N = TBS[tb]
            lc = c0 - ycol0
            if stat is not None:
                ps, hp = stat[tb]
            else:
                ps, hp = sumsq_stat(lambda c: ysb[:, c, lc:lc + N], lambda c: [hh[("ysb", c, tb)]], N, sqbuf)
            rstd_from_ps(ps, hp, N, 1.0 / 1024, rstd, hh["rstd"], tmp, hh["rtmp"])
            for c in range(8):
                yt = ytmp[c % 2]
                hyt = hh[("ytmp", c % 2)]
                STT(yt[:, 0:N], ysb[:, c, lc:lc + N], gcol(l, w, c), rstd[:, 0:N], ALU.mult, ALU.mult,
                    [hh[("ysb", c, tb)], hh["rstd"], hconst], [hyt])
                TT(hT[:, c, c0:c0 + N], hT[:, c, c0:c0 + N], yt[:, 0:N], ALU.add, [hyt, hh[("hT", c, tb)]], [hh[("hT", c, tb)]],
                   eng=("pool" if POOL_ADD else "dve"))

        mmrot5 = Rot([0, 1, 2, 3, 4])

        def out_proj(wd, row0, Kc, rhs_fn, tbs, ysb, ycol0, accumulate=False, sqbuf=None):
            stat = None
            if sqbuf is not None:
                stat = {tb: (psb[5 + i], hps[5 + i]) for i, tb in enumerate(tbs)}
            rot_ = mmrot5 if sqbuf is not None else mmrot
            pend = []
            it_ = 0
            for m in range(8):
                v, hs = load_wchunks(wd, row0, Kc, [m * 128])
                for tb in tbs:
                    c0, N = TBS[tb]
                    ps, hp = rot_.next()
                    for k in range(Kc):
                        rap, rh = rhs_fn(k, tb)
                        MM(ps[:, 0:N], v[:, 0, k, :], rap, k == 0, k == Kc - 1, [hs] + rh, [hp])
                    while pend:
                        a_ = pend.pop(0)
                        MM(*a_)
                    dst = ysb[:, m, c0 - ycol0:c0 - ycol0 + N]
                    if not accumulate:
                        ACT(dst, ps[:, 0:N], AF.Copy, [hp], [hh[("ysb", m, tb)]])
                    else:
                        TT(dst, dst, ps[:, 0:N], ALU.add, [hp, hh[("ysb", m, tb)]], [hh[("ysb", m, tb)]])
                    if stat is not None:
                        sq = sqbuf[it_ % 2]
                        hsq = hh[("sq", it_ % 2)]
                        it_ += 1
                        ACT(sq[:, 0:N], dst, AF.Square, [hh[("ysb", m, tb)]], [hsq])
                        pst, hpst = stat[tb]
                        pend.append((pst[:, 0:N], onesb, sq[:, 0:N], m == 0, m == 7, [hsq, hconst], [hpst]))
            while pend:
                MM(*pend.pop(0))
            return stat

        def pre_l0a():
            preload(w_in_conv, 0, 8, [0, 512, 1024, 2048])
            preload(w_in_conv, 0, 8, [1536])

        def pre_outproj(wd, Kc):
            preload(wd, 0, Kc, [0])
            preload(wd, 0, Kc, [128])

        def pre_mlp(l):
            preload(w1_d[l], 0, 8, [0, 128, 256, 384])
            preload(w1_d[l], 0, 8, [512, 640, 768, 896])

        def pre_l1a():
            preload(wq_d, 0, 8, [0, 128, 256, 384])
            preload(wq_d, 0, 8, [512, 640, 768, 896])

        def mlp_half(l, col0, tbs, pre_next=None):
            m1 = A.mark()
            W = HW_
            uT = A.alloc(8 * W, BF16).rearrange("p (c n) -> p c n", c=8)
            hid = A.alloc(16 * W, BF16).rearrange("p (c n) -> p c n", c=16)
            ysb = A.alloc(8 * W, F32).rearrange("p (c n) -> p c n", c=8)
            sqbuf = [A.alloc(512, BF16), A.alloc(512, BF16)]
            rl = [A.alloc(512, BF16), A.alloc(512, BF16)]
            tmp = A.alloc(512, F32)
            rstd = A.alloc(512, F32)
            ytmp = [A.alloc(512, F32), A.alloc(512, F32)]
            nb_ = push_slots(2)
            for tb in tbs:
                norm_block(l, 2, tb, uT, col0, sqbuf, tmp, rstd)
            for hf in range(2):
                for mg in range(4):
                    ms = [hf * 16 + mg * 4 + i for i in range(4)]
                    v, hs = load_wchunks(w1_d[l], 0, 8, [m * 128 for m in ms])
                    for i, m in enumerate(ms):
                        for tb in tbs:
                            c0, N = TBS[tb]
                            lc = c0 - col0
                            ps, hp = mmrot.next()
                            for k in range(8):
                                MM(ps[:, 0:N], v[:, i, k, :], uT[:, k, lc:lc + N], k == 0, k == 7, [hs, hh[("uT", k, tb)]], [hp])
                            r = rl[(m + tb) % 2]
                            hr = hh[("rl", (m + tb) % 2)]
                            ACT(r[:, 0:N], ps[:, 0:N], AF.Relu, [hp], [hr])
                            TT(hid[:, m % 16, lc:lc + N], r[:, 0:N], r[:, 0:N], ALU.mult, [hr], [hh[("hid", m % 16, tb)]])
                stat_ = out_proj(w2_d[l], hf * 2048, 16,
                                 lambda k, tb: (hid[:, k, TBS[tb][0] - col0:TBS[tb][0] - col0 + TBS[tb][1]], [hh[("hid", k, tb)]]),
                                 tbs, ysb, col0, accumulate=(hf == 1), sqbuf=(sqbuf if hf == 1 else None))
            for tb in tbs:
                postnorm_residual(l, 3, tb, ysb, col0, sqbuf, tmp, rstd, ytmp, stat=stat_)
            pop_slots(nb_)
            if pre_next is not None:
                pre_next()
            P.fence()
            A.reset(m1)

        def layer0_half(hi, col0, tbs):
            m1 = A.mark()
            W = HW_
            uT = A.alloc(8 * W, BF16).rearrange("p (c n) -> p c n", c=8)
            mixT = uT
            ga = A.alloc(4 * (30 + W), BF16).rearrange("p (c n) -> p c n", c=4)
            zb = A.alloc(4 * (2 + W), BF16).rearrange("p (c n) -> p c n", c=4)
            gb = A.alloc(4 * W, BF16).rearrange("p (c n) -> p c n", c=4)
            m2 = A.mark()
            diag = A.alloc(31 * 128, BF16).rearrange("p (j m) -> p j m", j=31)
            diagB = A.alloc(12 * 128, BF16).rearrange("p (j m) -> p j m", j=12)
            cvo = A.alloc(4 * W, F32).rearrange("p (c n) -> p c n", c=4)
            reg32 = A.alloc(2048, F32)
            xq16 = A.t16[:, 2 * reg32.offset:2 * reg32.offset + 4096]
            xb = xq16[:, 0:2048].rearrange("p (c n) -> p c n", c=4)
            sq4 = xq16[:, 2048:4096].rearrange("p (c n) -> p c n", c=4)
            sqbuf = [A.alloc(512, BF16), A.alloc(512, BF16)]
            tmpa = A.alloc(512, F32)
            tmpg = A.alloc(512, F32)
            tmp = A.alloc(512, F32)
            rstd = A.alloc(512, F32)
            mean = A.alloc(512, F32)
            msq = A.alloc(512, F32)
            t1 = [A.alloc(512, F32), A.alloc(512, F32)]
            tails = A.alloc(4 * 64, F32).rearrange("p (c n) -> p c n", c=4)
            sAT = A.alloc(4 * NS * 30, F32).rearrange("p (c n) -> p c n", c=4)
            sBT = A.alloc(4 * NS * 2, F32).rearrange("p (c n) -> p c n", c=4)
            stg = A.alloc(512, F32)
            gas = A.alloc(4 * NS, F32).rearrange("p (c n) -> p c n", c=4)
            prodA = A.alloc(NS * 30, F32)
            hga = [hh[("ga", c)] for c in range(4)]
            hzb = [hh[("zb", c)] for c in range(4)]
            nb2a = push_slots(1)

            if hi == 0:
                MEMSET(ga[:, :, 0:30], 0.0, hga)
                MEMSET(zb[:, :, 0:2], 0.0, hzb)
            else:
                CP(ga[:, :, 0:30], halo[:, 0:4, 0:30], [hh["halo"]], hga)
                CP(zb[:, :, 0:2], halo[:, 4:8, 0:2], [hh["halo"]], hzb)

            for tb in tbs:
                norm_block(0, 0, tb, uT, col0, sqbuf, tmp, rstd)

            for c in range(4):
                mcols = [c * 128, 512 + c * 128, 1024 + c * 128, 2048 + c * 128, 1536 + c * 128]
                v, hs = load_wchunks(w_in_conv, 0, 8, mcols[0:4])
                v2, hs2 = load_wchunks(w_in_conv, 0, 8, mcols[4:5])
                for tb in tbs:
                    c0, N = TBS[tb]
                    lc = c0 - col0
                    pss = []
                    for i in range(5):
                        ps, hp = mmrot.next()
                        vv, hv, ii = (v, hs, i) if i < 4 else (v2, hs2, 0)
                        for k in range(8):
                            MM(ps[:, 0:N], vv[:, ii, k, :], uT[:, k, lc:lc + N], k == 0, k == 7, [hv, hh[("uT", k, tb)]], [hp])
                        pss.append((ps, hp))
                    (pa, ha), (pg, hg), (px, hx_), (pc, hc), (pb, hb) = pss
                    ACT(tmpg[:, 0:N], pg[:, 0:N], AF.Sigmoid, [hg], [hh["tmpg"]])
                    TT(ga[:, c, 30 + lc:30 + lc + N], pa[:, 0:N], tmpg[:, 0:N], ALU.mult, [ha, hh["tmpg"]], [hga[c]])
                    if tb == 3:
                        TT(tails[:, c, 0:30], pa[:, 482:512], tmpg[:, 482:512], ALU.mult, [ha, hh["tmpg"]], [hh[("tails", c)]])
                    if tb == 4:
                        TT(gas[:, c, :], pa[:, 0:NS], tmpg[:, 0:NS], ALU.mult, [ha, hh["tmpg"]], [hh[("gas", c)]])
                    ACT(tmpa[:, 0:N], px[:, 0:N], AF.Copy, [hx_], [hh["tmpa"]])
                    TT(zb[:, c, 2 + lc:2 + lc + N], pc[:, 0:N], tmpa[:, 0:N], ALU.mult, [hc, hh["tmpa"]], [hzb[c]])
                    if tb == 3:
                        TT(tails[:, c, 32:34], pc[:, 510:512], tmpa[:, 510:512], ALU.mult, [hc, hh["tmpa"]], [hh[("tails", c)]])
                    if tb == 4:
                        TT(tails[:, c, 40:56], pc[:, 0:NS], tmpa[:, 0:NS], ALU.mult, [hc, hh["tmpa"]], [hh[("tails", c)]])
                    ACT(gb[:, c, lc:lc + N], pb[:, 0:N], AF.Copy, [hb], [hh[("gb", c)]])

            if hi == 1:
                stA = reg32.rearrange("p (g n) -> p g n", g=4)
                stB = tmp
                hstA = [hh[("xb", c)] for c in range(4)] + [hh[("sq4", c)] for c in range(4)]
                DMA("sp", stA[0:120, :, :], sca.rearrange("(g r) n -> r g n", r=120), writes=hstA)
                DMA("sp", stB[0:32, :], scb, writes=[hh["rtmp"]])
                for c in range(4):
                    ps, hp = mmrot.next()
                    for g in range(4):
                        TR(ps[:, g * 120:(g + 1) * 120], stA[0:120, g, c * 128:(c + 1) * 128], ident[0:120, 0:120], hstA + [hconst], [hp])
                    ACT(sAT[:, c, :], ps[:, 0:480], AF.Copy, [hp], [hh[("sAT", c)]])
                ps, hp = mmrot.next()
                for c in range(4):
                    TR(ps[:, c * 32:(c + 1) * 32], stB[0:32, c * 128:(c + 1) * 128], ident[0:32, 0:32], [hh["rtmp"], hconst], [hp])
                ACT(sBT[:, :, :], ps[:, 0:128].rearrange("p (c n) -> p c n", c=4), AF.Copy, [hp], [hh["sBT"]])

            for j in range(3):
                for c in range(4):
                    TS(diagB[:, j * 4 + c, :], identb, prmT[:, 76 + j * 4 + c:77 + j * 4 + c], ALU.mult, [hconst], [hh["diagB"]])
            for c in range(4):
                for j in range(31):
                    if j % 2 == 0:
                        TS(diag[:, j, :], identb, wAT[:, j * 4 + c:j * 4 + c + 1], ALU.mult, [hconst], [hh[("diag", j)]])
                    else:
                        ACT(diag[:, j, :], identb, AF.Copy, [hconst], [hh[("diag", j)]], scale=wAT[:, j * 4 + c:j * 4 + c + 1])
                for tb in tbs:
                    c0, N = TBS[tb]
                    lc = c0 - col0
                    hcv = hh[("cvo", c, tb)]
                    if tb < 4:
                        ps, hp = mmrot.next()
                        for j in range(31):
                            MM(ps[:, 0:N], diag[:, j, :], ga[:, c, lc + j:lc + j + N], j == 0, j == 30, [hh[("diag", j)], hga[c]], [hp])
                        ACT(cvo[:, c, lc:lc + N], ps[:, 0:N], AF.Identity, [hp, hconst], [hcv], bias=prmT[:, 64 + c:65 + c], scale=1.0)
                    else:
                        wv_ = bcast_ap(wAT[:, c:c + 1], [(0, NS), (4, 30)])
                        pA = prodA.rearrange("p (b j) -> p b j", b=NS)
                        TT(pA, sAT[:, c, :].rearrange("p (b j) -> p b j", b=NS), wv_, ALU.mult, [hh[("sAT", c)], hconst], [hh["prodA"]])
                        RED(cvo[:, c, lc:lc + NS], pA, [hh["prodA"]], [hcv])
                        STT(cvo[:, c, lc:lc + NS], gas[:, c, :], wAT[:, 120 + c:121 + c], cvo[:, c, lc:lc + NS], ALU.mult, ALU.add,
                            [hh[("gas", c)], hcv, hconst], [hcv])
                        TS(cvo[:, c, lc:lc + NS], cvo[:, c, lc:lc + NS], prmT[:, 64 + c:65 + c], ALU.add, [hcv, hconst], [hcv])
            for tb in tbs:
                c0, N = TBS[tb]
                lc = c0 - col0
                psm, hpm = statrot.next()
                pse, hpe = statrot.next()
                for c in range(4):
                    CP(xb[:, c, 0:N], cvo[:, c, lc:lc + N], [hh[("cvo", c, tb)]], [hh[("xb", c)]])
                    ACT(sq4[:, c, 0:N], cvo[:, c, lc:lc + N], AF.Square, [hh[("cvo", c, tb)]], [hh[("sq4", c)]])
                for c in range(4):
                    MM(psm[:, 0:N], onesb, xb[:, c, 0:N], c == 0, c == 3, [hh[("xb", c)], hconst], [hpm])
                for c in range(4):
                    MM(pse[:, 0:N], onesb, sq4[:, c, 0:N], c == 0, c == 3, [hh[("sq4", c)], hconst], [hpe])
                ACT(mean[:, 0:N], psm[:, 0:N], AF.Copy, [hpm], [hh["mean"]], scale=1.0 / 512)
                TT(msq[:, 0:N], mean[:, 0:N], mean[:, 0:N], ALU.mult, [hh["mean"]], [hh["msq"]])
                STT(msq[:, 0:N], pse[:, 0:N], 1.0 / 512, msq[:, 0:N], ALU.mult, ALU.subtract, [hpe, hh["msq"]], [hh["msq"]])
                ACT(tmp[:, 0:N], msq[:, 0:N], AF.Sqrt, [hh["msq"], hh["smalls"]], [hh["rtmp"]], bias=epsc, scale=1.0)
                RECIP(rstd[:, 0:N], tmp[:, 0:N], [hh["rtmp"]], [hh["rstd"]])
                for c in range(4):
                    tt = t1[c % 2]
                    ht = hh[("t1", c % 2)]
                    TT(tt[:, 0:N], cvo[:, c, lc:lc + N], mean[:, 0:N], ALU.subtract, [hh[("cvo", c, tb)], hh["mean"]], [ht])
                    TT(tt[:, 0:N], tt[:, 0:N], rstd[:, 0:N], ALU.mult, [ht, hh["rstd"]], [ht])
                    ACT(mixT[:, c, lc:lc + N], tt[:, 0:N], AF.Silu, [ht, hconst], [hh[("uT", c, tb)]],
                        bias=prmT[:, 72 + c:73 + c], scale=prmT[:, 68 + c:69 + c])
                for c in range(4):
                    hmx = hh[("uT", 4 + c, tb)]
                    if tb < 4:
                        ps, hp = mmrot.next()
                        for j in range(3):
                            MM(ps[:, 0:N], diagB[:, j * 4 + c, :], zb[:, c, lc + j:lc + j + N], j == 0, j == 2, [hh["diagB"], hzb[c]], [hp])
                        TT(mixT[:, 4 + c, lc:lc + N], ps[:, 0:N], gb[:, c, lc:lc + N], ALU.mult, [hp, hh[("gb", c)]], [hmx])
                    else:
                        tt = t1[c % 2]
                        ht = hh[("t1", c % 2)]
                        sb_ = sBT[:, c, :].rearrange("p (b j) -> p b j", j=2)
                        TS(tt[:, 0:NS], sb_[:, :, 0], prmT[:, 76 + c:77 + c], ALU.mult, [hh["sBT"], hconst], [ht])
                        STT(tt[:, 0:NS], sb_[:, :, 1], prmT[:, 80 + c:81 + c], tt[:, 0:NS], ALU.mult, ALU.add, [hh["sBT"], ht, hconst], [ht])
                        STT(tt[:, 0:NS], tails[:, c, 40:56], prmT[:, 84 + c:85 + c], tt[:, 0:NS], ALU.mult, ALU.add, [hh[("tails", c)], ht, hconst], [ht])
                        TT(mixT[:, 4 + c, lc:lc + NS], tt[:, 0:NS], gb[:, c, lc:lc + NS], ALU.mult, [ht, hh[("gb", c)]], [hmx])

            if hi == 1:
                def tm_out(src_fn, rows, dst, rd_fn):
                    ps, hp = mmrot.next()
                    for c in range(4):
                        TR(ps[0:rows, c * 128:(c + 1) * 128], src_fn(c), ident, rd_fn(c) + [hconst], [hp])
                    ACT(stg[0:rows, :], ps[0:rows, :], AF.Copy, [hp], [hh["stg"]])
                    DMA("sp", dst, stg[0:rows, :], reads=[hh["stg"]])
                tm_out(lambda c: tails[:, c, 0:30], 30, a_p, lambda c: [hh[("tails", c)]])
                tm_out(lambda c: tails[:, c, 32:34], 2, b_p, lambda c: [hh[("tails", c)]])
                DMA("sp", a_s[:, 0:29, :], sca.rearrange("(b j) n -> b j n", j=30)[:, 1:30, :])
                DMA("sp", b_s[:, 0:1, :], scb.rearrange("(b j) n -> b j n", j=2)[:, 1:2, :])
                tm_out(lambda c: gas[:, c, :], NS, a_s[:, 29, :], lambda c: [hh[("gas", c)]])
                tm_out(lambda c: tails[:, c, 40:56], NS, b_s[:, 1, :], lambda c: [hh[("tails", c)]])
            else:
                CP(halo[:, 0:4, 0:30], ga[:, :, 1024:1054], hga, [hh["halo"]])
                CP(halo[:, 4:8, 0:2], zb[:, :, 1024:1026], hzb, [hh["halo"]])
            pop_slots(nb2a)
            pre_outproj(w_out_conv, 8)
            P.fence()
            A.reset(m2)
            ysb = A.alloc(8 * W, F32).rearrange("p (c n) -> p c n", c=8)
            sqbuf2 = [A.alloc(512, BF16), A.alloc(512, BF16)]
            tmp2 = A.alloc(512, F32)
            rstd2 = A.alloc(512, F32)
            ytmp = [A.alloc(512, F32), A.alloc(512, F32)]
            nb_ = push_slots(2)
            stat_ = out_proj(w_out_conv, 0, 8,
                             lambda k, tb: (mixT[:, k, TBS[tb][0] - col0:TBS[tb][0] - col0 + TBS[tb][1]], [hh[("uT", k, tb)]]),
                             tbs, ysb, col0, sqbuf=sqbuf2)
            for tb in tbs:
                postnorm_residual(0, 1, tb, ysb, col0, sqbuf2, tmp2, rstd2, ytmp, stat=stat_)
            pop_slots(nb_)
            if STOP >= 2:
                pre_mlp(0)
            P.fence()
            A.reset(m1)

        RUN0 = STOP >= 1 and not int(os.environ.get('MK_SKIP0', '0'))
        if RUN0:
            pre_l0a()
        P.fence()
        A.reset(m0)
        if RUN0:
            for hi, (col0, tbs) in enumerate(HALVES):
                layer0_half(hi, col0, tbs)
                if STOP >= 2:
                    nxt = pre_l0a if hi == 0 else (pre_l1a if STOP >= 3 else None)
                    mlp_half(0, col0, tbs, pre_next=nxt)

        def load_wcols(wd, row0, Kc, col0, ncols):
            slot, hs = wslot_next()
            assert Kc * ncols <= SLOT
            v = slot[:, 0:Kc * ncols].rearrange("p (k m) -> p k m", k=Kc)
            DMA("pool", v, wd[row0:row0 + Kc * 128, col0:col0 + ncols].rearrange("(k p) m -> p k m", p=128), writes=[hs])
            return v, hs

        def layer1_mixer():
            mL = A.mark()
            QT = A.alloc(10 * NT, BF16).rearrange("p (c n) -> p c n", c=10)
            KT = A.alloc(3 * NT, BF16).rearrange("p (c n) -> p c n", c=3)
            Vd = A.alloc(16 * 5 * 128, BF16).rearrange("p (t k m) -> p t k m", t=16, k=5)
            QTs = A.alloc(10 * NS, F32).rearrange("p (c n) -> p c n", c=10)
            KTs = A.alloc(3 * NS, F32).rearrange("p (c n) -> p c n", c=3)
            VTs = A.alloc(5 * NS, F32).rearrange("p (c n) -> p c n", c=5)
            VsTM = A.alloc(320, F32)
            knew = A.alloc(384, F32)
            smix = A.alloc(6 * NS, BF16).rearrange("p (c n) -> p c n", c=6)
            mP = A.mark()
            uT = A.alloc(8 * HW_, BF16).rearrange("p (c n) -> p c n", c=8)
            sqbuf = [A.alloc(512, BF16), A.alloc(512, BF16)]
            tmp = A.alloc(512, F32)
            rstd = A.alloc(512, F32)
            cst = A.alloc(2 * HW_, F32).rearrange("p (c n) -> p c n", c=2)
            qbs = [A.alloc(512, BF16), A.alloc(512, BF16)]
            t1s = [A.alloc(512, F32), A.alloc(512, F32)]
            t2s = [A.alloc(512, F32), A.alloc(512, F32)]
            rctr = [0]
            K32 = A.alloc(512, F32)
            vst = A.alloc(320, F32)
            kst = A.alloc(128, F32)


            def rope(ps, hp, N, lc=0):
                i = rctr[0] % 2
                rctr[0] += 1
                qb, t1, t2 = qbs[i], t1s[i], t2s[i]
                hq, h1, h2 = hh[("qb", i)], hh[("t1", i)], hh[("t2", i)]
                ACT(qb[:, 0:N], ps[:, 0:N], AF.Copy, [hp], [hq])
                pr, hpr = rrot.next()
                MM(pr[:, 0:N], prot, qb[:, 0:N], True, True, [hq, hconst], [hpr])
                TT(t1[:, 0:N], ps[:, 0:N], cst[:, 0, lc:lc + N], ALU.mult, [hp, hh["cst"]], [h1])
                TT(t2[:, 0:N], pr[:, 0:N], cst[:, 1, lc:lc + N], ALU.mult, [hpr, hh["cst"]], [h2])
                return t1, t2, h1, h2

            def perm_write(dstT, m, rows, kind, tb, hw, R):
                t1, t2, h1, h2 = R
                c0, N = TBS[tb]
                r0, r1 = rows
                if kind == 1:
                    o = dstT[r0:r1, m, c0:c0 + N]
                    a, b = t1[r0:r1, 0:N], t2[r0:r1, 0:N]
                else:
                    o = dstT[r0:r1, m, 0:NTOK].rearrange("p (r i) -> p r i", r=kind)[:, :, c0 // kind:c0 // kind + N // kind]
                    a = t1[r0:r1, 0:N].rearrange("p (i r) -> p r i", r=kind)
                    b = t2[r0:r1, 0:N].rearrange("p (i r) -> p r i", r=kind)
                TT(o, a, b, ALU.add, [h1, h2], [hw])

            QKIND = [((1, 1),)] * 4 + [((1, 4),)] * 4 + [((16, 16),)] * 2
            KKIND = [(1, 1), (1, 4), (16, 16)]

            nbase = len(wslots)
            PASSES = [[0, 1], [2, 3, 4]]
            psrot = Rot([0, 1, 2])
            rrot = Rot([3, 4, 5])
            for tbs_ in PASSES:
                pc0 = TBS[tbs_[0]][0]
                PW = sum(TBS[tb][1] for tb in tbs_)
                for tb in tbs_:
                    norm_block(1, 0, tb, uT, pc0, sqbuf, tmp, rstd, htb="L1")
                DMA("sp", cst[:, :, 0:PW], cs_d[:, :, pc0:pc0 + PW], writes=[hh["cst"]])
                pend = [None]

                def flush():
                    if pend[0] is not None:
                        f_, a_ = pend[0]
                        pend[0] = None
                        f_(*a_)

                def post_q(ps, hp, m, tb):
                    c0, N = TBS[tb]
                    lc = c0 - pc0
                    R = rope(ps, hp, N, lc)
                    if tb == 4:
                        TT(QTs[:, m, :], R[0][:, 0:N], R[1][:, 0:N], ALU.add, [R[2], R[3]], [hh[("QTs", m)]])
                    else:
                        ka, kb = QKIND[m][0]
                        if ka == kb:
                            perm_write(QT, m, (0, 128), ka, tb, hh[("QT", m, tb)], R)
                        else:
                            perm_write(QT, m, (0, 64), ka, tb, hh[("QT", m, tb, 0)], R)
                            perm_write(QT, m, (64, 128), kb, tb, hh[("QT", m, tb, 1)], R)

                def post_k(ps, hp, kc, tb):
                    c0, N = TBS[tb]
                    lc = c0 - pc0
                    R = rope(ps, hp, N, lc)
                    if tb == 4:
                        TT(KTs[:, kc, :], R[0][:, 0:N], R[1][:, 0:N], ALU.add, [R[2], R[3]], [hh[("KTs", kc)]])
                        return
                    ka, kb = KKIND[kc]
                    if ka == kb:
                        perm_write(KT, kc, (0, 128), ka, tb, hh[("KT", kc, tb)], R)
                    else:
                        perm_write(KT, kc, (0, 64), ka, tb, hh[("KT", kc, tb, 0)], R)
                        perm_write(KT, kc, (64, 128), kb, tb, hh[("KT", kc, tb, 1)], R)
                    need = [t for t in range(4) if (M4A & 4) and (kc == 2 or (kc == 1 and 4 * tb + t >= 12) or (kc == 0 and 4 * tb + t == 15))]
                    if need:
                        TT(K32[:, 0:N], R[0][:, 0:N], R[1][:, 0:N], ALU.add, [R[2], R[3]], [hh["K32"]])
                    for t in need:
                        T = 4 * tb + t
                        pt, hpt = rrot.next()
                        TR(pt[:, 0:128], K32[:, t * 128:(t + 1) * 128], ident, [hh["K32"], hconst], [hpt])
                        ACT(kst[:, 0:128], pt[:, 0:128], AF.Copy, [hpt], [hh["kst"]])
                        if kc == 2:
                            DMA("sp", d_p[2][T * 128:(T + 1) * 128, 0:64], kst[:, 0:64], reads=[hh["kst"]], writes=[hh[("d2pk", T)]])
                        elif kc == 1:
                            DMA("sp", d_p[1][(T - 12) * 128:(T - 11) * 128, 0:64], kst[:, 64:128], reads=[hh["kst"]])
                            if T == 15:
                                DMA("sp", d_p[0][:, 0:64], kst[:, 0:64], reads=[hh["kst"]])
                        else:
                            DMA("sp", swa_p[:, 0:128], kst[:, 0:128], reads=[hh["kst"]])

                def main_mm(v, i, hs, tb):
                    c0, N = TBS[tb]
                    lc = c0 - pc0
                    ps, hp = psrot.next()
                    for k in range(8):
                        MM(ps[:, 0:N], v[:, i, k, :], uT[:, k, lc:lc + N], k == 0, k == 7, [hs, hh[("uT", k, "L1")]], [hp])
                    return ps, hp

                for mg in range(3):
                    ms = list(range(mg * 4, min(10, mg * 4 + 4)))
                    v, hs = load_wchunks(wq_d, 0, 8, [m * 128 for m in ms])
                    for i, m in enumerate(ms):
                        for tb in tbs_:
                            ps, hp = main_mm(v, i, hs, tb)
                            flush()
                            pend[0] = (post_q, (ps, hp, m, tb))
                v, hs = load_wchunks(wk_d, 0, 8, [0, 128, 256])
                for kc in range(3):
                    for tb in tbs_:
                        ps, hp = main_mm(v, kc, hs, tb)
                        flush()
                        pend[0] = (post_k, (ps, hp, kc, tb))
                flush()
                vw, hvw = load_wcols(wv_d, 0, 8, 0, 320)
                for tb in tbs_:
                    c0, N = TBS[tb]
                    lc = c0 - pc0
                    if tb == 4:
                        ps, hp = mmrot.next()
                        for k in range(8):
                            MM(ps[0:NS, 0:320], uT[:, k, lc:lc + NS], vw[:, k, :], k == 0, k == 7, [hvw, hh[("uT", k, "L1")]], [hp])
                        ACT(VsTM[0:NS, :], ps[0:NS, 0:320], AF.Copy, [hp], [hh["VsTM"]])
                        vd_, hvd = load_wchunks(wvd_d, 0, 8, [0, 128, 256, 384])
                        vd2, hvd2 = load_wchunks(wvd_d, 0, 8, [512])
                        for i in range(5):
                            vv, hv_, ii = (vd_, hvd, i) if i < 4 else (vd2, hvd2, 0)
                            ps, hp = mmrot.next()
                            for k in range(8):
                                MM(ps[:, 0:NS], vv[:, ii, k, :], uT[:, k, lc:lc + NS], k == 0, k == 7, [hv_, hh[("uT", k, "L1")]], [hp])
                            ACT(VTs[:, i, :], ps[:, 0:NS], AF.Copy, [hp], [hh[("VTs", i)]])
                        ps, hp = mmrot.next()
                        for kc in range(3):
                            TR(ps[0:NS, kc * 128:(kc + 1) * 128], KTs[:, kc, :], ident, [hh[("KTs", kc)], hconst], [hp])
                        ACT(knew[0:NS, :], ps[0:NS, 0:384], AF.Copy, [hp], [hh["knew"]])
                        DMA("sp", swa_s[:, 127, 0:128], knew[0:NS, 0:128], reads=[hh["knew"]])
                        DMA("sp", swa_s[:, 127, 128:256], VsTM[0:NS, 0:128], reads=[hh["VsTM"]])
                        for g, Wg in enumerate((128, 512, 2048)):
                            DMA("sp", d_s[g][:, Wg - 1, 0:64], knew[0:NS, 128 + 64 * g:192 + 64 * g], reads=[hh["knew"]])
                            DMA("sp", d_s[g][:, Wg - 1, 64:128], VsTM[0:NS, 128 + 64 * g:192 + 64 * g], reads=[hh["VsTM"]])
                        continue
                    for t in range(4):
                        T = 4 * tb + t
                        ps, hp = mmrot.next()
                        for k in range(8):
                            MM(ps[:, 0:320], uT[:, k, lc + t * 128:lc + (t + 1) * 128], vw[:, k, :], k == 0, k == 7, [hvw, hh[("uT", k, "L1")]], [hp])
                        src3 = ps[:, 0:192].rearrange("p (k m) -> p k m", k=3)
                        ACT(Vd[:, T, 0:3, 0:64], src3, AF.Copy, [hp], [hh[("Vd", T, 0)]])
                        CP(Vd[:, T, 0:3, 64:128], src3, [hp], [hh[("Vd", T, 1)]])
                        ACT(vst[:, 0:320], ps[:, 0:320], AF.Copy, [hp], [hh["vst"]])
                        DMA("sp", d_p[2][T * 128:(T + 1) * 128, 64:128], vst[:, 256:320], reads=[hh["vst"]], writes=[hh[("d2pv", T)]])
                        if T >= 12:
                            DMA("sp", d_p[1][(T - 12) * 128:(T - 11) * 128, 64:128], vst[:, 192:256], reads=[hh["vst"]])
                        if T == 15:
                            DMA("sp", d_p[0][:, 64:128], vst[:, 128:192], reads=[hh["vst"]])
                            DMA("sp", swa_p[:, 128:256], vst[:, 0:128], reads=[hh["vst"]])
                    for r in range(4):
                        ps, hp = mmrot.next()
                        for k in range(8):
                            MM(ps[:, 0:64], uT[:, k, lc + r:lc + 512:4], vw[:, k, 192:256], k == 0, k == 7, [hvw, hh[("uT", k, "L1")]], [hp])
                        ACT(Vd[:, 4 * r + tb, 3, 0:64], ps[:, 0:64], AF.Copy, [hp], [hh[("Vd3", r, tb, 0)]])
                        CP(Vd[:, 4 * r + tb, 3, 64:128], ps[:, 0:64], [hp], [hh[("Vd3", r, tb, 1)]])
            del wslots[nbase:]
            NSL[0] = len(wslots)
            if L1STOP > 3:
                pre_outproj(w_out_attn, 6)
            src = d_p[2].rearrange("(i r) n -> i r n", r=16)[:, :, 64:128]
            rds = [hh[("d2pv", T)] for T in range(16)]
            if not int(os.environ.get("MK_NORB", "0")):
                for r in range(16):
                    DMA("pool", Vd[:, r, 4, 0:64], src[:, r, :], reads=rds, writes=[hh[("Vd4a", r)]])
                    DMA("pool", Vd[:, r, 4, 64:128], src[:, r, :], reads=rds, writes=[hh[("Vd4b", r)]])
            P.fence()
            if L1STOP <= 1:
                A.reset(mL)
                return

            if not NOCOPY:
                DMA("act", swa_s[:, 0:127, :], c_swa[:, 1:128, :])
                for g, Wg in enumerate((128, 512, 2048)):
                    DMA("act", d_s[g][:, 0:Wg - 1, :], c_d[g][:, 1:Wg, :])
            A.reset(mP)
            mixT = A.alloc(6 * NT, BF16).rearrange("p (c n) -> p c n", c=6)
            PT = [A.alloc(256, BF16), A.alloc(256, BF16)]
            accN = A.alloc(NTOK, F32)
            accD = A.alloc(NTOK, F32)
            rec = A.alloc(512, F32)
            PT = PT + [A.alloc(256, BF16), A.alloc(256, BF16)]
            srot = Rot([0, 1, 6, 7])
            orot = Rot([2, 4])
            items = []

            def add_head(qc, base, kc, kv, kind, evac):
                cfg = dict(qc=qc, base=base, kc=kc, kv=kv, kind=kind, evac=evac)
                for T in range(16):
                    items.append((cfg, T))

            for h in range(8):
                kv, g = h // 4, h % 4
                mrows = slice((h % 2) * 64, (h % 2) * 64 + 64)
                mch = h // 2

                def evac(b, po, hpo, pd, hpd, h=h, mrows=mrows, mch=mch):
                    TS(rec[mrows, :], pd[mrows, :], smalls[mrows, 16 + h:17 + h], ALU.add, [hpd, hh["expsink"]], [hh["rec"]])
                    RECIP(rec[mrows, :], rec[mrows, :], [hh["rec"]], [hh["rec"]])
                    TT(mixT[mrows, mch, b * 512:(b + 1) * 512], po[mrows, :], rec[mrows, :], ALU.mult, [hpo, hh["rec"]], [hh[("mixT", mch, h % 2, b)]])
                add_head(g, kv * 64, 0, kv, 1, evac)
            for s_ in range(4):
                mrows = slice((s_ % 2) * 64, (s_ % 2) * 64 + 64)
                mch = 4 + s_ // 2
                for grp in range(3):
                    kind = (1, 4, 16)[grp]

                    def evac(b, po, hpo, pd, hpd, grp=grp, kind=kind, mrows=mrows, mch=mch, s_=s_):
                        if kind == 1:
                            on = accN[mrows, b * 512:(b + 1) * 512]
                            od = accD[mrows, b * 512:(b + 1) * 512]
                            sn, sd = po[mrows, :], pd[mrows, :]
                        elif kind == 4:
                            on = accN[mrows, b:NTOK:4]
                            od = accD[mrows, b:NTOK:4]
                            sn, sd = po[mrows, :], pd[mrows, :]
                        else:
                            on = accN[mrows, :].rearrange("p (i r) -> p i r", r=16)[:, :, 4 * b:4 * b + 4]
                            od = accD[mrows, :].rearrange("p (i r) -> p i r", r=16)[:, :, 4 * b:4 * b + 4]
                            sn = po[mrows, :].rearrange("p (t i) -> p i t", t=4)
                            sd = pd[mrows, :].rearrange("p (t i) -> p i t", t=4)
                        if grp == 0:
                            ACT(on, sn, AF.Copy, [hpo], [hh["accN"]])
                            CP(od, sd, [hpd], [hh["accD"]])
                        else:
                            TT(on, on, sn, ALU.add, [hpo, hh["accN"]], [hh["accN"]])
                            TT(od, od, sd, ALU.add, [hpd, hh["accD"]], [hh["accD"]])
                        if grp == 2 and b == 3:
                            for bb in range(4):
                                RECIP(rec[mrows, :], accD[mrows, bb * 512:(bb + 1) * 512], [hh["accD"]], [hh["rec"]])
                                TT(mixT[mrows, mch, bb * 512:(bb + 1) * 512], accN[mrows, bb * 512:(bb + 1) * 512], rec[mrows, :], ALU.mult,
                                   [hh["accN"], hh["rec"]], [hh[("mixT", mch, s_ % 2, bb)]])
                    if grp == 0:
                        add_head(4 + s_, 0, 1, 2, 1, evac)
                    elif grp == 1:
                        add_head(4 + s_, 64, 1, 3, 4, evac)
                    else:
                        add_head(8 + s_ // 2, (s_ % 2) * 64, 2, 4, 16, evac)

            state = {}

            def issue_S(i):
                cfg, T = items[i]
                kind, base = cfg["kind"], cfg["base"]
                rows = slice(base, base + 64)
                j = T if kind == 1 else (T % 4 if kind == 4 else 0)
                pss, hs_ = srot.next()
                lo = 0 if j > 0 else 128
                qsl = QT[rows, cfg["qc"], T * 128:(T + 1) * 128]
                if j > 0:
                    MM(pss[:, 0:128], KT[rows, cfg["kc"], (T - 1) * 128:T * 128], qsl, True, True, [], [hs_])
                MM(pss[:, 128:256], KT[rows, cfg["kc"], T * 128:(T + 1) * 128], qsl, True, True, [], [hs_])
                pt = PT[i % 4]
                hpt = hh[("PT", i % 4)]
                ACT(pt[:, lo:256], pss[:, lo:256], AF.Exp, [hs_], [hpt], scale=0.125)
                TT(pt[:, lo:256], pt[:, lo:256], maskpd[:, lo:256], ALU.mult, [hpt, hconst], [hpt], eng="pool")
                state[i] = (pt, hpt, j)

            def issue_PV(i):
                cfg, T = items[i]
                pt, hpt, j = state.pop(i)
                kv = cfg["kv"]
                if T % 4 == 0:
                    po, hpo = orot.next()
                    ob = orot.banks[(orot.i - 1) % 2]
                    cfg["o"] = (po, hpo, psb[ob + 1], hps[ob + 1])
                po, hpo, pd, hpd = cfg["o"]
                osl = slice((T % 4) * 128, (T % 4 + 1) * 128)
                if j > 0:
                    MM(po[:, osl], Vd[:, T - 1, kv, :], pt[:, 0:128], True, False, [hpt], [hpo])
                    MM(po[:, osl], Vd[:, T, kv, :], pt[:, 128:256], False, True, [hpt], [hpo])
                    MM(pd[:, osl], onesb, pt[:, 0:128], True, False, [hpt, hconst], [hpd])
                    MM(pd[:, osl], onesb, pt[:, 128:256], False, True, [hpt, hconst], [hpd])
                else:
                    MM(po[:, osl], Vd[:, T, kv, :], pt[:, 128:256], True, True, [hpt], [hpo])
                    MM(pd[:, osl], onesb, pt[:, 128:256], True, True, [hpt, hconst], [hpd])
                if T % 4 == 3:
                    cfg["evac"](T // 4, po, hpo, pd, hpd)

            LOOK = 2
            for i in range(len(items) + LOOK):
                if i < len(items):
                    issue_S(i)
                if i >= LOOK:
                    issue_PV(i - LOOK)
            P.fence()
            if L1STOP <= 2:
                A.reset(mL)
                return

            A.reset(mL)
            _skip = A.alloc(1, F32)
            A.reset(mL)
            Kc = A.alloc(NS * 64, F32).rearrange("p (b d) -> p b d", b=NS)
            Vc = A.alloc(NS * 128, F32).rearrange("p (b d) -> p b d", b=NS)
            Qs = A.alloc(256, F32)
            Qd = A.alloc(NS * 256, F32).rearrange("p (b c) -> p b c", b=NS)
            prod = A.alloc(512, F32)
            Sall = A.alloc(64, F32)
            Pn = A.alloc(64, F32)
            Pnew = A.alloc(64, F32)
            den = A.alloc(64, F32)
            num = A.alloc(64, F32)
            numD = A.alloc(64, F32)
            denD = A.alloc(64, F32)
            prT = A.alloc(NS, F32)
            outv = A.alloc(64, F32)
            assert A.off <= mP - 0 or True

            def v3(ap):
                return ap.rearrange("p (b s) -> p b s", s=4)

            groups = []
            for kv in range(2):
                groups.append(dict(heads=[(g, kv * 64) for g in range(4)], kc=0, kbase=kv * 64, vi=kv, cache=c_swa, kcol=kv * 64, vcol=128 + kv * 64,
                                   step=1, swa=kv))
            groups.append(dict(heads=[(4 + s, 0) for s in range(4)], kc=1, kbase=0, vi=2, cache=c_d[0], kcol=0, vcol=64, step=1, swa=None))
            groups.append(dict(heads=[(4 + s, 64) for s in range(4)], kc=1, kbase=64, vi=3, cache=c_d[1], kcol=0, vcol=64, step=4, swa=None))
            groups.append(dict(heads=[(8 + s // 2, (s % 2) * 64) for s in range(4)], kc=2, kbase=None, vi=4, cache=c_d[2], kcol=0, vcol=64, step=16, swa=None))
            for gi, G in enumerate(groups):
                cview = G["cache"].rearrange("b j n -> j b n")
                st_ = G["step"]
                for q4 in range(4):
                    bs = slice(4 * q4, 4 * q4 + 4)
                    DMA("sp", Kc[:, bs, :], cview[0:128 * st_:st_, bs, G["kcol"]:G["kcol"] + 64], writes=[hh[("Kc", q4)]])
                    DMA("sp", Vc[:, bs, 0:64], cview[0:128 * st_:st_, bs, G["vcol"]:G["vcol"] + 64], writes=[hh[("Vc", q4, 0)]])
                    DMA("sp", Vc[:, bs, 64:128], cview[0:128 * st_:st_, bs, G["vcol"]:G["vcol"] + 64], writes=[hh[("Vc", q4, 1)]])
                pq = [mmrot.next(), mmrot.next()]
                for s, (qc, qb_) in enumerate(G["heads"]):
                    ps, hp = pq[qb_ // 64]
                    TR(ps[0:NS, s * 64:(s + 1) * 64], QTs[qb_:qb_ + 64, qc, :], ident[qb_:qb_ + 64, qb_:qb_ + 64], [hh[("QTs", qc)], hconst], [hp])
                for s, (qc, qb_) in enumerate(G["heads"]):
                    ps, hp = pq[qb_ // 64]
                    ACT(Qs[0:NS, s * 64:(s + 1) * 64], ps[0:NS, s * 64:(s + 1) * 64], AF.Copy, [hp], [hh["Qs"]])
                TT(Qd[0:NS, :, :], bcast_ap(Qs[0:NS, 0:1], [(0, NS), (1, 256)]), bcast_ap(ident[0:NS, 0:1], [(1, NS), (0, 256)]), ALU.mult,
                   [hh["Qs"], hconst], [hh["Qd"]])
                for bp in range(NS // 2):
                    ps, hp = mmrot.next()
                    MM(ps[:, 0:512], ones32[0:NS, :], Qd[0:NS, 2 * bp:2 * bp + 2, :], True, True, [hh["Qd"], hconst], [hp])
                    TT(prod[:, :].rearrange("p (b s d) -> p b s d", b=2, s=4), ps[:, 0:512].rearrange("p (b s d) -> p b s d", b=2, s=4),
                       bcast_ap(Kc[:, 2 * bp, 0:1], [(64, 2), (0, 4), (1, 64)]), ALU.mult, [hp] + [hh[("Kc", q4)] for q4 in range(4)], [hh["prod"]])
                    o = Sall[:, 8 * bp:8 * bp + 8].rearrange("p (b s) -> p b s", b=2)
                    RED(o, prod[:, :].rearrange("p (b s d) -> p b s d", b=2, s=4), [hh["prod"]], [hh["Sall"]])
                psn, hpn = mmrot.next()
                for s, (qc, qb_) in enumerate(G["heads"]):
                    TT(prT[:, :], QTs[:, qc, :], KTs[:, G["kc"], :], ALU.mult, [hh[("QTs", qc)], hh[("KTs", G["kc"])]], [hh["prT"]])
                    MM(psn[:, s * NS:(s + 1) * NS], selh[qb_ // 64], prT[:, :], True, True, [hh["prT"], hconst], [hpn])
                ACT(Pn[:, :], Sall[:, :], AF.Exp, [hh["Sall"]], [hh["Pn"]], scale=0.125)
                ACT(Pnew[:, :].rearrange("p (b s) -> p s b", s=4), psn[:, 0:64].rearrange("p (s b) -> p s b", s=4), AF.Exp, [hpn], [hh["Pnew"]], scale=0.125)
                psd, hpd = mmrot.next()
                MM(psd[:, 0:64], ones32, Pn[:, :], True, True, [hh["Pn"], hconst], [hpd])
                TT(den[:, :], psd[:, 0:64], Pnew[:, :], ALU.add, [hpd, hh["Pnew"]], [hh["den"]])
                if G["swa"] is not None:
                    kv = G["swa"]
                    TT(v3(den[:, :]), v3(den[:, :]), bcast_ap(smalls[:, 16 + 4 * kv:17 + 4 * kv], [(0, NS), (1, 4)]), ALU.add,
                       [hh["den"], hh["expsink"]], [hh["den"]])
                psv, hpv = mmrot.next()
                for b in range(NS):
                    MM(psv[:, 4 * b:4 * b + 4], Vc[:, b, :], Pn[:, 4 * b:4 * b + 4], True, True, [hh[("Vc", q4, i)] for q4 in range(4) for i in range(2)] + [hh["Pn"]], [hpv])
                TT(v3(num[:, :]), v3(Pnew[:, :]), bcast_ap(VTs[:, G["vi"], 0:1], [(1, NS), (0, 4)]), ALU.mult, [hh["Pnew"], hh[("VTs", G["vi"])]], [hh["num"]])
                TT(num[:, :], num[:, :], psv[:, 0:64], ALU.add, [hh["num"], hpv], [hh["num"]])
                if G["swa"] is not None:
                    kv = G["swa"]
                    RECIP(den[:, :], den[:, :], [hh["den"]], [hh["den"]])
                    TT(outv[:, :], num[:, :], den[:, :], ALU.mult, [hh["num"], hh["den"]], [hh["outv"]])
                    for s in range(4):
                        h = kv * 4 + s
                        rows = slice((h % 2) * 64, (h % 2) * 64 + 64)
                        CP(smix[rows, h // 2, :], outv[rows, s:64:4], [hh["outv"]], [hh[("smix", h // 2, h % 2)]])
                else:
                    if gi == 2:
                        CP(numD[:, :], num[:, :], [hh["num"]], [hh["numD"]])
                        CP(denD[:, :], den[:, :], [hh["den"]], [hh["denD"]])
                    else:
                        TT(numD[:, :], numD[:, :], num[:, :], ALU.add, [hh["num"], hh["numD"]], [hh["numD"]])
                        TT(denD[:, :], denD[:, :], den[:, :], ALU.add, [hh["den"], hh["denD"]], [hh["denD"]])
            RECIP(denD[:, :], denD[:, :], [hh["denD"]], [hh["denD"]])
            TT(outv[:, :], numD[:, :], denD[:, :], ALU.mult, [hh["numD"], hh["denD"]], [hh["outv"]])
            for s in range(4):
                rows = slice((s % 2) * 64, (s % 2) * 64 + 64)
                CP(smix[rows, 4 + s // 2, :], outv[rows, s:64:4], [hh["outv"]], [hh[("smix", 4 + s // 2, s % 2)]])
            P.fence()
            if L1STOP <= 3:
                A.reset(mL)
                return

            A.reset(mL)
            W = HW_
            ysb = A.alloc(8 * W, F32).rearrange("p (c n) -> p c n", c=8)
            sqbuf2 = [A.alloc(512, BF16), A.alloc(512, BF16)]
            tmp2 = A.alloc(512, F32)
            rstd2 = A.alloc(512, F32)
            ytmp = [A.alloc(512, F32), A.alloc(512, F32)]
            nb_ = push_slots(2)
            assert A.off <= mP

            def rhs_fn(k, tb):
                if tb == 4:
                    return smix[:, k, :], []
                return mixT[:, k, TBS[tb][0]:TBS[tb][0] + 512], []
            for hi_, (col0, tbs) in enumerate(HALVES):
                stat_ = out_proj(w_out_attn, 0, 6, rhs_fn, tbs, ysb, col0, sqbuf=sqbuf2)
                for tb in tbs:
                    postnorm_residual(1, 1, tb, ysb, col0, sqbuf2, tmp2, rstd2, ytmp, stat=stat_)
                if hi_ == 1:
                    pop_slots(nb_)
                    if STOP >= 4:
                        pre_mlp(1)
                P.fence()
            A.reset(mL)

        if STOP >= 3:
            layer1_mixer()
        if STOP >= 4:
            for hi_, (col0, tbs) in enumerate(HALVES):
                mlp_half(1, col0, tbs, pre_next=(lambda: pre_mlp(1)) if hi_ == 0 else None)

        A.reset(m0)
        yo = [A.alloc(1024, F32), A.alloc(1024, F32)]
        rot = Rot([0, 1, 2, 3])
        for t in range(17):
            y = yo[t % 2]
            hy = hh[("yo", t % 2)]
            rows = 128 if t < 16 else NS
            tbk = t // 4 if t < 16 else 4
            for g in range(2):
                ps, hp = rot.next()
                for i in range(4):
                    c = g * 4 + i
                    TR(ps[0:rows, i * 128:(i + 1) * 128], hT[:, c, t * 128:t * 128 + rows], ident, [hh[("hT", c, tbk)], hconst], [hp])
                if g == 0:
                    ACT(y[0:rows, 0:512], ps[0:rows, :], AF.Copy, [hp], [hy])
                else:
                    CP(y[0:rows, 512:1024], ps[0:rows, :], [hp], [hy])
            dst = y_p[t * 128:(t + 1) * 128, :] if t < 16 else y_s
            DMA("sp", dst, y[0:rows, :], reads=[hy])

        assert not PRE, list(PRE.keys())
        P.emit(st)
        build_program.stats = dict(P.stats)
        build_program.arena_peak = A.peak
    return nc


def _consts():
    c32 = np.zeros((128, 512), np.float32)
    c32[:, 0:128] = np.eye(128, dtype=np.float32)
    c32[0:64, 128:256] = 1.0
    c32[64:128, 256:384] = 1.0
    c32[:, 384:512] = 1.0
    cb = np.zeros((128, 768), np.float32)
    prot = np.zeros((128, 128), np.float32)
    for base in (0, 64):
        for i in range(8):
            prot[base + i + 8, base + i] = 1.0
            prot[base + i, base + i + 8] = 1.0
    cb[:, 0:128] = prot
    cb[:, 128:256] = 1.0
    p = np.arange(128)[:, None]
    f = np.arange(128)[None, :]
    cb[:, 256:384] = (f <= p)
    cb[:, 384:512] = (f >= p)
    cb[:, 512:640] = np.eye(128, dtype=np.float32)
    half = 8
    inv = (np.float32(500000.0) ** (-np.arange(half, dtype=np.float32) / half)).astype(np.float32)
    pos = np.concatenate([np.arange(NTOK, dtype=np.float32), np.full(NS, 8192.0, np.float32)])
    ang = (pos[None, :] * inv[:, None]).astype(np.float32)
    cs = np.zeros((128, 2, NT), np.float32)
    cs[:, 0, :] = 1.0
    for base in (0, 64):
        cs[base:base + 8, 0] = np.cos(ang)
        cs[base + 8:base + 16, 0] = np.cos(ang)
        cs[base:base + 8, 1] = -np.sin(ang)
        cs[base + 8:base + 16, 1] = np.sin(ang)
    return c32, cb, cs


_NC_CACHE = {}


def kernel(x_prompt, x_sample, state_conv_a, state_conv_b, cache_swa_kv, cache_dil0_kv, cache_dil1_kv,
           cache_dil2_kv, norm_g, w_in_conv, conv_a_w, conv_a_b, conv_a_ln_g, conv_a_ln_b, conv_b_w,
           w_out_conv, w_in_attn, attn_sinks, w_out_attn, mlp_w1, mlp_w2):
    f = lambda a: np.ascontiguousarray(np.asarray(a, dtype=np.float32))
    x_prompt, x_sample = f(x_prompt), f(x_sample)
    if "nc" not in _NC_CACHE:
        _NC_CACHE["nc"] = build_program()
    nc = _NC_CACHE["nc"]
    c32, cb, cs = _consts()
    prm1 = np.concatenate([f(norm_g).reshape(64, 128), f(conv_a_b).reshape(4, 128), f(conv_a_ln_g).reshape(4, 128),
                           f(conv_a_ln_b).reshape(4, 128), f(conv_b_w).reshape(12, 128)], 0)
    prm2 = f(conv_a_w).reshape(124, 128)
    wi = f(w_in_attn)[0]
    qs = lambda h: wi[:, h * 64:(h + 1) * 64]
    qd = lambda g, h: wi[:, 768 + 384 * g + h * 64: 768 + 384 * g + (h + 1) * 64]
    kd = lambda g: wi[:, 768 + 384 * g + 256: 768 + 384 * g + 320]
    vd = lambda g: wi[:, 768 + 384 * g + 320: 768 + 384 * g + 384]
    ks = lambda kv: wi[:, 512 + kv * 64: 512 + (kv + 1) * 64]
    vs = lambda kv: wi[:, 640 + kv * 64: 640 + (kv + 1) * 64]
    wq = np.concatenate([np.concatenate([qs(g), qs(4 + g)], 1) for g in range(4)] +
                        [np.concatenate([qd(0, g), qd(1, g)], 1) for g in range(4)] +
                        [np.concatenate([qd(2, 0), qd(2, 1)], 1), np.concatenate([qd(2, 2), qd(2, 3)], 1)], 1)
    wk = np.concatenate([ks(0), ks(1), kd(0), kd(1), kd(2), kd(2)], 1)
    wv = np.concatenate([vs(0), vs(1), vd(0), vd(1), vd(2)], 1)
    wvd = np.concatenate([vs(0), vs(0), vs(1), vs(1), vd(0), vd(0), vd(1), vd(1), vd(2), vd(2)], 1)
    shared = {
        "prm1": f(prm1), "prm2": prm2, "sinks": f(attn_sinks).reshape(1, 8), "cst32": c32, "cstb": cb, "cs": cs,
        "w_in_conv": f(w_in_conv)[0], "w_out_conv": f(w_out_conv)[0], "wq": f(wq), "wk": f(wk), "wv": f(wv), "wvd": f(wvd),
        "w_out_attn": f(w_out_attn)[0], "w1": f(mlp_w1), "w2": f(mlp_w2),
    }
    sca = f(state_conv_a)[0]
    scb = f(state_conv_b)[0]
    cswa = f(cache_swa_kv)[0]
    cds = [f(cache_dil0_kv)[0], f(cache_dil1_kv)[0], f(cache_dil2_kv)[0]]
    in_maps = []
    for i in range(8):
        s = slice(NS * i, NS * (i + 1))
        m = dict(shared)
        m["xp"] = x_prompt[i]
        m["xs"] = x_sample[s, 0, :]
        m["sca"] = sca[s].reshape(NS * 30, 512)
        m["scb"] = scb[s].reshape(NS * 2, 512)
        m["c_swa"] = cswa[s].reshape(NS, 128, 256)
        for g in range(3):
            m["c_d%d" % g] = cds[g][s].reshape(NS, cds[g].shape[1], 128)
        in_maps.append(m)
    res = run_bass_kernel_spmd(nc, in_maps, core_ids=list(range(8)))
    R = res.results
    cat = lambda k: np.stack([np.asarray(r[k]) for r in R], 0)
    y_prompt = cat("y_p")
    y_sample = np.concatenate([np.asarray(r["y_s"]) for r in R], 0).reshape(128, 1, 1024)
    a_p = cat("a_p")[None]
    a_s = np.concatenate([np.asarray(r["a_s"]) for r in R], 0)[None]
    b_p = cat("b_p")[None]
    b_s = np.concatenate([np.asarray(r["b_s"]) for r in R], 0)[None]
    swa_p = cat("swa_p").reshape(1, 8, 128, 2, 2, 64)
    swa_s = np.concatenate([np.asarray(r["swa_s"]) for r in R], 0).reshape(1, 128, 128, 2, 2, 64)
    outs = [y_prompt, y_sample, a_p, a_s, b_p, b_s, swa_p, swa_s]
    for g, Wg in enumerate((128, 512, 2048)):
        outs.append(cat("d%d_p" % g).reshape(1, 8, Wg, 2, 1, 64))
        outs.append(np.concatenate([np.asarray(r["d%d_s" % g]) for r in R], 0).reshape(1, 128, Wg, 2, 1, 64))
    return tuple(np.ascontiguousarray(o, dtype=np.float32) for o in outs)
```

```python
import os
import numpy as np
from contextlib import ExitStack
import concourse.bass as bass
import concourse.mybir as mybir
from concourse.bass_utils import run_bass_kernel_spmd

F32 = mybir.dt.float32
BF16 = mybir.dt.bfloat16
ALU = mybir.AluOpType
AF = mybir.ActivationFunctionType
AX = mybir.AxisListType

ENGS = ("pe", "act", "dve", "pool", "sp")

NTOK = 2048
NS = 16
NT = NTOK + NS
TBS = [(0, 512), (512, 512), (1024, 512), (1536, 512), (2048, 16)]
HALVES = [(0, [0, 1]), (1024, [2, 3, 4])]
HW_ = 1040
EPS = 1e-6
STOP = int(os.environ.get("MK_STOP", "99"))
SAFE_WAR = int(os.environ.get("MK_SAFE_WAR", "1"))
M4A = int(os.environ.get("MK_4A", "63"))
POOL_ADD = int(os.environ.get("MK_POOL_ADD", "0"))
L1STOP = int(os.environ.get("MK_L1STOP", "99"))
NOCOPY = int(os.environ.get("MK_NOCOPY", "0"))


class H:
    __slots__ = ("name", "w", "r", "excl")

    def __init__(self, name="", excl=False):
        self.name = name
        self.w = None
        self.r = []
        self.excl = excl


class HD(dict):
    def __missing__(self, k):
        v = H(str(k))
        self[k] = v
        return v


class Op:
    __slots__ = ("eng", "fn", "deps", "signal", "tick", "dma", "sem", "val", "fenced")


class Prog:
    def __init__(self, nc, ndma=8):
        self.nc = nc
        self.ops = {e: [] for e in ENGS}
        self.all = []
        self.ndma = ndma

    def op(self, eng, fn, reads=(), writes=(), dma=False):
        o = Op()
        o.eng, o.fn, o.dma, o.signal, o.tick, o.sem, o.val, o.fenced = eng, fn, dma, dma, 0, None, 0, False
        deps = []
        if any(h.excl for h in reads):
            writes = list(writes) + [h for h in reads if h.excl and h not in writes]
            reads = [h for h in reads if not h.excl]

        def add(p, kind):
            if p is None or p is o:
                return
            if p.eng == eng and not p.dma and not dma:
                if eng == "pe" or (kind == "WAR" and not SAFE_WAR):
                    return
            if p not in deps:
                deps.append(p)

        for h in reads:
            add(h.w, "RAW")
        for h in writes:
            add(h.w, "WAW")
            for r in h.r:
                add(r, "WAR")
        for h in reads:
            h.r.append(o)
        for h in writes:
            h.w = o
            h.r = []
        o.deps = deps
        self.ops[eng].append(o)
        self.all.append(o)
        return o

    def fence(self):
        lasts = [self.ops[e][-1] for e in ENGS if self.ops[e] and self.ops[e][-1].fn is not None]
        dmas = [o for o in self.all if o.dma and not o.fenced]
        for o in dmas:
            o.fenced = True
        for e in ENGS:
            o = Op()
            o.eng, o.fn, o.dma, o.signal, o.tick, o.sem, o.val, o.fenced = e, None, False, False, 0, None, 0, True
            o.deps = [p for p in lasts if p.eng != e or p.dma] + [d for d in dmas if d not in lasts]
            self.ops[e].append(o)
            self.all.append(o)

    def emit(self, stack):
        nc = self.nc
        for o in self.all:
            for d in o.deps:
                d.signal = True
        esem = {e: stack.enter_context(nc.semaphore("es_" + e)) for e in ENGS}
        dsem = {e: [stack.enter_context(nc.semaphore("ds_%s_%d" % (e, i))) for i in range(self.ndma)]
                for e in ENGS if any(o.dma for o in self.ops[e])}
        for e in ENGS:
            c = 0
            nd = 0
            dmas = []
            for o in self.ops[e]:
                if o.dma:
                    o.sem = dsem[e][nd % self.ndma]
                    o.val = 16 * (nd // self.ndma + 1)
                    if nd >= self.ndma:
                        prev = dmas[nd - self.ndma]
                        if prev not in o.deps:
                            o.deps.append(prev)
                    dmas.append(o)
                    nd += 1
                elif o.signal:
                    c += 1
                    o.tick = c
        block = stack.enter_context(nc.Block())
        prog = self
        self.stats = {}

        def section(e):
            def body(eng):
                known = {}
                nwait = 0
                for o in prog.ops[e]:
                    for d in o.deps:
                        if d.dma:
                            sem, val = d.sem, d.val
                        else:
                            sem, val = esem[d.eng], d.tick
                        key = id(sem)
                        if known.get(key, 0) >= val:
                            continue
                        eng.wait_ge(sem, val)
                        nwait += 1
                        known[key] = val
                    if o.fn is None:
                        continue
                    ins = o.fn(eng)
                    if o.dma:
                        ins.then_inc(o.sem, 16)
                    elif o.signal:
                        ins.then_inc(esem[e], 1)
                last = {}
                for o in prog.ops[e]:
                    if o.dma:
                        last[id(o.sem)] = (o.sem, o.val)
                for sem, val in last.values():
                    if known.get(id(sem), 0) < val:
                        eng.wait_ge(sem, val)
                prog.stats[e] = (len(prog.ops[e]), nwait)
            return body

        block.tensor(section("pe"))
        block.scalar(section("act"))
        block.vector(section("dve"))
        block.gpsimd(section("pool"))
        block.sync(section("sp"))


class Arena:
    def __init__(self, t32, cap_bytes):
        self.t32 = t32
        self.t16 = t32.bitcast(BF16)
        self.cap = cap_bytes
        self.off = 0
        self.peak = 0

    def alloc(self, ncols, dtype):
        esz = 4 if dtype == F32 else 2
        off = (self.off + 31) // 32 * 32
        nb = ncols * esz
        assert off + nb <= self.cap, ("arena overflow", off, nb, self.cap)
        self.off = off + nb
        self.peak = max(self.peak, self.off)
        if dtype == F32:
            return self.t32[:, off // 4: off // 4 + ncols]
        return self.t16[:, off // 2: off // 2 + ncols]

    def mark(self):
        return self.off

    def reset(self, m=0):
        self.off = m


def bcast_ap(ap, dims):
    return bass.AP(ap.tensor, ap.offset, [list(ap.ap[0])] + [[s, n] for s, n in dims])


def build_program():
    nc = bass.Bass("TRN2", target_bir_lowering=False)

    def din(name, shape):
        return nc.dram_tensor(name, list(shape), F32, kind="ExternalInput").ap()

    def dout(name, shape):
        return nc.dram_tensor(name, list(shape), F32, kind="ExternalOutput").ap()

    xp = din("xp", [NTOK, 1024])
    xs = din("xs", [NS, 1024])
    sca = din("sca", [NS * 30, 512])
    scb = din("scb", [NS * 2, 512])
    c_swa = din("c_swa", [NS, 128, 256])
    c_d = [din("c_d0", [NS, 128, 128]), din("c_d1", [NS, 512, 128]), din("c_d2", [NS, 2048, 128])]
    prm1 = din("prm1", [88, 128])
    prm2 = din("prm2", [124, 128])
    sinks = din("sinks", [1, 8])
    cst32_d = din("cst32", [128, 512])
    cstb_d = din("cstb", [128, 768])
    cs_d = din("cs", [128, 2, NT])
    w_in_conv = din("w_in_conv", [1024, 2560])
    w_out_conv = din("w_out_conv", [1024, 1024])
    wq_d = din("wq", [1024, 1280])
    wk_d = din("wk", [1024, 384])
    wv_d = din("wv", [1024, 320])
    wvd_d = din("wvd", [1024, 640])
    w_out_attn = din("w_out_attn", [768, 1024])
    w1_d = din("w1", [2, 1024, 4096])
    w2_d = din("w2", [2, 4096, 1024])

    y_p = dout("y_p", [NTOK, 1024])
    y_s = dout("y_s", [NS, 1024])
    a_p = dout("a_p", [30, 512])
    a_s = dout("a_s", [NS, 30, 512])
    b_p = dout("b_p", [2, 512])
    b_s = dout("b_s", [NS, 2, 512])
    swa_p = dout("swa_p", [128, 256])
    swa_s = dout("swa_s", [NS, 128, 256])
    d_p = [dout("d0_p", [128, 128]), dout("d1_p", [512, 128]), dout("d2_p", [2048, 128])]
    d_s = [dout("d0_s", [NS, 128, 128]), dout("d1_s", [NS, 512, 128]), dout("d2_s", [NS, 2048, 128])]

    st = ExitStack()
    with st:
        P = Prog(nc)
        sb = lambda name, shape, dt: st.enter_context(nc.sbuf_tensor(name, list(shape), dt))
        hT = sb("hT", [128, 8, NT], F32)
        cst32 = sb("cst32s", [128, 512], F32)
        cstb = sb("cstbs", [128, 768], BF16)
        prmT = sb("prmT", [128, 88], F32)
        wAT = sb("wAT", [128, 124], F32)
        smalls = sb("smalls", [128, 32], F32)
        halo = sb("halo", [128, 8, 30], BF16)
        NSLOT = 2
        SLOT = 4096
        wslots = [sb("wslot%d" % i, [128, SLOT], BF16) for i in range(NSLOT)]
        ARENA_BYTES = (nc.sbuf_bytes_remaining // 64) * 64 - 256
        A = Arena(sb("arena", [128, ARENA_BYTES // 4], F32), ARENA_BYTES)
        psb = [st.enter_context(nc.psum_tensor("psb%d" % i, [128, 512], F32)) for i in range(8)]
        hps = [H("ps%d" % i, excl=True) for i in range(8)]

        ident = cst32[:, 0:128]
        selh = [cst32[:, 128:256], cst32[:, 256:384]]
        ones32 = cst32[:, 384:512]
        prot = cstb[:, 0:128]
        onesb = cstb[:, 128:256]
        maskpd = cstb[:, 256:512]
        identb = cstb[:, 512:640]
        epsc = smalls[:, 0:1]

        hh = HD()
        hconst = hh["const"]

        def MM(ps_ap, lhsT, rhs, start, stop, reads, writes):
            P.op("pe", lambda e: e.matmul(ps_ap, lhsT=lhsT, rhs=rhs, start=start, stop=stop), reads=reads, writes=writes)

        def TR(out, in_, idn, reads, writes):
            P.op("pe", lambda e: e.transpose(out=out, in_=in_, identity=idn), reads=reads, writes=writes)

        def ACT(out, in_, func, reads, writes, bias=None, scale=None):
            kw = {}
            if bias is not None:
                kw["bias"] = bias
            if scale is not None:
                kw["scale"] = scale
            P.op("act", lambda e: e.activation(out=out, in_=in_, func=func, **kw), reads=reads, writes=writes)

        def TT(out, in0, in1, op, reads, writes, eng="dve"):
            P.op(eng, lambda e: e.tensor_tensor(out=out, in0=in0, in1=in1, op=op), reads=reads, writes=writes)

        def STT(out, in0, scalar, in1, op0, op1, reads, writes):
            P.op("dve", lambda e: e.scalar_tensor_tensor(out=out, in0=in0, scalar=scalar, in1=in1, op0=op0, op1=op1), reads=reads, writes=writes)

        def TS(out, in0, s1, op0, reads, writes, s2=None, op1=None, eng="dve"):
            if op1 is None:
                P.op(eng, lambda e: e.tensor_scalar(out=out, in0=in0, scalar1=s1, scalar2=None, op0=op0), reads=reads, writes=writes)
            else:
                P.op(eng, lambda e: e.tensor_scalar(out=out, in0=in0, scalar1=s1, scalar2=s2, op0=op0, op1=op1), reads=reads, writes=writes)

        def CP(out, in_, reads, writes, eng="dve"):
            P.op(eng, lambda e: e.tensor_copy(out=out, in_=in_), reads=reads, writes=writes)

        def RED(out, in_, reads, writes):
            P.op("dve", lambda e: e.tensor_reduce(out=out, in_=in_, axis=AX.X, op=ALU.add), reads=reads, writes=writes)

        def RECIP(out, in_, reads, writes):
            P.op("dve", lambda e: e.reciprocal(out=out, in_=in_), reads=reads, writes=writes)

        def MEMSET(ap, val, writes, eng="dve"):
            P.op(eng, lambda e: e.memset(ap, val), writes=writes)

        def DMA(eng, out, in_, reads=(), writes=()):
            P.op(eng, lambda e: e.dma_start(out=out, in_=in_), reads=reads, writes=writes, dma=True)

        class Rot:
            def __init__(self, banks):
                self.banks = banks
                self.i = 0

            def next(self):
                b = self.banks[self.i % len(self.banks)]
                self.i += 1
                return psb[b], hps[b]

        wrot = [0]

        NSL = [NSLOT]

        def wslot_next():
            i = wrot[0] % NSL[0]
            wrot[0] += 1
            return wslots[i], hh[("wslot", i)]

        def gcol(l, w, c):
            j = (l * 4 + w) * 8 + c
            return prmT[:, j:j + 1]

        DMA("sp", cst32[:], cst32_d, writes=[hconst])
        DMA("pool", cstb[:], cstb_d, writes=[hconst])
        MEMSET(smalls[:, 0:8], EPS, [hh["smalls"]])
        DMA("sp", smalls[:, 8:16], bass.AP(sinks.tensor, 0, [[0, 128], [1, 8]]), writes=[hh["sinks"]])
        ACT(smalls[:, 16:24], smalls[:, 8:16], AF.Exp, [hh["sinks"]], [hh["expsink"]])
        m0 = A.mark()
        p1 = A.alloc(128, F32)
        p2 = A.alloc(128, F32)
        DMA("sp", p1[0:88, :], prm1, writes=[hh["p1"]])
        DMA("sp", p2[0:124, :], prm2, writes=[hh["p2"]])
        TR(psb[0][:, 0:88], p1[0:88, :], ident[0:88, 0:88], [hh["p1"], hconst], [hps[0]])
        ACT(prmT[:], psb[0][:, 0:88], AF.Copy, [hps[0]], [hconst])
        TR(psb[1][:, 0:124], p2[0:124, :], ident[0:124, 0:124], [hh["p2"], hconst], [hps[1]])
        ACT(wAT[:], psb[1][:, 0:124], AF.Copy, [hps[1]], [hconst])

        xin = [A.alloc(1024, F32), A.alloc(1024, F32)]
        rot = Rot([2, 3, 4, 5])
        for t in range(17):
            xi = xin[t % 2]
            hx = hh[("xin", t % 2)]
            rows = 128 if t < 16 else NS
            src = xp[t * 128:(t + 1) * 128, :] if t < 16 else xs
            DMA("sp", xi[0:rows, :], src, writes=[hx])
            for g in range(2):
                ps, hp = rot.next()
                for i in range(4):
                    c = g * 4 + i
                    TR(ps[:, i * rows:(i + 1) * rows], xi[0:rows, c * 128:(c + 1) * 128], ident[0:rows, 0:rows], [hx, hconst], [hp])
                dst = hT[:, g * 4:(g + 1) * 4, t * 128:t * 128 + rows]
                srcp = ps[:, 0:4 * rows].rearrange("p (i r) -> p i r", i=4)
                wr = [hh[("hT", g * 4 + i, t // 4 if t < 16 else 4)] for i in range(4)]
                if g == 0:
                    ACT(dst, srcp, AF.Copy, [hp], wr)
                else:
                    CP(dst, srcp, [hp], wr)
        PHASE1_FENCE = True

        statrot = Rot([6, 7])
        mmrot = Rot([0, 1, 2, 3, 4, 5])

        def rstd_from_ps(ps_stat, hstat, N, scale, out_rstd, hout, tmp, htmp):
            ACT(tmp[:, 0:N], ps_stat[:, 0:N], AF.Sqrt, [hstat, hh["smalls"]], [htmp], bias=epsc, scale=scale)
            RECIP(out_rstd[:, 0:N], tmp[:, 0:N], [htmp], [hout])

        def sumsq_stat(src_fn, rd_fn, N, sqbuf):
            ps, hp = statrot.next()
            for c in range(8):
                sq = sqbuf[c % 2]
                hsq = hh[("sq", c % 2)]
                ACT(sq[:, 0:N], src_fn(c), AF.Square, rd_fn(c), [hsq])
                MM(ps[:, 0:N], onesb, sq[:, 0:N], c == 0, c == 7, [hsq, hconst], [hp])
            return ps, hp

        def norm_block(l, w, tb, dstT, dst_col0, sqbuf, tmp, rstd, htb=None):
            c0, N = TBS[tb]
            ps, hp = sumsq_stat(lambda c: hT[:, c, c0:c0 + N], lambda c: [hh[("hT", c, tb)]], N, sqbuf)
            rstd_from_ps(ps, hp, N, 1.0 / 1024, rstd, hh["rstd"], tmp, hh["rtmp"])
            for c in range(8):
                STT(dstT[:, c, c0 - dst_col0:c0 - dst_col0 + N], hT[:, c, c0:c0 + N], gcol(l, w, c), rstd[:, 0:N], ALU.mult, ALU.mult,
                    [hh[("hT", c, tb)], hh["rstd"], hconst], [hh[("uT", c, tb if htb is None else htb)]])

        def push_slots(n):
            nb = len(wslots)
            wslots.extend([A.alloc(SLOT, BF16) for _ in range(n)])
            NSL[0] = len(wslots)
            return nb

        def pop_slots(nb):
            del wslots[nb:]
            NSL[0] = len(wslots)

        PRE = {}

        def wkey(wd, row0, Kc, cols):
            return (wd.tensor.name, int(wd.offset), row0, Kc, tuple(cols))

        def preload(wd, row0, Kc, cols):
            assert NSL[0] == NSLOT
            PRE[wkey(wd, row0, Kc, cols)] = load_wchunks(wd, row0, Kc, cols)

        def load_wchunks(wd, row0, Kc, cols):
            k_ = wkey(wd, row0, Kc, cols)
            if k_ in PRE:
                return PRE.pop(k_)
            slot, hs = wslot_next()
            n = len(cols)
            assert n * Kc * 128 <= SLOT
            v = slot[:, 0:n * Kc * 128].rearrange("p (i k m) -> p i k m", i=n, k=Kc)
            for i, col0 in enumerate(cols):
                src = wd[row0:row0 + Kc * 128, col0:col0 + 128].rearrange("(k p) m -> p k m", p=128)
                DMA("pool", v[:, i], src, writes=[hs])
            return v, hs

        def postnorm_residual(l, w, tb, ysb, ycol0, sqbuf, tmp, rstd, ytmp):
            c0, N = TBS[tb]
            lc = c0 - ycol0
            ps, hp = sumsq_stat(lambda c: ysb[:, c, lc:lc + N], lambda c: [hh[("ysb", c, tb)]], N, sqbuf)
            rstd_from_ps(ps, hp, N, 1.0 / 1024, rstd, hh["rstd"], tmp, hh["rtmp"])
            for c in range(8):
                yt = ytmp[c % 2]
                hyt = hh[("ytmp", c % 2)]
                STT(yt[:, 0:N], ysb[:, c, lc:lc + N], gcol(l, w, c), rstd[:, 0:N], ALU.mult, ALU.mult,
                    [hh[("ysb", c, tb)], hh["rstd"], hconst], [hyt])
                TT(hT[:, c, c0:c0 + N], hT[:, c, c0:c0 + N], yt[:, 0:N], ALU.add, [hyt, hh[("hT", c, tb)]], [hh[("hT", c, tb)]],
                   eng=("pool" if POOL_ADD else "dve"))

        def out_proj(wd, row0, Kc, rhs_fn, tbs, ysb, ycol0, accumulate=False):
            for m in range(8):
                v, hs = load_wchunks(wd, row0, Kc, [m * 128])
                for tb in tbs:
                    c0, N = TBS[tb]
                    ps, hp = mmrot.next()
                    for k in range(Kc):
                        rap, rh = rhs_fn(k, tb)
                        MM(ps[:, 0:N], v[:, 0, k, :], rap, k == 0, k == Kc - 1, [hs] + rh, [hp])
                    dst = ysb[:, m, c0 - ycol0:c0 - ycol0 + N]
                    if not accumulate:
                        ACT(dst, ps[:, 0:N], AF.Copy, [hp], [hh[("ysb", m, tb)]])
                    else:
                        TT(dst, dst, ps[:, 0:N], ALU.add, [hp, hh[("ysb", m, tb)]], [hh[("ysb", m, tb)]])

        def pre_l0a():
            preload(w_in_conv, 0, 8, [0, 512, 1024, 2048])
            preload(w_in_conv, 0, 8, [1536])

        def pre_outproj(wd, Kc):
            preload(wd, 0, Kc, [0])
            preload(wd, 0, Kc, [128])

        def pre_mlp(l):
            preload(w1_d[l], 0, 8, [0, 128, 256, 384])
            preload(w1_d[l], 0, 8, [512, 640, 768, 896])

        def pre_l1a():
            preload(wq_d, 0, 8, [0, 128, 256, 384])
            preload(wq_d, 0, 8, [512, 640, 768, 896])

        def mlp_half(l, col0, tbs, pre_next=None):
            m1 = A.mark()
            W = HW_
            uT = A.alloc(8 * W, BF16).rearrange("p (c n) -> p c n", c=8)
            hid = A.alloc(16 * W, BF16).rearrange("p (c n) -> p c n", c=16)
            ysb = A.alloc(8 * W, F32).rearrange("p (c n) -> p c n", c=8)
            sqbuf = [A.alloc(512, BF16), A.alloc(512, BF16)]
            rl = [A.alloc(512, BF16), A.alloc(512, BF16)]
            tmp = A.alloc(512, F32)
            rstd = A.alloc(512, F32)
            ytmp = [A.alloc(512, F32), A.alloc(512, F32)]
            nb_ = push_slots(2)
            for tb in tbs:
                norm_block(l, 2, tb, uT, col0, sqbuf, tmp, rstd)
            for hf in range(2):
                for mg in range(4):
                    ms = [hf * 16 + mg * 4 + i for i in range(4)]
                    v, hs = load_wchunks(w1_d[l], 0, 8, [m * 128 for m in ms])
                    for i, m in enumerate(ms):
                        for tb in tbs:
                            c0, N = TBS[tb]
                            lc = c0 - col0
                            ps, hp = mmrot.next()
                            for k in range(8):
                                MM(ps[:, 0:N], v[:, i, k, :], uT[:, k, lc:lc + N], k == 0, k == 7, [hs, hh[("uT", k, tb)]], [hp])
                            r = rl[(m + tb) % 2]
                            hr = hh[("rl", (m + tb) % 2)]
                            ACT(r[:, 0:N], ps[:, 0:N], AF.Relu, [hp], [hr])
                            TT(hid[:, m % 16, lc:lc + N], r[:, 0:N], r[:, 0:N], ALU.mult, [hr], [hh[("hid", m % 16, tb)]])
                out_proj(w2_d[l], hf * 2048, 16,
                         lambda k, tb: (hid[:, k, TBS[tb][0] - col0:TBS[tb][0] - col0 + TBS[tb][1]], [hh[("hid", k, tb)]]),
                         tbs, ysb, col0, accumulate=(hf == 1))
            for tb in tbs:
                postnorm_residual(l, 3, tb, ysb, col0, sqbuf, tmp, rstd, ytmp)
            pop_slots(nb_)
            if pre_next is not None:
                pre_next()
            P.fence()
            A.reset(m1)

        def layer0_half(hi, col0, tbs):
            m1 = A.mark()
            W = HW_
            uT = A.alloc(8 * W, BF16).rearrange("p (c n) -> p c n", c=8)
            mixT = uT
            ga = A.alloc(4 * (30 + W), BF16).rearrange("p (c n) -> p c n", c=4)
            zb = A.alloc(4 * (2 + W), BF16).rearrange("p (c n) -> p c n", c=4)
            gb = A.alloc(4 * W, BF16).rearrange("p (c n) -> p c n", c=4)
            m2 = A.mark()
            diag = A.alloc(31 * 128, BF16).rearrange("p (j m) -> p j m", j=31)
            diagB = A.alloc(12 * 128, BF16).rearrange("p (j m) -> p j m", j=12)
            cvo = A.alloc(4 * W, F32).rearrange("p (c n) -> p c n", c=4)
            reg32 = A.alloc(2048, F32)
            xq16 = A.t16[:, 2 * reg32.offset:2 * reg32.offset + 4096]
            xb = xq16[:, 0:2048].rearrange("p (c n) -> p c n", c=4)
            sq4 = xq16[:, 2048:4096].rearrange("p (c n) -> p c n", c=4)
            sqbuf = [A.alloc(512, BF16), A.alloc(512, BF16)]
            tmpa = A.alloc(512, F32)
            tmpg = A.alloc(512, F32)
            tmp = A.alloc(512, F32)
            rstd = A.alloc(512, F32)
            mean = A.alloc(512, F32)
            msq = A.alloc(512, F32)
            t1 = [A.alloc(512, F32), A.alloc(512, F32)]
            tails = A.alloc(4 * 64, F32).rearrange("p (c n) -> p c n", c=4)
            sAT = A.alloc(4 * NS * 30, F32).rearrange("p (c n) -> p c n", c=4)
            sBT = A.alloc(4 * NS * 2, F32).rearrange("p (c n) -> p c n", c=4)
            stg = A.alloc(512, F32)
            gas = A.alloc(4 * NS, F32).rearrange("p (c n) -> p c n", c=4)
            prodA = A.alloc(NS * 30, F32)
            hga = [hh[("ga", c)] for c in range(4)]
            hzb = [hh[("zb", c)] for c in range(4)]
            nb2a = push_slots(1)

            if hi == 0:
                MEMSET(ga[:, :, 0:30], 0.0, hga)
                MEMSET(zb[:, :, 0:2], 0.0, hzb)
            else:
                CP(ga[:, :, 0:30], halo[:, 0:4, 0:30], [hh["halo"]], hga)
                CP(zb[:, :, 0:2], halo[:, 4:8, 0:2], [hh["halo"]], hzb)

            for tb in tbs:
                norm_block(0, 0, tb, uT, col0, sqbuf, tmp, rstd)

            for c in range(4):
                mcols = [c * 128, 512 + c * 128, 1024 + c * 128, 2048 + c * 128, 1536 + c * 128]
                v, hs = load_wchunks(w_in_conv, 0, 8, mcols[0:4])
                v2, hs2 = load_wchunks(w_in_conv, 0, 8, mcols[4:5])
                for tb in tbs:
                    c0, N = TBS[tb]
                    lc = c0 - col0
                    pss = []
                    for i in range(5):
                        ps, hp = mmrot.next()
                        vv, hv, ii = (v, hs, i) if i < 4 else (v2, hs2, 0)
                        for k in range(8):
                            MM(ps[:, 0:N], vv[:, ii, k, :], uT[:, k, lc:lc + N], k == 0, k == 7, [hv, hh[("uT", k, tb)]], [hp])
                        pss.append((ps, hp))
                    (pa, ha), (pg, hg), (px, hx_), (pc, hc), (pb, hb) = pss
                    ACT(tmpg[:, 0:N], pg[:, 0:N], AF.Sigmoid, [hg], [hh["tmpg"]])
                    TT(ga[:, c, 30 + lc:30 + lc + N], pa[:, 0:N], tmpg[:, 0:N], ALU.mult, [ha, hh["tmpg"]], [hga[c]])
                    if tb == 3:
                        TT(tails[:, c, 0:30], pa[:, 482:512], tmpg[:, 482:512], ALU.mult, [ha, hh["tmpg"]], [hh[("tails", c)]])
                    if tb == 4:
                        TT(gas[:, c, :], pa[:, 0:NS], tmpg[:, 0:NS], ALU.mult, [ha, hh["tmpg"]], [hh[("gas", c)]])
                    ACT(tmpa[:, 0:N], px[:, 0:N], AF.Copy, [hx_], [hh["tmpa"]])
                    TT(zb[:, c, 2 + lc:2 + lc + N], pc[:, 0:N], tmpa[:, 0:N], ALU.mult, [hc, hh["tmpa"]], [hzb[c]])
                    if tb == 3:
                        TT(tails[:, c, 32:34], pc[:, 510:512], tmpa[:, 510:512], ALU.mult, [hc, hh["tmpa"]], [hh[("tails", c)]])
                    if tb == 4:
                        TT(tails[:, c, 40:56], pc[:, 0:NS], tmpa[:, 0:NS], ALU.mult, [hc, hh["tmpa"]], [hh[("tails", c)]])
                    ACT(gb[:, c, lc:lc + N], pb[:, 0:N], AF.Copy, [hb], [hh[("gb", c)]])

            if hi == 1:
                stA = reg32.rearrange("p (g n) -> p g n", g=4)
                stB = tmp
                hstA = [hh[("xb", c)] for c in range(4)] + [hh[("sq4", c)] for c in range(4)]
                DMA("sp", stA[0:120, :, :], sca.rearrange("(g r) n -> r g n", r=120), writes=hstA)
                DMA("sp", stB[0:32, :], scb, writes=[hh["rtmp"]])
                for c in range(4):
                    ps, hp = mmrot.next()
                    for g in range(4):
                        TR(ps[:, g * 120:(g + 1) * 120], stA[0:120, g, c * 128:(c + 1) * 128], ident[0:120, 0:120], hstA + [hconst], [hp])
                    ACT(sAT[:, c, :], ps[:, 0:480], AF.Copy, [hp], [hh[("sAT", c)]])
                ps, hp = mmrot.next()
                for c in range(4):
                    TR(ps[:, c * 32:(c + 1) * 32], stB[0:32, c * 128:(c + 1) * 128], ident[0:32, 0:32], [hh["rtmp"], hconst], [hp])
                ACT(sBT[:, :, :], ps[:, 0:128].rearrange("p (c n) -> p c n", c=4), AF.Copy, [hp], [hh["sBT"]])

            for j in range(3):
                for c in range(4):
                    TS(diagB[:, j * 4 + c, :], identb, prmT[:, 76 + j * 4 + c:77 + j * 4 + c], ALU.mult, [hconst], [hh["diagB"]])
            for c in range(4):
                for j in range(31):
                    if j % 2 == 0:
                        TS(diag[:, j, :], identb, wAT[:, j * 4 + c:j * 4 + c + 1], ALU.mult, [hconst], [hh[("diag", j)]])
                    else:
                        ACT(diag[:, j, :], identb, AF.Copy, [hconst], [hh[("diag", j)]], scale=wAT[:, j * 4 + c:j * 4 + c + 1])
                for tb in tbs:
                    c0, N = TBS[tb]
                    lc = c0 - col0
                    hcv = hh[("cvo", c, tb)]
                    if tb < 4:
                        ps, hp = mmrot.next()
                        for j in range(31):
                            MM(ps[:, 0:N], diag[:, j, :], ga[:, c, lc + j:lc + j + N], j == 0, j == 30, [hh[("diag", j)], hga[c]], [hp])
                        ACT(cvo[:, c, lc:lc + N], ps[:, 0:N], AF.Identity, [hp, hconst], [hcv], bias=prmT[:, 64 + c:65 + c], scale=1.0)
                    else:
                        wv_ = bcast_ap(wAT[:, c:c + 1], [(0, NS), (4, 30)])
                        pA = prodA.rearrange("p (b j) -> p b j", b=NS)
                        TT(pA, sAT[:, c, :].rearrange("p (b j) -> p b j", b=NS), wv_, ALU.mult, [hh[("sAT", c)], hconst], [hh["prodA"]])
                        RED(cvo[:, c, lc:lc + NS], pA, [hh["prodA"]], [hcv])
                        STT(cvo[:, c, lc:lc + NS], gas[:, c, :], wAT[:, 120 + c:121 + c], cvo[:, c, lc:lc + NS], ALU.mult, ALU.add,
                            [hh[("gas", c)], hcv, hconst], [hcv])
                        TS(cvo[:, c, lc:lc + NS], cvo[:, c, lc:lc + NS], prmT[:, 64 + c:65 + c], ALU.add, [hcv, hconst], [hcv])
            for tb in tbs:
                c0, N = TBS[tb]
                lc = c0 - col0
                psm, hpm = statrot.next()
                pse, hpe = statrot.next()
                for c in range(4):
                    CP(xb[:, c, 0:N], cvo[:, c, lc:lc + N], [hh[("cvo", c, tb)]], [hh[("xb", c)]])
                    ACT(sq4[:, c, 0:N], cvo[:, c, lc:lc + N], AF.Square, [hh[("cvo", c, tb)]], [hh[("sq4", c)]])
                for c in range(4):
                    MM(psm[:, 0:N], onesb, xb[:, c, 0:N], c == 0, c == 3, [hh[("xb", c)], hconst], [hpm])
                for c in range(4):
                    MM(pse[:, 0:N], onesb, sq4[:, c, 0:N], c == 0, c == 3, [hh[("sq4", c)], hconst], [hpe])
                ACT(mean[:, 0:N], psm[:, 0:N], AF.Copy, [hpm], [hh["mean"]], scale=1.0 / 512)
                TT(msq[:, 0:N], mean[:, 0:N], mean[:, 0:N], ALU.mult, [hh["mean"]], [hh["msq"]])
                STT(msq[:, 0:N], pse[:, 0:N], 1.0 / 512, msq[:, 0:N], ALU.mult, ALU.subtract, [hpe, hh["msq"]], [hh["msq"]])
                ACT(tmp[:, 0:N], msq[:, 0:N], AF.Sqrt, [hh["msq"], hh["smalls"]], [hh["rtmp"]], bias=epsc, scale=1.0)
                RECIP(rstd[:, 0:N], tmp[:, 0:N], [hh["rtmp"]], [hh["rstd"]])
                for c in range(4):
                    tt = t1[c % 2]
                    ht = hh[("t1", c % 2)]
                    TT(tt[:, 0:N], cvo[:, c, lc:lc + N], mean[:, 0:N], ALU.subtract, [hh[("cvo", c, tb)], hh["mean"]], [ht])
                    TT(tt[:, 0:N], tt[:, 0:N], rstd[:, 0:N], ALU.mult, [ht, hh["rstd"]], [ht])
                    ACT(mixT[:, c, lc:lc + N], tt[:, 0:N], AF.Silu, [ht, hconst], [hh[("uT", c, tb)]],
                        bias=prmT[:, 72 + c:73 + c], scale=prmT[:, 68 + c:69 + c])
                for c in range(4):
                    hmx = hh[("uT", 4 + c, tb)]
                    if tb < 4:
                        ps, hp = mmrot.next()
                        for j in range(3):
                            MM(ps[:, 0:N], diagB[:, j * 4 + c, :], zb[:, c, lc + j:lc + j + N], j == 0, j == 2, [hh["diagB"], hzb[c]], [hp])
                        TT(mixT[:, 4 + c, lc:lc + N], ps[:, 0:N], gb[:, c, lc:lc + N], ALU.mult, [hp, hh[("gb", c)]], [hmx])
                    else:
                        tt = t1[c % 2]
                        ht = hh[("t1", c % 2)]
                        sb_ = sBT[:, c, :].rearrange("p (b j) -> p b j", j=2)
                        TS(tt[:, 0:NS], sb_[:, :, 0], prmT[:, 76 + c:77 + c], ALU.mult, [hh["sBT"], hconst], [ht])
                        STT(tt[:, 0:NS], sb_[:, :, 1], prmT[:, 80 + c:81 + c], tt[:, 0:NS], ALU.mult, ALU.add, [hh["sBT"], ht, hconst], [ht])
                        STT(tt[:, 0:NS], tails[:, c, 40:56], prmT[:, 84 + c:85 + c], tt[:, 0:NS], ALU.mult, ALU.add, [hh[("tails", c)], ht, hconst], [ht])
                        TT(mixT[:, 4 + c, lc:lc + NS], tt[:, 0:NS], gb[:, c, lc:lc + NS], ALU.mult, [ht, hh[("gb", c)]], [hmx])

            if hi == 1:
                def tm_out(src_fn, rows, dst, rd_fn):
                    ps, hp = mmrot.next()
                    for c in range(4):
                        TR(ps[0:rows, c * 128:(c + 1) * 128], src_fn(c), ident, rd_fn(c) + [hconst], [hp])
                    ACT(stg[0:rows, :], ps[0:rows, :], AF.Copy, [hp], [hh["stg"]])
                    DMA("sp", dst, stg[0:rows, :], reads=[hh["stg"]])
                tm_out(lambda c: tails[:, c, 0:30], 30, a_p, lambda c: [hh[("tails", c)]])
                tm_out(lambda c: tails[:, c, 32:34], 2, b_p, lambda c: [hh[("tails", c)]])
                DMA("sp", a_s[:, 0:29, :], sca.rearrange("(b j) n -> b j n", j=30)[:, 1:30, :])
                DMA("sp", b_s[:, 0:1, :], scb.rearrange("(b j) n -> b j n", j=2)[:, 1:2, :])
                tm_out(lambda c: gas[:, c, :], NS, a_s[:, 29, :], lambda c: [hh[("gas", c)]])
                tm_out(lambda c: tails[:, c, 40:56], NS, b_s[:, 1, :], lambda c: [hh[("tails", c)]])
            else:
                CP(halo[:, 0:4, 0:30], ga[:, :, 1024:1054], hga, [hh["halo"]])
                CP(halo[:, 4:8, 0:2], zb[:, :, 1024:1026], hzb, [hh["halo"]])
            pop_slots(nb2a)
            pre_outproj(w_out_conv, 8)
            P.fence()
            A.reset(m2)
            ysb = A.alloc(8 * W, F32).rearrange("p (c n) -> p c n", c=8)
            sqbuf2 = [A.alloc(512, BF16), A.alloc(512, BF16)]
            tmp2 = A.alloc(512, F32)
            rstd2 = A.alloc(512, F32)
            ytmp = [A.alloc(512, F32), A.alloc(512, F32)]
            nb_ = push_slots(2)
            out_proj(w_out_conv, 0, 8,
                     lambda k, tb: (mixT[:, k, TBS[tb][0] - col0:TBS[tb][0] - col0 + TBS[tb][1]], [hh[("uT", k, tb)]]),
                     tbs, ysb, col0)
            for tb in tbs:
                postnorm_residual(0, 1, tb, ysb, col0, sqbuf2, tmp2, rstd2, ytmp)
            pop_slots(nb_)
            if STOP >= 2:
                pre_mlp(0)
            P.fence()
            A.reset(m1)

        RUN0 = STOP >= 1 and not int(os.environ.get('MK_SKIP0', '0'))
        if RUN0:
            pre_l0a()
        P.fence()
        A.reset(m0)
        if RUN0:
            for hi, (col0, tbs) in enumerate(HALVES):
                layer0_half(hi, col0, tbs)
                if STOP >= 2:
                    nxt = pre_l0a if hi == 0 else (pre_l1a if STOP >= 3 else None)
                    mlp_half(0, col0, tbs, pre_next=nxt)

        def load_wcols(wd, row0, Kc, col0, ncols):
            slot, hs = wslot_next()
            assert Kc * ncols <= SLOT
            v = slot[:, 0:Kc * ncols].rearrange("p (k m) -> p k m", k=Kc)
            DMA("pool", v, wd[row0:row0 + Kc * 128, col0:col0 + ncols].rearrange("(k p) m -> p k m", p=128), writes=[hs])
            return v, hs

        def layer1_mixer():
            mL = A.mark()
            QT = A.alloc(10 * NT, BF16).rearrange("p (c n) -> p c n", c=10)
            KT = A.alloc(3 * NT, BF16).rearrange("p (c n) -> p c n", c=3)
            Vd = A.alloc(16 * 5 * 128, BF16).rearrange("p (t k m) -> p t k m", t=16, k=5)
            QTs = A.alloc(10 * NS, F32).rearrange("p (c n) -> p c n", c=10)
            KTs = A.alloc(3 * NS, F32).rearrange("p (c n) -> p c n", c=3)
            VTs = A.alloc(5 * NS, F32).rearrange("p (c n) -> p c n", c=5)
            VsTM = A.alloc(320, F32)
            knew = A.alloc(384, F32)
            smix = A.alloc(6 * NS, BF16).rearrange("p (c n) -> p c n", c=6)
            mP = A.mark()
            uT = A.alloc(8 * HW_, BF16).rearrange("p (c n) -> p c n", c=8)
            sqbuf = [A.alloc(512, BF16), A.alloc(512, BF16)]
            tmp = A.alloc(512, F32)
            rstd = A.alloc(512, F32)
            cst = A.alloc(2 * HW_, F32).rearrange("p (c n) -> p c n", c=2)
            qbs = [A.alloc(512, BF16), A.alloc(512, BF16)]
            t1s = [A.alloc(512, F32), A.alloc(512, F32)]
            t2s = [A.alloc(512, F32), A.alloc(512, F32)]
            rctr = [0]
            K32 = A.alloc(512, F32)
            vst = A.alloc(320, F32)
            kst = A.alloc(128, F32)


            def rope(ps, hp, N, lc=0):
                i = rctr[0] % 2
                rctr[0] += 1
                qb, t1, t2 = qbs[i], t1s[i], t2s[i]
                hq, h1, h2 = hh[("qb", i)], hh[("t1", i)], hh[("t2", i)]
                ACT(qb[:, 0:N], ps[:, 0:N], AF.Copy, [hp], [hq])
                pr, hpr = rrot.next()
                MM(pr[:, 0:N], prot, qb[:, 0:N], True, True, [hq, hconst], [hpr])
                TT(t1[:, 0:N], ps[:, 0:N], cst[:, 0, lc:lc + N], ALU.mult, [hp, hh["cst"]], [h1])
                TT(t2[:, 0:N], pr[:, 0:N], cst[:, 1, lc:lc + N], ALU.mult, [hpr, hh["cst"]], [h2])
                return t1, t2, h1, h2

            def perm_write(dstT, m, rows, kind, tb, hw, R):
                t1, t2, h1, h2 = R
                c0, N = TBS[tb]
                r0, r1 = rows
                if kind == 1:
                    o = dstT[r0:r1, m, c0:c0 + N]
                    a, b = t1[r0:r1, 0:N], t2[r0:r1, 0:N]
                else:
                    o = dstT[r0:r1, m, 0:NTOK].rearrange("p (r i) -> p r i", r=kind)[:, :, c0 // kind:c0 // kind + N // kind]
                    a = t1[r0:r1, 0:N].rearrange("p (i r) -> p r i", r=kind)
                    b = t2[r0:r1, 0:N].rearrange("p (i r) -> p r i", r=kind)
                TT(o, a, b, ALU.add, [h1, h2], [hw])

            QKIND = [((1, 1),)] * 4 + [((1, 4),)] * 4 + [((16, 16),)] * 2
            KKIND = [(1, 1), (1, 4), (16, 16)]

            nbase = len(wslots)
            PASSES = [[0, 1], [2, 3, 4]]
            psrot = Rot([0, 1, 2])
            rrot = Rot([3, 4, 5])
            for tbs_ in PASSES:
                pc0 = TBS[tbs_[0]][0]
                PW = sum(TBS[tb][1] for tb in tbs_)
                for tb in tbs_:
                    norm_block(1, 0, tb, uT, pc0, sqbuf, tmp, rstd, htb="L1")
                DMA("sp", cst[:, :, 0:PW], cs_d[:, :, pc0:pc0 + PW], writes=[hh["cst"]])
                pend = [None]

                def flush():
                    if pend[0] is not None:
                        f_, a_ = pend[0]
                        pend[0] = None
                        f_(*a_)

                def post_q(ps, hp, m, tb):
                    c0, N = TBS[tb]
                    lc = c0 - pc0
                    R = rope(ps, hp, N, lc)
                    if tb == 4:
                        TT(QTs[:, m, :], R[0][:, 0:N], R[1][:, 0:N], ALU.add, [R[2], R[3]], [hh[("QTs", m)]])
                    else:
                        ka, kb = QKIND[m][0]
                        if ka == kb:
                            perm_write(QT, m, (0, 128), ka, tb, hh[("QT", m, tb)], R)
                        else:
                            perm_write(QT, m, (0, 64), ka, tb, hh[("QT", m, tb, 0)], R)
                            perm_write(QT, m, (64, 128), kb, tb, hh[("QT", m, tb, 1)], R)

                def post_k(ps, hp, kc, tb):
                    c0, N = TBS[tb]
                    lc = c0 - pc0
                    R = rope(ps, hp, N, lc)
                    if tb == 4:
                        TT(KTs[:, kc, :], R[0][:, 0:N], R[1][:, 0:N], ALU.add, [R[2], R[3]], [hh[("KTs", kc)]])
                        return
                    ka, kb = KKIND[kc]
                    if ka == kb:
                        perm_write(KT, kc, (0, 128), ka, tb, hh[("KT", kc, tb)], R)
                    else:
                        perm_write(KT, kc, (0, 64), ka, tb, hh[("KT", kc, tb, 0)], R)
                        perm_write(KT, kc, (64, 128), kb, tb, hh[("KT", kc, tb, 1)], R)
                    need = [t for t in range(4) if (M4A & 4) and (kc == 2 or (kc == 1 and 4 * tb + t >= 12) or (kc == 0 and 4 * tb + t == 15))]
                    if need:
                        TT(K32[:, 0:N], R[0][:, 0:N], R[1][:, 0:N], ALU.add, [R[2], R[3]], [hh["K32"]])
                    for t in need:
                        T = 4 * tb + t
                        pt, hpt = rrot.next()
                        TR(pt[:, 0:128], K32[:, t * 128:(t + 1) * 128], ident, [hh["K32"], hconst], [hpt])
                        ACT(kst[:, 0:128], pt[:, 0:128], AF.Copy, [hpt], [hh["kst"]])
                        if kc == 2:
                            DMA("sp", d_p[2][T * 128:(T + 1) * 128, 0:64], kst[:, 0:64], reads=[hh["kst"]], writes=[hh[("d2pk", T)]])
                        elif kc == 1:
                            DMA("sp", d_p[1][(T - 12) * 128:(T - 11) * 128, 0:64], kst[:, 64:128], reads=[hh["kst"]])
                            if T == 15:
                                DMA("sp", d_p[0][:, 0:64], kst[:, 0:64], reads=[hh["kst"]])
                        else:
                            DMA("sp", swa_p[:, 0:128], kst[:, 0:128], reads=[hh["kst"]])

                def main_mm(v, i, hs, tb):
                    c0, N = TBS[tb]
                    lc = c0 - pc0
                    ps, hp = psrot.next()
                    for k in range(8):
                        MM(ps[:, 0:N], v[:, i, k, :], uT[:, k, lc:lc + N], k == 0, k == 7, [hs, hh[("uT", k, "L1")]], [hp])
                    return ps, hp

                for mg in range(3):
                    ms = list(range(mg * 4, min(10, mg * 4 + 4)))
                    v, hs = load_wchunks(wq_d, 0, 8, [m * 128 for m in ms])
                    for i, m in enumerate(ms):
                        for tb in tbs_:
                            ps, hp = main_mm(v, i, hs, tb)
                            flush()
                            pend[0] = (post_q, (ps, hp, m, tb))
                v, hs = load_wchunks(wk_d, 0, 8, [0, 128, 256])
                for kc in range(3):
                    for tb in tbs_:
                        ps, hp = main_mm(v, kc, hs, tb)
                        flush()
                        pend[0] = (post_k, (ps, hp, kc, tb))
                flush()
                vw, hvw = load_wcols(wv_d, 0, 8, 0, 320)
                for tb in tbs_:
                    c0, N = TBS[tb]
                    lc = c0 - pc0
                    if tb == 4:
                        ps, hp = mmrot.next()
                        for k in range(8):
                            MM(ps[0:NS, 0:320], uT[:, k, lc:lc + NS], vw[:, k, :], k == 0, k == 7, [hvw, hh[("uT", k, "L1")]], [hp])
                        ACT(VsTM[0:NS, :], ps[0:NS, 0:320], AF.Copy, [hp], [hh["VsTM"]])
                        vd_, hvd = load_wchunks(wvd_d, 0, 8, [0, 128, 256, 384])
                        vd2, hvd2 = load_wchunks(wvd_d, 0, 8, [512])
                        for i in range(5):
                            vv, hv_, ii = (vd_, hvd, i) if i < 4 else (vd2, hvd2, 0)
                            ps, hp = mmrot.next()
                            for k in range(8):
                                MM(ps[:, 0:NS], vv[:, ii, k, :], uT[:, k, lc:lc + NS], k == 0, k == 7, [hv_, hh[("uT", k, "L1")]], [hp])
                            ACT(VTs[:, i, :], ps[:, 0:NS], AF.Copy, [hp], [hh[("VTs", i)]])
                        ps, hp = mmrot.next()
                        for kc in range(3):
                            TR(ps[0:NS, kc * 128:(kc + 1) * 128], KTs[:, kc, :], ident, [hh[("KTs", kc)], hconst], [hp])
                        ACT(knew[0:NS, :], ps[0:NS, 0:384], AF.Copy, [hp], [hh["knew"]])
                        DMA("sp", swa_s[:, 127, 0:128], knew[0:NS, 0:128], reads=[hh["knew"]])
                        DMA("sp", swa_s[:, 127, 128:256], VsTM[0:NS, 0:128], reads=[hh["VsTM"]])
                        for g, Wg in enumerate((128, 512, 2048)):
                            DMA("sp", d_s[g][:, Wg - 1, 0:64], knew[0:NS, 128 + 64 * g:192 + 64 * g], reads=[hh["knew"]])
                            DMA("sp", d_s[g][:, Wg - 1, 64:128], VsTM[0:NS, 128 + 64 * g:192 + 64 * g], reads=[hh["VsTM"]])
                        continue
                    for t in range(4):
                        T = 4 * tb + t
                        ps, hp = mmrot.next()
                        for k in range(8):
                            MM(ps[:, 0:320], uT[:, k, lc + t * 128:lc + (t + 1) * 128], vw[:, k, :], k == 0, k == 7, [hvw, hh[("uT", k, "L1")]], [hp])
                        src3 = ps[:, 0:192].rearrange("p (k m) -> p k m", k=3)
                        ACT(Vd[:, T, 0:3, 0:64], src3, AF.Copy, [hp], [hh[("Vd", T, 0)]])
                        CP(Vd[:, T, 0:3, 64:128], src3, [hp], [hh[("Vd", T, 1)]])
                        ACT(vst[:, 0:320], ps[:, 0:320], AF.Copy, [hp], [hh["vst"]])
                        DMA("sp", d_p[2][T * 128:(T + 1) * 128, 64:128], vst[:, 256:320], reads=[hh["vst"]], writes=[hh[("d2pv", T)]])
                        if T >= 12:
                            DMA("sp", d_p[1][(T - 12) * 128:(T - 11) * 128, 64:128], vst[:, 192:256], reads=[hh["vst"]])
                        if T == 15:
                            DMA("sp", d_p[0][:, 64:128], vst[:, 128:192], reads=[hh["vst"]])
                            DMA("sp", swa_p[:, 128:256], vst[:, 0:128], reads=[hh["vst"]])
                    for r in range(4):
                        ps, hp = mmrot.next()
                        for k in range(8):
                            MM(ps[:, 0:64], uT[:, k, lc + r:lc + 512:4], vw[:, k, 192:256], k == 0, k == 7, [hvw, hh[("uT", k, "L1")]], [hp])
                        ACT(Vd[:, 4 * r + tb, 3, 0:64], ps[:, 0:64], AF.Copy, [hp], [hh[("Vd3", r, tb, 0)]])
                        CP(Vd[:, 4 * r + tb, 3, 64:128], ps[:, 0:64], [hp], [hh[("Vd3", r, tb, 1)]])
            del wslots[nbase:]
            NSL[0] = len(wslots)
            if L1STOP > 3:
                pre_outproj(w_out_attn, 6)
            src = d_p[2].rearrange("(i r) n -> i r n", r=16)[:, :, 64:128]
            rds = [hh[("d2pv", T)] for T in range(16)]
            if not int(os.environ.get("MK_NORB", "0")):
                for r in range(16):
                    DMA("pool", Vd[:, r, 4, 0:64], src[:, r, :], reads=rds, writes=[hh[("Vd4a", r)]])
                    DMA("pool", Vd[:, r, 4, 64:128], src[:, r, :], reads=rds, writes=[hh[("Vd4b", r)]])
            P.fence()
            if L1STOP <= 1:
                A.reset(mL)
                return

            if not NOCOPY:
                DMA("act", swa_s[:, 0:127, :], c_swa[:, 1:128, :])
                for g, Wg in enumerate((128, 512, 2048)):
                    DMA("act", d_s[g][:, 0:Wg - 1, :], c_d[g][:, 1:Wg, :])
            A.reset(mP)
            mixT = A.alloc(6 * NT, BF16).rearrange("p (c n) -> p c n", c=6)
            PT = [A.alloc(256, BF16), A.alloc(256, BF16)]
            accN = A.alloc(NTOK, F32)
            accD = A.alloc(NTOK, F32)
            rec = A.alloc(512, F32)
            PT = PT + [A.alloc(256, BF16), A.alloc(256, BF16)]
            srot = Rot([0, 1, 6, 7])
            orot = Rot([2, 4])
            items = []

            def add_head(qc, base, kc, kv, kind, evac):
                cfg = dict(qc=qc, base=base, kc=kc, kv=kv, kind=kind, evac=evac)
                for T in range(16):
                    items.append((cfg, T))

            for h in range(8):
                kv, g = h // 4, h % 4
                mrows = slice((h % 2) * 64, (h % 2) * 64 + 64)
                mch = h // 2

                def evac(b, po, hpo, pd, hpd, h=h, mrows=mrows, mch=mch):
                    TS(rec[mrows, :], pd[mrows, :], smalls[mrows, 16 + h:17 + h], ALU.add, [hpd, hh["expsink"]], [hh["rec"]])
                    RECIP(rec[mrows, :], rec[mrows, :], [hh["rec"]], [hh["rec"]])
                    TT(mixT[mrows, mch, b * 512:(b + 1) * 512], po[mrows, :], rec[mrows, :], ALU.mult, [hpo, hh["rec"]], [hh[("mixT", mch, h % 2, b)]])
                add_head(g, kv * 64, 0, kv, 1, evac)
            for s_ in range(4):
                mrows = slice((s_ % 2) * 64, (s_ % 2) * 64 + 64)
                mch = 4 + s_ // 2
                for grp in range(3):
                    kind = (1, 4, 16)[grp]

                    def evac(b, po, hpo, pd, hpd, grp=grp, kind=kind, mrows=mrows, mch=mch, s_=s_):
                        if kind == 1:
                            on = accN[mrows, b * 512:(b + 1) * 512]
                            od = accD[mrows, b * 512:(b + 1) * 512]
                            sn, sd = po[mrows, :], pd[mrows, :]
                        elif kind == 4:
                            on = accN[mrows, b:NTOK:4]
                            od = accD[mrows, b:NTOK:4]
                            sn, sd = po[mrows, :], pd[mrows, :]
                        else:
                            on = accN[mrows, :].rearrange("p (i r) -> p i r", r=16)[:, :, 4 * b:4 * b + 4]
                            od = accD[mrows, :].rearrange("p (i r) -> p i r", r=16)[:, :, 4 * b:4 * b + 4]
                            sn = po[mrows, :].rearrange("p (t i) -> p i t", t=4)
                            sd = pd[mrows, :].rearrange("p (t i) -> p i t", t=4)
                        if grp == 0:
                            ACT(on, sn, AF.Copy, [hpo], [hh["accN"]])
                            CP(od, sd, [hpd], [hh["accD"]])
                        else:
                            TT(on, on, sn, ALU.add, [hpo, hh["accN"]], [hh["accN"]])
                            TT(od, od, sd, ALU.add, [hpd, hh["accD"]], [hh["accD"]])
                        if grp == 2 and b == 3:
                            for bb in range(4):
                                RECIP(rec[mrows, :], accD[mrows, bb * 512:(bb + 1) * 512], [hh["accD"]], [hh["rec"]])
                                TT(mixT[mrows, mch, bb * 512:(bb + 1) * 512], accN[mrows, bb * 512:(bb + 1) * 512], rec[mrows, :], ALU.mult,
                                   [hh["accN"], hh["rec"]], [hh[("mixT", mch, s_ % 2, bb)]])
                    if grp == 0:
                        add_head(4 + s_, 0, 1, 2, 1, evac)
                    elif grp == 1:
                        add_head(4 + s_, 64, 1, 3, 4, evac)
                    else:
                        add_head(8 + s_ // 2, (s_ % 2) * 64, 2, 4, 16, evac)

            state = {}

            def issue_S(i):
                cfg, T = items[i]
                kind, base = cfg["kind"], cfg["base"]
                rows = slice(base, base + 64)
                j = T if kind == 1 else (T % 4 if kind == 4 else 0)
                pss, hs_ = srot.next()
                lo = 0 if j > 0 else 128
                qsl = QT[rows, cfg["qc"], T * 128:(T + 1) * 128]
                if j > 0:
                    MM(pss[:, 0:128], KT[rows, cfg["kc"], (T - 1) * 128:T * 128], qsl, True, True, [], [hs_])
                MM(pss[:, 128:256], KT[rows, cfg["kc"], T * 128:(T + 1) * 128], qsl, True, True, [], [hs_])
                pt = PT[i % 4]
                hpt = hh[("PT", i % 4)]
                ACT(pt[:, lo:256], pss[:, lo:256], AF.Exp, [hs_], [hpt], scale=0.125)
                TT(pt[:, lo:256], pt[:, lo:256], maskpd[:, lo:256], ALU.mult, [hpt, hconst], [hpt], eng="pool")
                state[i] = (pt, hpt, j)

            def issue_PV(i):
                cfg, T = items[i]
                pt, hpt, j = state.pop(i)
                kv = cfg["kv"]
                if T % 4 == 0:
                    po, hpo = orot.next()
                    ob = orot.banks[(orot.i - 1) % 2]
                    cfg["o"] = (po, hpo, psb[ob + 1], hps[ob + 1])
                po, hpo, pd, hpd = cfg["o"]
                osl = slice((T % 4) * 128, (T % 4 + 1) * 128)
                if j > 0:
                    MM(po[:, osl], Vd[:, T - 1, kv, :], pt[:, 0:128], True, False, [hpt], [hpo])
                    MM(po[:, osl], Vd[:, T, kv, :], pt[:, 128:256], False, True, [hpt], [hpo])
                    MM(pd[:, osl], onesb, pt[:, 0:128], True, False, [hpt, hconst], [hpd])
                    MM(pd[:, osl], onesb, pt[:, 128:256], False, True, [hpt, hconst], [hpd])
                else:
                    MM(po[:, osl], Vd[:, T, kv, :], pt[:, 128:256], True, True, [hpt], [hpo])
                    MM(pd[:, osl], onesb, pt[:, 128:256], True, True, [hpt, hconst], [hpd])
                if T % 4 == 3:
                    cfg["evac"](T // 4, po, hpo, pd, hpd)

            LOOK = 3
            for i in range(len(items) + LOOK):
                if i < len(items):
                    issue_S(i)
                if i >= LOOK:
                    issue_PV(i - LOOK)
            P.fence()
            if L1STOP <= 2:
                A.reset(mL)
                return

            A.reset(mL)
            _skip = A.alloc(1, F32)
            A.reset(mL)
            Kc = A.alloc(NS * 64, F32).rearrange("p (b d) -> p b d", b=NS)
            Vc = A.alloc(NS * 128, F32).rearrange("p (b d) -> p b d", b=NS)
            Qs = A.alloc(256, F32)
            Qd = A.alloc(NS * 256, F32).rearrange("p (b c) -> p b c", b=NS)
            prod = A.alloc(512, F32)
            Sall = A.alloc(64, F32)
            Pn = A.alloc(64, F32)
            Pnew = A.alloc(64, F32)
            den = A.alloc(64, F32)
            num = A.alloc(64, F32)
            numD = A.alloc(64, F32)
            denD = A.alloc(64, F32)
            prT = A.alloc(NS, F32)
            outv = A.alloc(64, F32)
            assert A.off <= mP - 0 or True

            def v3(ap):
                return ap.rearrange("p (b s) -> p b s", s=4)

            groups = []
            for kv in range(2):
                groups.append(dict(heads=[(g, kv * 64) for g in range(4)], kc=0, kbase=kv * 64, vi=kv, cache=c_swa, kcol=kv * 64, vcol=128 + kv * 64,
                                   step=1, swa=kv))
            groups.append(dict(heads=[(4 + s, 0) for s in range(4)], kc=1, kbase=0, vi=2, cache=c_d[0], kcol=0, vcol=64, step=1, swa=None))
            groups.append(dict(heads=[(4 + s, 64) for s in range(4)], kc=1, kbase=64, vi=3, cache=c_d[1], kcol=0, vcol=64, step=4, swa=None))
            groups.append(dict(heads=[(8 + s // 2, (s % 2) * 64) for s in range(4)], kc=2, kbase=None, vi=4, cache=c_d[2], kcol=0, vcol=64, step=16, swa=None))
            for gi, G in enumerate(groups):
                cview = G["cache"].rearrange("b j n -> j b n")
                st_ = G["step"]
                for q4 in range(4):
                    bs = slice(4 * q4, 4 * q4 + 4)
                    DMA("sp", Kc[:, bs, :], cview[0:128 * st_:st_, bs, G["kcol"]:G["kcol"] + 64], writes=[hh[("Kc", q4)]])
                    DMA("sp", Vc[:, bs, 0:64], cview[0:128 * st_:st_, bs, G["vcol"]:G["vcol"] + 64], writes=[hh[("Vc", q4, 0)]])
                    DMA("sp", Vc[:, bs, 64:128], cview[0:128 * st_:st_, bs, G["vcol"]:G["vcol"] + 64], writes=[hh[("Vc", q4, 1)]])
                pq = [mmrot.next(), mmrot.next()]
                for s, (qc, qb_) in enumerate(G["heads"]):
                    ps, hp = pq[qb_ // 64]
                    TR(ps[0:NS, s * 64:(s + 1) * 64], QTs[qb_:qb_ + 64, qc, :], ident[qb_:qb_ + 64, qb_:qb_ + 64], [hh[("QTs", qc)], hconst], [hp])
                for s, (qc, qb_) in enumerate(G["heads"]):
                    ps, hp = pq[qb_ // 64]
                    ACT(Qs[0:NS, s * 64:(s + 1) * 64], ps[0:NS, s * 64:(s + 1) * 64], AF.Copy, [hp], [hh["Qs"]])
                TT(Qd[0:NS, :, :], bcast_ap(Qs[0:NS, 0:1], [(0, NS), (1, 256)]), bcast_ap(ident[0:NS, 0:1], [(1, NS), (0, 256)]), ALU.mult,
                   [hh["Qs"], hconst], [hh["Qd"]])
                for bp in range(NS // 2):
                    ps, hp = mmrot.next()
                    MM(ps[:, 0:512], ones32[0:NS, :], Qd[0:NS, 2 * bp:2 * bp + 2, :], True, True, [hh["Qd"], hconst], [hp])
                    TT(prod[:, :].rearrange("p (b s d) -> p b s d", b=2, s=4), ps[:, 0:512].rearrange("p (b s d) -> p b s d", b=2, s=4),
                       bcast_ap(Kc[:, 2 * bp, 0:1], [(64, 2), (0, 4), (1, 64)]), ALU.mult, [hp] + [hh[("Kc", q4)] for q4 in range(4)], [hh["prod"]])
                    o = Sall[:, 8 * bp:8 * bp + 8].rearrange("p (b s) -> p b s", b=2)
                    RED(o, prod[:, :].rearrange("p (b s d) -> p b s d", b=2, s=4), [hh["prod"]], [hh["Sall"]])
                psn, hpn = mmrot.next()
                for s, (qc, qb_) in enumerate(G["heads"]):
                    TT(prT[:, :], QTs[:, qc, :], KTs[:, G["kc"], :], ALU.mult, [hh[("QTs", qc)], hh[("KTs", G["kc"])]], [hh["prT"]])
                    MM(psn[:, s * NS:(s + 1) * NS], selh[qb_ // 64], prT[:, :], True, True, [hh["prT"], hconst], [hpn])
                ACT(Pn[:, :], Sall[:, :], AF.Exp, [hh["Sall"]], [hh["Pn"]], scale=0.125)
                ACT(Pnew[:, :].rearrange("p (b s) -> p s b", s=4), psn[:, 0:64].rearrange("p (s b) -> p s b", s=4), AF.Exp, [hpn], [hh["Pnew"]], scale=0.125)
                psd, hpd = mmrot.next()
                MM(psd[:, 0:64], ones32, Pn[:, :], True, True, [hh["Pn"], hconst], [hpd])
                TT(den[:, :], psd[:, 0:64], Pnew[:, :], ALU.add, [hpd, hh["Pnew"]], [hh["den"]])
                if G["swa"] is not None:
                    kv = G["swa"]
                    TT(v3(den[:, :]), v3(den[:, :]), bcast_ap(smalls[:, 16 + 4 * kv:17 + 4 * kv], [(0, NS), (1, 4)]), ALU.add,
                       [hh["den"], hh["expsink"]], [hh["den"]])
                psv, hpv = mmrot.next()
                for b in range(NS):
                    MM(psv[:, 4 * b:4 * b + 4], Vc[:, b, :], Pn[:, 4 * b:4 * b + 4], True, True, [hh[("Vc", q4, i)] for q4 in range(4) for i in range(2)] + [hh["Pn"]], [hpv])
                TT(v3(num[:, :]), v3(Pnew[:, :]), bcast_ap(VTs[:, G["vi"], 0:1], [(1, NS), (0, 4)]), ALU.mult, [hh["Pnew"], hh[("VTs", G["vi"])]], [hh["num"]])
                TT(num[:, :], num[:, :], psv[:, 0:64], ALU.add, [hh["num"], hpv], [hh["num"]])
                if G["swa"] is not None:
                    kv = G["swa"]
                    RECIP(den[:, :], den[:, :], [hh["den"]], [hh["den"]])
                    TT(outv[:, :], num[:, :], den[:, :], ALU.mult, [hh["num"], hh["den"]], [hh["outv"]])
                    for s in range(4):
                        h = kv * 4 + s
                        rows = slice((h % 2) * 64, (h % 2) * 64 + 64)
                        CP(smix[rows, h // 2, :], outv[rows, s:64:4], [hh["outv"]], [hh[("smix", h // 2, h % 2)]])
                else:
                    if gi == 2:
                        CP(numD[:, :], num[:, :], [hh["num"]], [hh["numD"]])
                        CP(denD[:, :], den[:, :], [hh["den"]], [hh["denD"]])
                    else:
                        TT(numD[:, :], numD[:, :], num[:, :], ALU.add, [hh["num"], hh["numD"]], [hh["numD"]])
                        TT(denD[:, :], denD[:, :], den[:, :], ALU.add, [hh["den"], hh["denD"]], [hh["denD"]])
            RECIP(denD[:, :], denD[:, :], [hh["denD"]], [hh["denD"]])
            TT(outv[:, :], numD[:, :], denD[:, :], ALU.mult, [hh["numD"], hh["denD"]], [hh["outv"]])
            for s in range(4):
                rows = slice((s % 2) * 64, (s % 2) * 64 + 64)
                CP(smix[rows, 4 + s // 2, :], outv[rows, s:64:4], [hh["outv"]], [hh[("smix", 4 + s // 2, s % 2)]])
            P.fence()
            if L1STOP <= 3:
                A.reset(mL)
                return

            A.reset(mL)
            W = HW_
            ysb = A.alloc(8 * W, F32).rearrange("p (c n) -> p c n", c=8)
            sqbuf2 = [A.alloc(512, BF16), A.alloc(512, BF16)]
            tmp2 = A.alloc(512, F32)
            rstd2 = A.alloc(512, F32)
            ytmp = [A.alloc(512, F32), A.alloc(512, F32)]
            nb_ = push_slots(2)
            assert A.off <= mP

            def rhs_fn(k, tb):
                if tb == 4:
                    return smix[:, k, :], []
                return mixT[:, k, TBS[tb][0]:TBS[tb][0] + 512], []
            for hi_, (col0, tbs) in enumerate(HALVES):
                out_proj(w_out_attn, 0, 6, rhs_fn, tbs, ysb, col0)
                for tb in tbs:
                    postnorm_residual(1, 1, tb, ysb, col0, sqbuf2, tmp2, rstd2, ytmp)
                if hi_ == 1:
                    pop_slots(nb_)
                    if STOP >= 4:
                        pre_mlp(1)
                P.fence()
            A.reset(mL)

        if STOP >= 3:
            layer1_mixer()
        if STOP >= 4:
            for hi_, (col0, tbs) in enumerate(HALVES):
                mlp_half(1, col0, tbs, pre_next=(lambda: pre_mlp(1)) if hi_ == 0 else None)

        A.reset(m0)
        yo = [A.alloc(1024, F32), A.alloc(1024, F32)]
        rot = Rot([0, 1, 2, 3])
        for t in range(17):
            y = yo[t % 2]
            hy = hh[("yo", t % 2)]
            rows = 128 if t < 16 else NS
            tbk = t // 4 if t < 16 else 4
            for g in range(2):
                ps, hp = rot.next()
                for i in range(4):
                    c = g * 4 + i
                    TR(ps[0:rows, i * 128:(i + 1) * 128], hT[:, c, t * 128:t * 128 + rows], ident, [hh[("hT", c, tbk)], hconst], [hp])
                if g == 0:
                    ACT(y[0:rows, 0:512], ps[0:rows, :], AF.Copy, [hp], [hy])
                else:
                    CP(y[0:rows, 512:1024], ps[0:rows, :], [hp], [hy])
            dst = y_p[t * 128:(t + 1) * 128, :] if t < 16 else y_s
            DMA("sp", dst, y[0:rows, :], reads=[hy])

        assert not PRE, list(PRE.keys())
        P.emit(st)
        build_program.stats = dict(P.stats)
        build_program.arena_peak = A.peak
    return nc


def _consts():
    c32 = np.zeros((128, 512), np.float32)
    c32[:, 0:128] = np.eye(128, dtype=np.float32)
    c32[0:64, 128:256] = 1.0
    c32[64:128, 256:384] = 1.0
    c32[:, 384:512] = 1.0
    cb = np.zeros((128, 768), np.float32)
    prot = np.zeros((128, 128), np.float32)
    for base in (0, 64):
        for i in range(8):
            prot[base + i + 8, base + i] = 1.0
            prot[base + i, base + i + 8] = 1.0
    cb[:, 0:128] = prot
    cb[:, 128:256] = 1.0
    p = np.arange(128)[:, None]
    f = np.arange(128)[None, :]
    cb[:, 256:384] = (f <= p)
    cb[:, 384:512] = (f >= p)
    cb[:, 512:640] = np.eye(128, dtype=np.float32)
    half = 8
    inv = (np.float32(500000.0) ** (-np.arange(half, dtype=np.float32) / half)).astype(np.float32)
    pos = np.concatenate([np.arange(NTOK, dtype=np.float32), np.full(NS, 8192.0, np.float32)])
    ang = (pos[None, :] * inv[:, None]).astype(np.float32)
    cs = np.zeros((128, 2, NT), np.float32)
    cs[:, 0, :] = 1.0
    for base in (0, 64):
        cs[base:base + 8, 0] = np.cos(ang)
        cs[base + 8:base + 16, 0] = np.cos(ang)
        cs[base:base + 8, 1] = -np.sin(ang)
        cs[base + 8:base + 16, 1] = np.sin(ang)
    return c32, cb, cs


_NC_CACHE = {}


def kernel(x_prompt, x_sample, state_conv_a, state_conv_b, cache_swa_kv, cache_dil0_kv, cache_dil1_kv,
           cache_dil2_kv, norm_g, w_in_conv, conv_a_w, conv_a_b, conv_a_ln_g, conv_a_ln_b, conv_b_w,
           w_out_conv, w_in_attn, attn_sinks, w_out_attn, mlp_w1, mlp_w2):
    f = lambda a: np.ascontiguousarray(np.asarray(a, dtype=np.float32))
    x_prompt, x_sample = f(x_prompt), f(x_sample)
    if "nc" not in _NC_CACHE:
        _NC_CACHE["nc"] = build_program()
    nc = _NC_CACHE["nc"]
    c32, cb, cs = _consts()
    prm1 = np.concatenate([f(norm_g).reshape(64, 128), f(conv_a_b).reshape(4, 128), f(conv_a_ln_g).reshape(4, 128),
                           f(conv_a_ln_b).reshape(4, 128), f(conv_b_w).reshape(12, 128)], 0)
    prm2 = f(conv_a_w).reshape(124, 128)
    wi = f(w_in_attn)[0]
    qs = lambda h: wi[:, h * 64:(h + 1) * 64]
    qd = lambda g, h: wi[:, 768 + 384 * g + h * 64: 768 + 384 * g + (h + 1) * 64]
    kd = lambda g: wi[:, 768 + 384 * g + 256: 768 + 384 * g + 320]
    vd = lambda g: wi[:, 768 + 384 * g + 320: 768 + 384 * g + 384]
    ks = lambda kv: wi[:, 512 + kv * 64: 512 + (kv + 1) * 64]
    vs = lambda kv: wi[:, 640 + kv * 64: 640 + (kv + 1) * 64]
    wq = np.concatenate([np.concatenate([qs(g), qs(4 + g)], 1) for g in range(4)] +
                        [np.concatenate([qd(0, g), qd(1, g)], 1) for g in range(4)] +
                        [np.concatenate([qd(2, 0), qd(2, 1)], 1), np.concatenate([qd(2, 2), qd(2, 3)], 1)], 1)
    wk = np.concatenate([ks(0), ks(1), kd(0), kd(1), kd(2), kd(2)], 1)
    wv = np.concatenate([vs(0), vs(1), vd(0), vd(1), vd(2)], 1)
    wvd = np.concatenate([vs(0), vs(0), vs(1), vs(1), vd(0), vd(0), vd(1), vd(1), vd(2), vd(2)], 1)
    shared = {
        "prm1": f(prm1), "prm2": prm2, "sinks": f(attn_sinks).reshape(1, 8), "cst32": c32, "cstb": cb, "cs": cs,
        "w_in_conv": f(w_in_conv)[0], "w_out_conv": f(w_out_conv)[0], "wq": f(wq), "wk": f(wk), "wv": f(wv), "wvd": f(wvd),
        "w_out_attn": f(w_out_attn)[0], "w1": f(mlp_w1), "w2": f(mlp_w2),
    }
    sca = f(state_conv_a)[0]
    scb = f(state_conv_b)[0]
    cswa = f(cache_swa_kv)[0]
    cds = [f(cache_dil0_kv)[0], f(cache_dil1_kv)[0], f(cache_dil2_kv)[0]]
    in_maps = []
    for i in range(8):
        s = slice(NS * i, NS * (i + 1))
        m = dict(shared)
        m["xp"] = x_prompt[i]
        m["xs"] = x_sample[s, 0, :]
        m["sca"] = sca[s].reshape(NS * 30, 512)
        m["scb"] = scb[s].reshape(NS * 2, 512)
        m["c_swa"] = cswa[s].reshape(NS, 128, 256)
        for g in range(3):
            m["c_d%d" % g] = cds[g][s].reshape(NS, cds[g].shape[1], 128)
        in_maps.append(m)
    res = run_bass_kernel_spmd(nc, in_maps, core_ids=list(range(8)))
    R = res.results
    cat = lambda k: np.stack([np.asarray(r[k]) for r in R], 0)
    y_prompt = cat("y_p")
    y_sample = np.concatenate([np.asarray(r["y_s"]) for r in R], 0).reshape(128, 1, 1024)
    a_p = cat("a_p")[None]
    a_s = np.concatenate([np.asarray(r["a_s"]) for r in R], 0)[None]
    b_p = cat("b_p")[None]
    b_s = np.concatenate([np.asarray(r["b_s"]) for r in R], 0)[None]
    swa_p = cat("swa_p").reshape(1, 8, 128, 2, 2, 64)
    swa_s = np.concatenate([np.asarray(r["swa_s"]) for r in R], 0).reshape(1, 128, 128, 2, 2, 64)
    outs = [y_prompt, y_sample, a_p, a_s, b_p, b_s, swa_p, swa_s]
    for g, Wg in enumerate((128, 512, 2048)):
        outs.append(cat("d%d_p" % g).reshape(1, 8, Wg, 2, 1, 64))
        outs.append(np.concatenate([np.asarray(r["d%d_s" % g]) for r in R], 0).reshape(1, 128, Wg, 2, 1, 64))
    return tuple(np.ascontiguousarray(o, dtype=np.float32) for o in outs)
```

```python
import os
import numpy as np
from contextlib import ExitStack
import concourse.bass as bass
import concourse.mybir as mybir
from concourse.bass_utils import run_bass_kernel_spmd

F32 = mybir.dt.float32
BF16 = mybir.dt.bfloat16
ALU = mybir.AluOpType
AF = mybir.ActivationFunctionType
AX = mybir.AxisListType

ENGS = ("pe", "act", "dve", "pool", "sp")

NTOK = 2048
NS = 16
NT = NTOK + NS
TBS = [(0, 512), (512, 512), (1024, 512), (1536, 512), (2048, 16)]
HALVES = [(0, [0, 1]), (1024, [2, 3, 4])]
HW_ = 1040
EPS = 1e-6
STOP = int(os.environ.get("MK_STOP", "99"))
SAFE_WAR = int(os.environ.get("MK_SAFE_WAR", "1"))
M4A = int(os.environ.get("MK_4A", "63"))
POOL_ADD = int(os.environ.get("MK_POOL_ADD", "0"))
L1STOP = int(os.environ.get("MK_L1STOP", "99"))
NOCOPY = int(os.environ.get("MK_NOCOPY", "0"))


class H:
    __slots__ = ("name", "w", "r", "excl")

    def __init__(self, name="", excl=False):
        self.name = name
        self.w = None
        self.r = []
        self.excl = excl


class HD(dict):
    def __missing__(self, k):
        v = H(str(k))
        self[k] = v
        return v


class Op:
    __slots__ = ("eng", "fn", "deps", "signal", "tick", "dma", "sem", "val", "fenced")


class Prog:
    def __init__(self, nc, ndma=8):
        self.nc = nc
        self.ops = {e: [] for e in ENGS}
        self.all = []
        self.ndma = ndma

    def op(self, eng, fn, reads=(), writes=(), dma=False):
        o = Op()
        o.eng, o.fn, o.dma, o.signal, o.tick, o.sem, o.val, o.fenced = eng, fn, dma, dma, 0, None, 0, False
        deps = []
        if any(h.excl for h in reads):
            writes = list(writes) + [h for h in reads if h.excl and h not in writes]
            reads = [h for h in reads if not h.excl]

        def add(p, kind):
            if p is None or p is o:
                return
            if p.eng == eng and not p.dma and not dma:
                if eng == "pe" or (kind == "WAR" and not SAFE_WAR):
                    return
            if p not in deps:
                deps.append(p)

        for h in reads:
            add(h.w, "RAW")
        for h in writes:
            add(h.w, "WAW")
            for r in h.r:
                add(r, "WAR")
        for h in reads:
            h.r.append(o)
        for h in writes:
            h.w = o
            h.r = []
        o.deps = deps
        self.ops[eng].append(o)
        self.all.append(o)
        return o

    def fence(self):
        lasts = [self.ops[e][-1] for e in ENGS if self.ops[e] and self.ops[e][-1].fn is not None]
        dmas = [o for o in self.all if o.dma and not o.fenced]
        for o in dmas:
            o.fenced = True
        for e in ENGS:
            o = Op()
            o.eng, o.fn, o.dma, o.signal, o.tick, o.sem, o.val, o.fenced = e, None, False, False, 0, None, 0, True
            o.deps = [p for p in lasts if p.eng != e or p.dma] + [d for d in dmas if d not in lasts]
            self.ops[e].append(o)
            self.all.append(o)

    def emit(self, stack):
        nc = self.nc
        for o in self.all:
            for d in o.deps:
                d.signal = True
        esem = {e: stack.enter_context(nc.semaphore("es_" + e)) for e in ENGS}
        dsem = {e: [stack.enter_context(nc.semaphore("ds_%s_%d" % (e, i))) for i in range(self.ndma)]
                for e in ENGS if any(o.dma for o in self.ops[e])}
        for e in ENGS:
            c = 0
            nd = 0
            dmas = []
            for o in self.ops[e]:
                if o.dma:
                    o.sem = dsem[e][nd % self.ndma]
                    o.val = 16 * (nd // self.ndma + 1)
                    if nd >= self.ndma:
                        prev = dmas[nd - self.ndma]
                        if prev not in o.deps:
                            o.deps.append(prev)
                    dmas.append(o)
                    nd += 1
                elif o.signal:
                    c += 1
                    o.tick = c
        block = stack.enter_context(nc.Block())
        prog = self
        self.stats = {}

        def section(e):
            def body(eng):
                known = {}
                nwait = 0
                for o in prog.ops[e]:
                    for d in o.deps:
                        if d.dma:
                            sem, val = d.sem, d.val
                        else:
                            sem, val = esem[d.eng], d.tick
                        key = id(sem)
                        if known.get(key, 0) >= val:
                            continue
                        eng.wait_ge(sem, val)
                        nwait += 1
                        known[key] = val
                    if o.fn is None:
                        continue
                    ins = o.fn(eng)
                    if o.dma:
                        ins.then_inc(o.sem, 16)
                    elif o.signal:
                        ins.then_inc(esem[e], 1)
                last = {}
                for o in prog.ops[e]:
                    if o.dma:
                        last[id(o.sem)] = (o.sem, o.val)
                for sem, val in last.values():
                    if known.get(id(sem), 0) < val:
                        eng.wait_ge(sem, val)
                prog.stats[e] = (len(prog.ops[e]), nwait)
            return body

        block.tensor(section("pe"))
        block.scalar(section("act"))
        block.vector(section("dve"))
        block.gpsimd(section("pool"))
        block.sync(section("sp"))


class Arena:
    def __init__(self, t32, cap_bytes):
        self.t32 = t32
        self.t16 = t32.bitcast(BF16)
        self.cap = cap_bytes
        self.off = 0
        self.peak = 0

    def alloc(self, ncols, dtype):
        esz = 4 if dtype == F32 else 2
        off = (self.off + 31) // 32 * 32
        nb = ncols * esz
        assert off + nb <= self.cap, ("arena overflow", off, nb, self.cap)
        self.off = off + nb
        self.peak = max(self.peak, self.off)
        if dtype == F32:
            return self.t32[:, off // 4: off // 4 + ncols]
        return self.t16[:, off // 2: off // 2 + ncols]

    def mark(self):
        return self.off

    def reset(self, m=0):
        self.off = m


def bcast_ap(ap, dims):
    return bass.AP(ap.tensor, ap.offset, [list(ap.ap[0])] + [[s, n] for s, n in dims])


def build_program():
    nc = bass.Bass("TRN2", target_bir_lowering=False)

    def din(name, shape):
        return nc.dram_tensor(name, list(shape), F32, kind="ExternalInput").ap()

    def dout(name, shape):
        return nc.dram_tensor(name, list(shape), F32, kind="ExternalOutput").ap()

    xp = din("xp", [NTOK, 1024])
    xs = din("xs", [NS, 1024])
    sca = din("sca", [NS * 30, 512])
    scb = din("scb", [NS * 2, 512])
    c_swa = din("c_swa", [NS, 128, 256])
    c_d = [din("c_d0", [NS, 128, 128]), din("c_d1", [NS, 512, 128]), din("c_d2", [NS, 2048, 128])]
    prm1 = din("prm1", [88, 128])
    prm2 = din("prm2", [124, 128])
    sinks = din("sinks", [1, 8])
    cst32_d = din("cst32", [128, 512])
    cstb_d = din("cstb", [128, 768])
    cs_d = din("cs", [128, 2, NT])
    w_in_conv = din("w_in_conv", [1024, 2560])
    w_out_conv = din("w_out_conv", [1024, 1024])
    wq_d = din("wq", [1024, 1280])
    wk_d = din("wk", [1024, 384])
    wv_d = din("wv", [1024, 320])
    wvd_d = din("wvd", [1024, 640])
    w_out_attn = din("w_out_attn", [768, 1024])
    w1_d = din("w1", [2, 1024, 4096])
    w2_d = din("w2", [2, 4096, 1024])

    y_p = dout("y_p", [NTOK, 1024])
    y_s = dout("y_s", [NS, 1024])
    a_p = dout("a_p", [30, 512])
    a_s = dout("a_s", [NS, 30, 512])
    b_p = dout("b_p", [2, 512])
    b_s = dout("b_s", [NS, 2, 512])
    swa_p = dout("swa_p", [128, 256])
    swa_s = dout("swa_s", [NS, 128, 256])
    d_p = [dout("d0_p", [128, 128]), dout("d1_p", [512, 128]), dout("d2_p", [2048, 128])]
    d_s = [dout("d0_s", [NS, 128, 128]), dout("d1_s", [NS, 512, 128]), dout("d2_s", [NS, 2048, 128])]

    st = ExitStack()
    with st:
        P = Prog(nc)
        sb = lambda name, shape, dt: st.enter_context(nc.sbuf_tensor(name, list(shape), dt))
        hT = sb("hT", [128, 8, NT], F32)
        cst32 = sb("cst32s", [128, 512], F32)
        cstb = sb("cstbs", [128, 768], BF16)
        prmT = sb("prmT", [128, 88], F32)
        wAT = sb("wAT", [128, 124], F32)
        smalls = sb("smalls", [128, 32], F32)
        halo = sb("halo", [128, 8, 30], BF16)
        NSLOT = 2
        SLOT = 4096
        wslots = [sb("wslot%d" % i, [128, SLOT], BF16) for i in range(NSLOT)]
        ARENA_BYTES = (nc.sbuf_bytes_remaining // 64) * 64 - 256
        A = Arena(sb("arena", [128, ARENA_BYTES // 4], F32), ARENA_BYTES)
        psb = [st.enter_context(nc.psum_tensor("psb%d" % i, [128, 512], F32)) for i in range(8)]
        hps = [H("ps%d" % i, excl=True) for i in range(8)]

        ident = cst32[:, 0:128]
        selh = [cst32[:, 128:256], cst32[:, 256:384]]
        ones32 = cst32[:, 384:512]
        prot = cstb[:, 0:128]
        onesb = cstb[:, 128:256]
        maskpd = cstb[:, 256:512]
        identb = cstb[:, 512:640]
        epsc = smalls[:, 0:1]

        hh = HD()
        hconst = hh["const"]

        def MM(ps_ap, lhsT, rhs, start, stop, reads, writes):
            P.op("pe", lambda e: e.matmul(ps_ap, lhsT=lhsT, rhs=rhs, start=start, stop=stop), reads=reads, writes=writes)

        def TR(out, in_, idn, reads, writes):
            P.op("pe", lambda e: e.transpose(out=out, in_=in_, identity=idn), reads=reads, writes=writes)

        def ACT(out, in_, func, reads, writes, bias=None, scale=None):
            kw = {}
            if bias is not None:
                kw["bias"] = bias
            if scale is not None:
                kw["scale"] = scale
            P.op("act", lambda e: e.activation(out=out, in_=in_, func=func, **kw), reads=reads, writes=writes)

        def TT(out, in0, in1, op, reads, writes, eng="dve"):
            P.op(eng, lambda e: e.tensor_tensor(out=out, in0=in0, in1=in1, op=op), reads=reads, writes=writes)

        def STT(out, in0, scalar, in1, op0, op1, reads, writes):
            P.op("dve", lambda e: e.scalar_tensor_tensor(out=out, in0=in0, scalar=scalar, in1=in1, op0=op0, op1=op1), reads=reads, writes=writes)

        def TS(out, in0, s1, op0, reads, writes, s2=None, op1=None, eng="dve"):
            if op1 is None:
                P.op(eng, lambda e: e.tensor_scalar(out=out, in0=in0, scalar1=s1, scalar2=None, op0=op0), reads=reads, writes=writes)
            else:
                P.op(eng, lambda e: e.tensor_scalar(out=out, in0=in0, scalar1=s1, scalar2=s2, op0=op0, op1=op1), reads=reads, writes=writes)

        def CP(out, in_, reads, writes, eng="dve"):
            P.op(eng, lambda e: e.tensor_copy(out=out, in_=in_), reads=reads, writes=writes)

        def RED(out, in_, reads, writes):
            P.op("dve", lambda e: e.tensor_reduce(out=out, in_=in_, axis=AX.X, op=ALU.add), reads=reads, writes=writes)

        def RECIP(out, in_, reads, writes):
            P.op("dve", lambda e: e.reciprocal(out=out, in_=in_), reads=reads, writes=writes)

        def MEMSET(ap, val, writes, eng="dve"):
            P.op(eng, lambda e: e.memset(ap, val), writes=writes)

        def DMA(eng, out, in_, reads=(), writes=()):
            P.op(eng, lambda e: e.dma_start(out=out, in_=in_), reads=reads, writes=writes, dma=True)

        class Rot:
            def __init__(self, banks):
                self.banks = banks
                self.i = 0

            def next(self):
                b = self.banks[self.i % len(self.banks)]
                self.i += 1
                return psb[b], hps[b]

        wrot = [0]

        NSL = [NSLOT]

        def wslot_next():
            i = wrot[0] % NSL[0]
            wrot[0] += 1
            return wslots[i], hh[("wslot", i)]

        def gcol(l, w, c):
            j = (l * 4 + w) * 8 + c
            return prmT[:, j:j + 1]

        DMA("sp", cst32[:], cst32_d, writes=[hconst])
        DMA("pool", cstb[:], cstb_d, writes=[hconst])
        MEMSET(smalls[:, 0:8], EPS, [hh["smalls"]])
        DMA("sp", smalls[:, 8:16], bass.AP(sinks.tensor, 0, [[0, 128], [1, 8]]), writes=[hh["sinks"]])
        ACT(smalls[:, 16:24], smalls[:, 8:16], AF.Exp, [hh["sinks"]], [hh["expsink"]])
        m0 = A.mark()
        p1 = A.alloc(128, F32)
        p2 = A.alloc(128, F32)
        DMA("sp", p1[0:88, :], prm1, writes=[hh["p1"]])
        DMA("sp", p2[0:124, :], prm2, writes=[hh["p2"]])
        TR(psb[0][:, 0:88], p1[0:88, :], ident[0:88, 0:88], [hh["p1"], hconst], [hps[0]])
        ACT(prmT[:], psb[0][:, 0:88], AF.Copy, [hps[0]], [hconst])
        TR(psb[1][:, 0:124], p2[0:124, :], ident[0:124, 0:124], [hh["p2"], hconst], [hps[1]])
        ACT(wAT[:], psb[1][:, 0:124], AF.Copy, [hps[1]], [hconst])

        xin = [A.alloc(1024, F32) for _ in range(4)]
        rot = Rot([2, 3, 4, 5, 6, 7])
        for t in range(17):
            xi = xin[t % 4]
            hx = hh[("xin", t % 4)]
            rows = 128 if t < 16 else NS
            src = xp[t * 128:(t + 1) * 128, :] if t < 16 else xs
            DMA("sp", xi[0:rows, :], src, writes=[hx])
            for g in range(2):
                ps, hp = rot.next()
                for i in range(4):
                    c = g * 4 + i
                    TR(ps[:, i * rows:(i + 1) * rows], xi[0:rows, c * 128:(c + 1) * 128], ident[0:rows, 0:rows], [hx, hconst], [hp])
                dst = hT[:, g * 4:(g + 1) * 4, t * 128:t * 128 + rows]
                srcp = ps[:, 0:4 * rows].rearrange("p (i r) -> p i r", i=4)
                wr = [hh[("hT", g * 4 + i, t // 4 if t < 16 else 4)] for i in range(4)]
                if g == 0:
                    ACT(dst, srcp, AF.Copy, [hp], wr)
                else:
                    CP(dst, srcp, [hp], wr)
        PHASE1_FENCE = True

        statrot = Rot([6, 7])
        mmrot = Rot([0, 1, 2, 3, 4, 5])

        def rstd_from_ps(ps_stat, hstat, N, scale, out_rstd, hout, tmp, htmp):
            ACT(tmp[:, 0:N], ps_stat[:, 0:N], AF.Sqrt, [hstat, hh["smalls"]], [htmp], bias=epsc, scale=scale)
            RECIP(out_rstd[:, 0:N], tmp[:, 0:N], [htmp], [hout])

        def sumsq_stat(src_fn, rd_fn, N, sqbuf):
            ps, hp = statrot.next()
            for c in range(8):
                sq = sqbuf[c % 2]
                hsq = hh[("sq", c % 2)]
                ACT(sq[:, 0:N], src_fn(c), AF.Square, rd_fn(c), [hsq])
                MM(ps[:, 0:N], onesb, sq[:, 0:N], c == 0, c == 7, [hsq, hconst], [hp])
            return ps, hp

        def norm_block(l, w, tb, dstT, dst_col0, sqbuf, tmp, rstd, htb=None):
            c0, N = TBS[tb]
            ps, hp = sumsq_stat(lambda c: hT[:, c, c0:c0 + N], lambda c: [hh[("hT", c, tb)]], N, sqbuf)
            rstd_from_ps(ps, hp, N, 1.0 / 1024, rstd, hh["rstd"], tmp, hh["rtmp"])
            for c in range(8):
                STT(dstT[:, c, c0 - dst_col0:c0 - dst_col0 + N], hT[:, c, c0:c0 + N], gcol(l, w, c), rstd[:, 0:N], ALU.mult, ALU.mult,
                    [hh[("hT", c, tb)], hh["rstd"], hconst], [hh[("uT", c, tb if htb is None else htb)]])

        def push_slots(n):
            nb = len(wslots)
            wslots.extend([A.alloc(SLOT, BF16) for _ in range(n)])
            NSL[0] = len(wslots)
            return nb

        def pop_slots(nb):
            del wslots[nb:]
            NSL[0] = len(wslots)

        PRE = {}

        def wkey(wd, row0, Kc, cols):
            return (wd.tensor.name, int(wd.offset), row0, Kc, tuple(cols))

        def preload(wd, row0, Kc, cols):
            assert NSL[0] == NSLOT
            PRE[wkey(wd, row0, Kc, cols)] = load_wchunks(wd, row0, Kc, cols)

        def load_wchunks(wd, row0, Kc, cols):
            k_ = wkey(wd, row0, Kc, cols)
            if k_ in PRE:
                return PRE.pop(k_)
            slot, hs = wslot_next()
            n = len(cols)
            assert n * Kc * 128 <= SLOT
            v = slot[:, 0:n * Kc * 128].rearrange("p (i k m) -> p i k m", i=n, k=Kc)
            for i, col0 in enumerate(cols):
                src = wd[row0:row0 + Kc * 128, col0:col0 + 128].rearrange("(k p) m -> p k m", p=128)
                DMA("pool", v[:, i], src, writes=[hs])
            return v, hs

        def postnorm_residual(l, w, tb, ysb, ycol0, sqbuf, tmp, rstd, ytmp):
            c0, N = TBS[tb]
            lc = c0 - ycol0
            ps, hp = sumsq_stat(lambda c: ysb[:, c, lc:lc + N], lambda c: [hh[("ysb", c, tb)]], N, sqbuf)
            rstd_from_ps(ps, hp, N, 1.0 / 1024, rstd, hh["rstd"], tmp, hh["rtmp"])
            for c in range(8):
                yt = ytmp[c % 2]
                hyt = hh[("ytmp", c % 2)]
                STT(yt[:, 0:N], ysb[:, c, lc:lc + N], gcol(l, w, c), rstd[:, 0:N], ALU.mult, ALU.mult,
                    [hh[("ysb", c, tb)], hh["rstd"], hconst], [hyt])
                TT(hT[:, c, c0:c0 + N], hT[:, c, c0:c0 + N], yt[:, 0:N], ALU.add, [hyt, hh[("hT", c, tb)]], [hh[("hT", c, tb)]],
                   eng=("pool" if POOL_ADD else "dve"))

        def out_proj(wd, row0, Kc, rhs_fn, tbs, ysb, ycol0, accumulate=False):
            for m in range(8):
                v, hs = load_wchunks(wd, row0, Kc, [m * 128])
                for tb in tbs:
                    c0, N = TBS[tb]
                    ps, hp = mmrot.next()
                    for k in range(Kc):
                        rap, rh = rhs_fn(k, tb)
                        MM(ps[:, 0:N], v[:, 0, k, :], rap, k == 0, k == Kc - 1, [hs] + rh, [hp])
                    dst = ysb[:, m, c0 - ycol0:c0 - ycol0 + N]
                    if not accumulate:
                        ACT(dst, ps[:, 0:N], AF.Copy, [hp], [hh[("ysb", m, tb)]])
                    else:
                        TT(dst, dst, ps[:, 0:N], ALU.add, [hp, hh[("ysb", m, tb)]], [hh[("ysb", m, tb)]])

        def pre_l0a():
            preload(w_in_conv, 0, 8, [0, 512, 1024, 2048])
            preload(w_in_conv, 0, 8, [1536])

        def pre_outproj(wd, Kc):
            preload(wd, 0, Kc, [0])
            preload(wd, 0, Kc, [128])

        def pre_mlp(l):
            preload(w1_d[l], 0, 8, [0, 128, 256, 384])
            preload(w1_d[l], 0, 8, [512, 640, 768, 896])

        def pre_l1a():
            preload(wq_d, 0, 8, [0, 128, 256, 384])
            preload(wq_d, 0, 8, [512, 640, 768, 896])

        def mlp_half(l, col0, tbs, pre_next=None):
            m1 = A.mark()
            W = HW_
            uT = A.alloc(8 * W, BF16).rearrange("p (c n) -> p c n", c=8)
            hid = A.alloc(16 * W, BF16).rearrange("p (c n) -> p c n", c=16)
            ysb = A.alloc(8 * W, F32).rearrange("p (c n) -> p c n", c=8)
            sqbuf = [A.alloc(512, BF16), A.alloc(512, BF16)]
            rl = [A.alloc(512, BF16), A.alloc(512, BF16)]
            tmp = A.alloc(512, F32)
            rstd = A.alloc(512, F32)
            ytmp = [A.alloc(512, F32), A.alloc(512, F32)]
            nb_ = push_slots(2)
            for tb in tbs:
                norm_block(l, 2, tb, uT, col0, sqbuf, tmp, rstd)
            for hf in range(2):
                for mg in range(4):
                    ms = [hf * 16 + mg * 4 + i for i in range(4)]
                    v, hs = load_wchunks(w1_d[l], 0, 8, [m * 128 for m in ms])
                    for i, m in enumerate(ms):
                        for tb in tbs:
                            c0, N = TBS[tb]
                            lc = c0 - col0
                            ps, hp = mmrot.next()
                            for k in range(8):
                                MM(ps[:, 0:N], v[:, i, k, :], uT[:, k, lc:lc + N], k == 0, k == 7, [hs, hh[("uT", k, tb)]], [hp])
                            r = rl[(m + tb) % 2]
                            hr = hh[("rl", (m + tb) % 2)]
                            ACT(r[:, 0:N], ps[:, 0:N], AF.Relu, [hp], [hr])
                            TT(hid[:, m % 16, lc:lc + N], r[:, 0:N], r[:, 0:N], ALU.mult, [hr], [hh[("hid", m % 16, tb)]])
                out_proj(w2_d[l], hf * 2048, 16,
                         lambda k, tb: (hid[:, k, TBS[tb][0] - col0:TBS[tb][0] - col0 + TBS[tb][1]], [hh[("hid", k, tb)]]),
                         tbs, ysb, col0, accumulate=(hf == 1))
            for tb in tbs:
                postnorm_residual(l, 3, tb, ysb, col0, sqbuf, tmp, rstd, ytmp)
            pop_slots(nb_)
            if pre_next is not None:
                pre_next()
            P.fence()
            A.reset(m1)

        def layer0_half(hi, col0, tbs):
            m1 = A.mark()
            W = HW_
            uT = A.alloc(8 * W, BF16).rearrange("p (c n) -> p c n", c=8)
            mixT = uT
            ga = A.alloc(4 * (30 + W), BF16).rearrange("p (c n) -> p c n", c=4)
            zb = A.alloc(4 * (2 + W), BF16).rearrange("p (c n) -> p c n", c=4)
            gb = A.alloc(4 * W, BF16).rearrange("p (c n) -> p c n", c=4)
            m2 = A.mark()
            diag = A.alloc(31 * 128, BF16).rearrange("p (j m) -> p j m", j=31)
            diagB = A.alloc(12 * 128, BF16).rearrange("p (j m) -> p j m", j=12)
            cvo = A.alloc(4 * W, F32).rearrange("p (c n) -> p c n", c=4)
            reg32 = A.alloc(2048, F32)
            xq16 = A.t16[:, 2 * reg32.offset:2 * reg32.offset + 4096]
            xb = xq16[:, 0:2048].rearrange("p (c n) -> p c n", c=4)
            sq4 = xq16[:, 2048:4096].rearrange("p (c n) -> p c n", c=4)
            sqbuf = [A.alloc(512, BF16), A.alloc(512, BF16)]
            tmpa = A.alloc(512, F32)
            tmpg = A.alloc(512, F32)
            tmp = A.alloc(512, F32)
            rstd = A.alloc(512, F32)
            mean = A.alloc(512, F32)
            msq = A.alloc(512, F32)
            t1 = [A.alloc(512, F32), A.alloc(512, F32)]
            tails = A.alloc(4 * 64, F32).rearrange("p (c n) -> p c n", c=4)
            sAT = A.alloc(4 * NS * 30, F32).rearrange("p (c n) -> p c n", c=4)
            sBT = A.alloc(4 * NS * 2, F32).rearrange("p (c n) -> p c n", c=4)
            stg = A.alloc(512, F32)
            gas = A.alloc(4 * NS, F32).rearrange("p (c n) -> p c n", c=4)
            prodA = A.alloc(NS * 30, F32)
            hga = [hh[("ga", c)] for c in range(4)]
            hzb = [hh[("zb", c)] for c in range(4)]
            nb2a = push_slots(1)

            if hi == 0:
                MEMSET(ga[:, :, 0:30], 0.0, hga)
                MEMSET(zb[:, :, 0:2], 0.0, hzb)
            else:
                CP(ga[:, :, 0:30], halo[:, 0:4, 0:30], [hh["halo"]], hga)
                CP(zb[:, :, 0:2], halo[:, 4:8, 0:2], [hh["halo"]], hzb)

            for tb in tbs:
                norm_block(0, 0, tb, uT, col0, sqbuf, tmp, rstd)

            for c in range(4):
                mcols = [c * 128, 512 + c * 128, 1024 + c * 128, 2048 + c * 128, 1536 + c * 128]
                v, hs = load_wchunks(w_in_conv, 0, 8, mcols[0:4])
                v2, hs2 = load_wchunks(w_in_conv, 0, 8, mcols[4:5])
                for tb in tbs:
                    c0, N = TBS[tb]
                    lc = c0 - col0
                    pss = []
                    for i in range(5):
                        ps, hp = mmrot.next()
                        vv, hv, ii = (v, hs, i) if i < 4 else (v2, hs2, 0)
                        for k in range(8):
                            MM(ps[:, 0:N], vv[:, ii, k, :], uT[:, k, lc:lc + N], k == 0, k == 7, [hv, hh[("uT", k, tb)]], [hp])
                        pss.append((ps, hp))
                    (pa, ha), (pg, hg), (px, hx_), (pc, hc), (pb, hb) = pss
                    ACT(tmpg[:, 0:N], pg[:, 0:N], AF.Sigmoid, [hg], [hh["tmpg"]])
                    TT(ga[:, c, 30 + lc:30 + lc + N], pa[:, 0:N], tmpg[:, 0:N], ALU.mult, [ha, hh["tmpg"]], [hga[c]])
                    if tb == 3:
                        TT(tails[:, c, 0:30], pa[:, 482:512], tmpg[:, 482:512], ALU.mult, [ha, hh["tmpg"]], [hh[("tails", c)]])
                    if tb == 4:
                        TT(gas[:, c, :], pa[:, 0:NS], tmpg[:, 0:NS], ALU.mult, [ha, hh["tmpg"]], [hh[("gas", c)]])
                    ACT(tmpa[:, 0:N], px[:, 0:N], AF.Copy, [hx_], [hh["tmpa"]])
                    TT(zb[:, c, 2 + lc:2 + lc + N], pc[:, 0:N], tmpa[:, 0:N], ALU.mult, [hc, hh["tmpa"]], [hzb[c]])
                    if tb == 3:
                        TT(tails[:, c, 32:34], pc[:, 510:512], tmpa[:, 510:512], ALU.mult, [hc, hh["tmpa"]], [hh[("tails", c)]])
                    if tb == 4:
                        TT(tails[:, c, 40:56], pc[:, 0:NS], tmpa[:, 0:NS], ALU.mult, [hc, hh["tmpa"]], [hh[("tails", c)]])
                    ACT(gb[:, c, lc:lc + N], pb[:, 0:N], AF.Copy, [hb], [hh[("gb", c)]])

            if hi == 1:
                stA = reg32.rearrange("p (g n) -> p g n", g=4)
                stB = tmp
                hstA = [hh[("xb", c)] for c in range(4)] + [hh[("sq4", c)] for c in range(4)]
                DMA("sp", stA[0:120, :, :], sca.rearrange("(g r) n -> r g n", r=120), writes=hstA)
                DMA("sp", stB[0:32, :], scb, writes=[hh["rtmp"]])
                for c in range(4):
                    ps, hp = mmrot.next()
                    for g in range(4):
                        TR(ps[:, g * 120:(g + 1) * 120], stA[0:120, g, c * 128:(c + 1) * 128], ident[0:120, 0:120], hstA + [hconst], [hp])
                    ACT(sAT[:, c, :], ps[:, 0:480], AF.Copy, [hp], [hh[("sAT", c)]])
                ps, hp = mmrot.next()
                for c in range(4):
                    TR(ps[:, c * 32:(c + 1) * 32], stB[0:32, c * 128:(c + 1) * 128], ident[0:32, 0:32], [hh["rtmp"], hconst], [hp])
                ACT(sBT[:, :, :], ps[:, 0:128].rearrange("p (c n) -> p c n", c=4), AF.Copy, [hp], [hh["sBT"]])

            for j in range(3):
                for c in range(4):
                    TS(diagB[:, j * 4 + c, :], identb, prmT[:, 76 + j * 4 + c:77 + j * 4 + c], ALU.mult, [hconst], [hh["diagB"]])
            for c in range(4):
                for j in range(31):
                    if j % 2 == 0:
                        TS(diag[:, j, :], identb, wAT[:, j * 4 + c:j * 4 + c + 1], ALU.mult, [hconst], [hh[("diag", j)]])
                    else:
                        ACT(diag[:, j, :], identb, AF.Copy, [hconst], [hh[("diag", j)]], scale=wAT[:, j * 4 + c:j * 4 + c + 1])
                for tb in tbs:
                    c0, N = TBS[tb]
                    lc = c0 - col0
                    hcv = hh[("cvo", c, tb)]
                    if tb < 4:
                        ps, hp = mmrot.next()
                        for j in range(31):
                            MM(ps[:, 0:N], diag[:, j, :], ga[:, c, lc + j:lc + j + N], j == 0, j == 30, [hh[("diag", j)], hga[c]], [hp])
                        ACT(cvo[:, c, lc:lc + N], ps[:, 0:N], AF.Identity, [hp, hconst], [hcv], bias=prmT[:, 64 + c:65 + c], scale=1.0)
                    else:
                        wv_ = bcast_ap(wAT[:, c:c + 1], [(0, NS), (4, 30)])
                        pA = prodA.rearrange("p (b j) -> p b j", b=NS)
                        TT(pA, sAT[:, c, :].rearrange("p (b j) -> p b j", b=NS), wv_, ALU.mult, [hh[("sAT", c)], hconst], [hh["prodA"]])
                        RED(cvo[:, c, lc:lc + NS], pA, [hh["prodA"]], [hcv])
                        STT(cvo[:, c, lc:lc + NS], gas[:, c, :], wAT[:, 120 + c:121 + c], cvo[:, c, lc:lc + NS], ALU.mult, ALU.add,
                            [hh[("gas", c)], hcv, hconst], [hcv])
                        TS(cvo[:, c, lc:lc + NS], cvo[:, c, lc:lc + NS], prmT[:, 64 + c:65 + c], ALU.add, [hcv, hconst], [hcv])
            for tb in tbs:
                c0, N = TBS[tb]
                lc = c0 - col0
                psm, hpm = statrot.next()
                pse, hpe = statrot.next()
                for c in range(4):
                    CP(xb[:, c, 0:N], cvo[:, c, lc:lc + N], [hh[("cvo", c, tb)]], [hh[("xb", c)]])
                    ACT(sq4[:, c, 0:N], cvo[:, c, lc:lc + N], AF.Square, [hh[("cvo", c, tb)]], [hh[("sq4", c)]])
                for c in range(4):
                    MM(psm[:, 0:N], onesb, xb[:, c, 0:N], c == 0, c == 3, [hh[("xb", c)], hconst], [hpm])
                for c in range(4):
                    MM(pse[:, 0:N], onesb, sq4[:, c, 0:N], c == 0, c == 3, [hh[("sq4", c)], hconst], [hpe])
                ACT(mean[:, 0:N], psm[:, 0:N], AF.Copy, [hpm], [hh["mean"]], scale=1.0 / 512)
                TT(msq[:, 0:N], mean[:, 0:N], mean[:, 0:N], ALU.mult, [hh["mean"]], [hh["msq"]])
                STT(msq[:, 0:N], pse[:, 0:N], 1.0 / 512, msq[:, 0:N], ALU.mult, ALU.subtract, [hpe, hh["msq"]], [hh["msq"]])
                ACT(tmp[:, 0:N], msq[:, 0:N], AF.Sqrt, [hh["msq"], hh["smalls"]], [hh["rtmp"]], bias=epsc, scale=1.0)
                RECIP(rstd[:, 0:N], tmp[:, 0:N], [hh["rtmp"]], [hh["rstd"]])
                for c in range(4):
                    tt = t1[c % 2]
                    ht = hh[("t1", c % 2)]
                    TT(tt[:, 0:N], cvo[:, c, lc:lc + N], mean[:, 0:N], ALU.subtract, [hh[("cvo", c, tb)], hh["mean"]], [ht])
                    TT(tt[:, 0:N], tt[:, 0:N], rstd[:, 0:N], ALU.mult, [ht, hh["rstd"]], [ht])
                    ACT(mixT[:, c, lc:lc + N], tt[:, 0:N], AF.Silu, [ht, hconst], [hh[("uT", c, tb)]],
                        bias=prmT[:, 72 + c:73 + c], scale=prmT[:, 68 + c:69 + c])
                for c in range(4):
                    hmx = hh[("uT", 4 + c, tb)]
                    if tb < 4:
                        ps, hp = mmrot.next()
                        for j in range(3):
                            MM(ps[:, 0:N], diagB[:, j * 4 + c, :], zb[:, c, lc + j:lc + j + N], j == 0, j == 2, [hh["diagB"], hzb[c]], [hp])
                        TT(mixT[:, 4 + c, lc:lc + N], ps[:, 0:N], gb[:, c, lc:lc + N], ALU.mult, [hp, hh[("gb", c)]], [hmx])
                    else:
                        tt = t1[c % 2]
                        ht = hh[("t1", c % 2)]
                        sb_ = sBT[:, c, :].rearrange("p (b j) -> p b j", j=2)
                        TS(tt[:, 0:NS], sb_[:, :, 0], prmT[:, 76 + c:77 + c], ALU.mult, [hh["sBT"], hconst], [ht])
                        STT(tt[:, 0:NS], sb_[:, :, 1], prmT[:, 80 + c:81 + c], tt[:, 0:NS], ALU.mult, ALU.add, [hh["sBT"], ht, hconst], [ht])
                        STT(tt[:, 0:NS], tails[:, c, 40:56], prmT[:, 84 + c:85 + c], tt[:, 0:NS], ALU.mult, ALU.add, [hh[("tails", c)], ht, hconst], [ht])
                        TT(mixT[:, 4 + c, lc:lc + NS], tt[:, 0:NS], gb[:, c, lc:lc + NS], ALU.mult, [ht, hh[("gb", c)]], [hmx])

            if hi == 1:
                def tm_out(src_fn, rows, dst, rd_fn):
                    ps, hp = mmrot.next()
                    for c in range(4):
                        TR(ps[0:rows, c * 128:(c + 1) * 128], src_fn(c), ident, rd_fn(c) + [hconst], [hp])
                    ACT(stg[0:rows, :], ps[0:rows, :], AF.Copy, [hp], [hh["stg"]])
                    DMA("sp", dst, stg[0:rows, :], reads=[hh["stg"]])
                tm_out(lambda c: tails[:, c, 0:30], 30, a_p, lambda c: [hh[("tails", c)]])
                tm_out(lambda c: tails[:, c, 32:34], 2, b_p, lambda c: [hh[("tails", c)]])
                DMA("sp", a_s[:, 0:29, :], sca.rearrange("(b j) n -> b j n", j=30)[:, 1:30, :])
                DMA("sp", b_s[:, 0:1, :], scb.rearrange("(b j) n -> b j n", j=2)[:, 1:2, :])
                tm_out(lambda c: gas[:, c, :], NS, a_s[:, 29, :], lambda c: [hh[("gas", c)]])
                tm_out(lambda c: tails[:, c, 40:56], NS, b_s[:, 1, :], lambda c: [hh[("tails", c)]])
            else:
                CP(halo[:, 0:4, 0:30], ga[:, :, 1024:1054], hga, [hh["halo"]])
                CP(halo[:, 4:8, 0:2], zb[:, :, 1024:1026], hzb, [hh["halo"]])
            pop_slots(nb2a)
            pre_outproj(w_out_conv, 8)
            P.fence()
            A.reset(m2)
            ysb = A.alloc(8 * W, F32).rearrange("p (c n) -> p c n", c=8)
            sqbuf2 = [A.alloc(512, BF16), A.alloc(512, BF16)]
            tmp2 = A.alloc(512, F32)
            rstd2 = A.alloc(512, F32)
            ytmp = [A.alloc(512, F32), A.alloc(512, F32)]
            nb_ = push_slots(2)
            out_proj(w_out_conv, 0, 8,
                     lambda k, tb: (mixT[:, k, TBS[tb][0] - col0:TBS[tb][0] - col0 + TBS[tb][1]], [hh[("uT", k, tb)]]),
                     tbs, ysb, col0)
            for tb in tbs:
                postnorm_residual(0, 1, tb, ysb, col0, sqbuf2, tmp2, rstd2, ytmp)
            pop_slots(nb_)
            if STOP >= 2:
                pre_mlp(0)
            P.fence()
            A.reset(m1)

        RUN0 = STOP >= 1 and not int(os.environ.get('MK_SKIP0', '0'))
        if RUN0:
            pre_l0a()
        P.fence()
        A.reset(m0)
        if RUN0:
            for hi, (col0, tbs) in enumerate(HALVES):
                layer0_half(hi, col0, tbs)
                if STOP >= 2:
                    nxt = pre_l0a if hi == 0 else (pre_l1a if STOP >= 3 else None)
                    mlp_half(0, col0, tbs, pre_next=nxt)

        def load_wcols(wd, row0, Kc, col0, ncols):
            slot, hs = wslot_next()
            assert Kc * ncols <= SLOT
            v = slot[:, 0:Kc * ncols].rearrange("p (k m) -> p k m", k=Kc)
            DMA("pool", v, wd[row0:row0 + Kc * 128, col0:col0 + ncols].rearrange("(k p) m -> p k m", p=128), writes=[hs])
            return v, hs

        def layer1_mixer():
            mL = A.mark()
            QT = A.alloc(10 * NT, BF16).rearrange("p (c n) -> p c n", c=10)
            KT = A.alloc(3 * NT, BF16).rearrange("p (c n) -> p c n", c=3)
            Vd = A.alloc(16 * 5 * 128, BF16).rearrange("p (t k m) -> p t k m", t=16, k=5)
            QTs = A.alloc(10 * NS, F32).rearrange("p (c n) -> p c n", c=10)
            KTs = A.alloc(3 * NS, F32).rearrange("p (c n) -> p c n", c=3)
            VTs = A.alloc(5 * NS, F32).rearrange("p (c n) -> p c n", c=5)
            VsTM = A.alloc(320, F32)
            knew = A.alloc(384, F32)
            smix = A.alloc(6 * NS, BF16).rearrange("p (c n) -> p c n", c=6)
            mP = A.mark()
            uT = A.alloc(8 * HW_, BF16).rearrange("p (c n) -> p c n", c=8)
            sqbuf = [A.alloc(512, BF16), A.alloc(512, BF16)]
            tmp = A.alloc(512, F32)
            rstd = A.alloc(512, F32)
            cst = A.alloc(2 * HW_, F32).rearrange("p (c n) -> p c n", c=2)
            qbs = [A.alloc(512, BF16), A.alloc(512, BF16)]
            t1s = [A.alloc(512, F32), A.alloc(512, F32)]
            t2s = [A.alloc(512, F32), A.alloc(512, F32)]
            rctr = [0]
            K32 = A.alloc(512, F32)
            vst = A.alloc(320, F32)
            kst = A.alloc(128, F32)


            def rope(ps, hp, N, lc=0):
                i = rctr[0] % 2
                rctr[0] += 1
                qb, t1, t2 = qbs[i], t1s[i], t2s[i]
                hq, h1, h2 = hh[("qb", i)], hh[("t1", i)], hh[("t2", i)]
                ACT(qb[:, 0:N], ps[:, 0:N], AF.Copy, [hp], [hq])
                pr, hpr = rrot.next()
                MM(pr[:, 0:N], prot, qb[:, 0:N], True, True, [hq, hconst], [hpr])
                TT(t1[:, 0:N], ps[:, 0:N], cst[:, 0, lc:lc + N], ALU.mult, [hp, hh["cst"]], [h1])
                TT(t2[:, 0:N], pr[:, 0:N], cst[:, 1, lc:lc + N], ALU.mult, [hpr, hh["cst"]], [h2])
                return t1, t2, h1, h2

            def perm_write(dstT, m, rows, kind, tb, hw, R):
                t1, t2, h1, h2 = R
                c0, N = TBS[tb]
                r0, r1 = rows
                if kind == 1:
                    o = dstT[r0:r1, m, c0:c0 + N]
                    a, b = t1[r0:r1, 0:N], t2[r0:r1, 0:N]
                else:
                    o = dstT[r0:r1, m, 0:NTOK].rearrange("p (r i) -> p r i", r=kind)[:, :, c0 // kind:c0 // kind + N // kind]
                    a = t1[r0:r1, 0:N].rearrange("p (i r) -> p r i", r=kind)
                    b = t2[r0:r1, 0:N].rearrange("p (i r) -> p r i", r=kind)
                TT(o, a, b, ALU.add, [h1, h2], [hw])

            QKIND = [((1, 1),)] * 4 + [((1, 4),)] * 4 + [((16, 16),)] * 2
            KKIND = [(1, 1), (1, 4), (16, 16)]

            nbase = len(wslots)
            PASSES = [[0, 1], [2, 3, 4]]
            psrot = Rot([0, 1, 2])
            rrot = Rot([3, 4, 5])
            for tbs_ in PASSES:
                pc0 = TBS[tbs_[0]][0]
                PW = sum(TBS[tb][1] for tb in tbs_)
                for tb in tbs_:
                    norm_block(1, 0, tb, uT, pc0, sqbuf, tmp, rstd, htb="L1")
                DMA("sp", cst[:, :, 0:PW], cs_d[:, :, pc0:pc0 + PW], writes=[hh["cst"]])
                pend = [None]

                def flush():
                    if pend[0] is not None:
                        f_, a_ = pend[0]
                        pend[0] = None
                        f_(*a_)

                def post_q(ps, hp, m, tb):
                    c0, N = TBS[tb]
                    lc = c0 - pc0
                    R = rope(ps, hp, N, lc)
                    if tb == 4:
                        TT(QTs[:, m, :], R[0][:, 0:N], R[1][:, 0:N], ALU.add, [R[2], R[3]], [hh[("QTs", m)]])
                    else:
                        ka, kb = QKIND[m][0]
                        if ka == kb:
                            perm_write(QT, m, (0, 128), ka, tb, hh[("QT", m, tb)], R)
                        else:
                            perm_write(QT, m, (0, 64), ka, tb, hh[("QT", m, tb, 0)], R)
                            perm_write(QT, m, (64, 128), kb, tb, hh[("QT", m, tb, 1)], R)

                def post_k(ps, hp, kc, tb):
                    c0, N = TBS[tb]
                    lc = c0 - pc0
                    R = rope(ps, hp, N, lc)
                    if tb == 4:
                        TT(KTs[:, kc, :], R[0][:, 0:N], R[1][:, 0:N], ALU.add, [R[2], R[3]], [hh[("KTs", kc)]])
                        return
                    ka, kb = KKIND[kc]
                    if ka == kb:
                        perm_write(KT, kc, (0, 128), ka, tb, hh[("KT", kc, tb)], R)
                    else:
                        perm_write(KT, kc, (0, 64), ka, tb, hh[("KT", kc, tb, 0)], R)
                        perm_write(KT, kc, (64, 128), kb, tb, hh[("KT", kc, tb, 1)], R)
                    need = [t for t in range(4) if (M4A & 4) and (kc == 2 or (kc == 1 and 4 * tb + t >= 12) or (kc == 0 and 4 * tb + t == 15))]
                    if need:
                        TT(K32[:, 0:N], R[0][:, 0:N], R[1][:, 0:N], ALU.add, [R[2], R[3]], [hh["K32"]])
                    for t in need:
                        T = 4 * tb + t
                        pt, hpt = rrot.next()
                        TR(pt[:, 0:128], K32[:, t * 128:(t + 1) * 128], ident, [hh["K32"], hconst], [hpt])
                        ACT(kst[:, 0:128], pt[:, 0:128], AF.Copy, [hpt], [hh["kst"]])
                        if kc == 2:
                            DMA("sp", d_p[2][T * 128:(T + 1) * 128, 0:64], kst[:, 0:64], reads=[hh["kst"]], writes=[hh[("d2pk", T)]])
                        elif kc == 1:
                            DMA("sp", d_p[1][(T - 12) * 128:(T - 11) * 128, 0:64], kst[:, 64:128], reads=[hh["kst"]])
                            if T == 15:
                                DMA("sp", d_p[0][:, 0:64], kst[:, 0:64], reads=[hh["kst"]])
                        else:
                            DMA("sp", swa_p[:, 0:128], kst[:, 0:128], reads=[hh["kst"]])

                def main_mm(v, i, hs, tb):
                    c0, N = TBS[tb]
                    lc = c0 - pc0
                    ps, hp = psrot.next()
                    for k in range(8):
                        MM(ps[:, 0:N], v[:, i, k, :], uT[:, k, lc:lc + N], k == 0, k == 7, [hs, hh[("uT", k, "L1")]], [hp])
                    return ps, hp

                for mg in range(3):
                    ms = list(range(mg * 4, min(10, mg * 4 + 4)))
                    v, hs = load_wchunks(wq_d, 0, 8, [m * 128 for m in ms])
                    for i, m in enumerate(ms):
                        for tb in tbs_:
                            ps, hp = main_mm(v, i, hs, tb)
                            flush()
                            pend[0] = (post_q, (ps, hp, m, tb))
                v, hs = load_wchunks(wk_d, 0, 8, [0, 128, 256])
                for kc in range(3):
                    for tb in tbs_:
                        ps, hp = main_mm(v, kc, hs, tb)
                        flush()
                        pend[0] = (post_k, (ps, hp, kc, tb))
                flush()
                vw, hvw = load_wcols(wv_d, 0, 8, 0, 320)
                for tb in tbs_:
                    c0, N = TBS[tb]
                    lc = c0 - pc0
                    if tb == 4:
                        ps, hp = mmrot.next()
                        for k in range(8):
                            MM(ps[0:NS, 0:320], uT[:, k, lc:lc + NS], vw[:, k, :], k == 0, k == 7, [hvw, hh[("uT", k, "L1")]], [hp])
                        ACT(VsTM[0:NS, :], ps[0:NS, 0:320], AF.Copy, [hp], [hh["VsTM"]])
                        vd_, hvd = load_wchunks(wvd_d, 0, 8, [0, 128, 256, 384])
                        vd2, hvd2 = load_wchunks(wvd_d, 0, 8, [512])
                        for i in range(5):
                            vv, hv_, ii = (vd_, hvd, i) if i < 4 else (vd2, hvd2, 0)
                            ps, hp = mmrot.next()
                            for k in range(8):
                                MM(ps[:, 0:NS], vv[:, ii, k, :], uT[:, k, lc:lc + NS], k == 0, k == 7, [hv_, hh[("uT", k, "L1")]], [hp])
                            ACT(VTs[:, i, :], ps[:, 0:NS], AF.Copy, [hp], [hh[("VTs", i)]])
                        ps, hp = mmrot.next()
                        for kc in range(3):
                            TR(ps[0:NS, kc * 128:(kc + 1) * 128], KTs[:, kc, :], ident, [hh[("KTs", kc)], hconst], [hp])
                        ACT(knew[0:NS, :], ps[0:NS, 0:384], AF.Copy, [hp], [hh["knew"]])
                        DMA("sp", swa_s[:, 127, 0:128], knew[0:NS, 0:128], reads=[hh["knew"]])
                        DMA("sp", swa_s[:, 127, 128:256], VsTM[0:NS, 0:128], reads=[hh["VsTM"]])
                        for g, Wg in enumerate((128, 512, 2048)):
                            DMA("sp", d_s[g][:, Wg - 1, 0:64], knew[0:NS, 128 + 64 * g:192 + 64 * g], reads=[hh["knew"]])
                            DMA("sp", d_s[g][:, Wg - 1, 64:128], VsTM[0:NS, 128 + 64 * g:192 + 64 * g], reads=[hh["VsTM"]])
                        continue
                    for t in range(4):
                        T = 4 * tb + t
                        ps, hp = mmrot.next()
                        for k in range(8):
                            MM(ps[:, 0:320], uT[:, k, lc + t * 128:lc + (t + 1) * 128], vw[:, k, :], k == 0, k == 7, [hvw, hh[("uT", k, "L1")]], [hp])
                        src3 = ps[:, 0:192].rearrange("p (k m) -> p k m", k=3)
                        ACT(Vd[:, T, 0:3, 0:64], src3, AF.Copy, [hp], [hh[("Vd", T, 0)]])
                        CP(Vd[:, T, 0:3, 64:128], src3, [hp], [hh[("Vd", T, 1)]])
                        ACT(vst[:, 0:320], ps[:, 0:320], AF.Copy, [hp], [hh["vst"]])
                        DMA("sp", d_p[2][T * 128:(T + 1) * 128, 64:128], vst[:, 256:320], reads=[hh["vst"]], writes=[hh[("d2pv", T)]])
                        if T >= 12:
                            DMA("sp", d_p[1][(T - 12) * 128:(T - 11) * 128, 64:128], vst[:, 192:256], reads=[hh["vst"]])
                        if T == 15:
                            DMA("sp", d_p[0][:, 64:128], vst[:, 128:192], reads=[hh["vst"]])
                            DMA("sp", swa_p[:, 128:256], vst[:, 0:128], reads=[hh["vst"]])
                    for r in range(4):
                        ps, hp = mmrot.next()
                        for k in range(8):
                            MM(ps[:, 0:64], uT[:, k, lc + r:lc + 512:4], vw[:, k, 192:256], k == 0, k == 7, [hvw, hh[("uT", k, "L1")]], [hp])
                        ACT(Vd[:, 4 * r + tb, 3, 0:64], ps[:, 0:64], AF.Copy, [hp], [hh[("Vd3", r, tb, 0)]])
                        CP(Vd[:, 4 * r + tb, 3, 64:128], ps[:, 0:64], [hp], [hh[("Vd3", r, tb, 1)]])
            del wslots[nbase:]
            NSL[0] = len(wslots)
            if L1STOP > 3:
                pre_outproj(w_out_attn, 6)
            src = d_p[2].rearrange("(i r) n -> i r n", r=16)[:, :, 64:128]
            rds = [hh[("d2pv", T)] for T in range(16)]
            if not int(os.environ.get("MK_NORB", "0")):
                for r in range(16):
                    DMA("pool", Vd[:, r, 4, 0:64], src[:, r, :], reads=rds, writes=[hh[("Vd4a", r)]])
                    DMA("pool", Vd[:, r, 4, 64:128], src[:, r, :], reads=rds, writes=[hh[("Vd4b", r)]])
            P.fence()
            if L1STOP <= 1:
                A.reset(mL)
                return

            if not NOCOPY:
                DMA("act", swa_s[:, 0:127, :], c_swa[:, 1:128, :])
                for g, Wg in enumerate((128, 512, 2048)):
                    DMA("act", d_s[g][:, 0:Wg - 1, :], c_d[g][:, 1:Wg, :])
            A.reset(mP)
            mixT = A.alloc(6 * NT, BF16).rearrange("p (c n) -> p c n", c=6)
            PT = [A.alloc(256, BF16), A.alloc(256, BF16)]
            accN = A.alloc(NTOK, F32)
            accD = A.alloc(NTOK, F32)
            rec = A.alloc(512, F32)
            PT = PT + [A.alloc(256, BF16), A.alloc(256, BF16)]
            srot = Rot([0, 1, 6, 7])
            orot = Rot([2, 4])
            items = []

            def add_head(qc, base, kc, kv, kind, evac):
                cfg = dict(qc=qc, base=base, kc=kc, kv=kv, kind=kind, evac=evac)
                for T in range(16):
                    items.append((cfg, T))

            for h in range(8):
                kv, g = h // 4, h % 4
                mrows = slice((h % 2) * 64, (h % 2) * 64 + 64)
                mch = h // 2

                def evac(b, po, hpo, pd, hpd, h=h, mrows=mrows, mch=mch):
                    TS(rec[mrows, :], pd[mrows, :], smalls[mrows, 16 + h:17 + h], ALU.add, [hpd, hh["expsink"]], [hh["rec"]])
                    RECIP(rec[mrows, :], rec[mrows, :], [hh["rec"]], [hh["rec"]])
                    TT(mixT[mrows, mch, b * 512:(b + 1) * 512], po[mrows, :], rec[mrows, :], ALU.mult, [hpo, hh["rec"]], [hh[("mixT", mch, h % 2, b)]])
                add_head(g, kv * 64, 0, kv, 1, evac)
            for s_ in range(4):
                mrows = slice((s_ % 2) * 64, (s_ % 2) * 64 + 64)
                mch = 4 + s_ // 2
                for grp in range(3):
                    kind = (1, 4, 16)[grp]

                    def evac(b, po, hpo, pd, hpd, grp=grp, kind=kind, mrows=mrows, mch=mch, s_=s_):
                        if kind == 1:
                            on = accN[mrows, b * 512:(b + 1) * 512]
                            od = accD[mrows, b * 512:(b + 1) * 512]
                            sn, sd = po[mrows, :], pd[mrows, :]
                        elif kind == 4:
                            on = accN[mrows, b:NTOK:4]
                            od = accD[mrows, b:NTOK:4]
                            sn, sd = po[mrows, :], pd[mrows, :]
                        else:
                            on = accN[mrows, :].rearrange("p (i r) -> p i r", r=16)[:, :, 4 * b:4 * b + 4]
                            od = accD[mrows, :].rearrange("p (i r) -> p i r", r=16)[:, :, 4 * b:4 * b + 4]
                            sn = po[mrows, :].rearrange("p (t i) -> p i t", t=4)
                            sd = pd[mrows, :].rearrange("p (t i) -> p i t", t=4)
                        if grp == 0:
                            ACT(on, sn, AF.Copy, [hpo], [hh["accN"]])
                            CP(od, sd, [hpd], [hh["accD"]])
                        else:
                            TT(on, on, sn, ALU.add, [hpo, hh["accN"]], [hh["accN"]])
                            TT(od, od, sd, ALU.add, [hpd, hh["accD"]], [hh["accD"]])
                        if grp == 2 and b == 3:
                            for bb in range(4):
                                RECIP(rec[mrows, :], accD[mrows, bb * 512:(bb + 1) * 512], [hh["accD"]], [hh["rec"]])
                                TT(mixT[mrows, mch, bb * 512:(bb + 1) * 512], accN[mrows, bb * 512:(bb + 1) * 512], rec[mrows, :], ALU.mult,
                                   [hh["accN"], hh["rec"]], [hh[("mixT", mch, s_ % 2, bb)]])
                    if grp == 0:
                        add_head(4 + s_, 0, 1, 2, 1, evac)
                    elif grp == 1:
                        add_head(4 + s_, 64, 1, 3, 4, evac)
                    else:
                        add_head(8 + s_ // 2, (s_ % 2) * 64, 2, 4, 16, evac)

            state = {}

            def issue_S(i):
                cfg, T = items[i]
                kind, base = cfg["kind"], cfg["base"]
                rows = slice(base, base + 64)
                j = T if kind == 1 else (T % 4 if kind == 4 else 0)
                pss, hs_ = srot.next()
                lo = 0 if j > 0 else 128
                qsl = QT[rows, cfg["qc"], T * 128:(T + 1) * 128]
                if j > 0:
                    MM(pss[:, 0:128], KT[rows, cfg["kc"], (T - 1) * 128:T * 128], qsl, True, True, [], [hs_])
                MM(pss[:, 128:256], KT[rows, cfg["kc"], T * 128:(T + 1) * 128], qsl, True, True, [], [hs_])
                pt = PT[i % 4]
                hpt = hh[("PT", i % 4)]
                ACT(pt[:, lo:256], pss[:, lo:256], AF.Exp, [hs_], [hpt], scale=0.125)
                TT(pt[:, lo:256], pt[:, lo:256], maskpd[:, lo:256], ALU.mult, [hpt, hconst], [hpt], eng="pool")
                state[i] = (pt, hpt, j)

            def issue_PV(i):
                cfg, T = items[i]
                pt, hpt, j = state.pop(i)
                kv = cfg["kv"]
                if T % 4 == 0:
                    po, hpo = orot.next()
                    ob = orot.banks[(orot.i - 1) % 2]
                    cfg["o"] = (po, hpo, psb[ob + 1], hps[ob + 1])
                po, hpo, pd, hpd = cfg["o"]
                osl = slice((T % 4) * 128, (T % 4 + 1) * 128)
                if j > 0:
                    MM(po[:, osl], Vd[:, T - 1, kv, :], pt[:, 0:128], True, False, [hpt], [hpo])
                    MM(po[:, osl], Vd[:, T, kv, :], pt[:, 128:256], False, True, [hpt], [hpo])
                    MM(pd[:, osl], onesb, pt[:, 0:128], True, False, [hpt, hconst], [hpd])
                    MM(pd[:, osl], onesb, pt[:, 128:256], False, True, [hpt, hconst], [hpd])
                else:
                    MM(po[:, osl], Vd[:, T, kv, :], pt[:, 128:256], True, True, [hpt], [hpo])
                    MM(pd[:, osl], onesb, pt[:, 128:256], True, True, [hpt, hconst], [hpd])
                if T % 4 == 3:
                    cfg["evac"](T // 4, po, hpo, pd, hpd)

            LOOK = 2
            for i in range(len(items) + LOOK):
                if i < len(items):
                    issue_S(i)
                if i >= LOOK:
                    issue_PV(i - LOOK)
            P.fence()
            if L1STOP <= 2:
                A.reset(mL)
                return

            A.reset(mL)
            _skip = A.alloc(1, F32)
            A.reset(mL)
            Kc = A.alloc(NS * 64, F32).rearrange("p (b d) -> p b d", b=NS)
            Vc = A.alloc(NS * 128, F32).rearrange("p (b d) -> p b d", b=NS)
            Qs = A.alloc(256, F32)
            Qd = A.alloc(NS * 256, F32).rearrange("p (b c) -> p b c", b=NS)
            prod = A.alloc(512, F32)
            Sall = A.alloc(64, F32)
            Pn = A.alloc(64, F32)
            Pnew = A.alloc(64, F32)
            den = A.alloc(64, F32)
            num = A.alloc(64, F32)
            numD = A.alloc(64, F32)
            denD = A.alloc(64, F32)
            prT = A.alloc(NS, F32)
            outv = A.alloc(64, F32)
            assert A.off <= mP - 0 or True

            def v3(ap):
                return ap.rearrange("p (b s) -> p b s", s=4)

            groups = []
            for kv in range(2):
                groups.append(dict(heads=[(g, kv * 64) for g in range(4)], kc=0, kbase=kv * 64, vi=kv, cache=c_swa, kcol=kv * 64, vcol=128 + kv * 64,
                                   step=1, swa=kv))
            groups.append(dict(heads=[(4 + s, 0) for s in range(4)], kc=1, kbase=0, vi=2, cache=c_d[0], kcol=0, vcol=64, step=1, swa=None))
            groups.append(dict(heads=[(4 + s, 64) for s in range(4)], kc=1, kbase=64, vi=3, cache=c_d[1], kcol=0, vcol=64, step=4, swa=None))
            groups.append(dict(heads=[(8 + s // 2, (s % 2) * 64) for s in range(4)], kc=2, kbase=None, vi=4, cache=c_d[2], kcol=0, vcol=64, step=16, swa=None))
            for gi, G in enumerate(groups):
                cview = G["cache"].rearrange("b j n -> j b n")
                st_ = G["step"]
                for q4 in range(4):
                    bs = slice(4 * q4, 4 * q4 + 4)
                    DMA("sp", Kc[:, bs, :], cview[0:128 * st_:st_, bs, G["kcol"]:G["kcol"] + 64], writes=[hh[("Kc", q4)]])
                    DMA("sp", Vc[:, bs, 0:64], cview[0:128 * st_:st_, bs, G["vcol"]:G["vcol"] + 64], writes=[hh[("Vc", q4, 0)]])
                    DMA("sp", Vc[:, bs, 64:128], cview[0:128 * st_:st_, bs, G["vcol"]:G["vcol"] + 64], writes=[hh[("Vc", q4, 1)]])
                pq = [mmrot.next(), mmrot.next()]
                for s, (qc, qb_) in enumerate(G["heads"]):
                    ps, hp = pq[qb_ // 64]
                    TR(ps[0:NS, s * 64:(s + 1) * 64], QTs[qb_:qb_ + 64, qc, :], ident[qb_:qb_ + 64, qb_:qb_ + 64], [hh[("QTs", qc)], hconst], [hp])
                for s, (qc, qb_) in enumerate(G["heads"]):
                    ps, hp = pq[qb_ // 64]
                    ACT(Qs[0:NS, s * 64:(s + 1) * 64], ps[0:NS, s * 64:(s + 1) * 64], AF.Copy, [hp], [hh["Qs"]])
                TT(Qd[0:NS, :, :], bcast_ap(Qs[0:NS, 0:1], [(0, NS), (1, 256)]), bcast_ap(ident[0:NS, 0:1], [(1, NS), (0, 256)]), ALU.mult,
                   [hh["Qs"], hconst], [hh["Qd"]])
                for bp in range(NS // 2):
                    ps, hp = mmrot.next()
                    MM(ps[:, 0:512], ones32[0:NS, :], Qd[0:NS, 2 * bp:2 * bp + 2, :], True, True, [hh["Qd"], hconst], [hp])
                    TT(prod[:, :].rearrange("p (b s d) -> p b s d", b=2, s=4), ps[:, 0:512].rearrange("p (b s d) -> p b s d", b=2, s=4),
                       bcast_ap(Kc[:, 2 * bp, 0:1], [(64, 2), (0, 4), (1, 64)]), ALU.mult, [hp] + [hh[("Kc", q4)] for q4 in range(4)], [hh["prod"]])
                    o = Sall[:, 8 * bp:8 * bp + 8].rearrange("p (b s) -> p b s", b=2)
                    RED(o, prod[:, :].rearrange("p (b s d) -> p b s d", b=2, s=4), [hh["prod"]], [hh["Sall"]])
                psn, hpn = mmrot.next()
                for s, (qc, qb_) in enumerate(G["heads"]):
                    TT(prT[:, :], QTs[:, qc, :], KTs[:, G["kc"], :], ALU.mult, [hh[("QTs", qc)], hh[("KTs", G["kc"])]], [hh["prT"]])
                    MM(psn[:, s * NS:(s + 1) * NS], selh[qb_ // 64], prT[:, :], True, True, [hh["prT"], hconst], [hpn])
                ACT(Pn[:, :], Sall[:, :], AF.Exp, [hh["Sall"]], [hh["Pn"]], scale=0.125)
                ACT(Pnew[:, :].rearrange("p (b s) -> p s b", s=4), psn[:, 0:64].rearrange("p (s b) -> p s b", s=4), AF.Exp, [hpn], [hh["Pnew"]], scale=0.125)
                psd, hpd = mmrot.next()
                MM(psd[:, 0:64], ones32, Pn[:, :], True, True, [hh["Pn"], hconst], [hpd])
                TT(den[:, :], psd[:, 0:64], Pnew[:, :], ALU.add, [hpd, hh["Pnew"]], [hh["den"]])
                if G["swa"] is not None:
                    kv = G["swa"]
                    TT(v3(den[:, :]), v3(den[:, :]), bcast_ap(smalls[:, 16 + 4 * kv:17 + 4 * kv], [(0, NS), (1, 4)]), ALU.add,
                       [hh["den"], hh["expsink"]], [hh["den"]])
                psv, hpv = mmrot.next()
                for b in range(NS):
                    MM(psv[:, 4 * b:4 * b + 4], Vc[:, b, :], Pn[:, 4 * b:4 * b + 4], True, True, [hh[("Vc", q4, i)] for q4 in range(4) for i in range(2)] + [hh["Pn"]], [hpv])
                TT(v3(num[:, :]), v3(Pnew[:, :]), bcast_ap(VTs[:, G["vi"], 0:1], [(1, NS), (0, 4)]), ALU.mult, [hh["Pnew"], hh[("VTs", G["vi"])]], [hh["num"]])
                TT(num[:, :], num[:, :], psv[:, 0:64], ALU.add, [hh["num"], hpv], [hh["num"]])
                if G["swa"] is not None:
                    kv = G["swa"]
                    RECIP(den[:, :], den[:, :], [hh["den"]], [hh["den"]])
                    TT(outv[:, :], num[:, :], den[:, :], ALU.mult, [hh["num"], hh["den"]], [hh["outv"]])
                    for s in range(4):
                        h = kv * 4 + s
                        rows = slice((h % 2) * 64, (h % 2) * 64 + 64)
                        CP(smix[rows, h // 2, :], outv[rows, s:64:4], [hh["outv"]], [hh[("smix", h // 2, h % 2)]])
                else:
                    if gi == 2:
                        CP(numD[:, :], num[:, :], [hh["num"]], [hh["numD"]])
                        CP(denD[:, :], den[:, :], [hh["den"]], [hh["denD"]])
                    else:
                        TT(numD[:, :], numD[:, :], num[:, :], ALU.add, [hh["num"], hh["numD"]], [hh["numD"]])
                        TT(denD[:, :], denD[:, :], den[:, :], ALU.add, [hh["den"], hh["denD"]], [hh["denD"]])
            RECIP(denD[:, :], denD[:, :], [hh["denD"]], [hh["denD"]])
            TT(outv[:, :], numD[:, :], denD[:, :], ALU.mult, [hh["numD"], hh["denD"]], [hh["outv"]])
            for s in range(4):
                rows = slice((s % 2) * 64, (s % 2) * 64 + 64)
                CP(smix[rows, 4 + s // 2, :], outv[rows, s:64:4], [hh["outv"]], [hh[("smix", 4 + s // 2, s % 2)]])
            P.fence()
            if L1STOP <= 3:
                A.reset(mL)
                return

            A.reset(mL)
            W = HW_
            ysb = A.alloc(8 * W, F32).rearrange("p (c n) -> p c n", c=8)
            sqbuf2 = [A.alloc(512, BF16), A.alloc(512, BF16)]
            tmp2 = A.alloc(512, F32)
            rstd2 = A.alloc(512, F32)
            ytmp = [A.alloc(512, F32), A.alloc(512, F32)]
            nb_ = push_slots(2)
            assert A.off <= mP

            def rhs_fn(k, tb):
                if tb == 4:
                    return smix[:, k, :], []
                return mixT[:, k, TBS[tb][0]:TBS[tb][0] + 512], []
            for hi_, (col0, tbs) in enumerate(HALVES):
                out_proj(w_out_attn, 0, 6, rhs_fn, tbs, ysb, col0)
                for tb in tbs:
                    postnorm_residual(1, 1, tb, ysb, col0, sqbuf2, tmp2, rstd2, ytmp)
                if hi_ == 1:
                    pop_slots(nb_)
                    if STOP >= 4:
                        pre_mlp(1)
                P.fence()
            A.reset(mL)

        if STOP >= 3:
            layer1_mixer()
        if STOP >= 4:
            for hi_, (col0, tbs) in enumerate(HALVES):
                mlp_half(1, col0, tbs, pre_next=(lambda: pre_mlp(1)) if hi_ == 0 else None)

        A.reset(m0)
        yo = [A.alloc(1024, F32) for _ in range(4)]
        rot = Rot([0, 1, 2, 3, 4, 5])
        for t in range(17):
            y = yo[t % 4]
            hy = hh[("yo", t % 4)]
            rows = 128 if t < 16 else NS
            tbk = t // 4 if t < 16 else 4
            for g in range(2):
                ps, hp = rot.next()
                for i in range(4):
                    c = g * 4 + i
                    TR(ps[0:rows, i * 128:(i + 1) * 128], hT[:, c, t * 128:t * 128 + rows], ident, [hh[("hT", c, tbk)], hconst], [hp])
                if g == 0:
                    ACT(y[0:rows, 0:512], ps[0:rows, :], AF.Copy, [hp], [hy])
                else:
                    CP(y[0:rows, 512:1024], ps[0:rows, :], [hp], [hy])
            dst = y_p[t * 128:(t + 1) * 128, :] if t < 16 else y_s
            DMA("sp", dst, y[0:rows, :], reads=[hy])

        assert not PRE, list(PRE.keys())
        P.emit(st)
        build_program.stats = dict(P.stats)
        build_program.arena_peak = A.peak
    return nc


def _consts():
    c32 = np.zeros((128, 512), np.float32)
    c32[:, 0:128] = np.eye(128, dtype=np.float32)
    c32[0:64, 128:256] = 1.0
    c32[64:128, 256:384] = 1.0
    c32[:, 384:512] = 1.0
    cb = np.zeros((128, 768), np.float32)
    prot = np.zeros((128, 128), np.float32)
    for base in (0, 64):
        for i in range(8):
            prot[base + i + 8, base + i] = 1.0
            prot[base + i, base + i + 8] = 1.0
    cb[:, 0:128] = prot
    cb[:, 128:256] = 1.0
    p = np.arange(128)[:, None]
    f = np.arange(128)[None, :]
    cb[:, 256:384] = (f <= p)
    cb[:, 384:512] = (f >= p)
    cb[:, 512:640] = np.eye(128, dtype=np.float32)
    half = 8
    inv = (np.float32(500000.0) ** (-np.arange(half, dtype=np.float32) / half)).astype(np.float32)
    pos = np.concatenate([np.arange(NTOK, dtype=np.float32), np.full(NS, 8192.0, np.float32)])
    ang = (pos[None, :] * inv[:, None]).astype(np.float32)
    cs = np.zeros((128, 2, NT), np.float32)
    cs[:, 0, :] = 1.0
    for base in (0, 64):
        cs[base:base + 8, 0] = np.cos(ang)
        cs[base + 8:base + 16, 0] = np.cos(ang)
        cs[base:base + 8, 1] = -np.sin(ang)
        cs[base + 8:base + 16, 1] = np.sin(ang)
    return c32, cb, cs


_NC_CACHE = {}


def kernel(x_prompt, x_sample, state_conv_a, state_conv_b, cache_swa_kv, cache_dil0_kv, cache_dil1_kv,
           cache_dil2_kv, norm_g, w_in_conv, conv_a_w, conv_a_b, conv_a_ln_g, conv_a_ln_b, conv_b_w,
           w_out_conv, w_in_attn, attn_sinks, w_out_attn, mlp_w1, mlp_w2):
    f = lambda a: np.ascontiguousarray(np.asarray(a, dtype=np.float32))
    x_prompt, x_sample = f(x_prompt), f(x_sample)
    if "nc" not in _NC_CACHE:
        _NC_CACHE["nc"] = build_program()
    nc = _NC_CACHE["nc"]
    c32, cb, cs = _consts()
    prm1 = np.concatenate([f(norm_g).reshape(64, 128), f(conv_a_b).reshape(4, 128), f(conv_a_ln_g).reshape(4, 128),
                           f(conv_a_ln_b).reshape(4, 128), f(conv_b_w).reshape(12, 128)], 0)
    prm2 = f(conv_a_w).reshape(124, 128)
    wi = f(w_in_attn)[0]
    qs = lambda h: wi[:, h * 64:(h + 1) * 64]
    qd = lambda g, h: wi[:, 768 + 384 * g + h * 64: 768 + 384 * g + (h + 1) * 64]
    kd = lambda g: wi[:, 768 + 384 * g + 256: 768 + 384 * g + 320]
    vd = lambda g: wi[:, 768 + 384 * g + 320: 768 + 384 * g + 384]
    ks = lambda kv: wi[:, 512 + kv * 64: 512 + (kv + 1) * 64]
    vs = lambda kv: wi[:, 640 + kv * 64: 640 + (kv + 1) * 64]
    wq = np.concatenate([np.concatenate([qs(g), qs(4 + g)], 1) for g in range(4)] +
                        [np.concatenate([qd(0, g), qd(1, g)], 1) for g in range(4)] +
                        [np.concatenate([qd(2, 0), qd(2, 1)], 1), np.concatenate([qd(2, 2), qd(2, 3)], 1)], 1)
    wk = np.concatenate([ks(0), ks(1), kd(0), kd(1), kd(2), kd(2)], 1)
    wv = np.concatenate([vs(0), vs(1), vd(0), vd(1), vd(2)], 1)
    wvd = np.concatenate([vs(0), vs(0), vs(1), vs(1), vd(0), vd(0), vd(1), vd(1), vd(2), vd(2)], 1)
    shared = {
        "prm1": f(prm1), "prm2": prm2, "sinks": f(attn_sinks).reshape(1, 8), "cst32": c32, "cstb": cb, "cs": cs,
        "w_in_conv": f(w_in_conv)[0], "w_out_conv": f(w_out_conv)[0], "wq": f(wq), "wk": f(wk), "wv": f(wv), "wvd": f(wvd),
        "w_out_attn": f(w_out_attn)[0], "w1": f(mlp_w1), "w2": f(mlp_w2),
    }
    sca = f(state_conv_a)[0]
    scb = f(state_conv_b)[0]
    cswa = f(cache_swa_kv)[0]
    cds = [f(cache_dil0_kv)[0], f(cache_dil1_kv)[0], f(cache_dil2_kv)[0]]
    in_maps = []
    for i in range(8):
        s = slice(NS * i, NS * (i + 1))
        m = dict(shared)
        m["xp"] = x_prompt[i]
        m["xs"] = x_sample[s, 0, :]
        m["sca"] = sca[s].reshape(NS * 30, 512)
        m["scb"] = scb[s].reshape(NS * 2, 512)
        m["c_swa"] = cswa[s].reshape(NS, 128, 256)
        for g in range(3):
            m["c_d%d" % g] = cds[g][s].reshape(NS, cds[g].shape[1], 128)
        in_maps.append(m)
    res = run_bass_kernel_spmd(nc, in_maps, core_ids=list(range(8)))
    R = res.results
    cat = lambda k: np.stack([np.asarray(r[k]) for r in R], 0)
    y_prompt = cat("y_p")
    y_sample = np.concatenate([np.asarray(r["y_s"]) for r in R], 0).reshape(128, 1, 1024)
    a_p = cat("a_p")[None]
    a_s = np.concatenate([np.asarray(r["a_s"]) for r in R], 0)[None]
    b_p = cat("b_p")[None]
    b_s = np.concatenate([np.asarray(r["b_s"]) for r in R], 0)[None]
    swa_p = cat("swa_p").reshape(1, 8, 128, 2, 2, 64)
    swa_s = np.concatenate([np.asarray(r["swa_s"]) for r in R], 0).reshape(1, 128, 128, 2, 2, 64)
    outs = [y_prompt, y_sample, a_p, a_s, b_p, b_s, swa_p, swa_s]
    for g, Wg in enumerate((128, 512, 2048)):
        outs.append(cat("d%d_p" % g).reshape(1, 8, Wg, 2, 1, 64))
        outs.append(np.concatenate([np.asarray(r["d%d_s" % g]) for r in R], 0).reshape(1, 128, Wg, 2, 1, 64))
    return tuple(np.ascontiguousarray(o, dtype=np.float32) for o in outs)
```

```python
import os
import numpy as np
from contextlib import ExitStack
import concourse.bass as bass
import concourse.mybir as mybir
from concourse.bass_utils import run_bass_kernel_spmd

F32 = mybir.dt.float32
BF16 = mybir.dt.bfloat16
ALU = mybir.AluOpType
AF = mybir.ActivationFunctionType
AX = mybir.AxisListType

ENGS = ("pe", "act", "dve", "pool", "sp")

NTOK = 2048
NS = 16
NT = NTOK + NS
TBS = [(0, 512), (512, 512), (1024, 512), (1536, 512), (2048, 16)]
HALVES = [(0, [0, 1]), (1024, [2, 3, 4])]
HW_ = 1040
EPS = 1e-6
STOP = int(os.environ.get("MK_STOP", "99"))
SAFE_WAR = int(os.environ.get("MK_SAFE_WAR", "1"))
M4A = int(os.environ.get("MK_4A", "63"))
POOL_ADD = int(os.environ.get("MK_POOL_ADD", "0"))
L1STOP = int(os.environ.get("MK_L1STOP", "99"))
NOCOPY = int(os.environ.get("MK_NOCOPY", "0"))


class H:
    __slots__ = ("name", "w", "r", "excl")

    def __init__(self, name="", excl=False):
        self.name = name
        self.w = None
        self.r = []
        self.excl = excl


class HD(dict):
    def __missing__(self, k):
        v = H(str(k))
        self[k] = v
        return v


class Op:
    __slots__ = ("eng", "fn", "deps", "signal", "tick", "dma", "sem", "val", "fenced")


class Prog:
    def __init__(self, nc, ndma=8):
        self.nc = nc
        self.ops = {e: [] for e in ENGS}
        self.all = []
        self.ndma = ndma

    def op(self, eng, fn, reads=(), writes=(), dma=False):
        o = Op()
        o.eng, o.fn, o.dma, o.signal, o.tick, o.sem, o.val, o.fenced = eng, fn, dma, dma, 0, None, 0, False
        deps = []
        if any(h.excl for h in reads):
            writes = list(writes) + [h for h in reads if h.excl and h not in writes]
            reads = [h for h in reads if not h.excl]

        def add(p, kind):
            if p is None or p is o:
                return
            if p.eng == eng and not p.dma and not dma:
                if eng == "pe" or (kind == "WAR" and not SAFE_WAR):
                    return
            if p not in deps:
                deps.append(p)

        for h in reads:
            add(h.w, "RAW")
        for h in writes:
            add(h.w, "WAW")
            for r in h.r:
                add(r, "WAR")
        for h in reads:
            h.r.append(o)
        for h in writes:
            h.w = o
            h.r = []
        o.deps = deps
        self.ops[eng].append(o)
        self.all.append(o)
        return o

    def fence(self):
        lasts = [self.ops[e][-1] for e in ENGS if self.ops[e] and self.ops[e][-1].fn is not None]
        dmas = [o for o in self.all if o.dma and not o.fenced]
        for o in dmas:
            o.fenced = True
        for e in ENGS:
            o = Op()
            o.eng, o.fn, o.dma, o.signal, o.tick, o.sem, o.val, o.fenced = e, None, False, False, 0, None, 0, True
            o.deps = [p for p in lasts if p.eng != e or p.dma] + [d for d in dmas if d not in lasts]
            self.ops[e].append(o)
            self.all.append(o)

    def emit(self, stack):
        nc = self.nc
        for o in self.all:
            for d in o.deps:
                d.signal = True
        esem = {e: stack.enter_context(nc.semaphore("es_" + e)) for e in ENGS}
        dsem = {e: [stack.enter_context(nc.semaphore("ds_%s_%d" % (e, i))) for i in range(self.ndma)]
                for e in ENGS if any(o.dma for o in self.ops[e])}
        for e in ENGS:
            c = 0
            nd = 0
            dmas = []
            for o in self.ops[e]:
                if o.dma:
                    o.sem = dsem[e][nd % self.ndma]
                    o.val = 16 * (nd // self.ndma + 1)
                    if nd >= self.ndma:
                        prev = dmas[nd - self.ndma]
                        if prev not in o.deps:
                            o.deps.append(prev)
                    dmas.append(o)
                    nd += 1
                elif o.signal:
                    c += 1
                    o.tick = c
        block = stack.enter_context(nc.Block())
        prog = self
        self.stats = {}

        def section(e):
            def body(eng):
                known = {}
                nwait = 0
                for o in prog.ops[e]:
                    for d in o.deps:
                        if d.dma:
                            sem, val = d.sem, d.val
                        else:
                            sem, val = esem[d.eng], d.tick
                        key = id(sem)
                        if known.get(key, 0) >= val:
                            continue
                        eng.wait_ge(sem, val)
                        nwait += 1
                        known[key] = val
                    if o.fn is None:
                        continue
                    ins = o.fn(eng)
                    if o.dma:
                        ins.then_inc(o.sem, 16)
                    elif o.signal:
                        ins.then_inc(esem[e], 1)
                last = {}
                for o in prog.ops[e]:
                    if o.dma:
                        last[id(o.sem)] = (o.sem, o.val)
                for sem, val in last.values():
                    if known.get(id(sem), 0) < val:
                        eng.wait_ge(sem, val)
                prog.stats[e] = (len(prog.ops[e]), nwait)
            return body

        block.tensor(section("pe"))
        block.scalar(section("act"))
        block.vector(section("dve"))
        block.gpsimd(section("pool"))
        block.sync(section("sp"))


class Arena:
    def __init__(self, t32, cap_bytes):
        self.t32 = t32
        self.t16 = t32.bitcast(BF16)
        self.cap = cap_bytes
        self.off = 0
        self.peak = 0

    def alloc(self, ncols, dtype):
        esz = 4 if dtype == F32 else 2
        off = (self.off + 31) // 32 * 32
        nb = ncols * esz
        assert off + nb <= self.cap, ("arena overflow", off, nb, self.cap)
        self.off = off + nb
        self.peak = max(self.peak, self.off)
        if dtype == F32:
            return self.t32[:, off // 4: off // 4 + ncols]
        return self.t16[:, off // 2: off // 2 + ncols]

    def mark(self):
        return self.off

    def reset(self, m=0):
        self.off = m


def bcast_ap(ap, dims):
    return bass.AP(ap.tensor, ap.offset, [list(ap.ap[0])] + [[s, n] for s, n in dims])


def build_program():
    nc = bass.Bass("TRN2", target_bir_lowering=False)

    def din(name, shape):
        return nc.dram_tensor(name, list(shape), F32, kind="ExternalInput").ap()

    def dout(name, shape):
        return nc.dram_tensor(name, list(shape), F32, kind="ExternalOutput").ap()

    xp = din("xp", [NTOK, 1024])
    xs = din("xs", [NS, 1024])
    sca = din("sca", [NS * 30, 512])
    scb = din("scb", [NS * 2, 512])
    c_swa = din("c_swa", [NS, 128, 256])
    c_d = [din("c_d0", [NS, 128, 128]), din("c_d1", [NS, 512, 128]), din("c_d2", [NS, 2048, 128])]
    prm1 = din("prm1", [88, 128])
    prm2 = din("prm2", [124, 128])
    sinks = din("sinks", [1, 8])
    cst32_d = din("cst32", [128, 512])
    cstb_d = din("cstb", [128, 768])
    cs_d = din("cs", [128, 2, NT])
    w_in_conv = din("w_in_conv", [1024, 2560])
    w_out_conv = din("w_out_conv", [1024, 1024])
    wq_d = din("wq", [1024, 1280])
    wk_d = din("wk", [1024, 384])
    wv_d = din("wv", [1024, 320])
    wvd_d = din("wvd", [1024, 640])
    w_out_attn = din("w_out_attn", [768, 1024])
    w1_d = din("w1", [2, 1024, 4096])
    w2_d = din("w2", [2, 4096, 1024])

    y_p = dout("y_p", [NTOK, 1024])
    y_s = dout("y_s", [NS, 1024])
    a_p = dout("a_p", [30, 512])
    a_s = dout("a_s", [NS, 30, 512])
    b_p = dout("b_p", [2, 512])
    b_s = dout("b_s", [NS, 2, 512])
    swa_p = dout("swa_p", [128, 256])
    swa_s = dout("swa_s", [NS, 128, 256])
    d_p = [dout("d0_p", [128, 128]), dout("d1_p", [512, 128]), dout("d2_p", [2048, 128])]
    d_s = [dout("d0_s", [NS, 128, 128]), dout("d1_s", [NS, 512, 128]), dout("d2_s", [NS, 2048, 128])]

    st = ExitStack()
    with st:
        P = Prog(nc)
        sb = lambda name, shape, dt: st.enter_context(nc.sbuf_tensor(name, list(shape), dt))
        hT = sb("hT", [128, 8, NT], F32)
        cst32 = sb("cst32s", [128, 512], F32)
        cstb = sb("cstbs", [128, 768], BF16)
        prmT = sb("prmT", [128, 88], F32)
        wAT = sb("wAT", [128, 124], F32)
        smalls = sb("smalls", [128, 32], F32)
        halo = sb("halo", [128, 8, 30], BF16)
        NSLOT = 2
        SLOT = 4096
        wslots = [sb("wslot%d" % i, [128, SLOT], BF16) for i in range(NSLOT)]
        ARENA_BYTES = (nc.sbuf_bytes_remaining // 64) * 64 - 256
        A = Arena(sb("arena", [128, ARENA_BYTES // 4], F32), ARENA_BYTES)
        psb = [st.enter_context(nc.psum_tensor("psb%d" % i, [128, 512], F32)) for i in range(8)]
        hps = [H("ps%d" % i, excl=True) for i in range(8)]

        ident = cst32[:, 0:128]
        selh = [cst32[:, 128:256], cst32[:, 256:384]]
        ones32 = cst32[:, 384:512]
        prot = cstb[:, 0:128]
        onesb = cstb[:, 128:256]
        maskpd = cstb[:, 256:512]
        identb = cstb[:, 512:640]
        epsc = smalls[:, 0:1]

        hh = HD()
        hconst = hh["const"]

        def MM(ps_ap, lhsT, rhs, start, stop, reads, writes):
            P.op("pe", lambda e: e.matmul(ps_ap, lhsT=lhsT, rhs=rhs, start=start, stop=stop), reads=reads, writes=writes)

        def TR(out, in_, idn, reads, writes):
            P.op("pe", lambda e: e.transpose(out=out, in_=in_, identity=idn), reads=reads, writes=writes)

        def ACT(out, in_, func, reads, writes, bias=None, scale=None):
            kw = {}
            if bias is not None:
                kw["bias"] = bias
            if scale is not None:
                kw["scale"] = scale
            P.op("act", lambda e: e.activation(out=out, in_=in_, func=func, **kw), reads=reads, writes=writes)

        def TT(out, in0, in1, op, reads, writes, eng="dve"):
            P.op(eng, lambda e: e.tensor_tensor(out=out, in0=in0, in1=in1, op=op), reads=reads, writes=writes)

        def STT(out, in0, scalar, in1, op0, op1, reads, writes):
            P.op("dve", lambda e: e.scalar_tensor_tensor(out=out, in0=in0, scalar=scalar, in1=in1, op0=op0, op1=op1), reads=reads, writes=writes)

        def TS(out, in0, s1, op0, reads, writes, s2=None, op1=None, eng="dve"):
            if op1 is None:
                P.op(eng, lambda e: e.tensor_scalar(out=out, in0=in0, scalar1=s1, scalar2=None, op0=op0), reads=reads, writes=writes)
            else:
                P.op(eng, lambda e: e.tensor_scalar(out=out, in0=in0, scalar1=s1, scalar2=s2, op0=op0, op1=op1), reads=reads, writes=writes)

        def CP(out, in_, reads, writes, eng="dve"):
            P.op(eng, lambda e: e.tensor_copy(out=out, in_=in_), reads=reads, writes=writes)

        def RED(out, in_, reads, writes):
            P.op("dve", lambda e: e.tensor_reduce(out=out, in_=in_, axis=AX.X, op=ALU.add), reads=reads, writes=writes)

        def RECIP(out, in_, reads, writes):
            P.op("dve", lambda e: e.reciprocal(out=out, in_=in_), reads=reads, writes=writes)

        def MEMSET(ap, val, writes, eng="dve"):
            P.op(eng, lambda e: e.memset(ap, val), writes=writes)

        def DMA(eng, out, in_, reads=(), writes=()):
            P.op(eng, lambda e: e.dma_start(out=out, in_=in_), reads=reads, writes=writes, dma=True)

        class Rot:
            def __init__(self, banks):
                self.banks = banks
                self.i = 0

            def next(self):
                b = self.banks[self.i % len(self.banks)]
                self.i += 1
                return psb[b], hps[b]

        wrot = [0]

        NSL = [NSLOT]

        def wslot_next():
            i = wrot[0] % NSL[0]
            wrot[0] += 1
            return wslots[i], hh[("wslot", i)]

        def gcol(l, w, c):
            j = (l * 4 + w) * 8 + c
            return prmT[:, j:j + 1]

        DMA("sp", cst32[:], cst32_d, writes=[hconst])
        DMA("pool", cstb[:], cstb_d, writes=[hconst])
        MEMSET(smalls[:, 0:8], EPS, [hh["smalls"]])
        DMA("sp", smalls[:, 8:16], bass.AP(sinks.tensor, 0, [[0, 128], [1, 8]]), writes=[hh["sinks"]])
        ACT(smalls[:, 16:24], smalls[:, 8:16], AF.Exp, [hh["sinks"]], [hh["expsink"]])
        m0 = A.mark()
        p1 = A.alloc(128, F32)
        p2 = A.alloc(128, F32)
        DMA("sp", p1[0:88, :], prm1, writes=[hh["p1"]])
        DMA("sp", p2[0:124, :], prm2, writes=[hh["p2"]])
        TR(psb[0][:, 0:88], p1[0:88, :], ident[0:88, 0:88], [hh["p1"], hconst], [hps[0]])
        ACT(prmT[:], psb[0][:, 0:88], AF.Copy, [hps[0]], [hconst])
        TR(psb[1][:, 0:124], p2[0:124, :], ident[0:124, 0:124], [hh["p2"], hconst], [hps[1]])
        ACT(wAT[:], psb[1][:, 0:124], AF.Copy, [hps[1]], [hconst])

        xin = [A.alloc(1024, F32) for _ in range(4)]
        rot = Rot([2, 3, 4, 5, 6, 7])
        for t in range(17):
            xi = xin[t % 4]
            hx = hh[("xin", t % 4)]
            rows = 128 if t < 16 else NS
            src = xp[t * 128:(t + 1) * 128, :] if t < 16 else xs
            DMA("sp", xi[0:rows, :], src, writes=[hx])
            for g in range(2):
                ps, hp = rot.next()
                for i in range(4):
                    c = g * 4 + i
                    TR(ps[:, i * rows:(i + 1) * rows], xi[0:rows, c * 128:(c + 1) * 128], ident[0:rows, 0:rows], [hx, hconst], [hp])
                dst = hT[:, g * 4:(g + 1) * 4, t * 128:t * 128 + rows]
                srcp = ps[:, 0:4 * rows].rearrange("p (i r) -> p i r", i=4)
                wr = [hh[("hT", g * 4 + i, t // 4 if t < 16 else 4)] for i in range(4)]
                if g == 0:
                    ACT(dst, srcp, AF.Copy, [hp], wr)
                else:
                    CP(dst, srcp, [hp], wr)
        PHASE1_FENCE = True

        statrot = Rot([6, 7])
        mmrot = Rot([0, 1, 2, 3, 4, 5])

        def rstd_from_ps(ps_stat, hstat, N, scale, out_rstd, hout, tmp, htmp):
            ACT(tmp[:, 0:N], ps_stat[:, 0:N], AF.Sqrt, [hstat, hh["smalls"]], [htmp], bias=epsc, scale=scale)
            RECIP(out_rstd[:, 0:N], tmp[:, 0:N], [htmp], [hout])

        def sumsq_stat(src_fn, rd_fn, N, sqbuf):
            ps, hp = statrot.next()
            for c in range(8):
                sq = sqbuf[c % 2]
                hsq = hh[("sq", c % 2)]
                ACT(sq[:, 0:N], src_fn(c), AF.Square, rd_fn(c), [hsq])
                MM(ps[:, 0:N], onesb, sq[:, 0:N], c == 0, c == 7, [hsq, hconst], [hp])
            return ps, hp

        def norm_block(l, w, tb, dstT, dst_col0, sqbuf, tmp, rstd, htb=None):
            c0, N = TBS[tb]
            ps, hp = sumsq_stat(lambda c: hT[:, c, c0:c0 + N], lambda c: [hh[("hT", c, tb)]], N, sqbuf)
            rstd_from_ps(ps, hp, N, 1.0 / 1024, rstd, hh["rstd"], tmp, hh["rtmp"])
            for c in range(8):
                STT(dstT[:, c, c0 - dst_col0:c0 - dst_col0 + N], hT[:, c, c0:c0 + N], gcol(l, w, c), rstd[:, 0:N], ALU.mult, ALU.mult,
                    [hh[("hT", c, tb)], hh["rstd"], hconst], [hh[("uT", c, tb if htb is None else htb)]])

        def push_slots(n):
            nb = len(wslots)
            wslots.extend([A.alloc(SLOT, BF16) for _ in range(n)])
            NSL[0] = len(wslots)
            return nb

        def pop_slots(nb):
            del wslots[nb:]
            NSL[0] = len(wslots)

        PRE = {}

        def wkey(wd, row0, Kc, cols):
            return (wd.tensor.name, int(wd.offset), row0, Kc, tuple(cols))

        def preload(wd, row0, Kc, cols):
            assert NSL[0] == NSLOT
            PRE[wkey(wd, row0, Kc, cols)] = load_wchunks(wd, row0, Kc, cols)

        def load_wchunks(wd, row0, Kc, cols):
            k_ = wkey(wd, row0, Kc, cols)
            if k_ in PRE:
                return PRE.pop(k_)
            slot, hs = wslot_next()
            n = len(cols)
            assert n * Kc * 128 <= SLOT
            if n > 1 and all(cols[i + 1] == cols[i] + 128 for i in range(n - 1)):
                vk = slot[:, 0:n * Kc * 128].rearrange("p (k c) -> p k c", k=Kc)
                src = wd[row0:row0 + Kc * 128, cols[0]:cols[0] + n * 128].rearrange("(k p) c -> p k c", p=128)
                DMA("pool", vk, src, writes=[hs])
                v = slot[:, 0:n * Kc * 128].rearrange("p (k i m) -> p i k m", k=Kc, i=n)
                return v, hs
            v = slot[:, 0:n * Kc * 128].rearrange("p (i k m) -> p i k m", i=n, k=Kc)
            for i, col0 in enumerate(cols):
                src = wd[row0:row0 + Kc * 128, col0:col0 + 128].rearrange("(k p) m -> p k m", p=128)
                DMA("pool", v[:, i], src, writes=[hs])
            return v, hs

        def postnorm_residual(l, w, tb, ysb, ycol0, sqbuf, tmp, rstd, ytmp):
            c0, N = TBS[tb]
            lc = c0 - ycol0
            ps, hp = sumsq_stat(lambda c: ysb[:, c, lc:lc + N], lambda c: [hh[("ysb", c, tb)]], N, sqbuf)
            rstd_from_ps(ps, hp, N, 1.0 / 1024, rstd, hh["rstd"], tmp, hh["rtmp"])
            for c in range(8):
                yt = ytmp[c % 2]
                hyt = hh[("ytmp", c % 2)]
                STT(yt[:, 0:N], ysb[:, c, lc:lc + N], gcol(l, w, c), rstd[:, 0:N], ALU.mult, ALU.mult,
                    [hh[("ysb", c, tb)], hh["rstd"], hconst], [hyt])
                TT(hT[:, c, c0:c0 + N], hT[:, c, c0:c0 + N], yt[:, 0:N], ALU.add, [hyt, hh[("hT", c, tb)]], [hh[("hT", c, tb)]],
                   eng=("pool" if POOL_ADD else "dve"))

        def out_proj(wd, row0, Kc, rhs_fn, tbs, ysb, ycol0, accumulate=False):
            for m in range(8):
                if m % 2 == 0:
                    v, hs = load_wchunks(wd, row0, Kc, [m * 128, (m + 1) * 128])
                for tb in tbs:
                    c0, N = TBS[tb]
                    ps, hp = mmrot.next()
                    for k in range(Kc):
                        rap, rh = rhs_fn(k, tb)
                        MM(ps[:, 0:N], v[:, m % 2, k, :], rap, k == 0, k == Kc - 1, [hs] + rh, [hp])
                    dst = ysb[:, m, c0 - ycol0:c0 - ycol0 + N]
                    if not accumulate:
                        ACT(dst, ps[:, 0:N], AF.Copy, [hp], [hh[("ysb", m, tb)]])
                    else:
                        TT(dst, dst, ps[:, 0:N], ALU.add, [hp, hh[("ysb", m, tb)]], [hh[("ysb", m, tb)]])

        def pre_l0a():
            preload(w_in_conv, 0, 8, [0, 512, 1024, 2048])
            preload(w_in_conv, 0, 8, [1536])

        def pre_outproj(wd, Kc):
            preload(wd, 0, Kc, [0, 128])
            preload(wd, 0, Kc, [256, 384])

        def pre_mlp(l):
            preload(w1_d[l], 0, 8, [0, 128, 256, 384])
            preload(w1_d[l], 0, 8, [512, 640, 768, 896])

        def pre_l1a():
            preload(wq_d, 0, 8, [0, 128, 256, 384])
            preload(wq_d, 0, 8, [512, 640, 768, 896])

        def mlp_half(l, col0, tbs, pre_next=None):
            m1 = A.mark()
            W = HW_
            uT = A.alloc(8 * W, BF16).rearrange("p (c n) -> p c n", c=8)
            hid = A.alloc(16 * W, BF16).rearrange("p (c n) -> p c n", c=16)
            ysb = A.alloc(8 * W, F32).rearrange("p (c n) -> p c n", c=8)
            sqbuf = [A.alloc(512, BF16), A.alloc(512, BF16)]
            rl = [A.alloc(512, BF16), A.alloc(512, BF16)]
            tmp = A.alloc(512, F32)
            rstd = A.alloc(512, F32)
            ytmp = [A.alloc(512, F32), A.alloc(512, F32)]
            nb_ = push_slots(2)
            for tb in tbs:
                norm_block(l, 2, tb, uT, col0, sqbuf, tmp, rstd)
            for hf in range(2):
                for mg in range(4):
                    ms = [hf * 16 + mg * 4 + i for i in range(4)]
                    v, hs = load_wchunks(w1_d[l], 0, 8, [m * 128 for m in ms])
                    for i, m in enumerate(ms):
                        for tb in tbs:
                            c0, N = TBS[tb]
                            lc = c0 - col0
                            ps, hp = mmrot.next()
                            for k in range(8):
                                MM(ps[:, 0:N], v[:, i, k, :], uT[:, k, lc:lc + N], k == 0, k == 7, [hs, hh[("uT", k, tb)]], [hp])
                            r = rl[(m + tb) % 2]
                            hr = hh[("rl", (m + tb) % 2)]
                            ACT(r[:, 0:N], ps[:, 0:N], AF.Relu, [hp], [hr])
                            TT(hid[:, m % 16, lc:lc + N], r[:, 0:N], r[:, 0:N], ALU.mult, [hr], [hh[("hid", m % 16, tb)]])
                out_proj(w2_d[l], hf * 2048, 16,
                         lambda k, tb: (hid[:, k, TBS[tb][0] - col0:TBS[tb][0] - col0 + TBS[tb][1]], [hh[("hid", k, tb)]]),
                         tbs, ysb, col0, accumulate=(hf == 1))
            for tb in tbs:
                postnorm_residual(l, 3, tb, ysb, col0, sqbuf, tmp, rstd, ytmp)
            pop_slots(nb_)
            if pre_next is not None:
                pre_next()
            P.fence()
            A.reset(m1)

        def layer0_half(hi, col0, tbs):
            m1 = A.mark()
            W = HW_
            uT = A.alloc(8 * W, BF16).rearrange("p (c n) -> p c n", c=8)
            mixT = uT
            ga = A.alloc(4 * (30 + W), BF16).rearrange("p (c n) -> p c n", c=4)
            zb = A.alloc(4 * (2 + W), BF16).rearrange("p (c n) -> p c n", c=4)
            gb = A.alloc(4 * W, BF16).rearrange("p (c n) -> p c n", c=4)
            m2 = A.mark()
            diag = A.alloc(31 * 128, BF16).rearrange("p (j m) -> p j m", j=31)
            diagB = A.alloc(12 * 128, BF16).rearrange("p (j m) -> p j m", j=12)
            cvo = A.alloc(4 * W, F32).rearrange("p (c n) -> p c n", c=4)
            reg32 = A.alloc(2048, F32)
            xq16 = A.t16[:, 2 * reg32.offset:2 * reg32.offset + 4096]
            xb = xq16[:, 0:2048].rearrange("p (c n) -> p c n", c=4)
            sq4 = xq16[:, 2048:4096].rearrange("p (c n) -> p c n", c=4)
            sqbuf = [A.alloc(512, BF16), A.alloc(512, BF16)]
            tmpa = A.alloc(512, F32)
            tmpg = A.alloc(512, F32)
            tmp = A.alloc(512, F32)
            rstd = A.alloc(512, F32)
            mean = A.alloc(512, F32)
            msq = A.alloc(512, F32)
            t1 = [A.alloc(512, F32), A.alloc(512, F32)]
            tails = A.alloc(4 * 64, F32).rearrange("p (c n) -> p c n", c=4)
            sAT = A.alloc(4 * NS * 30, F32).rearrange("p (c n) -> p c n", c=4)
            sBT = A.alloc(4 * NS * 2, F32).rearrange("p (c n) -> p c n", c=4)
            stg = A.alloc(512, F32)
            gas = A.alloc(4 * NS, F32).rearrange("p (c n) -> p c n", c=4)
            prodA = A.alloc(NS * 30, F32)
            hga = [hh[("ga", c)] for c in range(4)]
            hzb = [hh[("zb", c)] for c in range(4)]
            nb2a = push_slots(1)

            if hi == 0:
                MEMSET(ga[:, :, 0:30], 0.0, hga)
                MEMSET(zb[:, :, 0:2], 0.0, hzb)
            else:
                CP(ga[:, :, 0:30], halo[:, 0:4, 0:30], [hh["halo"]], hga)
                CP(zb[:, :, 0:2], halo[:, 4:8, 0:2], [hh["halo"]], hzb)

            for tb in tbs:
                norm_block(0, 0, tb, uT, col0, sqbuf, tmp, rstd)

            for c in range(4):
                mcols = [c * 128, 512 + c * 128, 1024 + c * 128, 2048 + c * 128, 1536 + c * 128]
                v, hs = load_wchunks(w_in_conv, 0, 8, mcols[0:4])
                v2, hs2 = load_wchunks(w_in_conv, 0, 8, mcols[4:5])
                for tb in tbs:
                    c0, N = TBS[tb]
                    lc = c0 - col0
                    pss = []
                    for i in range(5):
                        ps, hp = mmrot.next()
                        vv, hv, ii = (v, hs, i) if i < 4 else (v2, hs2, 0)
                        for k in range(8):
                            MM(ps[:, 0:N], vv[:, ii, k, :], uT[:, k, lc:lc + N], k == 0, k == 7, [hv, hh[("uT", k, tb)]], [hp])
                        pss.append((ps, hp))
                    (pa, ha), (pg, hg), (px, hx_), (pc, hc), (pb, hb) = pss
                    ACT(tmpg[:, 0:N], pg[:, 0:N], AF.Sigmoid, [hg], [hh["tmpg"]])
                    TT(ga[:, c, 30 + lc:30 + lc + N], pa[:, 0:N], tmpg[:, 0:N], ALU.mult, [ha, hh["tmpg"]], [hga[c]])
                    if tb == 3:
                        TT(tails[:, c, 0:30], pa[:, 482:512], tmpg[:, 482:512], ALU.mult, [ha, hh["tmpg"]], [hh[("tails", c)]])
                    if tb == 4:
                        TT(gas[:, c, :], pa[:, 0:NS], tmpg[:, 0:NS], ALU.mult, [ha, hh["tmpg"]], [hh[("gas", c)]])
                    ACT(tmpa[:, 0:N], px[:, 0:N], AF.Copy, [hx_], [hh["tmpa"]])
                    TT(zb[:, c, 2 + lc:2 + lc + N], pc[:, 0:N], tmpa[:, 0:N], ALU.mult, [hc, hh["tmpa"]], [hzb[c]])
                    if tb == 3:
                        TT(tails[:, c, 32:34], pc[:, 510:512], tmpa[:, 510:512], ALU.mult, [hc, hh["tmpa"]], [hh[("tails", c)]])
                    if tb == 4:
                        TT(tails[:, c, 40:56], pc[:, 0:NS], tmpa[:, 0:NS], ALU.mult, [hc, hh["tmpa"]], [hh[("tails", c)]])
                    ACT(gb[:, c, lc:lc + N], pb[:, 0:N], AF.Copy, [hb], [hh[("gb", c)]])

            if hi == 1:
                stA = reg32.rearrange("p (g n) -> p g n", g=4)
                stB = tmp
                hstA = [hh[("xb", c)] for c in range(4)] + [hh[("sq4", c)] for c in range(4)]
                DMA("sp", stA[0:120, :, :], sca.rearrange("(g r) n -> r g n", r=120), writes=hstA)
                DMA("sp", stB[0:32, :], scb, writes=[hh["rtmp"]])
                for c in range(4):
                    ps, hp = mmrot.next()
                    for g in range(4):
                        TR(ps[:, g * 120:(g + 1) * 120], stA[0:120, g, c * 128:(c + 1) * 128], ident[0:120, 0:120], hstA + [hconst], [hp])
                    ACT(sAT[:, c, :], ps[:, 0:480], AF.Copy, [hp], [hh[("sAT", c)]])
                ps, hp = mmrot.next()
                for c in range(4):
                    TR(ps[:, c * 32:(c + 1) * 32], stB[0:32, c * 128:(c + 1) * 128], ident[0:32, 0:32], [hh["rtmp"], hconst], [hp])
                ACT(sBT[:, :, :], ps[:, 0:128].rearrange("p (c n) -> p c n", c=4), AF.Copy, [hp], [hh["sBT"]])

            for j in range(3):
                for c in range(4):
                    TS(diagB[:, j * 4 + c, :], identb, prmT[:, 76 + j * 4 + c:77 + j * 4 + c], ALU.mult, [hconst], [hh["diagB"]])
            for c in range(4):
                for j in range(31):
                    if j % 2 == 0:
                        TS(diag[:, j, :], identb, wAT[:, j * 4 + c:j * 4 + c + 1], ALU.mult, [hconst], [hh[("diag", j)]])
                    else:
                        ACT(diag[:, j, :], identb, AF.Copy, [hconst], [hh[("diag", j)]], scale=wAT[:, j * 4 + c:j * 4 + c + 1])
                for tb in tbs:
                    c0, N = TBS[tb]
                    lc = c0 - col0
                    hcv = hh[("cvo", c, tb)]
                    if tb < 4:
                        ps, hp = mmrot.next()
                        for j in range(31):
                            MM(ps[:, 0:N], diag[:, j, :], ga[:, c, lc + j:lc + j + N], j == 0, j == 30, [hh[("diag", j)], hga[c]], [hp])
                        ACT(cvo[:, c, lc:lc + N], ps[:, 0:N], AF.Identity, [hp, hconst], [hcv], bias=prmT[:, 64 + c:65 + c], scale=1.0)
                    else:
                        wv_ = bcast_ap(wAT[:, c:c + 1], [(0, NS), (4, 30)])
                        pA = prodA.rearrange("p (b j) -> p b j", b=NS)
                        TT(pA, sAT[:, c, :].rearrange("p (b j) -> p b j", b=NS), wv_, ALU.mult, [hh[("sAT", c)], hconst], [hh["prodA"]])
                        RED(cvo[:, c, lc:lc + NS], pA, [hh["prodA"]], [hcv])
                        STT(cvo[:, c, lc:lc + NS], gas[:, c, :], wAT[:, 120 + c:121 + c], cvo[:, c, lc:lc + NS], ALU.mult, ALU.add,
                            [hh[("gas", c)], hcv, hconst], [hcv])
                        TS(cvo[:, c, lc:lc + NS], cvo[:, c, lc:lc + NS], prmT[:, 64 + c:65 + c], ALU.add, [hcv, hconst], [hcv])
            for tb in tbs:
                c0, N = TBS[tb]
                lc = c0 - col0
                psm, hpm = statrot.next()
                pse, hpe = statrot.next()
                for c in range(4):
                    CP(xb[:, c, 0:N], cvo[:, c, lc:lc + N], [hh[("cvo", c, tb)]], [hh[("xb", c)]])
                    ACT(sq4[:, c, 0:N], cvo[:, c, lc:lc + N], AF.Square, [hh[("cvo", c, tb)]], [hh[("sq4", c)]])
                for c in range(4):
                    MM(psm[:, 0:N], onesb, xb[:, c, 0:N], c == 0, c == 3, [hh[("xb", c)], hconst], [hpm])
                for c in range(4):
                    MM(pse[:, 0:N], onesb, sq4[:, c, 0:N], c == 0, c == 3, [hh[("sq4", c)], hconst], [hpe])
                ACT(mean[:, 0:N], psm[:, 0:N], AF.Copy, [hpm], [hh["mean"]], scale=1.0 / 512)
                TT(msq[:, 0:N], mean[:, 0:N], mean[:, 0:N], ALU.mult, [hh["mean"]], [hh["msq"]])
                STT(msq[:, 0:N], pse[:, 0:N], 1.0 / 512, msq[:, 0:N], ALU.mult, ALU.subtract, [hpe, hh["msq"]], [hh["msq"]])
                ACT(tmp[:, 0:N], msq[:, 0:N], AF.Sqrt, [hh["msq"], hh["smalls"]], [hh["rtmp"]], bias=epsc, scale=1.0)
                RECIP(rstd[:, 0:N], tmp[:, 0:N], [hh["rtmp"]], [hh["rstd"]])
                for c in range(4):
                    tt = t1[c % 2]
                    ht = hh[("t1", c % 2)]
                    TT(tt[:, 0:N], cvo[:, c, lc:lc + N], mean[:, 0:N], ALU.subtract, [hh[("cvo", c, tb)], hh["mean"]], [ht])
                    TT(tt[:, 0:N], tt[:, 0:N], rstd[:, 0:N], ALU.mult, [ht, hh["rstd"]], [ht])
                    ACT(mixT[:, c, lc:lc + N], tt[:, 0:N], AF.Silu, [ht, hconst], [hh[("uT", c, tb)]],
                        bias=prmT[:, 72 + c:73 + c], scale=prmT[:, 68 + c:69 + c])
                for c in range(4):
                    hmx = hh[("uT", 4 + c, tb)]
                    if tb < 4:
                        ps, hp = mmrot.next()
                        for j in range(3):
                            MM(ps[:, 0:N], diagB[:, j * 4 + c, :], zb[:, c, lc + j:lc + j + N], j == 0, j == 2, [hh["diagB"], hzb[c]], [hp])
                        TT(mixT[:, 4 + c, lc:lc + N], ps[:, 0:N], gb[:, c, lc:lc + N], ALU.mult, [hp, hh[("gb", c)]], [hmx])
                    else:
                        tt = t1[c % 2]
                        ht = hh[("t1", c % 2)]
                        sb_ = sBT[:, c, :].rearrange("p (b j) -> p b j", j=2)
                        TS(tt[:, 0:NS], sb_[:, :, 0], prmT[:, 76 + c:77 + c], ALU.mult, [hh["sBT"], hconst], [ht])
                        STT(tt[:, 0:NS], sb_[:, :, 1], prmT[:, 80 + c:81 + c], tt[:, 0:NS], ALU.mult, ALU.add, [hh["sBT"], ht, hconst], [ht])
                        STT(tt[:, 0:NS], tails[:, c, 40:56], prmT[:, 84 + c:85 + c], tt[:, 0:NS], ALU.mult, ALU.add, [hh[("tails", c)], ht, hconst], [ht])
                        TT(mixT[:, 4 + c, lc:lc + NS], tt[:, 0:NS], gb[:, c, lc:lc + NS], ALU.mult, [ht, hh[("gb", c)]], [hmx])

            if hi == 1:
                def tm_out(src_fn, rows, dst, rd_fn):
                    ps, hp = mmrot.next()
                    for c in range(4):
                        TR(ps[0:rows, c * 128:(c + 1) * 128], src_fn(c), ident, rd_fn(c) + [hconst], [hp])
                    ACT(stg[0:rows, :], ps[0:rows, :], AF.Copy, [hp], [hh["stg"]])
                    DMA("sp", dst, stg[0:rows, :], reads=[hh["stg"]])
                tm_out(lambda c: tails[:, c, 0:30], 30, a_p, lambda c: [hh[("tails", c)]])
                tm_out(lambda c: tails[:, c, 32:34], 2, b_p, lambda c: [hh[("tails", c)]])
                DMA("sp", a_s[:, 0:29, :], sca.rearrange("(b j) n -> b j n", j=30)[:, 1:30, :])
                DMA("sp", b_s[:, 0:1, :], scb.rearrange("(b j) n -> b j n", j=2)[:, 1:2, :])
                tm_out(lambda c: gas[:, c, :], NS, a_s[:, 29, :], lambda c: [hh[("gas", c)]])
                tm_out(lambda c: tails[:, c, 40:56], NS, b_s[:, 1, :], lambda c: [hh[("tails", c)]])
            else:
                CP(halo[:, 0:4, 0:30], ga[:, :, 1024:1054], hga, [hh["halo"]])
                CP(halo[:, 4:8, 0:2], zb[:, :, 1024:1026], hzb, [hh["halo"]])
            pop_slots(nb2a)
            pre_outproj(w_out_conv, 8)
            P.fence()
            A.reset(m2)
            ysb = A.alloc(8 * W, F32).rearrange("p (c n) -> p c n", c=8)
            sqbuf2 = [A.alloc(512, BF16), A.alloc(512, BF16)]
            tmp2 = A.alloc(512, F32)
            rstd2 = A.alloc(512, F32)
            ytmp = [A.alloc(512, F32), A.alloc(512, F32)]
            nb_ = push_slots(2)
            out_proj(w_out_conv, 0, 8,
                     lambda k, tb: (mixT[:, k, TBS[tb][0] - col0:TBS[tb][0] - col0 + TBS[tb][1]], [hh[("uT", k, tb)]]),
                     tbs, ysb, col0)
            for tb in tbs:
                postnorm_residual(0, 1, tb, ysb, col0, sqbuf2, tmp2, rstd2, ytmp)
            pop_slots(nb_)
            if STOP >= 2:
                pre_mlp(0)
            P.fence()
            A.reset(m1)

        RUN0 = STOP >= 1 and not int(os.environ.get('MK_SKIP0', '0'))
        if RUN0:
            pre_l0a()
        P.fence()
        A.reset(m0)
        if RUN0:
            for hi, (col0, tbs) in enumerate(HALVES):
                layer0_half(hi, col0, tbs)
                if STOP >= 2:
                    nxt = pre_l0a if hi == 0 else (pre_l1a if STOP >= 3 else None)
                    mlp_half(0, col0, tbs, pre_next=nxt)

        def load_wcols(wd, row0, Kc, col0, ncols):
            slot, hs = wslot_next()
            assert Kc * ncols <= SLOT
            v = slot[:, 0:Kc * ncols].rearrange("p (k m) -> p k m", k=Kc)
            DMA("pool", v, wd[row0:row0 + Kc * 128, col0:col0 + ncols].rearrange("(k p) m -> p k m", p=128), writes=[hs])
            return v, hs

        def layer1_mixer():
            mL = A.mark()
            QT = A.alloc(10 * NT, BF16).rearrange("p (c n) -> p c n", c=10)
            KT = A.alloc(3 * NT, BF16).rearrange("p (c n) -> p c n", c=3)
            Vd = A.alloc(16 * 5 * 128, BF16).rearrange("p (t k m) -> p t k m", t=16, k=5)
            QTs = A.alloc(10 * NS, F32).rearrange("p (c n) -> p c n", c=10)
            KTs = A.alloc(3 * NS, F32).rearrange("p (c n) -> p c n", c=3)
            VTs = A.alloc(5 * NS, F32).rearrange("p (c n) -> p c n", c=5)
            VsTM = A.alloc(320, F32)
            knew = A.alloc(384, F32)
            smix = A.alloc(6 * NS, BF16).rearrange("p (c n) -> p c n", c=6)
            mP = A.mark()
            uT = A.alloc(8 * HW_, BF16).rearrange("p (c n) -> p c n", c=8)
            sqbuf = [A.alloc(512, BF16), A.alloc(512, BF16)]
            tmp = A.alloc(512, F32)
            rstd = A.alloc(512, F32)
            cst = A.alloc(2 * HW_, F32).rearrange("p (c n) -> p c n", c=2)
            qbs = [A.alloc(512, BF16), A.alloc(512, BF16)]
            t1s = [A.alloc(512, F32), A.alloc(512, F32)]
            t2s = [A.alloc(512, F32), A.alloc(512, F32)]
            rctr = [0]
            K32 = A.alloc(512, F32)
            vst = A.alloc(320, F32)
            kst = A.alloc(128, F32)


            def rope(ps, hp, N, lc=0):
                i = rctr[0] % 2
                rctr[0] += 1
                qb, t1, t2 = qbs[i], t1s[i], t2s[i]
                hq, h1, h2 = hh[("qb", i)], hh[("t1", i)], hh[("t2", i)]
                ACT(qb[:, 0:N], ps[:, 0:N], AF.Copy, [hp], [hq])
                pr, hpr = rrot.next()
                MM(pr[:, 0:N], prot, qb[:, 0:N], True, True, [hq, hconst], [hpr])
                TT(t1[:, 0:N], ps[:, 0:N], cst[:, 0, lc:lc + N], ALU.mult, [hp, hh["cst"]], [h1])
                TT(t2[:, 0:N], pr[:, 0:N], cst[:, 1, lc:lc + N], ALU.mult, [hpr, hh["cst"]], [h2])
                return t1, t2, h1, h2

            def perm_write(dstT, m, rows, kind, tb, hw, R):
                t1, t2, h1, h2 = R
                c0, N = TBS[tb]
                r0, r1 = rows
                if kind == 1:
                    o = dstT[r0:r1, m, c0:c0 + N]
                    a, b = t1[r0:r1, 0:N], t2[r0:r1, 0:N]
                else:
                    o = dstT[r0:r1, m, 0:NTOK].rearrange("p (r i) -> p r i", r=kind)[:, :, c0 // kind:c0 // kind + N // kind]
                    a = t1[r0:r1, 0:N].rearrange("p (i r) -> p r i", r=kind)
                    b = t2[r0:r1, 0:N].rearrange("p (i r) -> p r i", r=kind)
                TT(o, a, b, ALU.add, [h1, h2], [hw])

            QKIND = [((1, 1),)] * 4 + [((1, 4),)] * 4 + [((16, 16),)] * 2
            KKIND = [(1, 1), (1, 4), (16, 16)]

            nbase = len(wslots)
            PASSES = [[0, 1], [2, 3, 4]]
            psrot = Rot([0, 1, 2])
            rrot = Rot([3, 4, 5])
            for tbs_ in PASSES:
                pc0 = TBS[tbs_[0]][0]
                PW = sum(TBS[tb][1] for tb in tbs_)
                for tb in tbs_:
                    norm_block(1, 0, tb, uT, pc0, sqbuf, tmp, rstd, htb="L1")
                DMA("sp", cst[:, :, 0:PW], cs_d[:, :, pc0:pc0 + PW], writes=[hh["cst"]])
                pend = [None]

                def flush():
                    if pend[0] is not None:
                        f_, a_ = pend[0]
                        pend[0] = None
                        f_(*a_)

                def post_q(ps, hp, m, tb):
                    c0, N = TBS[tb]
                    lc = c0 - pc0
                    R = rope(ps, hp, N, lc)
                    if tb == 4:
                        TT(QTs[:, m, :], R[0][:, 0:N], R[1][:, 0:N], ALU.add, [R[2], R[3]], [hh[("QTs", m)]])
                    else:
                        ka, kb = QKIND[m][0]
                        if ka == kb:
                            perm_write(QT, m, (0, 128), ka, tb, hh[("QT", m, tb)], R)
                        else:
                            perm_write(QT, m, (0, 64), ka, tb, hh[("QT", m, tb, 0)], R)
                            perm_write(QT, m, (64, 128), kb, tb, hh[("QT", m, tb, 1)], R)

                def post_k(ps, hp, kc, tb):
                    c0, N = TBS[tb]
                    lc = c0 - pc0
                    R = rope(ps, hp, N, lc)
                    if tb == 4:
                        TT(KTs[:, kc, :], R[0][:, 0:N], R[1][:, 0:N], ALU.add, [R[2], R[3]], [hh[("KTs", kc)]])
                        return
                    ka, kb = KKIND[kc]
                    if ka == kb:
                        perm_write(KT, kc, (0, 128), ka, tb, hh[("KT", kc, tb)], R)
                    else:
                        perm_write(KT, kc, (0, 64), ka, tb, hh[("KT", kc, tb, 0)], R)
                        perm_write(KT, kc, (64, 128), kb, tb, hh[("KT", kc, tb, 1)], R)
                    need = [t for t in range(4) if (M4A & 4) and (kc == 2 or (kc == 1 and 4 * tb + t >= 12) or (kc == 0 and 4 * tb + t == 15))]
                    if need:
                        TT(K32[:, 0:N], R[0][:, 0:N], R[1][:, 0:N], ALU.add, [R[2], R[3]], [hh["K32"]])
                    for t in need:
                        T = 4 * tb + t
                        pt, hpt = rrot.next()
                        TR(pt[:, 0:128], K32[:, t * 128:(t + 1) * 128], ident, [hh["K32"], hconst], [hpt])
                        ACT(kst[:, 0:128], pt[:, 0:128], AF.Copy, [hpt], [hh["kst"]])
                        if kc == 2:
                            DMA("sp", d_p[2][T * 128:(T + 1) * 128, 0:64], kst[:, 0:64], reads=[hh["kst"]], writes=[hh[("d2pk", T)]])
                        elif kc == 1:
                            DMA("sp", d_p[1][(T - 12) * 128:(T - 11) * 128, 0:64], kst[:, 64:128], reads=[hh["kst"]])
                            if T == 15:
                                DMA("sp", d_p[0][:, 0:64], kst[:, 0:64], reads=[hh["kst"]])
                        else:
                            DMA("sp", swa_p[:, 0:128], kst[:, 0:128], reads=[hh["kst"]])

                def main_mm(v, i, hs, tb):
                    c0, N = TBS[tb]
                    lc = c0 - pc0
                    ps, hp = psrot.next()
                    for k in range(8):
                        MM(ps[:, 0:N], v[:, i, k, :], uT[:, k, lc:lc + N], k == 0, k == 7, [hs, hh[("uT", k, "L1")]], [hp])
                    return ps, hp

                for mg in range(3):
                    ms = list(range(mg * 4, min(10, mg * 4 + 4)))
                    v, hs = load_wchunks(wq_d, 0, 8, [m * 128 for m in ms])
                    for i, m in enumerate(ms):
                        for tb in tbs_:
                            ps, hp = main_mm(v, i, hs, tb)
                            flush()
                            pend[0] = (post_q, (ps, hp, m, tb))
                v, hs = load_wchunks(wk_d, 0, 8, [0, 128, 256])
                for kc in range(3):
                    for tb in tbs_:
                        ps, hp = main_mm(v, kc, hs, tb)
                        flush()
                        pend[0] = (post_k, (ps, hp, kc, tb))
                flush()
                vw, hvw = load_wcols(wv_d, 0, 8, 0, 320)
                for tb in tbs_:
                    c0, N = TBS[tb]
                    lc = c0 - pc0
                    if tb == 4:
                        ps, hp = mmrot.next()
                        for k in range(8):
                            MM(ps[0:NS, 0:320], uT[:, k, lc:lc + NS], vw[:, k, :], k == 0, k == 7, [hvw, hh[("uT", k, "L1")]], [hp])
                        ACT(VsTM[0:NS, :], ps[0:NS, 0:320], AF.Copy, [hp], [hh["VsTM"]])
                        vd_, hvd = load_wchunks(wvd_d, 0, 8, [0, 128, 256, 384])
                        vd2, hvd2 = load_wchunks(wvd_d, 0, 8, [512])
                        for i in range(5):
                            vv, hv_, ii = (vd_, hvd, i) if i < 4 else (vd2, hvd2, 0)
                            ps, hp = mmrot.next()
                            for k in range(8):
                                MM(ps[:, 0:NS], vv[:, ii, k, :], uT[:, k, lc:lc + NS], k == 0, k == 7, [hv_, hh[("uT", k, "L1")]], [hp])
                            ACT(VTs[:, i, :], ps[:, 0:NS], AF.Copy, [hp], [hh[("VTs", i)]])
                        ps, hp = mmrot.next()
                        for kc in range(3):
                            TR(ps[0:NS, kc * 128:(kc + 1) * 128], KTs[:, kc, :], ident, [hh[("KTs", kc)], hconst], [hp])
                        ACT(knew[0:NS, :], ps[0:NS, 0:384], AF.Copy, [hp], [hh["knew"]])
                        DMA("sp", swa_s[:, 127, 0:128], knew[0:NS, 0:128], reads=[hh["knew"]])
                        DMA("sp", swa_s[:, 127, 128:256], VsTM[0:NS, 0:128], reads=[hh["VsTM"]])
                        for g, Wg in enumerate((128, 512, 2048)):
                            DMA("sp", d_s[g][:, Wg - 1, 0:64], knew[0:NS, 128 + 64 * g:192 + 64 * g], reads=[hh["knew"]])
                            DMA("sp", d_s[g][:, Wg - 1, 64:128], VsTM[0:NS, 128 + 64 * g:192 + 64 * g], reads=[hh["VsTM"]])
                        continue
                    for t in range(4):
                        T = 4 * tb + t
                        ps, hp = mmrot.next()
                        for k in range(8):
                            MM(ps[:, 0:320], uT[:, k, lc + t * 128:lc + (t + 1) * 128], vw[:, k, :], k == 0, k == 7, [hvw, hh[("uT", k, "L1")]], [hp])
                        src3 = ps[:, 0:192].rearrange("p (k m) -> p k m", k=3)
                        ACT(Vd[:, T, 0:3, 0:64], src3, AF.Copy, [hp], [hh[("Vd", T, 0)]])
                        CP(Vd[:, T, 0:3, 64:128], src3, [hp], [hh[("Vd", T, 1)]])
                        ACT(vst[:, 0:320], ps[:, 0:320], AF.Copy, [hp], [hh["vst"]])
                        DMA("sp", d_p[2][T * 128:(T + 1) * 128, 64:128], vst[:, 256:320], reads=[hh["vst"]], writes=[hh[("d2pv", T)]])
                        if T >= 12:
                            DMA("sp", d_p[1][(T - 12) * 128:(T - 11) * 128, 64:128], vst[:, 192:256], reads=[hh["vst"]])
                        if T == 15:
                            DMA("sp", d_p[0][:, 64:128], vst[:, 128:192], reads=[hh["vst"]])
                            DMA("sp", swa_p[:, 128:256], vst[:, 0:128], reads=[hh["vst"]])
                    for r in range(4):
                        ps, hp = mmrot.next()
                        for k in range(8):
                            MM(ps[:, 0:64], uT[:, k, lc + r:lc + 512:4], vw[:, k, 192:256], k == 0, k == 7, [hvw, hh[("uT", k, "L1")]], [hp])
                        ACT(Vd[:, 4 * r + tb, 3, 0:64], ps[:, 0:64], AF.Copy, [hp], [hh[("Vd3", r, tb, 0)]])
                        CP(Vd[:, 4 * r + tb, 3, 64:128], ps[:, 0:64], [hp], [hh[("Vd3", r, tb, 1)]])
            del wslots[nbase:]
            NSL[0] = len(wslots)
            if L1STOP > 3:
                pre_outproj(w_out_attn, 6)
            src = d_p[2].rearrange("(i r) n -> i r n", r=16)[:, :, 64:128]
            rds = [hh[("d2pv", T)] for T in range(16)]
            if not int(os.environ.get("MK_NORB", "0")):
                for r in range(16):
                    DMA("pool", Vd[:, r, 4, 0:64], src[:, r, :], reads=rds, writes=[hh[("Vd4a", r)]])
                    DMA("pool", Vd[:, r, 4, 64:128], src[:, r, :], reads=rds, writes=[hh[("Vd4b", r)]])
            P.fence()
            if L1STOP <= 1:
                A.reset(mL)
                return

            if not NOCOPY:
                DMA("act", swa_s[:, 0:127, :], c_swa[:, 1:128, :])
                for g, Wg in enumerate((128, 512, 2048)):
                    DMA("act", d_s[g][:, 0:Wg - 1, :], c_d[g][:, 1:Wg, :])
            A.reset(mP)
            mixT = A.alloc(6 * NT, BF16).rearrange("p (c n) -> p c n", c=6)
            PT = [A.alloc(256, BF16), A.alloc(256, BF16)]
            accN = A.alloc(NTOK, F32)
            accD = A.alloc(NTOK, F32)
            rec = A.alloc(512, F32)
            PT = PT + [A.alloc(256, BF16), A.alloc(256, BF16)]
            srot = Rot([0, 1, 6, 7])
            orot = Rot([2, 4])
            items = []

            def add_head(qc, base, kc, kv, kind, evac):
                cfg = dict(qc=qc, base=base, kc=kc, kv=kv, kind=kind, evac=evac)
                for T in range(16):
                    items.append((cfg, T))

            for h in range(8):
                kv, g = h // 4, h % 4
                mrows = slice((h % 2) * 64, (h % 2) * 64 + 64)
                mch = h // 2

                def evac(b, po, hpo, pd, hpd, h=h, mrows=mrows, mch=mch):
                    TS(rec[mrows, :], pd[mrows, :], smalls[mrows, 16 + h:17 + h], ALU.add, [hpd, hh["expsink"]], [hh["rec"]])
                    RECIP(rec[mrows, :], rec[mrows, :], [hh["rec"]], [hh["rec"]])
                    TT(mixT[mrows, mch, b * 512:(b + 1) * 512], po[mrows, :], rec[mrows, :], ALU.mult, [hpo, hh["rec"]], [hh[("mixT", mch, h % 2, b)]])
                add_head(g, kv * 64, 0, kv, 1, evac)
            for s_ in range(4):
                mrows = slice((s_ % 2) * 64, (s_ % 2) * 64 + 64)
                mch = 4 + s_ // 2
                for grp in range(3):
                    kind = (1, 4, 16)[grp]

                    def evac(b, po, hpo, pd, hpd, grp=grp, kind=kind, mrows=mrows, mch=mch, s_=s_):
                        if kind == 1:
                            on = accN[mrows, b * 512:(b + 1) * 512]
                            od = accD[mrows, b * 512:(b + 1) * 512]
                            sn, sd = po[mrows, :], pd[mrows, :]
                        elif kind == 4:
                            on = accN[mrows, b:NTOK:4]
                            od = accD[mrows, b:NTOK:4]
                            sn, sd = po[mrows, :], pd[mrows, :]
                        else:
                            on = accN[mrows, :].rearrange("p (i r) -> p i r", r=16)[:, :, 4 * b:4 * b + 4]
                            od = accD[mrows, :].rearrange("p (i r) -> p i r", r=16)[:, :, 4 * b:4 * b + 4]
                            sn = po[mrows, :].rearrange("p (t i) -> p i t", t=4)
                            sd = pd[mrows, :].rearrange("p (t i) -> p i t", t=4)
                        if grp == 0:
                            ACT(on, sn, AF.Copy, [hpo], [hh["accN"]])
                            CP(od, sd, [hpd], [hh["accD"]])
                        else:
                            TT(on, on, sn, ALU.add, [hpo, hh["accN"]], [hh["accN"]])
                            TT(od, od, sd, ALU.add, [hpd, hh["accD"]], [hh["accD"]])
                        if grp == 2 and b == 3:
                            for bb in range(4):
                                RECIP(rec[mrows, :], accD[mrows, bb * 512:(bb + 1) * 512], [hh["accD"]], [hh["rec"]])
                                TT(mixT[mrows, mch, bb * 512:(bb + 1) * 512], accN[mrows, bb * 512:(bb + 1) * 512], rec[mrows, :], ALU.mult,
                                   [hh["accN"], hh["rec"]], [hh[("mixT", mch, s_ % 2, bb)]])
                    if grp == 0:
                        add_head(4 + s_, 0, 1, 2, 1, evac)
                    elif grp == 1:
                        add_head(4 + s_, 64, 1, 3, 4, evac)
                    else:
                        add_head(8 + s_ // 2, (s_ % 2) * 64, 2, 4, 16, evac)

            state = {}

            def issue_S(i):
                cfg, T = items[i]
                kind, base = cfg["kind"], cfg["base"]
                rows = slice(base, base + 64)
                j = T if kind == 1 else (T % 4 if kind == 4 else 0)
                pss, hs_ = srot.next()
                lo = 0 if j > 0 else 128
                qsl = QT[rows, cfg["qc"], T * 128:(T + 1) * 128]
                if j > 0:
                    MM(pss[:, 0:128], KT[rows, cfg["kc"], (T - 1) * 128:T * 128], qsl, True, True, [], [hs_])
                MM(pss[:, 128:256], KT[rows, cfg["kc"], T * 128:(T + 1) * 128], qsl, True, True, [], [hs_])
                pt = PT[i % 4]
                hpt = hh[("PT", i % 4)]
                ACT(pt[:, lo:256], pss[:, lo:256], AF.Exp, [hs_], [hpt], scale=0.125)
                TT(pt[:, lo:256], pt[:, lo:256], maskpd[:, lo:256], ALU.mult, [hpt, hconst], [hpt], eng="pool")
                state[i] = (pt, hpt, j)

            def issue_PV(i):
                cfg, T = items[i]
                pt, hpt, j = state.pop(i)
                kv = cfg["kv"]
                if T % 4 == 0:
                    po, hpo = orot.next()
                    ob = orot.banks[(orot.i - 1) % 2]
                    cfg["o"] = (po, hpo, psb[ob + 1], hps[ob + 1])
                po, hpo, pd, hpd = cfg["o"]
                osl = slice((T % 4) * 128, (T % 4 + 1) * 128)
                if j > 0:
                    MM(po[:, osl], Vd[:, T - 1, kv, :], pt[:, 0:128], True, False, [hpt], [hpo])
                    MM(po[:, osl], Vd[:, T, kv, :], pt[:, 128:256], False, True, [hpt], [hpo])
                    MM(pd[:, osl], onesb, pt[:, 0:128], True, False, [hpt, hconst], [hpd])
                    MM(pd[:, osl], onesb, pt[:, 128:256], False, True, [hpt, hconst], [hpd])
                else:
                    MM(po[:, osl], Vd[:, T, kv, :], pt[:, 128:256], True, True, [hpt], [hpo])
                    MM(pd[:, osl], onesb, pt[:, 128:256], True, True, [hpt, hconst], [hpd])
                if T % 4 == 3:
                    cfg["evac"](T // 4, po, hpo, pd, hpd)

            LOOK = 2
            for i in range(len(items) + LOOK):
                if i < len(items):
                    issue_S(i)
                if i >= LOOK:
                    issue_PV(i - LOOK)
            P.fence()
            if L1STOP <= 2:
                A.reset(mL)
                return

            A.reset(mL)
            _skip = A.alloc(1, F32)
            A.reset(mL)
            Kc = A.alloc(NS * 64, F32).rearrange("p (b d) -> p b d", b=NS)
            Vc = A.alloc(NS * 128, F32).rearrange("p (b d) -> p b d", b=NS)
            Qs = A.alloc(256, F32)
            Qd = A.alloc(NS * 256, F32).rearrange("p (b c) -> p b c", b=NS)
            prod = A.alloc(512, F32)
            Sall = A.alloc(64, F32)
            Pn = A.alloc(64, F32)
            Pnew = A.alloc(64, F32)
            den = A.alloc(64, F32)
            num = A.alloc(64, F32)
            numD = A.alloc(64, F32)
            denD = A.alloc(64, F32)
            prT = A.alloc(NS, F32)
            outv = A.alloc(64, F32)
            assert A.off <= mP - 0 or True

            def v3(ap):
                return ap.rearrange("p (b s) -> p b s", s=4)

            groups = []
            for kv in range(2):
                groups.append(dict(heads=[(g, kv * 64) for g in range(4)], kc=0, kbase=kv * 64, vi=kv, cache=c_swa, kcol=kv * 64, vcol=128 + kv * 64,
                                   step=1, swa=kv))
            groups.append(dict(heads=[(4 + s, 0) for s in range(4)], kc=1, kbase=0, vi=2, cache=c_d[0], kcol=0, vcol=64, step=1, swa=None))
            groups.append(dict(heads=[(4 + s, 64) for s in range(4)], kc=1, kbase=64, vi=3, cache=c_d[1], kcol=0, vcol=64, step=4, swa=None))
            groups.append(dict(heads=[(8 + s // 2, (s % 2) * 64) for s in range(4)], kc=2, kbase=None, vi=4, cache=c_d[2], kcol=0, vcol=64, step=16, swa=None))
            for gi, G in enumerate(groups):
                cview = G["cache"].rearrange("b j n -> j b n")
                st_ = G["step"]
                for q4 in range(4):
                    bs = slice(4 * q4, 4 * q4 + 4)
                    DMA("sp", Kc[:, bs, :], cview[0:128 * st_:st_, bs, G["kcol"]:G["kcol"] + 64], writes=[hh[("Kc", q4)]])
                    DMA("sp", Vc[:, bs, 0:64], cview[0:128 * st_:st_, bs, G["vcol"]:G["vcol"] + 64], writes=[hh[("Vc", q4, 0)]])
                    DMA("sp", Vc[:, bs, 64:128], cview[0:128 * st_:st_, bs, G["vcol"]:G["vcol"] + 64], writes=[hh[("Vc", q4, 1)]])
                pq = [mmrot.next(), mmrot.next()]
                for s, (qc, qb_) in enumerate(G["heads"]):
                    ps, hp = pq[qb_ // 64]
                    TR(ps[0:NS, s * 64:(s + 1) * 64], QTs[qb_:qb_ + 64, qc, :], ident[qb_:qb_ + 64, qb_:qb_ + 64], [hh[("QTs", qc)], hconst], [hp])
                for s, (qc, qb_) in enumerate(G["heads"]):
                    ps, hp = pq[qb_ // 64]
                    ACT(Qs[0:NS, s * 64:(s + 1) * 64], ps[0:NS, s * 64:(s + 1) * 64], AF.Copy, [hp], [hh["Qs"]])
                TT(Qd[0:NS, :, :], bcast_ap(Qs[0:NS, 0:1], [(0, NS), (1, 256)]), bcast_ap(ident[0:NS, 0:1], [(1, NS), (0, 256)]), ALU.mult,
                   [hh["Qs"], hconst], [hh["Qd"]])
                for bp in range(NS // 2):
                    ps, hp = mmrot.next()
                    MM(ps[:, 0:512], ones32[0:NS, :], Qd[0:NS, 2 * bp:2 * bp + 2, :], True, True, [hh["Qd"], hconst], [hp])
                    TT(prod[:, :].rearrange("p (b s d) -> p b s d", b=2, s=4), ps[:, 0:512].rearrange("p (b s d) -> p b s d", b=2, s=4),
                       bcast_ap(Kc[:, 2 * bp, 0:1], [(64, 2), (0, 4), (1, 64)]), ALU.mult, [hp] + [hh[("Kc", q4)] for q4 in range(4)], [hh["prod"]])
                    o = Sall[:, 8 * bp:8 * bp + 8].rearrange("p (b s) -> p b s", b=2)
                    RED(o, prod[:, :].rearrange("p (b s d) -> p b s d", b=2, s=4), [hh["prod"]], [hh["Sall"]])
                psn, hpn = mmrot.next()
                for s, (qc, qb_) in enumerate(G["heads"]):
                    TT(prT[:, :], QTs[:, qc, :], KTs[:, G["kc"], :], ALU.mult, [hh[("QTs", qc)], hh[("KTs", G["kc"])]], [hh["prT"]])
                    MM(psn[:, s * NS:(s + 1) * NS], selh[qb_ // 64], prT[:, :], True, True, [hh["prT"], hconst], [hpn])
                ACT(Pn[:, :], Sall[:, :], AF.Exp, [hh["Sall"]], [hh["Pn"]], scale=0.125)
                ACT(Pnew[:, :].rearrange("p (b s) -> p s b", s=4), psn[:, 0:64].rearrange("p (s b) -> p s b", s=4), AF.Exp, [hpn], [hh["Pnew"]], scale=0.125)
                psd, hpd = mmrot.next()
                MM(psd[:, 0:64], ones32, Pn[:, :], True, True, [hh["Pn"], hconst], [hpd])
                TT(den[:, :], psd[:, 0:64], Pnew[:, :], ALU.add, [hpd, hh["Pnew"]], [hh["den"]])
                if G["swa"] is not None:
                    kv = G["swa"]
                    TT(v3(den[:, :]), v3(den[:, :]), bcast_ap(smalls[:, 16 + 4 * kv:17 + 4 * kv], [(0, NS), (1, 4)]), ALU.add,
                       [hh["den"], hh["expsink"]], [hh["den"]])
                psv, hpv = mmrot.next()
                for b in range(NS):
                    MM(psv[:, 4 * b:4 * b + 4], Vc[:, b, :], Pn[:, 4 * b:4 * b + 4], True, True, [hh[("Vc", q4, i)] for q4 in range(4) for i in range(2)] + [hh["Pn"]], [hpv])
                TT(v3(num[:, :]), v3(Pnew[:, :]), bcast_ap(VTs[:, G["vi"], 0:1], [(1, NS), (0, 4)]), ALU.mult, [hh["Pnew"], hh[("VTs", G["vi"])]], [hh["num"]])
                TT(num[:, :], num[:, :], psv[:, 0:64], ALU.add, [hh["num"], hpv], [hh["num"]])
                if G["swa"] is not None:
                    kv = G["swa"]
                    RECIP(den[:, :], den[:, :], [hh["den"]], [hh["den"]])
                    TT(outv[:, :], num[:, :], den[:, :], ALU.mult, [hh["num"], hh["den"]], [hh["outv"]])
                    for s in range(4):
                        h = kv * 4 + s
                        rows = slice((h % 2) * 64, (h % 2) * 64 + 64)
                        CP(smix[rows, h // 2, :], outv[rows, s:64:4], [hh["outv"]], [hh[("smix", h // 2, h % 2)]])
                else:
                    if gi == 2:
                        CP(numD[:, :], num[:, :], [hh["num"]], [hh["numD"]])
                        CP(denD[:, :], den[:, :], [hh["den"]], [hh["denD"]])
                    else:
                        TT(numD[:, :], numD[:, :], num[:, :], ALU.add, [hh["num"], hh["numD"]], [hh["numD"]])
                        TT(denD[:, :], denD[:, :], den[:, :], ALU.add, [hh["den"], hh["denD"]], [hh["denD"]])
            RECIP(denD[:, :], denD[:, :], [hh["denD"]], [hh["denD"]])
            TT(outv[:, :], numD[:, :], denD[:, :], ALU.mult, [hh["numD"], hh["denD"]], [hh["outv"]])
            for s in range(4):
                rows = slice((s % 2) * 64, (s % 2) * 64 + 64)
                CP(smix[rows, 4 + s // 2, :], outv[rows, s:64:4], [hh["outv"]], [hh[("smix", 4 + s // 2, s % 2)]])
            P.fence()
            if L1STOP <= 3:
                A.reset(mL)
                return

            A.reset(mL)
            W = HW_
            ysb = A.alloc(8 * W, F32).rearrange("p (c n) -> p c n", c=8)
            sqbuf2 = [A.alloc(512, BF16), A.alloc(512, BF16)]
            tmp2 = A.alloc(512, F32)
            rstd2 = A.alloc(512, F32)
            ytmp = [A.alloc(512, F32), A.alloc(512, F32)]
            nb_ = push_slots(2)
            assert A.off <= mP

            def rhs_fn(k, tb):
                if tb == 4:
                    return smix[:, k, :], []
                return mixT[:, k, TBS[tb][0]:TBS[tb][0] + 512], []
            for hi_, (col0, tbs) in enumerate(HALVES):
                out_proj(w_out_attn, 0, 6, rhs_fn, tbs, ysb, col0)
                for tb in tbs:
                    postnorm_residual(1, 1, tb, ysb, col0, sqbuf2, tmp2, rstd2, ytmp)
                if hi_ == 1:
                    pop_slots(nb_)
                    if STOP >= 4:
                        pre_mlp(1)
                P.fence()
            A.reset(mL)

        if STOP >= 3:
            layer1_mixer()
        if STOP >= 4:
            for hi_, (col0, tbs) in enumerate(HALVES):
                mlp_half(1, col0, tbs, pre_next=(lambda: pre_mlp(1)) if hi_ == 0 else None)

        A.reset(m0)
        yo = [A.alloc(1024, F32) for _ in range(4)]
        rot = Rot([0, 1, 2, 3, 4, 5])
        for t in range(17):
            y = yo[t % 4]
            hy = hh[("yo", t % 4)]
            rows = 128 if t < 16 else NS
            tbk = t // 4 if t < 16 else 4
            for g in range(2):
                ps, hp = rot.next()
                for i in range(4):
                    c = g * 4 + i
                    TR(ps[0:rows, i * 128:(i + 1) * 128], hT[:, c, t * 128:t * 128 + rows], ident, [hh[("hT", c, tbk)], hconst], [hp])
                if g == 0:
                    ACT(y[0:rows, 0:512], ps[0:rows, :], AF.Copy, [hp], [hy])
                else:
                    CP(y[0:rows, 512:1024], ps[0:rows, :], [hp], [hy])
            dst = y_p[t * 128:(t + 1) * 128, :] if t < 16 else y_s
            DMA("sp", dst, y[0:rows, :], reads=[hy])

        assert not PRE, list(PRE.keys())
        P.emit(st)
        build_program.stats = dict(P.stats)
        build_program.arena_peak = A.peak
    return nc


def _consts():
    c32 = np.zeros((128, 512), np.float32)
    c32[:, 0:128] = np.eye(128, dtype=np.float32)
    c32[0:64, 128:256] = 1.0
    c32[64:128, 256:384] = 1.0
    c32[:, 384:512] = 1.0
    cb = np.zeros((128, 768), np.float32)
    prot = np.zeros((128, 128), np.float32)
    for base in (0, 64):
        for i in range(8):
            prot[base + i + 8, base + i] = 1.0
            prot[base + i, base + i + 8] = 1.0
    cb[:, 0:128] = prot
    cb[:, 128:256] = 1.0
    p = np.arange(128)[:, None]
    f = np.arange(128)[None, :]
    cb[:, 256:384] = (f <= p)
    cb[:, 384:512] = (f >= p)
    cb[:, 512:640] = np.eye(128, dtype=np.float32)
    half = 8
    inv = (np.float32(500000.0) ** (-np.arange(half, dtype=np.float32) / half)).astype(np.float32)
    pos = np.concatenate([np.arange(NTOK, dtype=np.float32), np.full(NS, 8192.0, np.float32)])
    ang = (pos[None, :] * inv[:, None]).astype(np.float32)
    cs = np.zeros((128, 2, NT), np.float32)
    cs[:, 0, :] = 1.0
    for base in (0, 64):
        cs[base:base + 8, 0] = np.cos(ang)
        cs[base + 8:base + 16, 0] = np.cos(ang)
        cs[base:base + 8, 1] = -np.sin(ang)
        cs[base + 8:base + 16, 1] = np.sin(ang)
    return c32, cb, cs


_NC_CACHE = {}


def kernel(x_prompt, x_sample, state_conv_a, state_conv_b, cache_swa_kv, cache_dil0_kv, cache_dil1_kv,
           cache_dil2_kv, norm_g, w_in_conv, conv_a_w, conv_a_b, conv_a_ln_g, conv_a_ln_b, conv_b_w,
           w_out_conv, w_in_attn, attn_sinks, w_out_attn, mlp_w1, mlp_w2):
    f = lambda a: np.ascontiguousarray(np.asarray(a, dtype=np.float32))
    x_prompt, x_sample = f(x_prompt), f(x_sample)
    if "nc" not in _NC_CACHE:
        _NC_CACHE["nc"] = build_program()
    nc = _NC_CACHE["nc"]
    c32, cb, cs = _consts()
    prm1 = np.concatenate([f(norm_g).reshape(64, 128), f(conv_a_b).reshape(4, 128), f(conv_a_ln_g).reshape(4, 128),
                           f(conv_a_ln_b).reshape(4, 128), f(conv_b_w).reshape(12, 128)], 0)
    prm2 = f(conv_a_w).reshape(124, 128)
    wi = f(w_in_attn)[0]
    qs = lambda h: wi[:, h * 64:(h + 1) * 64]
    qd = lambda g, h: wi[:, 768 + 384 * g + h * 64: 768 + 384 * g + (h + 1) * 64]
    kd = lambda g: wi[:, 768 + 384 * g + 256: 768 + 384 * g + 320]
    vd = lambda g: wi[:, 768 + 384 * g + 320: 768 + 384 * g + 384]
    ks = lambda kv: wi[:, 512 + kv * 64: 512 + (kv + 1) * 64]
    vs = lambda kv: wi[:, 640 + kv * 64: 640 + (kv + 1) * 64]
    wq = np.concatenate([np.concatenate([qs(g), qs(4 + g)], 1) for g in range(4)] +
                        [np.concatenate([qd(0, g), qd(1, g)], 1) for g in range(4)] +
                        [np.concatenate([qd(2, 0), qd(2, 1)], 1), np.concatenate([qd(2, 2), qd(2, 3)], 1)], 1)
    wk = np.concatenate([ks(0), ks(1), kd(0), kd(1), kd(2), kd(2)], 1)
    wv = np.concatenate([vs(0), vs(1), vd(0), vd(1), vd(2)], 1)
    wvd = np.concatenate([vs(0), vs(0), vs(1), vs(1), vd(0), vd(0), vd(1), vd(1), vd(2), vd(2)], 1)
    shared = {
        "prm1": f(prm1), "prm2": prm2, "sinks": f(attn_sinks).reshape(1, 8), "cst32": c32, "cstb": cb, "cs": cs,
        "w_in_conv": f(w_in_conv)[0], "w_out_conv": f(w_out_conv)[0], "wq": f(wq), "wk": f(wk), "wv": f(wv), "wvd": f(wvd),
        "w_out_attn": f(w_out_attn)[0], "w1": f(mlp_w1), "w2": f(mlp_w2),
    }
    sca = f(state_conv_a)[0]
    scb = f(state_conv_b)[0]
    cswa = f(cache_swa_kv)[0]
    cds = [f(cache_dil0_kv)[0], f(cache_dil1_kv)[0], f(cache_dil2_kv)[0]]
    in_maps = []
    for i in range(8):
        s = slice(NS * i, NS * (i + 1))
        m = dict(shared)
        m["xp"] = x_prompt[i]
        m["xs"] = x_sample[s, 0, :]
        m["sca"] = sca[s].reshape(NS * 30, 512)
        m["scb"] = scb[s].reshape(NS * 2, 512)
        m["c_swa"] = cswa[s].reshape(NS, 128, 256)
        for g in range(3):
            m["c_d%d" % g] = cds[g][s].reshape(NS, cds[g].shape[1], 128)
        in_maps.append(m)
    res = run_bass_kernel_spmd(nc, in_maps, core_ids=list(range(8)))
    R = res.results
    cat = lambda k: np.stack([np.asarray(r[k]) for r in R], 0)
    y_prompt = cat("y_p")
    y_sample = np.concatenate([np.asarray(r["y_s"]) for r in R], 0).reshape(128, 1, 1024)
    a_p = cat("a_p")[None]
    a_s = np.concatenate([np.asarray(r["a_s"]) for r in R], 0)[None]
    b_p = cat("b_p")[None]
    b_s = np.concatenate([np.asarray(r["b_s"]) for r in R], 0)[None]
    swa_p = cat("swa_p").reshape(1, 8, 128, 2, 2, 64)
    swa_s = np.concatenate([np.asarray(r["swa_s"]) for r in R], 0).reshape(1, 128, 128, 2, 2, 64)
    outs = [y_prompt, y_sample, a_p, a_s, b_p, b_s, swa_p, swa_s]
    for g, Wg in enumerate((128, 512, 2048)):
        outs.append(cat("d%d_p" % g).reshape(1, 8, Wg, 2, 1, 64))
        outs.append(np.concatenate([np.asarray(r["d%d_s" % g]) for r in R], 0).reshape(1, 128, Wg, 2, 1, 64))
    return tuple(np.ascontiguousarray(o, dtype=np.float32) for o in outs)
```

```python
import os
import numpy as np
from contextlib import ExitStack
import concourse.bass as bass
import concourse.mybir as mybir
from concourse.bass_utils import run_bass_kernel_spmd

F32 = mybir.dt.float32
BF16 = mybir.dt.bfloat16
ALU = mybir.AluOpType
AF = mybir.ActivationFunctionType
AX = mybir.AxisListType

ENGS = ("pe", "act", "dve", "pool", "sp")

NTOK = 2048
NS = 16
NT = NTOK + NS
TBS = [(0, 512), (512, 512), (1024, 512), (1536, 512), (2048, 16)]
HALVES = [(0, [0, 1]), (1024, [2, 3, 4])]
HW_ = 1040
EPS = 1e-6
STOP = int(os.environ.get("MK_STOP", "99"))
SAFE_WAR = int(os.environ.get("MK_SAFE_WAR", "1"))
M4A = int(os.environ.get("MK_4A", "63"))
POOL_ADD = int(os.environ.get("MK_POOL_ADD", "0"))
L1STOP = int(os.environ.get("MK_L1STOP", "99"))
NOCOPY = int(os.environ.get("MK_NOCOPY", "0"))


class H:
    __slots__ = ("name", "w", "r", "excl")

    def __init__(self, name="", excl=False):
        self.name = name
        self.w = None
        self.r = []
        self.excl = excl


class HD(dict):
    def __missing__(self, k):
        v = H(str(k))
        self[k] = v
        return v


class Op:
    __slots__ = ("eng", "fn", "deps", "signal", "tick", "dma", "sem", "val", "fenced")


class Prog:
    def __init__(self, nc, ndma=8):
        self.nc = nc
        self.ops = {e: [] for e in ENGS}
        self.all = []
        self.ndma = ndma

    def op(self, eng, fn, reads=(), writes=(), dma=False):
        o = Op()
        o.eng, o.fn, o.dma, o.signal, o.tick, o.sem, o.val, o.fenced = eng, fn, dma, dma, 0, None, 0, False
        deps = []
        if any(h.excl for h in reads):
            writes = list(writes) + [h for h in reads if h.excl and h not in writes]
            reads = [h for h in reads if not h.excl]

        def add(p, kind):
            if p is None or p is o:
                return
            if p.eng == eng and not p.dma and not dma:
                if eng == "pe" or (kind == "WAR" and not SAFE_WAR):
                    return
            if p not in deps:
                deps.append(p)

        for h in reads:
            add(h.w, "RAW")
        for h in writes:
            add(h.w, "WAW")
            for r in h.r:
                add(r, "WAR")
        for h in reads:
            h.r.append(o)
        for h in writes:
            h.w = o
            h.r = []
        o.deps = deps
        self.ops[eng].append(o)
        self.all.append(o)
        return o

    def fence(self):
        lasts = [self.ops[e][-1] for e in ENGS if self.ops[e] and self.ops[e][-1].fn is not None]
        dmas = [o for o in self.all if o.dma and not o.fenced]
        for o in dmas:
            o.fenced = True
        for e in ENGS:
            o = Op()
            o.eng, o.fn, o.dma, o.signal, o.tick, o.sem, o.val, o.fenced = e, None, False, False, 0, None, 0, True
            o.deps = [p for p in lasts if p.eng != e or p.dma] + [d for d in dmas if d not in lasts]
            self.ops[e].append(o)
            self.all.append(o)

    def emit(self, stack):
        nc = self.nc
        for o in self.all:
            for d in o.deps:
                d.signal = True
        esem = {e: stack.enter_context(nc.semaphore("es_" + e)) for e in ENGS}
        dsem = {e: [stack.enter_context(nc.semaphore("ds_%s_%d" % (e, i))) for i in range(self.ndma)]
                for e in ENGS if any(o.dma for o in self.ops[e])}
        for e in ENGS:
            c = 0
            nd = 0
            dmas = []
            for o in self.ops[e]:
                if o.dma:
                    o.sem = dsem[e][nd % self.ndma]
                    o.val = 16 * (nd // self.ndma + 1)
                    if nd >= self.ndma:
                        prev = dmas[nd - self.ndma]
                        if prev not in o.deps:
                            o.deps.append(prev)
                    dmas.append(o)
                    nd += 1
                elif o.signal:
                    c += 1
                    o.tick = c
        block = stack.enter_context(nc.Block())
        prog = self
        self.stats = {}

        def section(e):
            def body(eng):
                known = {}
                nwait = 0
                for o in prog.ops[e]:
                    for d in o.deps:
                        if d.dma:
                            sem, val = d.sem, d.val
                        else:
                            sem, val = esem[d.eng], d.tick
                        key = id(sem)
                        if known.get(key, 0) >= val:
                            continue
                        eng.wait_ge(sem, val)
                        nwait += 1
                        known[key] = val
                    if o.fn is None:
                        continue
                    ins = o.fn(eng)
                    if o.dma:
                        ins.then_inc(o.sem, 16)
                    elif o.signal:
                        ins.then_inc(esem[e], 1)
                last = {}
                for o in prog.ops[e]:
                    if o.dma:
                        last[id(o.sem)] = (o.sem, o.val)
                for sem, val in last.values():
                    if known.get(id(sem), 0) < val:
                        eng.wait_ge(sem, val)
                prog.stats[e] = (len(prog.ops[e]), nwait)
            return body

        block.tensor(section("pe"))
        block.scalar(section("act"))
        block.vector(section("dve"))
        block.gpsimd(section("pool"))
        block.sync(section("sp"))


class Arena:
    def __init__(self, t32, cap_bytes):
        self.t32 = t32
        self.t16 = t32.bitcast(BF16)
        self.cap = cap_bytes
        self.off = 0
        self.peak = 0

    def alloc(self, ncols, dtype):
        esz = 4 if dtype == F32 else 2
        off = (self.off + 31) // 32 * 32
        nb = ncols * esz
        assert off + nb <= self.cap, ("arena overflow", off, nb, self.cap)
        self.off = off + nb
        self.peak = max(self.peak, self.off)
        if dtype == F32:
            return self.t32[:, off // 4: off // 4 + ncols]
        return self.t16[:, off // 2: off // 2 + ncols]

    def mark(self):
        return self.off

    def reset(self, m=0):
        self.off = m


def bcast_ap(ap, dims):
    return bass.AP(ap.tensor, ap.offset, [list(ap.ap[0])] + [[s, n] for s, n in dims])


def build_program():
    nc = bass.Bass("TRN2", target_bir_lowering=False)

    def din(name, shape):
        return nc.dram_tensor(name, list(shape), F32, kind="ExternalInput").ap()

    def dout(name, shape):
        return nc.dram_tensor(name, list(shape), F32, kind="ExternalOutput").ap()

    xp = din("xp", [NTOK, 1024])
    xs = din("xs", [NS, 1024])
    sca = din("sca", [NS * 30, 512])
    scb = din("scb", [NS * 2, 512])
    c_swa = din("c_swa", [NS, 128, 256])
    c_d = [din("c_d0", [NS, 128, 128]), din("c_d1", [NS, 512, 128]), din("c_d2", [NS, 2048, 128])]
    prm1 = din("prm1", [88, 128])
    prm2 = din("prm2", [124, 128])
    sinks = din("sinks", [1, 8])
    cst32_d = din("cst32", [128, 512])
    cstb_d = din("cstb", [128, 768])
    cs_d = din("cs", [128, 2, NT])
    w_in_conv = din("w_in_conv", [1024, 2560])
    w_out_conv = din("w_out_conv", [1024, 1024])
    wq_d = din("wq", [1024, 1280])
    wk_d = din("wk", [1024, 384])
    wv_d = din("wv", [1024, 320])
    wvd_d = din("wvd", [1024, 640])
    w_out_attn = din("w_out_attn", [768, 1024])
    w1_d = din("w1", [2, 1024, 4096])
    w2_d = din("w2", [2, 4096, 1024])

    y_p = dout("y_p", [NTOK, 1024])
    y_s = dout("y_s", [NS, 1024])
    a_p = dout("a_p", [30, 512])
    a_s = dout("a_s", [NS, 30, 512])
    b_p = dout("b_p", [2, 512])
    b_s = dout("b_s", [NS, 2, 512])
    swa_p = dout("swa_p", [128, 256])
    swa_s = dout("swa_s", [NS, 128, 256])
    d_p = [dout("d0_p", [128, 128]), dout("d1_p", [512, 128]), dout("d2_p", [2048, 128])]
    d_s = [dout("d0_s", [NS, 128, 128]), dout("d1_s", [NS, 512, 128]), dout("d2_s", [NS, 2048, 128])]

    st = ExitStack()
    with st:
        P = Prog(nc)
        sb = lambda name, shape, dt: st.enter_context(nc.sbuf_tensor(name, list(shape), dt))
        hT = sb("hT", [128, 8, NT], F32)
        cst32 = sb("cst32s", [128, 512], F32)
        cstb = sb("cstbs", [128, 768], BF16)
        prmT = sb("prmT", [128, 88], F32)
        wAT = sb("wAT", [128, 124], F32)
        smalls = sb("smalls", [128, 32], F32)
        halo = sb("halo", [128, 8, 30], BF16)
        NSLOT = 2
        SLOT = 4096
        wslots = [sb("wslot%d" % i, [128, SLOT], BF16) for i in range(NSLOT)]
        ARENA_BYTES = (nc.sbuf_bytes_remaining // 64) * 64 - 256
        A = Arena(sb("arena", [128, ARENA_BYTES // 4], F32), ARENA_BYTES)
        psb = [st.enter_context(nc.psum_tensor("psb%d" % i, [128, 512], F32)) for i in range(8)]
        hps = [H("ps%d" % i, excl=True) for i in range(8)]

        ident = cst32[:, 0:128]
        selh = [cst32[:, 128:256], cst32[:, 256:384]]
        ones32 = cst32[:, 384:512]
        prot = cstb[:, 0:128]
        onesb = cstb[:, 128:256]
        maskpd = cstb[:, 256:512]
        identb = cstb[:, 512:640]
        epsc = smalls[:, 0:1]

        hh = HD()
        hconst = hh["const"]

        def MM(ps_ap, lhsT, rhs, start, stop, reads, writes):
            P.op("pe", lambda e: e.matmul(ps_ap, lhsT=lhsT, rhs=rhs, start=start, stop=stop), reads=reads, writes=writes)

        def TR(out, in_, idn, reads, writes):
            P.op("pe", lambda e: e.transpose(out=out, in_=in_, identity=idn), reads=reads, writes=writes)

        def ACT(out, in_, func, reads, writes, bias=None, scale=None):
            kw = {}
            if bias is not None:
                kw["bias"] = bias
            if scale is not None:
                kw["scale"] = scale
            P.op("act", lambda e: e.activation(out=out, in_=in_, func=func, **kw), reads=reads, writes=writes)

        def TT(out, in0, in1, op, reads, writes, eng="dve"):
            P.op(eng, lambda e: e.tensor_tensor(out=out, in0=in0, in1=in1, op=op), reads=reads, writes=writes)

        def STT(out, in0, scalar, in1, op0, op1, reads, writes):
            P.op("dve", lambda e: e.scalar_tensor_tensor(out=out, in0=in0, scalar=scalar, in1=in1, op0=op0, op1=op1), reads=reads, writes=writes)

        def TS(out, in0, s1, op0, reads, writes, s2=None, op1=None, eng="dve"):
            if op1 is None:
                P.op(eng, lambda e: e.tensor_scalar(out=out, in0=in0, scalar1=s1, scalar2=None, op0=op0), reads=reads, writes=writes)
            else:
                P.op(eng, lambda e: e.tensor_scalar(out=out, in0=in0, scalar1=s1, scalar2=s2, op0=op0, op1=op1), reads=reads, writes=writes)

        def CP(out, in_, reads, writes, eng="dve"):
            P.op(eng, lambda e: e.tensor_copy(out=out, in_=in_), reads=reads, writes=writes)

        def RED(out, in_, reads, writes):
            P.op("dve", lambda e: e.tensor_reduce(out=out, in_=in_, axis=AX.X, op=ALU.add), reads=reads, writes=writes)

        def RECIP(out, in_, reads, writes):
            P.op("dve", lambda e: e.reciprocal(out=out, in_=in_), reads=reads, writes=writes)

        def MEMSET(ap, val, writes, eng="dve"):
            P.op(eng, lambda e: e.memset(ap, val), writes=writes)

        def DMA(eng, out, in_, reads=(), writes=()):
            P.op(eng, lambda e: e.dma_start(out=out, in_=in_), reads=reads, writes=writes, dma=True)

        class Rot:
            def __init__(self, banks):
                self.banks = banks
                self.i = 0

            def next(self):
                b = self.banks[self.i % len(self.banks)]
                self.i += 1
                return psb[b], hps[b]

        wrot = [0]

        NSL = [NSLOT]

        def wslot_next():
            i = wrot[0] % NSL[0]
            wrot[0] += 1
            return wslots[i], hh[("wslot", i)]

        def gcol(l, w, c):
            j = (l * 4 + w) * 8 + c
            return prmT[:, j:j + 1]

        DMA("sp", cst32[:], cst32_d, writes=[hconst])
        DMA("pool", cstb[:], cstb_d, writes=[hconst])
        MEMSET(smalls[:, 0:8], EPS, [hh["smalls"]])
        DMA("sp", smalls[:, 8:16], bass.AP(sinks.tensor, 0, [[0, 128], [1, 8]]), writes=[hh["sinks"]])
        ACT(smalls[:, 16:24], smalls[:, 8:16], AF.Exp, [hh["sinks"]], [hh["expsink"]])
        m0 = A.mark()
        p1 = A.alloc(128, F32)
        p2 = A.alloc(128, F32)
        DMA("sp", p1[0:88, :], prm1, writes=[hh["p1"]])
        DMA("sp", p2[0:124, :], prm2, writes=[hh["p2"]])
        TR(psb[0][:, 0:88], p1[0:88, :], ident[0:88, 0:88], [hh["p1"], hconst], [hps[0]])
        ACT(prmT[:], psb[0][:, 0:88], AF.Copy, [hps[0]], [hconst])
        TR(psb[1][:, 0:124], p2[0:124, :], ident[0:124, 0:124], [hh["p2"], hconst], [hps[1]])
        ACT(wAT[:], psb[1][:, 0:124], AF.Copy, [hps[1]], [hconst])

        xin = [A.alloc(1024, F32) for _ in range(4)]
        rot = Rot([2, 3, 4, 5, 6, 7])
        for t in range(17):
            xi = xin[t % 4]
            hx = hh[("xin", t % 4)]
            rows = 128 if t < 16 else NS
            src = xp[t * 128:(t + 1) * 128, :] if t < 16 else xs
            DMA("sp", xi[0:rows, :], src, writes=[hx])
            for g in range(2):
                ps, hp = rot.next()
                for i in range(4):
                    c = g * 4 + i
                    TR(ps[:, i * rows:(i + 1) * rows], xi[0:rows, c * 128:(c + 1) * 128], ident[0:rows, 0:rows], [hx, hconst], [hp])
                dst = hT[:, g * 4:(g + 1) * 4, t * 128:t * 128 + rows]
                srcp = ps[:, 0:4 * rows].rearrange("p (i r) -> p i r", i=4)
                wr = [hh[("hT", g * 4 + i, t // 4 if t < 16 else 4)] for i in range(4)]
                if g == 0:
                    ACT(dst, srcp, AF.Copy, [hp], wr)
                else:
                    CP(dst, srcp, [hp], wr)
        PHASE1_FENCE = True

        statrot = Rot([6, 7])
        mmrot = Rot([0, 1, 2, 3, 4, 5])

        def rstd_from_ps(ps_stat, hstat, N, scale, out_rstd, hout, tmp, htmp):
            ACT(tmp[:, 0:N], ps_stat[:, 0:N], AF.Sqrt, [hstat, hh["smalls"]], [htmp], bias=epsc, scale=scale)
            RECIP(out_rstd[:, 0:N], tmp[:, 0:N], [htmp], [hout])

        def sumsq_stat(src_fn, rd_fn, N, sqbuf):
            ps, hp = statrot.next()
            for c in range(8):
                sq = sqbuf[c % 2]
                hsq = hh[("sq", c % 2)]
                ACT(sq[:, 0:N], src_fn(c), AF.Square, rd_fn(c), [hsq])
                MM(ps[:, 0:N], onesb, sq[:, 0:N], c == 0, c == 7, [hsq, hconst], [hp])
            return ps, hp

        def norm_block(l, w, tb, dstT, dst_col0, sqbuf, tmp, rstd, htb=None):
            c0, N = TBS[tb]
            ps, hp = sumsq_stat(lambda c: hT[:, c, c0:c0 + N], lambda c: [hh[("hT", c, tb)]], N, sqbuf)
            rstd_from_ps(ps, hp, N, 1.0 / 1024, rstd, hh["rstd"], tmp, hh["rtmp"])
            for c in range(8):
                STT(dstT[:, c, c0 - dst_col0:c0 - dst_col0 + N], hT[:, c, c0:c0 + N], gcol(l, w, c), rstd[:, 0:N], ALU.mult, ALU.mult,
                    [hh[("hT", c, tb)], hh["rstd"], hconst], [hh[("uT", c, tb if htb is None else htb)]])

        def push_slots(n):
            nb = len(wslots)
            wslots.extend([A.alloc(SLOT, BF16) for _ in range(n)])
            NSL[0] = len(wslots)
            return nb

        def pop_slots(nb):
            del wslots[nb:]
            NSL[0] = len(wslots)

        PRE = {}

        def wkey(wd, row0, Kc, cols):
            return (wd.tensor.name, int(wd.offset), row0, Kc, tuple(cols))

        def preload(wd, row0, Kc, cols):
            assert NSL[0] == NSLOT
            PRE[wkey(wd, row0, Kc, cols)] = load_wchunks(wd, row0, Kc, cols)

        def load_wchunks(wd, row0, Kc, cols):
            k_ = wkey(wd, row0, Kc, cols)
            if k_ in PRE:
                return PRE.pop(k_)
            slot, hs = wslot_next()
            n = len(cols)
            assert n * Kc * 128 <= SLOT
            if n > 1 and all(cols[i + 1] == cols[i] + 128 for i in range(n - 1)):
                vk = slot[:, 0:n * Kc * 128].rearrange("p (k c) -> p k c", k=Kc)
                src = wd[row0:row0 + Kc * 128, cols[0]:cols[0] + n * 128].rearrange("(k p) c -> p k c", p=128)
                DMA("pool", vk, src, writes=[hs])
                v = slot[:, 0:n * Kc * 128].rearrange("p (k i m) -> p i k m", k=Kc, i=n)
                return v, hs
            v = slot[:, 0:n * Kc * 128].rearrange("p (i k m) -> p i k m", i=n, k=Kc)
            for i, col0 in enumerate(cols):
                src = wd[row0:row0 + Kc * 128, col0:col0 + 128].rearrange("(k p) m -> p k m", p=128)
                DMA("pool", v[:, i], src, writes=[hs])
            return v, hs

        def postnorm_residual(l, w, tb, ysb, ycol0, sqbuf, tmp, rstd, ytmp):
            c0, N = TBS[tb]
            lc = c0 - ycol0
            ps, hp = sumsq_stat(lambda c: ysb[:, c, lc:lc + N], lambda c: [hh[("ysb", c, tb)]], N, sqbuf)
            rstd_from_ps(ps, hp, N, 1.0 / 1024, rstd, hh["rstd"], tmp, hh["rtmp"])
            for c in range(8):
                yt = ytmp[c % 2]
                hyt = hh[("ytmp", c % 2)]
                STT(yt[:, 0:N], ysb[:, c, lc:lc + N], gcol(l, w, c), rstd[:, 0:N], ALU.mult, ALU.mult,
                    [hh[("ysb", c, tb)], hh["rstd"], hconst], [hyt])
                TT(hT[:, c, c0:c0 + N], hT[:, c, c0:c0 + N], yt[:, 0:N], ALU.add, [hyt, hh[("hT", c, tb)]], [hh[("hT", c, tb)]],
                   eng=("pool" if POOL_ADD else "dve"))

        def out_proj(wd, row0, Kc, rhs_fn, tbs, ysb, ycol0, accumulate=False):
            for m in range(8):
                if m % 2 == 0:
                    v, hs = load_wchunks(wd, row0, Kc, [m * 128, (m + 1) * 128])
                for tb in tbs:
                    c0, N = TBS[tb]
                    ps, hp = mmrot.next()
                    for k in range(Kc):
                        rap, rh = rhs_fn(k, tb)
                        MM(ps[:, 0:N], v[:, m % 2, k, :], rap, k == 0, k == Kc - 1, [hs] + rh, [hp])
                    dst = ysb[:, m, c0 - ycol0:c0 - ycol0 + N]
                    if not accumulate:
                        ACT(dst, ps[:, 0:N], AF.Copy, [hp], [hh[("ysb", m, tb)]])
                    else:
                        TT(dst, dst, ps[:, 0:N], ALU.add, [hp, hh[("ysb", m, tb)]], [hh[("ysb", m, tb)]])

        def pre_l0a():
            preload(w_in_conv, 0, 8, [0, 128, 256, 384])
            preload(w_in_conv, 0, 8, [512])

        def pre_outproj(wd, Kc):
            preload(wd, 0, Kc, [0, 128])
            preload(wd, 0, Kc, [256, 384])

        def pre_mlp(l):
            preload(w1_d[l], 0, 8, [0, 128, 256, 384])
            preload(w1_d[l], 0, 8, [512, 640, 768, 896])

        def pre_l1a():
            preload(wq_d, 0, 8, [0, 128, 256, 384])
            preload(wq_d, 0, 8, [512, 640, 768, 896])

        def mlp_half(l, col0, tbs, pre_next=None):
            m1 = A.mark()
            W = HW_
            uT = A.alloc(8 * W, BF16).rearrange("p (c n) -> p c n", c=8)
            hid = A.alloc(16 * W, BF16).rearrange("p (c n) -> p c n", c=16)
            ysb = A.alloc(8 * W, F32).rearrange("p (c n) -> p c n", c=8)
            sqbuf = [A.alloc(512, BF16), A.alloc(512, BF16)]
            rl = [A.alloc(512, BF16), A.alloc(512, BF16)]
            tmp = A.alloc(512, F32)
            rstd = A.alloc(512, F32)
            ytmp = [A.alloc(512, F32), A.alloc(512, F32)]
            nb_ = push_slots(2)
            for tb in tbs:
                norm_block(l, 2, tb, uT, col0, sqbuf, tmp, rstd)
            for hf in range(2):
                for mg in range(4):
                    ms = [hf * 16 + mg * 4 + i for i in range(4)]
                    v, hs = load_wchunks(w1_d[l], 0, 8, [m * 128 for m in ms])
                    for i, m in enumerate(ms):
                        for tb in tbs:
                            c0, N = TBS[tb]
                            lc = c0 - col0
                            ps, hp = mmrot.next()
                            for k in range(8):
                                MM(ps[:, 0:N], v[:, i, k, :], uT[:, k, lc:lc + N], k == 0, k == 7, [hs, hh[("uT", k, tb)]], [hp])
                            r = rl[(m + tb) % 2]
                            hr = hh[("rl", (m + tb) % 2)]
                            ACT(r[:, 0:N], ps[:, 0:N], AF.Relu, [hp], [hr])
                            TT(hid[:, m % 16, lc:lc + N], r[:, 0:N], r[:, 0:N], ALU.mult, [hr], [hh[("hid", m % 16, tb)]])
                out_proj(w2_d[l], hf * 2048, 16,
                         lambda k, tb: (hid[:, k, TBS[tb][0] - col0:TBS[tb][0] - col0 + TBS[tb][1]], [hh[("hid", k, tb)]]),
                         tbs, ysb, col0, accumulate=(hf == 1))
            for tb in tbs:
                postnorm_residual(l, 3, tb, ysb, col0, sqbuf, tmp, rstd, ytmp)
            pop_slots(nb_)
            if pre_next is not None:
                pre_next()
            P.fence()
            A.reset(m1)

        def layer0_half(hi, col0, tbs):
            m1 = A.mark()
            W = HW_
            uT = A.alloc(8 * W, BF16).rearrange("p (c n) -> p c n", c=8)
            mixT = uT
            ga = A.alloc(4 * (30 + W), BF16).rearrange("p (c n) -> p c n", c=4)
            zb = A.alloc(4 * (2 + W), BF16).rearrange("p (c n) -> p c n", c=4)
            gb = A.alloc(4 * W, BF16).rearrange("p (c n) -> p c n", c=4)
            m2 = A.mark()
            diag = A.alloc(31 * 128, BF16).rearrange("p (j m) -> p j m", j=31)
            diagB = A.alloc(12 * 128, BF16).rearrange("p (j m) -> p j m", j=12)
            cvo = A.alloc(4 * W, F32).rearrange("p (c n) -> p c n", c=4)
            reg32 = A.alloc(2048, F32)
            xq16 = A.t16[:, 2 * reg32.offset:2 * reg32.offset + 4096]
            xb = xq16[:, 0:2048].rearrange("p (c n) -> p c n", c=4)
            sq4 = xq16[:, 2048:4096].rearrange("p (c n) -> p c n", c=4)
            sqbuf = [A.alloc(512, BF16), A.alloc(512, BF16)]
            tmpa = A.alloc(512, F32)
            tmpg = A.alloc(512, F32)
            tmp = A.alloc(512, F32)
            rstd = A.alloc(512, F32)
            mean = A.alloc(512, F32)
            msq = A.alloc(512, F32)
            t1 = [A.alloc(512, F32), A.alloc(512, F32)]
            tails = A.alloc(4 * 64, F32).rearrange("p (c n) -> p c n", c=4)
            sAT = A.alloc(4 * NS * 30, F32).rearrange("p (c n) -> p c n", c=4)
            sBT = A.alloc(4 * NS * 2, F32).rearrange("p (c n) -> p c n", c=4)
            stg = A.alloc(512, F32)
            gas = A.alloc(4 * NS, F32).rearrange("p (c n) -> p c n", c=4)
            prodA = A.alloc(NS * 30, F32)
            hga = [hh[("ga", c)] for c in range(4)]
            hzb = [hh[("zb", c)] for c in range(4)]
            nb2a = push_slots(1)

            if hi == 0:
                MEMSET(ga[:, :, 0:30], 0.0, hga)
                MEMSET(zb[:, :, 0:2], 0.0, hzb)
            else:
                CP(ga[:, :, 0:30], halo[:, 0:4, 0:30], [hh["halo"]], hga)
                CP(zb[:, :, 0:2], halo[:, 4:8, 0:2], [hh["halo"]], hzb)

            for tb in tbs:
                norm_block(0, 0, tb, uT, col0, sqbuf, tmp, rstd)

            for c in range(4):
                mcols = [c * 640 + i * 128 for i in range(5)]
                v, hs = load_wchunks(w_in_conv, 0, 8, mcols[0:4])
                v2, hs2 = load_wchunks(w_in_conv, 0, 8, mcols[4:5])
                for tb in tbs:
                    c0, N = TBS[tb]
                    lc = c0 - col0
                    pss = []
                    for i in range(5):
                        ps, hp = mmrot.next()
                        vv, hv, ii = (v, hs, i) if i < 4 else (v2, hs2, 0)
                        for k in range(8):
                            MM(ps[:, 0:N], vv[:, ii, k, :], uT[:, k, lc:lc + N], k == 0, k == 7, [hv, hh[("uT", k, tb)]], [hp])
                        pss.append((ps, hp))
                    (pa, ha), (pg, hg), (px, hx_), (pc, hc), (pb, hb) = pss
                    ACT(tmpg[:, 0:N], pg[:, 0:N], AF.Sigmoid, [hg], [hh["tmpg"]])
                    TT(ga[:, c, 30 + lc:30 + lc + N], pa[:, 0:N], tmpg[:, 0:N], ALU.mult, [ha, hh["tmpg"]], [hga[c]])
                    if tb == 3:
                        TT(tails[:, c, 0:30], pa[:, 482:512], tmpg[:, 482:512], ALU.mult, [ha, hh["tmpg"]], [hh[("tails", c)]])
                    if tb == 4:
                        TT(gas[:, c, :], pa[:, 0:NS], tmpg[:, 0:NS], ALU.mult, [ha, hh["tmpg"]], [hh[("gas", c)]])
                    ACT(tmpa[:, 0:N], px[:, 0:N], AF.Copy, [hx_], [hh["tmpa"]])
                    TT(zb[:, c, 2 + lc:2 + lc + N], pc[:, 0:N], tmpa[:, 0:N], ALU.mult, [hc, hh["tmpa"]], [hzb[c]])
                    if tb == 3:
                        TT(tails[:, c, 32:34], pc[:, 510:512], tmpa[:, 510:512], ALU.mult, [hc, hh["tmpa"]], [hh[("tails", c)]])
                    if tb == 4:
                        TT(tails[:, c, 40:56], pc[:, 0:NS], tmpa[:, 0:NS], ALU.mult, [hc, hh["tmpa"]], [hh[("tails", c)]])
                    ACT(gb[:, c, lc:lc + N], pb[:, 0:N], AF.Copy, [hb], [hh[("gb", c)]])

            if hi == 1:
                stA = reg32.rearrange("p (g n) -> p g n", g=4)
                stB = tmp
                hstA = [hh[("xb", c)] for c in range(4)] + [hh[("sq4", c)] for c in range(4)]
                DMA("sp", stA[0:120, :, :], sca.rearrange("(g r) n -> r g n", r=120), writes=hstA)
                DMA("sp", stB[0:32, :], scb, writes=[hh["rtmp"]])
                for c in range(4):
                    ps, hp = mmrot.next()
                    for g in range(4):
                        TR(ps[:, g * 120:(g + 1) * 120], stA[0:120, g, c * 128:(c + 1) * 128], ident[0:120, 0:120], hstA + [hconst], [hp])
                    ACT(sAT[:, c, :], ps[:, 0:480], AF.Copy, [hp], [hh[("sAT", c)]])
                ps, hp = mmrot.next()
                for c in range(4):
                    TR(ps[:, c * 32:(c + 1) * 32], stB[0:32, c * 128:(c + 1) * 128], ident[0:32, 0:32], [hh["rtmp"], hconst], [hp])
                ACT(sBT[:, :, :], ps[:, 0:128].rearrange("p (c n) -> p c n", c=4), AF.Copy, [hp], [hh["sBT"]])

            for j in range(3):
                for c in range(4):
                    TS(diagB[:, j * 4 + c, :], identb, prmT[:, 76 + j * 4 + c:77 + j * 4 + c], ALU.mult, [hconst], [hh["diagB"]])
            for c in range(4):
                for j in range(31):
                    if j % 2 == 0:
                        TS(diag[:, j, :], identb, wAT[:, j * 4 + c:j * 4 + c + 1], ALU.mult, [hconst], [hh[("diag", j)]])
                    else:
                        ACT(diag[:, j, :], identb, AF.Copy, [hconst], [hh[("diag", j)]], scale=wAT[:, j * 4 + c:j * 4 + c + 1])
                for tb in tbs:
                    c0, N = TBS[tb]
                    lc = c0 - col0
                    hcv = hh[("cvo", c, tb)]
                    if tb < 4:
                        ps, hp = mmrot.next()
                        for j in range(31):
                            MM(ps[:, 0:N], diag[:, j, :], ga[:, c, lc + j:lc + j + N], j == 0, j == 30, [hh[("diag", j)], hga[c]], [hp])
                        ACT(cvo[:, c, lc:lc + N], ps[:, 0:N], AF.Identity, [hp, hconst], [hcv], bias=prmT[:, 64 + c:65 + c], scale=1.0)
                    else:
                        wv_ = bcast_ap(wAT[:, c:c + 1], [(0, NS), (4, 30)])
                        pA = prodA.rearrange("p (b j) -> p b j", b=NS)
                        TT(pA, sAT[:, c, :].rearrange("p (b j) -> p b j", b=NS), wv_, ALU.mult, [hh[("sAT", c)], hconst], [hh["prodA"]])
                        RED(cvo[:, c, lc:lc + NS], pA, [hh["prodA"]], [hcv])
                        STT(cvo[:, c, lc:lc + NS], gas[:, c, :], wAT[:, 120 + c:121 + c], cvo[:, c, lc:lc + NS], ALU.mult, ALU.add,
                            [hh[("gas", c)], hcv, hconst], [hcv])
                        TS(cvo[:, c, lc:lc + NS], cvo[:, c, lc:lc + NS], prmT[:, 64 + c:65 + c], ALU.add, [hcv, hconst], [hcv])
            for tb in tbs:
                c0, N = TBS[tb]
                lc = c0 - col0
                psm, hpm = statrot.next()
                pse, hpe = statrot.next()
                for c in range(4):
                    CP(xb[:, c, 0:N], cvo[:, c, lc:lc + N], [hh[("cvo", c, tb)]], [hh[("xb", c)]])
                    ACT(sq4[:, c, 0:N], cvo[:, c, lc:lc + N], AF.Square, [hh[("cvo", c, tb)]], [hh[("sq4", c)]])
                for c in range(4):
                    MM(psm[:, 0:N], onesb, xb[:, c, 0:N], c == 0, c == 3, [hh[("xb", c)], hconst], [hpm])
                for c in range(4):
                    MM(pse[:, 0:N], onesb, sq4[:, c, 0:N], c == 0, c == 3, [hh[("sq4", c)], hconst], [hpe])
                ACT(mean[:, 0:N], psm[:, 0:N], AF.Copy, [hpm], [hh["mean"]], scale=1.0 / 512)
                TT(msq[:, 0:N], mean[:, 0:N], mean[:, 0:N], ALU.mult, [hh["mean"]], [hh["msq"]])
                STT(msq[:, 0:N], pse[:, 0:N], 1.0 / 512, msq[:, 0:N], ALU.mult, ALU.subtract, [hpe, hh["msq"]], [hh["msq"]])
                ACT(tmp[:, 0:N], msq[:, 0:N], AF.Sqrt, [hh["msq"], hh["smalls"]], [hh["rtmp"]], bias=epsc, scale=1.0)
                RECIP(rstd[:, 0:N], tmp[:, 0:N], [hh["rtmp"]], [hh["rstd"]])
                for c in range(4):
                    tt = t1[c % 2]
                    ht = hh[("t1", c % 2)]
                    TT(tt[:, 0:N], cvo[:, c, lc:lc + N], mean[:, 0:N], ALU.subtract, [hh[("cvo", c, tb)], hh["mean"]], [ht])
                    TT(tt[:, 0:N], tt[:, 0:N], rstd[:, 0:N], ALU.mult, [ht, hh["rstd"]], [ht])
                    ACT(mixT[:, c, lc:lc + N], tt[:, 0:N], AF.Silu, [ht, hconst], [hh[("uT", c, tb)]],
                        bias=prmT[:, 72 + c:73 + c], scale=prmT[:, 68 + c:69 + c])
                for c in range(4):
                    hmx = hh[("uT", 4 + c, tb)]
                    if tb < 4:
                        ps, hp = mmrot.next()
                        for j in range(3):
                            MM(ps[:, 0:N], diagB[:, j * 4 + c, :], zb[:, c, lc + j:lc + j + N], j == 0, j == 2, [hh["diagB"], hzb[c]], [hp])
                        TT(mixT[:, 4 + c, lc:lc + N], ps[:, 0:N], gb[:, c, lc:lc + N], ALU.mult, [hp, hh[("gb", c)]], [hmx])
                    else:
                        tt = t1[c % 2]
                        ht = hh[("t1", c % 2)]
                        sb_ = sBT[:, c, :].rearrange("p (b j) -> p b j", j=2)
                        TS(tt[:, 0:NS], sb_[:, :, 0], prmT[:, 76 + c:77 + c], ALU.mult, [hh["sBT"], hconst], [ht])
                        STT(tt[:, 0:NS], sb_[:, :, 1], prmT[:, 80 + c:81 + c], tt[:, 0:NS], ALU.mult, ALU.add, [hh["sBT"], ht, hconst], [ht])
                        STT(tt[:, 0:NS], tails[:, c, 40:56], prmT[:, 84 + c:85 + c], tt[:, 0:NS], ALU.mult, ALU.add, [hh[("tails", c)], ht, hconst], [ht])
                        TT(mixT[:, 4 + c, lc:lc + NS], tt[:, 0:NS], gb[:, c, lc:lc + NS], ALU.mult, [ht, hh[("gb", c)]], [hmx])

            if hi == 1:
                def tm_out(src_fn, rows, dst, rd_fn):
                    ps, hp = mmrot.next()
                    for c in range(4):
                        TR(ps[0:rows, c * 128:(c + 1) * 128], src_fn(c), ident, rd_fn(c) + [hconst], [hp])
                    ACT(stg[0:rows, :], ps[0:rows, :], AF.Copy, [hp], [hh["stg"]])
                    DMA("sp", dst, stg[0:rows, :], reads=[hh["stg"]])
                tm_out(lambda c: tails[:, c, 0:30], 30, a_p, lambda c: [hh[("tails", c)]])
                tm_out(lambda c: tails[:, c, 32:34], 2, b_p, lambda c: [hh[("tails", c)]])
                DMA("sp", a_s[:, 0:29, :], sca.rearrange("(b j) n -> b j n", j=30)[:, 1:30, :])
                DMA("sp", b_s[:, 0:1, :], scb.rearrange("(b j) n -> b j n", j=2)[:, 1:2, :])
                tm_out(lambda c: gas[:, c, :], NS, a_s[:, 29, :], lambda c: [hh[("gas", c)]])
                tm_out(lambda c: tails[:, c, 40:56], NS, b_s[:, 1, :], lambda c: [hh[("tails", c)]])
            else:
                CP(halo[:, 0:4, 0:30], ga[:, :, 1024:1054], hga, [hh["halo"]])
                CP(halo[:, 4:8, 0:2], zb[:, :, 1024:1026], hzb, [hh["halo"]])
            pop_slots(nb2a)
            pre_outproj(w_out_conv, 8)
            P.fence()
            A.reset(m2)
            ysb = A.alloc(8 * W, F32).rearrange("p (c n) -> p c n", c=8)
            sqbuf2 = [A.alloc(512, BF16), A.alloc(512, BF16)]
            tmp2 = A.alloc(512, F32)
            rstd2 = A.alloc(512, F32)
            ytmp = [A.alloc(512, F32), A.alloc(512, F32)]
            nb_ = push_slots(2)
            out_proj(w_out_conv, 0, 8,
                     lambda k, tb: (mixT[:, k, TBS[tb][0] - col0:TBS[tb][0] - col0 + TBS[tb][1]], [hh[("uT", k, tb)]]),
                     tbs, ysb, col0)
            for tb in tbs:
                postnorm_residual(0, 1, tb, ysb, col0, sqbuf2, tmp2, rstd2, ytmp)
            pop_slots(nb_)
            if STOP >= 2:
                pre_mlp(0)
            P.fence()
            A.reset(m1)

        RUN0 = STOP >= 1 and not int(os.environ.get('MK_SKIP0', '0'))
        if RUN0:
            pre_l0a()
        P.fence()
        A.reset(m0)
        if RUN0:
            for hi, (col0, tbs) in enumerate(HALVES):
                layer0_half(hi, col0, tbs)
                if STOP >= 2:
                    nxt = pre_l0a if hi == 0 else (pre_l1a if STOP >= 3 else None)
                    mlp_half(0, col0, tbs, pre_next=nxt)

        def load_wcols(wd, row0, Kc, col0, ncols):
            slot, hs = wslot_next()
            assert Kc * ncols <= SLOT
            v = slot[:, 0:Kc * ncols].rearrange("p (k m) -> p k m", k=Kc)
            DMA("pool", v, wd[row0:row0 + Kc * 128, col0:col0 + ncols].rearrange("(k p) m -> p k m", p=128), writes=[hs])
            return v, hs

        def layer1_mixer():
            mL = A.mark()
            QT = A.alloc(10 * NT, BF16).rearrange("p (c n) -> p c n", c=10)
            KT = A.alloc(3 * NT, BF16).rearrange("p (c n) -> p c n", c=3)
            Vd = A.alloc(16 * 5 * 128, BF16).rearrange("p (t k m) -> p t k m", t=16, k=5)
            QTs = A.alloc(10 * NS, F32).rearrange("p (c n) -> p c n", c=10)
            KTs = A.alloc(3 * NS, F32).rearrange("p (c n) -> p c n", c=3)
            VTs = A.alloc(5 * NS, F32).rearrange("p (c n) -> p c n", c=5)
            VsTM = A.alloc(320, F32)
            knew = A.alloc(384, F32)
            smix = A.alloc(6 * NS, BF16).rearrange("p (c n) -> p c n", c=6)
            mP = A.mark()
            uT = A.alloc(8 * HW_, BF16).rearrange("p (c n) -> p c n", c=8)
            sqbuf = [A.alloc(512, BF16), A.alloc(512, BF16)]
            tmp = A.alloc(512, F32)
            rstd = A.alloc(512, F32)
            cst = A.alloc(2 * HW_, F32).rearrange("p (c n) -> p c n", c=2)
            qbs = [A.alloc(512, BF16), A.alloc(512, BF16)]
            t1s = [A.alloc(512, F32), A.alloc(512, F32)]
            t2s = [A.alloc(512, F32), A.alloc(512, F32)]
            rctr = [0]
            K32 = A.alloc(512, F32)
            vst = A.alloc(320, F32)
            kst = A.alloc(128, F32)


            def rope(ps, hp, N, lc=0):
                i = rctr[0] % 2
                rctr[0] += 1
                qb, t1, t2 = qbs[i], t1s[i], t2s[i]
                hq, h1, h2 = hh[("qb", i)], hh[("t1", i)], hh[("t2", i)]
                ACT(qb[:, 0:N], ps[:, 0:N], AF.Copy, [hp], [hq])
                pr, hpr = rrot.next()
                MM(pr[:, 0:N], prot, qb[:, 0:N], True, True, [hq, hconst], [hpr])
                TT(t1[:, 0:N], ps[:, 0:N], cst[:, 0, lc:lc + N], ALU.mult, [hp, hh["cst"]], [h1])
                TT(t2[:, 0:N], pr[:, 0:N], cst[:, 1, lc:lc + N], ALU.mult, [hpr, hh["cst"]], [h2])
                return t1, t2, h1, h2

            def perm_write(dstT, m, rows, kind, tb, hw, R):
                t1, t2, h1, h2 = R
                c0, N = TBS[tb]
                r0, r1 = rows
                if kind == 1:
                    o = dstT[r0:r1, m, c0:c0 + N]
                    a, b = t1[r0:r1, 0:N], t2[r0:r1, 0:N]
                else:
                    o = dstT[r0:r1, m, 0:NTOK].rearrange("p (r i) -> p r i", r=kind)[:, :, c0 // kind:c0 // kind + N // kind]
                    a = t1[r0:r1, 0:N].rearrange("p (i r) -> p r i", r=kind)
                    b = t2[r0:r1, 0:N].rearrange("p (i r) -> p r i", r=kind)
                TT(o, a, b, ALU.add, [h1, h2], [hw])

            QKIND = [((1, 1),)] * 4 + [((1, 4),)] * 4 + [((16, 16),)] * 2
            KKIND = [(1, 1), (1, 4), (16, 16)]

            nbase = len(wslots)
            PASSES = [[0, 1], [2, 3, 4]]
            psrot = Rot([0, 1, 2])
            rrot = Rot([3, 4, 5])
            for tbs_ in PASSES:
                pc0 = TBS[tbs_[0]][0]
                PW = sum(TBS[tb][1] for tb in tbs_)
                for tb in tbs_:
                    norm_block(1, 0, tb, uT, pc0, sqbuf, tmp, rstd, htb="L1")
                DMA("sp", cst[:, :, 0:PW], cs_d[:, :, pc0:pc0 + PW], writes=[hh["cst"]])
                pend = [None]

                def flush():
                    if pend[0] is not None:
                        f_, a_ = pend[0]
                        pend[0] = None
                        f_(*a_)

                def post_q(ps, hp, m, tb):
                    c0, N = TBS[tb]
                    lc = c0 - pc0
                    R = rope(ps, hp, N, lc)
                    if tb == 4:
                        TT(QTs[:, m, :], R[0][:, 0:N], R[1][:, 0:N], ALU.add, [R[2], R[3]], [hh[("QTs", m)]])
                    else:
                        ka, kb = QKIND[m][0]
                        if ka == kb:
                            perm_write(QT, m, (0, 128), ka, tb, hh[("QT", m, tb)], R)
                        else:
                            perm_write(QT, m, (0, 64), ka, tb, hh[("QT", m, tb, 0)], R)
                            perm_write(QT, m, (64, 128), kb, tb, hh[("QT", m, tb, 1)], R)

                def post_k(ps, hp, kc, tb):
                    c0, N = TBS[tb]
                    lc = c0 - pc0
                    R = rope(ps, hp, N, lc)
                    if tb == 4:
                        TT(KTs[:, kc, :], R[0][:, 0:N], R[1][:, 0:N], ALU.add, [R[2], R[3]], [hh[("KTs", kc)]])
                        return
                    ka, kb = KKIND[kc]
                    if ka == kb:
                        perm_write(KT, kc, (0, 128), ka, tb, hh[("KT", kc, tb)], R)
                    else:
                        perm_write(KT, kc, (0, 64), ka, tb, hh[("KT", kc, tb, 0)], R)
                        perm_write(KT, kc, (64, 128), kb, tb, hh[("KT", kc, tb, 1)], R)
                    need = [t for t in range(4) if (M4A & 4) and (kc == 2 or (kc == 1 and 4 * tb + t >= 12) or (kc == 0 and 4 * tb + t == 15))]
                    if need:
                        TT(K32[:, 0:N], R[0][:, 0:N], R[1][:, 0:N], ALU.add, [R[2], R[3]], [hh["K32"]])
                    for t in need:
                        T = 4 * tb + t
                        pt, hpt = rrot.next()
                        TR(pt[:, 0:128], K32[:, t * 128:(t + 1) * 128], ident, [hh["K32"], hconst], [hpt])
                        ACT(kst[:, 0:128], pt[:, 0:128], AF.Copy, [hpt], [hh["kst"]])
                        if kc == 2:
                            DMA("sp", d_p[2][T * 128:(T + 1) * 128, 0:64], kst[:, 0:64], reads=[hh["kst"]], writes=[hh[("d2pk", T)]])
                        elif kc == 1:
                            DMA("sp", d_p[1][(T - 12) * 128:(T - 11) * 128, 0:64], kst[:, 64:128], reads=[hh["kst"]])
                            if T == 15:
                                DMA("sp", d_p[0][:, 0:64], kst[:, 0:64], reads=[hh["kst"]])
                        else:
                            DMA("sp", swa_p[:, 0:128], kst[:, 0:128], reads=[hh["kst"]])

                def main_mm(v, i, hs, tb):
                    c0, N = TBS[tb]
                    lc = c0 - pc0
                    ps, hp = psrot.next()
                    for k in range(8):
                        MM(ps[:, 0:N], v[:, i, k, :], uT[:, k, lc:lc + N], k == 0, k == 7, [hs, hh[("uT", k, "L1")]], [hp])
                    return ps, hp

                for mg in range(3):
                    ms = list(range(mg * 4, min(10, mg * 4 + 4)))
                    v, hs = load_wchunks(wq_d, 0, 8, [m * 128 for m in ms])
                    for i, m in enumerate(ms):
                        for tb in tbs_:
                            ps, hp = main_mm(v, i, hs, tb)
                            flush()
                            pend[0] = (post_q, (ps, hp, m, tb))
                v, hs = load_wchunks(wk_d, 0, 8, [0, 128, 256])
                for kc in range(3):
                    for tb in tbs_:
                        ps, hp = main_mm(v, kc, hs, tb)
                        flush()
                        pend[0] = (post_k, (ps, hp, kc, tb))
                flush()
                vw, hvw = load_wcols(wv_d, 0, 8, 0, 320)
                for tb in tbs_:
                    c0, N = TBS[tb]
                    lc = c0 - pc0
                    if tb == 4:
                        ps, hp = mmrot.next()
                        for k in range(8):
                            MM(ps[0:NS, 0:320], uT[:, k, lc:lc + NS], vw[:, k, :], k == 0, k == 7, [hvw, hh[("uT", k, "L1")]], [hp])
                        ACT(VsTM[0:NS, :], ps[0:NS, 0:320], AF.Copy, [hp], [hh["VsTM"]])
                        vd_, hvd = load_wchunks(wvd_d, 0, 8, [0, 128, 256, 384])
                        vd2, hvd2 = load_wchunks(wvd_d, 0, 8, [512])
                        for i in range(5):
                            vv, hv_, ii = (vd_, hvd, i) if i < 4 else (vd2, hvd2, 0)
                            ps, hp = mmrot.next()
                            for k in range(8):
                                MM(ps[:, 0:NS], vv[:, ii, k, :], uT[:, k, lc:lc + NS], k == 0, k == 7, [hv_, hh[("uT", k, "L1")]], [hp])
                            ACT(VTs[:, i, :], ps[:, 0:NS], AF.Copy, [hp], [hh[("VTs", i)]])
                        ps, hp = mmrot.next()
                        for kc in range(3):
                            TR(ps[0:NS, kc * 128:(kc + 1) * 128], KTs[:, kc, :], ident, [hh[("KTs", kc)], hconst], [hp])
                        ACT(knew[0:NS, :], ps[0:NS, 0:384], AF.Copy, [hp], [hh["knew"]])
                        DMA("sp", swa_s[:, 127, 0:128], knew[0:NS, 0:128], reads=[hh["knew"]])
                        DMA("sp", swa_s[:, 127, 128:256], VsTM[0:NS, 0:128], reads=[hh["VsTM"]])
                        for g, Wg in enumerate((128, 512, 2048)):
                            DMA("sp", d_s[g][:, Wg - 1, 0:64], knew[0:NS, 128 + 64 * g:192 + 64 * g], reads=[hh["knew"]])
                            DMA("sp", d_s[g][:, Wg - 1, 64:128], VsTM[0:NS, 128 + 64 * g:192 + 64 * g], reads=[hh["VsTM"]])
                        continue
                    for t in range(4):
                        T = 4 * tb + t
                        ps, hp = mmrot.next()
                        for k in range(8):
                            MM(ps[:, 0:320], uT[:, k, lc + t * 128:lc + (t + 1) * 128], vw[:, k, :], k == 0, k == 7, [hvw, hh[("uT", k, "L1")]], [hp])
                        src3 = ps[:, 0:192].rearrange("p (k m) -> p k m", k=3)
                        ACT(Vd[:, T, 0:3, 0:64], src3, AF.Copy, [hp], [hh[("Vd", T, 0)]])
                        CP(Vd[:, T, 0:3, 64:128], src3, [hp], [hh[("Vd", T, 1)]])
                        ACT(vst[:, 0:320], ps[:, 0:320], AF.Copy, [hp], [hh["vst"]])
                        DMA("sp", d_p[2][T * 128:(T + 1) * 128, 64:128], vst[:, 256:320], reads=[hh["vst"]], writes=[hh[("d2pv", T)]])
                        if T >= 12:
                            DMA("sp", d_p[1][(T - 12) * 128:(T - 11) * 128, 64:128], vst[:, 192:256], reads=[hh["vst"]])
                        if T == 15:
                            DMA("sp", d_p[0][:, 64:128], vst[:, 128:192], reads=[hh["vst"]])
                            DMA("sp", swa_p[:, 128:256], vst[:, 0:128], reads=[hh["vst"]])
                    for r in range(4):
                        ps, hp = mmrot.next()
                        for k in range(8):
                            MM(ps[:, 0:64], uT[:, k, lc + r:lc + 512:4], vw[:, k, 192:256], k == 0, k == 7, [hvw, hh[("uT", k, "L1")]], [hp])
                        ACT(Vd[:, 4 * r + tb, 3, 0:64], ps[:, 0:64], AF.Copy, [hp], [hh[("Vd3", r, tb, 0)]])
                        CP(Vd[:, 4 * r + tb, 3, 64:128], ps[:, 0:64], [hp], [hh[("Vd3", r, tb, 1)]])
            del wslots[nbase:]
            NSL[0] = len(wslots)
            if L1STOP > 3:
                pre_outproj(w_out_attn, 6)
            src = d_p[2].rearrange("(i r) n -> i r n", r=16)[:, :, 64:128]
            rds = [hh[("d2pv", T)] for T in range(16)]
            if not int(os.environ.get("MK_NORB", "0")):
                for r in range(16):
                    DMA("pool", Vd[:, r, 4, 0:64], src[:, r, :], reads=rds, writes=[hh[("Vd4a", r)]])
                    DMA("pool", Vd[:, r, 4, 64:128], src[:, r, :], reads=rds, writes=[hh[("Vd4b", r)]])
            P.fence()
            if L1STOP <= 1:
                A.reset(mL)
                return

            if not NOCOPY:
                DMA("act", swa_s[:, 0:127, :], c_swa[:, 1:128, :])
                for g, Wg in enumerate((128, 512, 2048)):
                    DMA("act", d_s[g][:, 0:Wg - 1, :], c_d[g][:, 1:Wg, :])
            A.reset(mP)
            mixT = A.alloc(6 * NT, BF16).rearrange("p (c n) -> p c n", c=6)
            PT = [A.alloc(256, BF16), A.alloc(256, BF16)]
            accN = A.alloc(NTOK, F32)
            accD = A.alloc(NTOK, F32)
            rec = A.alloc(512, F32)
            PT = PT + [A.alloc(256, BF16), A.alloc(256, BF16)]
            srot = Rot([0, 1, 6, 7])
            orot = Rot([2, 4])
            items = []

            def add_head(qc, base, kc, kv, kind, evac):
                cfg = dict(qc=qc, base=base, kc=kc, kv=kv, kind=kind, evac=evac)
                for T in range(16):
                    items.append((cfg, T))

            for h in range(8):
                kv, g = h // 4, h % 4
                mrows = slice((h % 2) * 64, (h % 2) * 64 + 64)
                mch = h // 2

                def evac(b, po, hpo, pd, hpd, h=h, mrows=mrows, mch=mch):
                    TS(rec[mrows, :], pd[mrows, :], smalls[mrows, 16 + h:17 + h], ALU.add, [hpd, hh["expsink"]], [hh["rec"]])
                    RECIP(rec[mrows, :], rec[mrows, :], [hh["rec"]], [hh["rec"]])
                    TT(mixT[mrows, mch, b * 512:(b + 1) * 512], po[mrows, :], rec[mrows, :], ALU.mult, [hpo, hh["rec"]], [hh[("mixT", mch, h % 2, b)]])
                add_head(g, kv * 64, 0, kv, 1, evac)
            for s_ in range(4):
                mrows = slice((s_ % 2) * 64, (s_ % 2) * 64 + 64)
                mch = 4 + s_ // 2
                for grp in range(3):
                    kind = (1, 4, 16)[grp]

                    def evac(b, po, hpo, pd, hpd, grp=grp, kind=kind, mrows=mrows, mch=mch, s_=s_):
                        if kind == 1:
                            on = accN[mrows, b * 512:(b + 1) * 512]
                            od = accD[mrows, b * 512:(b + 1) * 512]
                            sn, sd = po[mrows, :], pd[mrows, :]
                        elif kind == 4:
                            on = accN[mrows, b:NTOK:4]
                            od = accD[mrows, b:NTOK:4]
                            sn, sd = po[mrows, :], pd[mrows, :]
                        else:
                            on = accN[mrows, :].rearrange("p (i r) -> p i r", r=16)[:, :, 4 * b:4 * b + 4]
                            od = accD[mrows, :].rearrange("p (i r) -> p i r", r=16)[:, :, 4 * b:4 * b + 4]
                            sn = po[mrows, :].rearrange("p (t i) -> p i t", t=4)
                            sd = pd[mrows, :].rearrange("p (t i) -> p i t", t=4)
                        if grp == 0:
                            ACT(on, sn, AF.Copy, [hpo], [hh["accN"]])
                            CP(od, sd, [hpd], [hh["accD"]])
                        else:
                            TT(on, on, sn, ALU.add, [hpo, hh["accN"]], [hh["accN"]])
                            TT(od, od, sd, ALU.add, [hpd, hh["accD"]], [hh["accD"]])
                        if grp == 2 and b == 3:
                            for bb in range(4):
                                RECIP(rec[mrows, :], accD[mrows, bb * 512:(bb + 1) * 512], [hh["accD"]], [hh["rec"]])
                                TT(mixT[mrows, mch, bb * 512:(bb + 1) * 512], accN[mrows, bb * 512:(bb + 1) * 512], rec[mrows, :], ALU.mult,
                                   [hh["accN"], hh["rec"]], [hh[("mixT", mch, s_ % 2, bb)]])
                    if grp == 0:
                        add_head(4 + s_, 0, 1, 2, 1, evac)
                    elif grp == 1:
                        add_head(4 + s_, 64, 1, 3, 4, evac)
                    else:
                        add_head(8 + s_ // 2, (s_ % 2) * 64, 2, 4, 16, evac)

            state = {}

            def issue_S(i):
                cfg, T = items[i]
                kind, base = cfg["kind"], cfg["base"]
                rows = slice(base, base + 64)
                j = T if kind == 1 else (T % 4 if kind == 4 else 0)
                pss, hs_ = srot.next()
                lo = 0 if j > 0 else 128
                qsl = QT[rows, cfg["qc"], T * 128:(T + 1) * 128]
                if j > 0:
                    MM(pss[:, 0:128], KT[rows, cfg["kc"], (T - 1) * 128:T * 128], qsl, True, True, [], [hs_])
                MM(pss[:, 128:256], KT[rows, cfg["kc"], T * 128:(T + 1) * 128], qsl, True, True, [], [hs_])
                pt = PT[i % 4]
                hpt = hh[("PT", i % 4)]
                ACT(pt[:, lo:256], pss[:, lo:256], AF.Exp, [hs_], [hpt], scale=0.125)
                TT(pt[:, lo:256], pt[:, lo:256], maskpd[:, lo:256], ALU.mult, [hpt, hconst], [hpt], eng="pool")
                state[i] = (pt, hpt, j)

            def issue_PV(i):
                cfg, T = items[i]
                pt, hpt, j = state.pop(i)
                kv = cfg["kv"]
                if T % 4 == 0:
                    po, hpo = orot.next()
                    ob = orot.banks[(orot.i - 1) % 2]
                    cfg["o"] = (po, hpo, psb[ob + 1], hps[ob + 1])
                po, hpo, pd, hpd = cfg["o"]
                osl = slice((T % 4) * 128, (T % 4 + 1) * 128)
                if j > 0:
                    MM(po[:, osl], Vd[:, T - 1, kv, :], pt[:, 0:128], True, False, [hpt], [hpo])
                    MM(po[:, osl], Vd[:, T, kv, :], pt[:, 128:256], False, True, [hpt], [hpo])
                    MM(pd[:, osl], onesb, pt[:, 0:128], True, False, [hpt, hconst], [hpd])
                    MM(pd[:, osl], onesb, pt[:, 128:256], False, True, [hpt, hconst], [hpd])
                else:
                    MM(po[:, osl], Vd[:, T, kv, :], pt[:, 128:256], True, True, [hpt], [hpo])
                    MM(pd[:, osl], onesb, pt[:, 128:256], True, True, [hpt, hconst], [hpd])
                if T % 4 == 3:
                    cfg["evac"](T // 4, po, hpo, pd, hpd)

            LOOK = 2
            for i in range(len(items) + LOOK):
                if i < len(items):
                    issue_S(i)
                if i >= LOOK:
                    issue_PV(i - LOOK)
            P.fence()
            if L1STOP <= 2:
                A.reset(mL)
                return

            A.reset(mL)
            _skip = A.alloc(1, F32)
            A.reset(mL)
            Kc = A.alloc(NS * 64, F32).rearrange("p (b d) -> p b d", b=NS)
            Vc = A.alloc(NS * 128, F32).rearrange("p (b d) -> p b d", b=NS)
            Qs = A.alloc(256, F32)
            Qd = A.alloc(NS * 256, F32).rearrange("p (b c) -> p b c", b=NS)
            prod = A.alloc(512, F32)
            Sall = A.alloc(64, F32)
            Pn = A.alloc(64, F32)
            Pnew = A.alloc(64, F32)
            den = A.alloc(64, F32)
            num = A.alloc(64, F32)
            numD = A.alloc(64, F32)
            denD = A.alloc(64, F32)
            prT = A.alloc(NS, F32)
            outv = A.alloc(64, F32)
            assert A.off <= mP - 0 or True

            def v3(ap):
                return ap.rearrange("p (b s) -> p b s", s=4)

            groups = []
            for kv in range(2):
                groups.append(dict(heads=[(g, kv * 64) for g in range(4)], kc=0, kbase=kv * 64, vi=kv, cache=c_swa, kcol=kv * 64, vcol=128 + kv * 64,
                                   step=1, swa=kv))
            groups.append(dict(heads=[(4 + s, 0) for s in range(4)], kc=1, kbase=0, vi=2, cache=c_d[0], kcol=0, vcol=64, step=1, swa=None))
            groups.append(dict(heads=[(4 + s, 64) for s in range(4)], kc=1, kbase=64, vi=3, cache=c_d[1], kcol=0, vcol=64, step=4, swa=None))
            groups.append(dict(heads=[(8 + s // 2, (s % 2) * 64) for s in range(4)], kc=2, kbase=None, vi=4, cache=c_d[2], kcol=0, vcol=64, step=16, swa=None))
            for gi, G in enumerate(groups):
                cview = G["cache"].rearrange("b j n -> j b n")
                st_ = G["step"]
                for q4 in range(4):
                    bs = slice(4 * q4, 4 * q4 + 4)
                    DMA("sp", Kc[:, bs, :], cview[0:128 * st_:st_, bs, G["kcol"]:G["kcol"] + 64], writes=[hh[("Kc", q4)]])
                    DMA("sp", Vc[:, bs, 0:64], cview[0:128 * st_:st_, bs, G["vcol"]:G["vcol"] + 64], writes=[hh[("Vc", q4, 0)]])
                    DMA("sp", Vc[:, bs, 64:128], cview[0:128 * st_:st_, bs, G["vcol"]:G["vcol"] + 64], writes=[hh[("Vc", q4, 1)]])
                pq = [mmrot.next(), mmrot.next()]
                for s, (qc, qb_) in enumerate(G["heads"]):
                    ps, hp = pq[qb_ // 64]
                    TR(ps[0:NS, s * 64:(s + 1) * 64], QTs[qb_:qb_ + 64, qc, :], ident[qb_:qb_ + 64, qb_:qb_ + 64], [hh[("QTs", qc)], hconst], [hp])
                for s, (qc, qb_) in enumerate(G["heads"]):
                    ps, hp = pq[qb_ // 64]
                    ACT(Qs[0:NS, s * 64:(s + 1) * 64], ps[0:NS, s * 64:(s + 1) * 64], AF.Copy, [hp], [hh["Qs"]])
                TT(Qd[0:NS, :, :], bcast_ap(Qs[0:NS, 0:1], [(0, NS), (1, 256)]), bcast_ap(ident[0:NS, 0:1], [(1, NS), (0, 256)]), ALU.mult,
                   [hh["Qs"], hconst], [hh["Qd"]])
                for bp in range(NS // 2):
                    ps, hp = mmrot.next()
                    MM(ps[:, 0:512], ones32[0:NS, :], Qd[0:NS, 2 * bp:2 * bp + 2, :], True, True, [hh["Qd"], hconst], [hp])
                    TT(prod[:, :].rearrange("p (b s d) -> p b s d", b=2, s=4), ps[:, 0:512].rearrange("p (b s d) -> p b s d", b=2, s=4),
                       bcast_ap(Kc[:, 2 * bp, 0:1], [(64, 2), (0, 4), (1, 64)]), ALU.mult, [hp] + [hh[("Kc", q4)] for q4 in range(4)], [hh["prod"]])
                    o = Sall[:, 8 * bp:8 * bp + 8].rearrange("p (b s) -> p b s", b=2)
                    RED(o, prod[:, :].rearrange("p (b s d) -> p b s d", b=2, s=4), [hh["prod"]], [hh["Sall"]])
                psn, hpn = mmrot.next()
                for s, (qc, qb_) in enumerate(G["heads"]):
                    TT(prT[:, :], QTs[:, qc, :], KTs[:, G["kc"], :], ALU.mult, [hh[("QTs", qc)], hh[("KTs", G["kc"])]], [hh["prT"]])
                    MM(psn[:, s * NS:(s + 1) * NS], selh[qb_ // 64], prT[:, :], True, True, [hh["prT"], hconst], [hpn])
                ACT(Pn[:, :], Sall[:, :], AF.Exp, [hh["Sall"]], [hh["Pn"]], scale=0.125)
                ACT(Pnew[:, :].rearrange("p (b s) -> p s b", s=4), psn[:, 0:64].rearrange("p (s b) -> p s b", s=4), AF.Exp, [hpn], [hh["Pnew"]], scale=0.125)
                psd, hpd = mmrot.next()
                MM(psd[:, 0:64], ones32, Pn[:, :], True, True, [hh["Pn"], hconst], [hpd])
                TT(den[:, :], psd[:, 0:64], Pnew[:, :], ALU.add, [hpd, hh["Pnew"]], [hh["den"]])
                if G["swa"] is not None:
                    kv = G["swa"]
                    TT(v3(den[:, :]), v3(den[:, :]), bcast_ap(smalls[:, 16 + 4 * kv:17 + 4 * kv], [(0, NS), (1, 4)]), ALU.add,
                       [hh["den"], hh["expsink"]], [hh["den"]])
                psv, hpv = mmrot.next()
                for b in range(NS):
                    MM(psv[:, 4 * b:4 * b + 4], Vc[:, b, :], Pn[:, 4 * b:4 * b + 4], True, True, [hh[("Vc", q4, i)] for q4 in range(4) for i in range(2)] + [hh["Pn"]], [hpv])
                TT(v3(num[:, :]), v3(Pnew[:, :]), bcast_ap(VTs[:, G["vi"], 0:1], [(1, NS), (0, 4)]), ALU.mult, [hh["Pnew"], hh[("VTs", G["vi"])]], [hh["num"]])
                TT(num[:, :], num[:, :], psv[:, 0:64], ALU.add, [hh["num"], hpv], [hh["num"]])
                if G["swa"] is not None:
                    kv = G["swa"]
                    RECIP(den[:, :], den[:, :], [hh["den"]], [hh["den"]])
                    TT(outv[:, :], num[:, :], den[:, :], ALU.mult, [hh["num"], hh["den"]], [hh["outv"]])
                    for s in range(4):
                        h = kv * 4 + s
                        rows = slice((h % 2) * 64, (h % 2) * 64 + 64)
                        CP(smix[rows, h // 2, :], outv[rows, s:64:4], [hh["outv"]], [hh[("smix", h // 2, h % 2)]])
                else:
                    if gi == 2:
                        CP(numD[:, :], num[:, :], [hh["num"]], [hh["numD"]])
                        CP(denD[:, :], den[:, :], [hh["den"]], [hh["denD"]])
                    else:
                        TT(numD[:, :], numD[:, :], num[:, :], ALU.add, [hh["num"], hh["numD"]], [hh["numD"]])
                        TT(denD[:, :], denD[:, :], den[:, :], ALU.add, [hh["den"], hh["denD"]], [hh["denD"]])
            RECIP(denD[:, :], denD[:, :], [hh["denD"]], [hh["denD"]])
            TT(outv[:, :], numD[:, :], denD[:, :], ALU.mult, [hh["numD"], hh["denD"]], [hh["outv"]])
            for s in range(4):
                rows = slice((s % 2) * 64, (s % 2) * 64 + 64)
                CP(smix[rows, 4 + s // 2, :], outv[rows, s:64:4], [hh["outv"]], [hh[("smix", 4 + s // 2, s % 2)]])
            P.fence()
            if L1STOP <= 3:
                A.reset(mL)
                return

            A.reset(mL)
            W = HW_
            ysb = A.alloc(8 * W, F32).rearrange("p (c n) -> p c n", c=8)
            sqbuf2 = [A.alloc(512, BF16), A.alloc(512, BF16)]
            tmp2 = A.alloc(512, F32)
            rstd2 = A.alloc(512, F32)
            ytmp = [A.alloc(512, F32), A.alloc(512, F32)]
            nb_ = push_slots(2)
            assert A.off <= mP

            def rhs_fn(k, tb):
                if tb == 4:
                    return smix[:, k, :], []
                return mixT[:, k, TBS[tb][0]:TBS[tb][0] + 512], []
            for hi_, (col0, tbs) in enumerate(HALVES):
                out_proj(w_out_attn, 0, 6, rhs_fn, tbs, ysb, col0)
                for tb in tbs:
                    postnorm_residual(1, 1, tb, ysb, col0, sqbuf2, tmp2, rstd2, ytmp)
                if hi_ == 1:
                    pop_slots(nb_)
                    if STOP >= 4:
                        pre_mlp(1)
                P.fence()
            A.reset(mL)

        if STOP >= 3:
            layer1_mixer()
        if STOP >= 4:
            for hi_, (col0, tbs) in enumerate(HALVES):
                mlp_half(1, col0, tbs, pre_next=(lambda: pre_mlp(1)) if hi_ == 0 else None)

        A.reset(m0)
        yo = [A.alloc(1024, F32) for _ in range(4)]
        rot = Rot([0, 1, 2, 3, 4, 5])
        for t in range(17):
            y = yo[t % 4]
            hy = hh[("yo", t % 4)]
            rows = 128 if t < 16 else NS
            tbk = t // 4 if t < 16 else 4
            for g in range(2):
                ps, hp = rot.next()
                for i in range(4):
                    c = g * 4 + i
                    TR(ps[0:rows, i * 128:(i + 1) * 128], hT[:, c, t * 128:t * 128 + rows], ident, [hh[("hT", c, tbk)], hconst], [hp])
                if g == 0:
                    ACT(y[0:rows, 0:512], ps[0:rows, :], AF.Copy, [hp], [hy])
                else:
                    CP(y[0:rows, 512:1024], ps[0:rows, :], [hp], [hy])
            dst = y_p[t * 128:(t + 1) * 128, :] if t < 16 else y_s
            DMA("sp", dst, y[0:rows, :], reads=[hy])

        assert not PRE, list(PRE.keys())
        P.emit(st)
        build_program.stats = dict(P.stats)
        build_program.arena_peak = A.peak
    return nc


def _consts():
    c32 = np.zeros((128, 512), np.float32)
    c32[:, 0:128] = np.eye(128, dtype=np.float32)
    c32[0:64, 128:256] = 1.0
    c32[64:128, 256:384] = 1.0
    c32[:, 384:512] = 1.0
    cb = np.zeros((128, 768), np.float32)
    prot = np.zeros((128, 128), np.float32)
    for base in (0, 64):
        for i in range(8):
            prot[base + i + 8, base + i] = 1.0
            prot[base + i, base + i + 8] = 1.0
    cb[:, 0:128] = prot
    cb[:, 128:256] = 1.0
    p = np.arange(128)[:, None]
    f = np.arange(128)[None, :]
    cb[:, 256:384] = (f <= p)
    cb[:, 384:512] = (f >= p)
    cb[:, 512:640] = np.eye(128, dtype=np.float32)
    half = 8
    inv = (np.float32(500000.0) ** (-np.arange(half, dtype=np.float32) / half)).astype(np.float32)
    pos = np.concatenate([np.arange(NTOK, dtype=np.float32), np.full(NS, 8192.0, np.float32)])
    ang = (pos[None, :] * inv[:, None]).astype(np.float32)
    cs = np.zeros((128, 2, NT), np.float32)
    cs[:, 0, :] = 1.0
    for base in (0, 64):
        cs[base:base + 8, 0] = np.cos(ang)
        cs[base + 8:base + 16, 0] = np.cos(ang)
        cs[base:base + 8, 1] = -np.sin(ang)
        cs[base + 8:base + 16, 1] = np.sin(ang)
    return c32, cb, cs


_NC_CACHE = {}


def kernel(x_prompt, x_sample, state_conv_a, state_conv_b, cache_swa_kv, cache_dil0_kv, cache_dil1_kv,
           cache_dil2_kv, norm_g, w_in_conv, conv_a_w, conv_a_b, conv_a_ln_g, conv_a_ln_b, conv_b_w,
           w_out_conv, w_in_attn, attn_sinks, w_out_attn, mlp_w1, mlp_w2):
    f = lambda a: np.ascontiguousarray(np.asarray(a, dtype=np.float32))
    x_prompt, x_sample = f(x_prompt), f(x_sample)
    if "nc" not in _NC_CACHE:
        _NC_CACHE["nc"] = build_program()
    nc = _NC_CACHE["nc"]
    c32, cb, cs = _consts()
    prm1 = np.concatenate([f(norm_g).reshape(64, 128), f(conv_a_b).reshape(4, 128), f(conv_a_ln_g).reshape(4, 128),
                           f(conv_a_ln_b).reshape(4, 128), f(conv_b_w).reshape(12, 128)], 0)
    prm2 = f(conv_a_w).reshape(124, 128)
    wi = f(w_in_attn)[0]
    qs = lambda h: wi[:, h * 64:(h + 1) * 64]
    qd = lambda g, h: wi[:, 768 + 384 * g + h * 64: 768 + 384 * g + (h + 1) * 64]
    kd = lambda g: wi[:, 768 + 384 * g + 256: 768 + 384 * g + 320]
    vd = lambda g: wi[:, 768 + 384 * g + 320: 768 + 384 * g + 384]
    ks = lambda kv: wi[:, 512 + kv * 64: 512 + (kv + 1) * 64]
    vs = lambda kv: wi[:, 640 + kv * 64: 640 + (kv + 1) * 64]
    wq = np.concatenate([np.concatenate([qs(g), qs(4 + g)], 1) for g in range(4)] +
                        [np.concatenate([qd(0, g), qd(1, g)], 1) for g in range(4)] +
                        [np.concatenate([qd(2, 0), qd(2, 1)], 1), np.concatenate([qd(2, 2), qd(2, 3)], 1)], 1)
    wk = np.concatenate([ks(0), ks(1), kd(0), kd(1), kd(2), kd(2)], 1)
    wv = np.concatenate([vs(0), vs(1), vd(0), vd(1), vd(2)], 1)
    wvd = np.concatenate([vs(0), vs(0), vs(1), vs(1), vd(0), vd(0), vd(1), vd(1), vd(2), vd(2)], 1)
    wic0 = f(w_in_conv)[0]
    wic = np.concatenate([wic0[:, o + c * 128:o + (c + 1) * 128] for c in range(4) for o in (0, 512, 1024, 2048, 1536)], 1)
    shared = {
        "prm1": f(prm1), "prm2": prm2, "sinks": f(attn_sinks).reshape(1, 8), "cst32": c32, "cstb": cb, "cs": cs,
        "w_in_conv": f(wic), "w_out_conv": f(w_out_conv)[0], "wq": f(wq), "wk": f(wk), "wv": f(wv), "wvd": f(wvd),
        "w_out_attn": f(w_out_attn)[0], "w1": f(mlp_w1), "w2": f(mlp_w2),
    }
    sca = f(state_conv_a)[0]
    scb = f(state_conv_b)[0]
    cswa = f(cache_swa_kv)[0]
    cds = [f(cache_dil0_kv)[0], f(cache_dil1_kv)[0], f(cache_dil2_kv)[0]]
    in_maps = []
    for i in range(8):
        s = slice(NS * i, NS * (i + 1))
        m = dict(shared)
        m["xp"] = x_prompt[i]
        m["xs"] = x_sample[s, 0, :]
        m["sca"] = sca[s].reshape(NS * 30, 512)
        m["scb"] = scb[s].reshape(NS * 2, 512)
        m["c_swa"] = cswa[s].reshape(NS, 128, 256)
        for g in range(3):
            m["c_d%d" % g] = cds[g][s].reshape(NS, cds[g].shape[1], 128)
        in_maps.append(m)
    res = run_bass_kernel_spmd(nc, in_maps, core_ids=list(range(8)))
    R = res.results
    cat = lambda k: np.stack([np.asarray(r[k]) for r in R], 0)
    y_prompt = cat("y_p")
    y_sample = np.concatenate([np.asarray(r["y_s"]) for r in R], 0).reshape(128, 1, 1024)
    a_p = cat("a_p")[None]
    a_s = np.concatenate([np.asarray(r["a_s"]) for r in R], 0)[None]
    b_p = cat("b_p")[None]
    b_s = np.concatenate([np.asarray(r["b_s"]) for r in R], 0)[None]
    swa_p = cat("swa_p").reshape(1, 8, 128, 2, 2, 64)
    swa_s = np.concatenate([np.asarray(r["swa_s"]) for r in R], 0).reshape(1, 128, 128, 2, 2, 64)
    outs = [y_prompt, y_sample, a_p, a_s, b_p, b_s, swa_p, swa_s]
    for g, Wg in enumerate((128, 512, 2048)):
        outs.append(cat("d%d_p" % g).reshape(1, 8, Wg, 2, 1, 64))
        outs.append(np.concatenate([np.asarray(r["d%d_s" % g]) for r in R], 0).reshape(1, 128, Wg, 2, 1, 64))
    return tuple(np.ascontiguousarray(o, dtype=np.float32) for o in outs)
```
